# Optimizing a Trainium2 kernel written in Bass

```python
import jax, jax.numpy as jnp
from jax import lax
import numpy as np

D_MODEL = 2048
BATCH = 2
SEQ = 8192
DEPTH = 2

GRID_W = 64
CTX_LEN = 256
ROPE_THETA = 10000.0
Q_BLOCK = 128
LN_EPS = 1e-6
RMS_EPS = 1e-6

A_HEADS = 8
A_Q_LORA = 768
A_KV_LORA = 512
A_NOPE = 128
A_ROPE = 64
A_V = 128
A_SCALE = (A_NOPE + A_ROPE) ** -0.5
B_HEADS = 8
B_KV_HEADS = 2
B_GROUP = B_HEADS // B_KV_HEADS
B_HEAD_DIM = 128
B_SCALE = B_HEAD_DIM ** -0.5
A_WIDTH = A_HEADS * A_V
B_WIDTH = B_HEADS * B_HEAD_DIM
ATTN_WIDTH = A_WIDTH + B_WIDTH
ATTN_IN = A_Q_LORA + A_KV_LORA + A_ROPE + B_WIDTH + 2 * B_KV_HEADS * B_HEAD_DIM + ATTN_WIDTH
C_EXPAND = 2
C_WIDTH = C_EXPAND * D_MODEL
C_GROUPS = 16
C_GROUP_DIM = C_WIDTH // C_GROUPS

N_EVEN = (DEPTH + 1) // 2
N_ODD = DEPTH // 2
DEEPNORM_ALPHA = (2.0 * DEPTH) ** 0.25
DEEPNORM_BETA = (8.0 * DEPTH) ** -0.25

kernel_name = "hybrid_mla_gqa_fnet_deepnorm_dit"


def layer_norm(x):
    xf = x.astype(jnp.float32)
    mu = jnp.mean(xf, axis=-1, keepdims=True)
    var = jnp.mean(jnp.square(xf - mu), axis=-1, keepdims=True)
    return ((xf - mu) * lax.rsqrt(var + LN_EPS)).astype(x.dtype)


def rms_norm(x, g):
    xf = x.astype(jnp.float32)
    y = xf * lax.rsqrt(jnp.mean(xf * xf, axis=-1, keepdims=True) + RMS_EPS)
    return (y * g.astype(jnp.float32)).astype(x.dtype)


def axial_rope_angles(rows, rot_dim):
    r, col = jnp.meshgrid(jnp.arange(rows), jnp.arange(GRID_W), indexing="ij")
    r = r.reshape(-1).astype(jnp.float32)
    col = col.reshape(-1).astype(jnp.float32)
    n_freq = rot_dim // 4
    inv = ROPE_THETA ** (-jnp.arange(n_freq, dtype=jnp.float32) / n_freq)
    return jnp.concatenate([r[:, None] * inv, col[:, None] * inv], axis=-1)


def apply_rope(x, ang):
    cos = jnp.cos(ang)[None, :, None, :]
    sin = jnp.sin(ang)[None, :, None, :]
    xf = x.astype(jnp.float32).reshape(*x.shape[:-1], -1, 2)
    x1, x2 = xf[..., 0], xf[..., 1]
    out = jnp.stack([x1 * cos - x2 * sin, x1 * sin + x2 * cos], axis=-1)
    return out.reshape(x.shape).astype(x.dtype)


def block_attention(q, k, v, scale):
    b, s, g, r, dk = q.shape
    nblk = s // Q_BLOCK
    qb = jnp.moveaxis(q.reshape(b, nblk, Q_BLOCK, g, r, dk), 1, 0)

    def one_block(qi):
        sc = jnp.einsum("bqgrd,bkgd->bgrqk", qi, k, preferred_element_type=jnp.float32) * scale
        p = jax.nn.softmax(sc, axis=-1).astype(v.dtype)
        return jnp.einsum("bgrqk,bkge->bqgre", p, v)

    out = lax.map(one_block, qb)
    return jnp.moveaxis(out, 0, 1).reshape(b, s, g, r, v.shape[-1])


def attn_project(h, w_in, wq_b, q_lora_g, kv_lora_g, wkv_b, qn_g, kn_g):
    b, l, _ = h.shape
    splits = np.cumsum([A_Q_LORA, A_KV_LORA, A_ROPE, B_WIDTH,
                        B_KV_HEADS * B_HEAD_DIM, B_KV_HEADS * B_HEAD_DIM]).tolist()
    z = h @ w_in
    c_q, c_kv, k_pe, q_b, k_b, v_b, gate = jnp.split(z, splits, axis=-1)
    q_a = (rms_norm(c_q, q_lora_g) @ wq_b).reshape(b, l, A_HEADS, A_NOPE + A_ROPE)
    kv_a = (rms_norm(c_kv, kv_lora_g) @ wkv_b).reshape(b, l, A_HEADS, A_NOPE + A_V)
    q_nope, q_pe = q_a[..., :A_NOPE], q_a[..., A_NOPE:]
    k_nope, v_a = kv_a[..., :A_NOPE], kv_a[..., A_NOPE:]
    k_pe = k_pe.reshape(b, l, 1, A_ROPE)
    q_b = rms_norm(q_b.reshape(b, l, B_HEADS, B_HEAD_DIM), qn_g)
    k_b = rms_norm(k_b.reshape(b, l, B_KV_HEADS, B_HEAD_DIM), kn_g)
    v_b = v_b.reshape(b, l, B_KV_HEADS, B_HEAD_DIM)
    return q_nope, q_pe, k_nope, k_pe, v_a, q_b, k_b, v_b, gate


def mla_qk(q_nope, q_pe, k_nope, k_pe):
    b, l, h, _ = q_nope.shape
    q = jnp.concatenate([q_nope, q_pe], axis=-1)[:, :, :, None, :]
    k = jnp.concatenate([k_nope, jnp.broadcast_to(k_pe, (b, l, h, A_ROPE))], axis=-1)
    return q, k


def gated_out(y_a, y_b, gate, w_out):
    b, l = y_a.shape[:2]
    y = jnp.concatenate([y_a.reshape(b, l, A_WIDTH), y_b.reshape(b, l, B_WIDTH)], axis=-1)
    return (y * jax.nn.silu(gate)) @ w_out


def attention_mixer(h_lat, h_ctx, ang_a, ang_b, w_in, wq_b, q_lora_g, kv_lora_g, wkv_b,
                    qn_g, kn_g, w_out, with_ctx_queries):
    b, s, _ = h_lat.shape
    lqn, lqp, lkn, lkp, lva, lqb, lkb, lvb, lgate = attn_project(
        h_lat, w_in, wq_b, q_lora_g, kv_lora_g, wkv_b, qn_g, kn_g)
    cqn, cqp, ckn, ckp, cva, cqb, ckb, cvb, cgate = attn_project(
        h_ctx, w_in, wq_b, q_lora_g, kv_lora_g, wkv_b, qn_g, kn_g)
    lqp, lkp = apply_rope(lqp, ang_a), apply_rope(lkp, ang_a)
    lqb, lkb = apply_rope(lqb, ang_b), apply_rope(lkb, ang_b)
    lq_a, lk_a = mla_qk(lqn, lqp, lkn, lkp)
    cq_a, ck_a = mla_qk(cqn, cqp, ckn, ckp)
    k_a_all = jnp.concatenate([ck_a, lk_a], axis=1)
    v_a_all = jnp.concatenate([cva, lva], axis=1)
    k_b_all = jnp.concatenate([ckb, lkb], axis=1)
    v_b_all = jnp.concatenate([cvb, lvb], axis=1)
    y_a = block_attention(lq_a, k_a_all, v_a_all, A_SCALE)
    y_b = block_attention(lqb.reshape(b, s, B_KV_HEADS, B_GROUP, B_HEAD_DIM), k_b_all, v_b_all, B_SCALE)
    out_lat = gated_out(y_a, y_b, lgate, w_out)
    out_ctx = None
    if with_ctx_queries:
        lc = h_ctx.shape[1]
        yc_a = block_attention(cq_a, ck_a, cva, A_SCALE)
        yc_b = block_attention(cqb.reshape(b, lc, B_KV_HEADS, B_GROUP, B_HEAD_DIM), ckb, cvb, B_SCALE)
        out_ctx = gated_out(yc_a, yc_b, cgate, w_out)
    return out_lat, out_ctx


def fourier_mixer(h, w_in, w_out):
    b, l, _ = h.shape
    u, gate = jnp.split(h @ w_in, 2, axis=-1)
    ug = u.reshape(b, l, C_GROUPS, C_GROUP_DIM).astype(jnp.float32)
    f = jnp.fft.fft2(ug, axes=(1, 3), norm="ortho").real.astype(h.dtype)
    y = f.reshape(b, l, C_WIDTH) * jax.nn.silu(gate)
    return y @ w_out


def modulation(cvec, ada_w, ada_b):
    m = jax.nn.silu(cvec) @ ada_w + ada_b
    return jnp.split(m, 3, axis=-1)


def modulate(x, shift, scale):
    return layer_norm(x) * (1.0 + scale) + shift


def post_norm(x, out, gate, g, bias):
    return layer_norm(DEEPNORM_ALPHA * x + gate * out) * g + bias


def setup_inputs(seed: int = 0) -> dict:
    key = jax.random.key(seed)
    ks = jax.random.split(key, 24)

    def nrm(k, shape, s):
        return jax.random.normal(k, shape, jnp.float32) * s

    def gain(k, shape):
        return 1.0 + 0.02 * jax.random.normal(k, shape, jnp.float32)

    return {
        "x": nrm(ks[0], (BATCH, SEQ, D_MODEL), 1.0),
        "c": nrm(ks[1], (BATCH, D_MODEL), 1.0),
        "ctx": nrm(ks[2], (BATCH, CTX_LEN, D_MODEL), 1.0),
        "c_ctx": nrm(ks[3], (D_MODEL,), 1.0),
        "ada_w": nrm(ks[4], (DEPTH, D_MODEL, 3 * D_MODEL), 0.5 * D_MODEL ** -0.5),
        "ada_b": nrm(ks[5], (DEPTH, 3 * D_MODEL), 0.01),
        "ln_g": gain(ks[6], (DEPTH, D_MODEL)),
        "ln_b": nrm(ks[7], (DEPTH, D_MODEL), 0.01),
        "w_in_attn": nrm(ks[8], (N_EVEN, D_MODEL, ATTN_IN), D_MODEL ** -0.5),
        "wq_b": nrm(ks[9], (N_EVEN, A_Q_LORA, A_HEADS * (A_NOPE + A_ROPE)), A_Q_LORA ** -0.5),
        "q_lora_norm": gain(ks[10], (N_EVEN, A_Q_LORA)),
        "kv_lora_norm": gain(ks[11], (N_EVEN, A_KV_LORA)),
        "wkv_b": nrm(ks[12], (N_EVEN, A_KV_LORA, A_HEADS * (A_NOPE + A_V)), A_KV_LORA ** -0.5),
        "q_norm_b": gain(ks[13], (N_EVEN, B_HEAD_DIM)),
        "k_norm_b": gain(ks[14], (N_EVEN, B_HEAD_DIM)),
        "w_out_attn": nrm(ks[15], (N_EVEN, ATTN_WIDTH, D_MODEL), DEEPNORM_BETA * ATTN_WIDTH ** -0.5),
        "w_in_fourier": nrm(ks[16], (N_ODD, D_MODEL, 2 * C_WIDTH), D_MODEL ** -0.5),
        "w_out_fourier": nrm(ks[17], (N_ODD, C_WIDTH, D_MODEL), DEEPNORM_BETA * C_WIDTH ** -0.5),
    }


def reference(x, c, ctx, c_ctx, ada_w, ada_b, ln_g, ln_b, w_in_attn, wq_b, q_lora_norm,
              kv_lora_norm, wkv_b, q_norm_b, k_norm_b, w_out_attn, w_in_fourier, w_out_fourier):
    rows = x.shape[1] // GRID_W
    ang_a = axial_rope_angles(rows, A_ROPE)
    ang_b = axial_rope_angles(rows, B_HEAD_DIM)
    xc = ctx
    for i in range(DEPTH):
        j = i // 2
        ctx_needed = i < DEPTH - 1
        sh_l, sc_l, g_l = [m[:, None, :] for m in modulation(c, ada_w[i], ada_b[i])]
        h_lat = modulate(x, sh_l, sc_l)
        o_ctx = None
        g_c = None
        if i % 2 == 0:
            sh_c, sc_c, g_c = modulation(c_ctx, ada_w[i], ada_b[i])
            h_ctx = modulate(xc, sh_c, sc_c)
            o_lat, o_ctx = attention_mixer(
                h_lat, h_ctx, ang_a, ang_b, w_in_attn[j], wq_b[j], q_lora_norm[j],
                kv_lora_norm[j], wkv_b[j], q_norm_b[j], k_norm_b[j], w_out_attn[j], ctx_needed)
        else:
            o_lat = fourier_mixer(h_lat, w_in_fourier[j], w_out_fourier[j])
            if ctx_needed:
                sh_c, sc_c, g_c = modulation(c_ctx, ada_w[i], ada_b[i])
                o_ctx = fourier_mixer(modulate(xc, sh_c, sc_c), w_in_fourier[j], w_out_fourier[j])
        x = post_norm(x, o_lat, g_l, ln_g[i], ln_b[i])
        if ctx_needed:
            xc = post_norm(xc, o_ctx, g_c, ln_g[i], ln_b[i])
    return x
```

```python
import numpy as np
from contextlib import ExitStack
import concourse.bass as bass
import concourse.mybir as mybir
from concourse.bass_utils import run_bass_kernel_spmd

F32 = mybir.dt.float32
BF16 = mybir.dt.bfloat16
ALU = mybir.AluOpType
AF = mybir.ActivationFunctionType
AX = mybir.AxisListType

SEM_LIM = 30000


class Buf:
    def __init__(self, name, t=None):
        self.name = name
        self.t = t
        self.w = {}
        self.r = {}
        self.dcnt = 0
        self.dsem = None
        self.is_dram = False

    def __getitem__(self, k):
        return self.t[k]


class Prog:
    ENGS = ("pe", "act", "dve", "pool", "sp")

    def __init__(self, nc, es):
        self.nc = nc
        self.es = es
        self.ops = {e: [] for e in self.ENGS}
        self.seen = {e: {} for e in self.ENGS}
        self.signal = {e: set() for e in self.ENGS}
        self.dbufs = []
        self.nbuf = 0

    def sb(self, name, shape, dt):
        t = self.es.enter_context(self.nc.sbuf_tensor(name, list(shape), dt))
        return Buf(name, t)

    def ps(self, name):
        t = self.es.enter_context(self.nc.psum_tensor(name, [128, 512], F32))
        return Buf(name, t)

    def dram(self, name, shape, dt, kind="Internal"):
        t = self.nc.dram_tensor(name, list(shape), dt, kind=kind)
        b = Buf(name, t.ap())
        b.is_dram = True
        return b

    def view(self, name, ap):
        b = Buf(name, ap)
        b.is_dram = True
        return b

    def op(self, eng, fn, reads=(), writes=(), dma_dst=None, acc=False, dma_inc=16):
        if dma_dst is not None:
            own = ("D", id(dma_dst))
        else:
            own = ("E", eng)
        need = {}

        def merge(d, skip_own=False):
            for k, v in d.items():
                if skip_own and k == own:
                    continue
                if need.get(k, -1) < v:
                    need[k] = v

        for b in reads:
            merge(b.w)
        for b in writes:
            merge(b.w, skip_own=acc)
            merge(b.r)
        waits = []
        seen = self.seen[eng]
        for k, v in need.items():
            if k == ("E", "pe") and eng == "pe":
                continue
            if seen.get(k, -1) >= v:
                continue
            seen[k] = v
            waits.append((k, v))
            if k[0] == "E":
                self.signal[k[1]].add(v)
        idx = len(self.ops[eng])
        if dma_dst is not None:
            if dma_dst.dsem is None:
                dma_dst.dsem = True
                self.dbufs.append(dma_dst)
            dma_dst.dcnt += dma_inc
            assert dma_dst.dcnt < 2 * SEM_LIM, dma_dst.name
            tok = (own, dma_dst.dcnt)
        else:
            tok = (own, idx)
        self.ops[eng].append((fn, waits, dma_dst, idx, dma_inc))
        for b in reads:
            if b.r.get(tok[0], -1) < tok[1]:
                b.r[tok[0]] = tok[1]
        for b in writes:
            if acc:
                b.w[tok[0]] = tok[1]
            else:
                b.w = {tok[0]: tok[1]}
            b.r = {}
        return tok

    def fence(self, eng, bufs):
        self.op(eng, None, reads=bufs)

    def mm(self, out, lhsT, rhs, start, stop, reads, w, **kw):
        self.op("pe", lambda e: e.matmul(out, lhsT, rhs, start=start, stop=stop, **kw),
                reads=reads, writes=[w], acc=True)

    def dma(self, eng, out, in_, src=None, dst=None, dbuf=None, **kw):
        if dst is None:
            dst = dbuf
        reads = [src] if src is not None else []
        writes = [dst] if dst is not None else []
        if src is not None and not src.is_dram and (dst is None or dst.is_dram):
            owner = src
        else:
            owner = dst
        self.op(eng, lambda e: e.dma_start(out=out, in_=in_, **kw), reads=reads, writes=writes,
                dma_dst=owner, acc=True)

    def emit(self):
        nc = self.nc
        es = self.es
        rank = {}
        esems = {}
        for e in self.ENGS:
            sig = sorted(self.signal[e])
            rank[e] = {idx: i + 1 for i, idx in enumerate(sig)}
            n = (len(sig) + SEM_LIM - 1) // SEM_LIM
            esems[e] = [es.enter_context(nc.semaphore(f"s_{e}{i}")) for i in range(max(n, 1))]
        for i, b in enumerate(self.dbufs):
            b.dsem = es.enter_context(nc.semaphore(f"d_{i}"))
        dmap = {id(b): b for b in self.dbufs}
        self.nsem = sum(len(v) for v in esems.values()) + len(self.dbufs)

        def sem_val(k, v):
            if k[0] == "E":
                r = rank[k[1]][v] - 1
                return esems[k[1]][r // SEM_LIM], r % SEM_LIM + 1
            return dmap[k[1]].dsem, v

        def run(e, name):
            for fn, waits, dma_dst, idx, dma_inc in self.ops[name]:
                for k, v in waits:
                    s, val = sem_val(k, v)
                    e.wait_ge(s, val)
                if fn is None:
                    continue
                ins = fn(e)
                if dma_dst is not None:
                    ins.then_inc(dma_dst.dsem, dma_inc)
                elif idx in rank[name]:
                    s, _ = sem_val(("E", name), idx)
                    ins.then_inc(s, 1)

        block = es.enter_context(nc.Block())

        @block.sync
        def _(e):
            run(e, "sp")

        @block.tensor
        def _(e):
            run(e, "pe")

        @block.scalar
        def _(e):
            run(e, "act")

        @block.vector
        def _(e):
            run(e, "dve")

        @block.gpsimd
        def _(e):
            run(e, "pool")


class Arena:
    def __init__(self, P, nbytes):
        self.P = P
        self.n = nbytes // 2
        self.t = P.es.enter_context(P.nc.sbuf_tensor("arena", [128, self.n], BF16))
        self.off = 0
        self.cur = []
        self.prev = {}

    def reset(self):
        for b in self.cur:
            for d in (b.w, b.r):
                for k, v in d.items():
                    if self.prev.get(k, -1) < v:
                        self.prev[k] = v
        self.cur = []
        self.off = 0

    def mark(self):
        return (self.off, len(self.cur))

    def release(self, m):
        for b in self.cur[m[1]:]:
            for d in (b.w, b.r):
                for k, v in d.items():
                    if self.prev.get(k, -1) < v:
                        self.prev[k] = v
        self.cur = self.cur[:m[1]]
        self.off = m[0]

    def alloc(self, name, shape, dt, parts=128):
        esz = 4 if dt == F32 else 2
        n = int(np.prod(shape))
        units = (n * esz + 1) // 2
        units = (units + 15) // 16 * 16
        assert self.off + units <= self.n, (name, self.off * 2, units * 2)
        ap = self.t[0:parts, self.off:self.off + units]
        self.off += units
        if dt == F32:
            ap = ap.bitcast(F32)
        ap = ap[:, 0:n]
        if len(shape) == 2:
            ap = ap.rearrange("p (a b) -> p a b", b=shape[1])
        elif len(shape) == 3:
            ap = ap.rearrange("p (a b c) -> p a b c", b=shape[1], c=shape[2])
        b = Buf(name, ap)
        b.r = dict(self.prev)
        self.cur.append(b)
        return b


ALPHA = (2.0 * 2) ** 0.25
A_SCALE = 192.0 ** -0.5
B_SCALE = 128.0 ** -0.5


class Ctx:
    pass


def setup_common(nc, es, consts_d, arena_kb=170):
    C = Ctx()
    P = Prog(nc, es)
    C.P = P
    C.nc = nc
    C.cst = P.sb("cst", [128, 3, 128], F32)
    P.dma("sp", C.cst.t[:], consts_d, dst=C.cst)
    C.idf = C.cst.t[:, 0, :]
    C.onesf = C.cst.t[:, 1, :]
    C.perm = C.cst.t[:, 2, :]
    C.cb = P.sb("cb", [128, 2, 128], BF16)
    P.op("dve", lambda e: e.tensor_copy(C.cb.t[:], C.cst.t[:, 0:2, :]), reads=[C.cst], writes=[C.cb])
    C.idb = C.cb.t[:, 0, :]
    C.onesb = C.cb.t[:, 1, :]
    C.eps = P.sb("eps", [128, 1], F32)
    P.op("pool", lambda e: e.memset(C.eps.t[:], 1e-6), writes=[C.eps])
    C.modT = P.sb("modT", [128, 32, 2], F32)
    C.stg = [P.sb(f"stg{i}", [128, 16 * 256], F32) for i in range(2)]
    C.nstg = 0
    C.psb = [P.ps(f"ps{i}") for i in range(8)]
    C.nps = 0
    C.ar = Arena(P, arena_kb * 1024)
    return C


def nextps(C, lo=0, hi=8):
    b = C.psb[lo + C.nps % (hi - lo)]
    C.nps += 1
    return b


def emit_modulation(C, cvec_d, ada_w_d, ada_b_d, gbc, want_gate_row=0):
    P, ar = C.P, C.ar
    cv = ar.alloc("cv", [32], F32)
    sv = ar.alloc("sv", [16, 2], F32)
    adb = ar.alloc("adb", [6144], F32, parts=2)
    m2 = ar.alloc("m2", [6144], F32, parts=2)
    P.dma("sp", cv.t, cvec_d, dst=cv)
    P.dma("sp", adb.t, ada_b_d, dst=adb)
    P.op("act", lambda e: e.activation(sv.t.rearrange("p a b -> p (a b)"), cv.t, AF.Silu), reads=[cv], writes=[sv])
    awv = ada_w_d.rearrange("(kc p) n -> p kc n", p=128)
    for nb in range(24):
        stg = C.stg[C.nstg % 2]
        C.nstg += 1
        sv3 = stg.t.rearrange("p (kc n) -> p kc n", n=256)
        P.dma("sp", sv3, awv[:, :, nb * 256:(nb + 1) * 256], dst=stg)
        pb = nextps(C)
        for kc in range(16):
            P.mm(pb.t[0:2, 0:256], sv.t[:, kc, :], sv3[:, kc, :], kc == 0, kc == 15, [sv, stg], pb)
        sl = slice(nb * 256, (nb + 1) * 256)
        P.op("dve", lambda e, pb=pb, sl=sl: e.tensor_tensor(m2.t[:, sl], pb.t[0:2, 0:256], adb.t[:, sl], ALU.add),
             reads=[pb, adb], writes=[m2])
    pT = nextps(C)
    for j in range(32):
        P.mm(pT.t[:, 2 * j:2 * j + 2], m2.t[0:2, j * 128:(j + 1) * 128], C.idf[0:2, 0:2], True, True, [m2, C.cst], pT)
    P.op("dve", lambda e: e.tensor_copy(C.modT.t[:, 0:16, :], pT.t[:, 0:32].rearrange("p (a b) -> p a b", b=2)),
         reads=[pT], writes=[C.modT])
    P.op("dve", lambda e: e.tensor_scalar_add(C.modT.t[:, 16:32, :], pT.t[:, 32:64].rearrange("p (a b) -> p a b", b=2), 1.0),
         reads=[pT], writes=[C.modT], acc=True)
    r = want_gate_row
    for q in range(4):
        pb = nextps(C)
        P.mm(pb.t[:, :], C.onesf[r:r + 1, :], m2.t[r:r + 1, 4096 + q * 512:4096 + (q + 1) * 512], True, True, [m2, C.cst], pb)
        P.op("act", lambda e, pb=pb, q=q: e.activation(gbc.t[:, q * 512:(q + 1) * 512], pb.t[:, :], AF.Copy),
             reads=[pb], writes=[gbc], acc=True)


def prep_panel(C, Wd, KC, c0, ncols, r, Wbuf, Wap, bbuf=None, bap=None):
    P = C.P
    stg = C.stg[C.nstg % 2]
    C.nstg += 1
    s3 = stg.t[:, 0:KC * ncols].rearrange("p (kc n) -> p kc n", n=ncols)
    P.dma("sp", s3, Wd.rearrange("(kc p) n -> p kc n", p=128)[:, :, c0:c0 + ncols], dst=stg)
    if r is None:
        P.op("dve", lambda e: e.tensor_copy(Wap, s3), reads=[stg], writes=[Wbuf], acc=True)
        return
    for kc in range(KC):
        if kc % 2 == 0:
            P.op("dve", lambda e, kc=kc: e.tensor_scalar_mul(Wap[:, kc, :], s3[:, kc, :], C.modT.t[:, 16 + kc, r:r + 1]),
                 reads=[stg, C.modT], writes=[Wbuf], acc=True)
        else:
            P.op("act", lambda e, kc=kc: e.activation(Wap[:, kc, :], s3[:, kc, :], AF.Copy, scale=C.modT.t[:, 16 + kc, r:r + 1]),
                 reads=[stg, C.modT], writes=[Wbuf], acc=True)
    pb = nextps(C)
    nch = (ncols + 127) // 128
    for j in range(nch):
        M = min(128, ncols - j * 128)
        for kc in range(KC):
            P.mm(pb.t[0:M, j:j + 1], s3[:, kc, j * 128:j * 128 + M], C.modT.t[:, kc, r:r + 1], kc == 0, kc == KC - 1,
                 [stg, C.modT], pb)
    nfull = ncols // 128
    if nfull:
        P.op("dve", lambda e: e.tensor_copy(bap[:, 0:nfull], pb.t[:, 0:nfull]), reads=[pb], writes=[bbuf], acc=True)
    if nch > nfull:
        Ml = ncols - nfull * 128
        P.op("dve", lambda e: e.tensor_copy(bap[0:Ml, nfull:nch], pb.t[0:Ml, nfull:nch]), reads=[pb], writes=[bbuf], acc=True)


def ln_tile(C, W, xrows_d, t):
    P = C.P
    xt = W.xt[t % 2]
    st = W.st[t % 2]
    mv = W.mv[t % 2]
    rs = W.rs[t % 2]
    xh = W.xh[t % 2]
    P.dma("sp", xt.t, xrows_d, src=getattr(W, "xsrc", None), dst=xt)
    for c in range(4):
        P.op("dve", lambda e, c=c: e.bn_stats(st.t[:, c, :], xt.t[:, c * 512:(c + 1) * 512]), reads=[xt], writes=[st], acc=True)
    P.op("dve", lambda e: e.bn_aggr(mv.t, st.t), reads=[st], writes=[mv])
    P.op("act", lambda e: e.activation(rs.t, mv.t[:, 1:2], AF.Sqrt, bias=C.eps.t[:, 0:1], scale=1.0), reads=[mv, C.eps], writes=[rs])
    P.op("dve", lambda e: e.reciprocal(rs.t, rs.t), reads=[rs], writes=[rs])
    P.op("dve", lambda e: e.tensor_scalar(xh.t, xt.t, mv.t[:, 0:1], rs.t[:, 0:1], ALU.subtract, ALU.mult),
         reads=[xt, mv, rs], writes=[xh])
    return xh


def ln_alloc(C, W):
    ar = C.ar
    W.xt = [ar.alloc(f"xt{i}", [2048], F32) for i in range(2)]
    W.st = [ar.alloc(f"st{i}", [4, 6], F32) for i in range(2)]
    W.mv = [ar.alloc(f"mv{i}", [2], F32) for i in range(2)]
    W.rs = [ar.alloc(f"rs{i}", [1], F32) for i in range(2)]
    W.xh = [ar.alloc(f"xh{i}", [2048], BF16) for i in range(2)]


def ln_transpose(C, W, x_d, row0, ntiles, hT, col0=0):
    P = C.P
    for t in range(ntiles):
        xh = ln_tile(C, W, x_d[row0 + t * 128: row0 + (t + 1) * 128, :], W.lnc)
        W.lnc += 1
        for half in range(2):
            pb = nextps(C)
            pv = pb.t[:].bitcast(BF16)
            for j in range(8):
                kc = half * 8 + j
                P.op("pe", lambda e, pv=pv, j=j, kc=kc, xh=xh: e.transpose(pv[:, j * 128:(j + 1) * 128], xh.t[:, kc * 128:(kc + 1) * 128], C.idb),
                     reads=[xh, C.cb], writes=[pb], acc=True)
            dst = hT.t[:, half * 8:(half + 1) * 8, col0 + t * 128: col0 + (t + 1) * 128]
            src = pv.rearrange("p (a b) -> p a b", b=128)
            if half == 0:
                P.op("act", lambda e, dst=dst, src=src: e.activation(dst, src, AF.Copy), reads=[pb], writes=[hT], acc=True)
            else:
                P.op("dve", lambda e, dst=dst, src=src: e.tensor_copy(dst, src), reads=[pb], writes=[hT], acc=True)


def proj(C, pb, M, nt, Wbuf, Wap, c0, KC, hT, tok0):
    for kc in range(KC):
        C.P.mm(pb.t[0:M, 0:nt], Wap[:, kc, c0:c0 + M], hT.t[:, kc, tok0:tok0 + nt], kc == 0, kc == KC - 1, [Wbuf, hT], pb)


def rstd_from(C, W, zs, nfeat, nt):
    P = C.P
    pss = nextps(C)
    for i, (zb, zap) in enumerate(zs):
        sq = W.sq[W.nsq % 2]
        W.nsq += 1
        P.op("act", lambda e, sq=sq, zap=zap: e.activation(sq.t[:, 0:nt], zap, AF.Square), reads=[zb], writes=[sq])
        P.mm(pss.t[:, 0:nt], C.onesf, sq.t[:, 0:nt], i == 0, i == len(zs) - 1, [sq, C.cst], pss)
    rstd = W.rstd[W.nrs % 2]
    W.nrs += 1
    P.op("act", lambda e: e.activation(rstd.t[:, 0:nt], pss.t[:, 0:nt], AF.Sqrt, bias=C.eps.t[:, 0:1], scale=1.0 / nfeat),
         reads=[pss, C.eps], writes=[rstd])
    P.op("dve", lambda e: e.reciprocal(rstd.t[:, 0:nt], rstd.t[:, 0:nt]), reads=[rstd], writes=[rstd])
    return rstd


def rope(C, W, dst_buf, dst_ap, src_buf, src_ap, tab, M, nt):
    P = C.P
    pw = nextps(C)
    P.mm(pw.t[0:M, 0:nt], C.perm[0:M, 0:M], src_ap, True, True, [src_buf, C.cst], pw)
    t1 = W.t1[W.nt1 % 2]
    t2 = W.t2[W.nt1 % 2]
    W.nt1 += 1
    P.op("pool", lambda e: e.tensor_tensor(t1.t[0:M, 0:nt], src_ap, tab.t[0:M, 0, 0:nt], ALU.mult), reads=[src_buf, tab], writes=[t1])
    P.op("dve", lambda e: e.tensor_tensor(t2.t[0:M, 0:nt], pw.t[0:M, 0:nt], tab.t[0:M, 1, 0:nt], ALU.mult), reads=[pw, tab], writes=[t2])
    P.op("pool", lambda e: e.tensor_tensor(dst_ap, t1.t[0:M, 0:nt], t2.t[0:M, 0:nt], ALU.add), reads=[t1, t2], writes=[dst_buf])


def work_alloc(C, W):
    ar = C.ar
    W.sq = [ar.alloc(f"sq{i}", [512], F32) for i in range(2)]
    W.rstd = [ar.alloc(f"rstd{i}", [512], F32) for i in range(2)]
    W.t1 = [ar.alloc(f"t1{i}", [512], F32) for i in range(2)]
    W.t2 = [ar.alloc(f"t2{i}", [512], F32) for i in range(2)]
    W.nsq = W.nrs = W.nt1 = 0


NKV = 8448
NKT = 66


def build_stage_a(fused=False):
    nc = bass.Bass("TRN2", target_bir_lowering=False)

    def din(name, shape, dt=F32):
        return nc.dram_tensor(name, list(shape), dt, kind="ExternalInput").ap()

    xkv = din("xkv", [NKV, 2048])
    xq = din("xq", [2048, 2048])
    cvec = din("cvec", [128, 32])
    ada_w = din("ada_w", [2048, 6144])
    ada_b = din("ada_b", [2, 6144])
    w_in = din("w_in", [2048, 4928])
    wq_b = din("wq_b", [768, 1536])
    wkv_b = din("wkv_b", [512, 2048])
    w_out = din("w_out", [2048, 2048])
    gains = din("gains", [128, 12])
    lngb = din("lngb", [2, 2048])
    consts = din("consts", [128, 3, 128])
    ropeB_kv = din("ropeB_kv", [128, 2, NKV])
    ropeA_kv = din("ropeA_kv", [64, 2, NKV])
    ropeB_q = din("ropeB_q", [128, 2, 2048])
    ropeA_q = din("ropeA_q", [64, 2, 2048])
    x1 = None if fused else nc.dram_tensor("x1", [2048, 2048], F32, kind="ExternalOutput").ap()

    with ExitStack() as es:
        C = setup_common(nc, es, consts)
        P, ar = C.P, C.ar
        gn = P.sb("gn", [128, 12], F32)
        P.dma("sp", gn.t[:], gains, dst=gn)
        KaT = P.dram("KaT", [8, 128, NKV], BF16)
        KpeT = P.dram("KpeT", [64, NKV], BF16)
        Va = P.dram("Va", [NKT, 128, 1024], BF16)
        KbT = P.dram("KbT", [2, 128, NKV], BF16)
        Vb = P.dram("Vb", [NKT, 128, 256], BF16)
        QaT = P.dram("QaT", [8, 128, 2048], BF16)
        QpeT = P.dram("QpeT", [8, 64, 2048], BF16)
        QbT = P.dram("QbT", [8, 128, 2048], BF16)
        Gd = P.dram("Gd", [16, 128, 2048], BF16)
        Yd = P.dram("Yd", [16, 128, 2048], BF16)

        gbc = ar.alloc("gbc_tmp", [2048], F32)
        emit_modulation(C, cvec, ada_w, ada_b[:, :], gbc, 0)
        Gbc_d = P.dram("Gbc_d", [128, 2048], F32)
        P.dma("sp", Gbc_d.t, gbc.t, src=gbc, dst=Gbc_d)

        def kv_phase(r, blocks):
            ar.reset()
            W = Ctx()
            Wkv = ar.alloc("Wkv", [16, 1088], BF16)
            Bkv = ar.alloc("Bkv", [9], F32)
            wkvb = ar.alloc("wkvb", [4, 2048], BF16)
            ln_alloc(C, W)
            W.lnc = 0
            work_alloc(C, W)
            hTs = [ar.alloc(f"hT{i}", [16, 512], BF16) for i in range(1)]
            zc = ar.alloc("zc", [4, 512], F32)
            ckvn = ar.alloc("ckvn", [4, 512], BF16)
            zk = [ar.alloc(f"zk{i}", [512], F32) for i in range(2)]
            kn = [ar.alloc(f"kn{i}", [512], F32) for i in range(2)]
            tabB = [ar.alloc(f"tabB{i}", [2, 512], F32) for i in range(1)]
            tabA = [ar.alloc(f"tabA{i}", [2, 512], F32) for i in range(1)]
            ko = [ar.alloc(f"ko{i}", [512], BF16) for i in range(4)]
            vT = [ar.alloc(f"vT{i}", [512], BF16) for i in range(2)]
            vo = [ar.alloc(f"vo{i}", [1024], BF16) for i in range(2)]
            vbo = [ar.alloc(f"vbo{i}", [256], BF16) for i in range(2)]
            panels = [(768, 256, 0, 0), (1024, 256, 256, 2), (1280, 64, 512, 4), (2368, 256, 576, 5), (2624, 256, 832, 7)]
            for (c0, ncols, l0, b0) in panels:
                nch = (ncols + 127) // 128
                prep_panel(C, w_in, 16, c0, ncols, r, Wkv, Wkv.t[:, :, l0:l0 + ncols], Bkv, Bkv.t[:, b0:b0 + nch])
            for j in range(8):
                prep_panel(C, wkv_b, 4, j * 256, 256, None, wkvb, wkvb.t[:, :, j * 256:(j + 1) * 256])
            nko = 0
            for bi, (row0, nt) in enumerate(blocks):
                ntl = nt // 128
                hT = hTs[0]
                ln_transpose(C, W, xkv, row0, ntl, hT)
                tb = tabB[0]
                ta = tabA[0]
                P.dma("sp", tb.t[:, :, 0:nt], ropeB_kv[:, :, row0:row0 + nt], dst=tb)
                P.dma("sp", ta.t[0:64, :, 0:nt], ropeA_kv[:, :, row0:row0 + nt], dst=ta)
                for j in range(4):
                    pb = nextps(C)
                    proj(C, pb, 128, nt, Wkv, Wkv.t, j * 128, 16, hT, 0)
                    P.op("act", lambda e, pb=pb, j=j: e.activation(zc.t[:, j, 0:nt], pb.t[:, 0:nt], AF.Identity, bias=Bkv.t[:, j:j + 1]),
                         reads=[pb, Bkv], writes=[zc], acc=True)
                rstd = rstd_from(C, W, [(zc, zc.t[:, j, 0:nt]) for j in range(4)], 512, nt)
                for j in range(4):
                    P.op("dve", lambda e, j=j, rstd=rstd: e.scalar_tensor_tensor(ckvn.t[:, j, 0:nt], zc.t[:, j, 0:nt], gn.t[:, 6 + j:7 + j], rstd.t[:, 0:nt], ALU.mult, ALU.mult),
                         reads=[zc, gn, rstd], writes=[ckvn], acc=True)
                pb = nextps(C)
                proj(C, pb, 64, nt, Wkv, Wkv.t, 512, 16, hT, 0)
                z = zk[0]
                P.op("act", lambda e, pb=pb, z=z: e.activation(z.t[0:64, 0:nt], pb.t[0:64, 0:nt], AF.Identity, bias=Bkv.t[0:64, 4:5]),
                     reads=[pb, Bkv], writes=[z])
                o = ko[nko % 4]; nko += 1
                rope(C, W, o, o.t[0:64, 0:nt], z, z.t[0:64, 0:nt], ta, 64, nt)
                P.dma("sp", KpeT.t[:, row0:row0 + nt], o.t[0:64, 0:nt], src=o, dst=KpeT)
                for hh in range(2):
                    pb = nextps(C)
                    proj(C, pb, 128, nt, Wkv, Wkv.t, 576 + hh * 128, 16, hT, 0)
                    z = zk[1 - hh % 2] if False else zk[hh % 2]
                    P.op("act", lambda e, pb=pb, z=z, hh=hh: e.activation(z.t[:, 0:nt], pb.t[:, 0:nt], AF.Identity, bias=Bkv.t[:, 5 + hh:6 + hh]),
                         reads=[pb, Bkv], writes=[z])
                    rstd = rstd_from(C, W, [(z, z.t[:, 0:nt])], 128, nt)
                    k_ = kn[hh % 2]
                    P.op("dve", lambda e, z=z, k_=k_, rstd=rstd: e.scalar_tensor_tensor(k_.t[:, 0:nt], z.t[:, 0:nt], gn.t[:, 11:12], rstd.t[:, 0:nt], ALU.mult, ALU.mult),
                         reads=[z, gn, rstd], writes=[k_])
                    o = ko[nko % 4]; nko += 1
                    rope(C, W, o, o.t[:, 0:nt], k_, k_.t[:, 0:nt], tb, 128, nt)
                    P.dma("sp", KbT.t[hh, :, row0:row0 + nt], o.t[:, 0:nt], src=o, dst=KbT)
                for hh in range(2):
                    pb = nextps(C)
                    proj(C, pb, 128, nt, Wkv, Wkv.t, 832 + hh * 128, 16, hT, 0)
                    v_ = vT[hh % 2]
                    P.op("act", lambda e, pb=pb, v_=v_, hh=hh: e.activation(v_.t[:, 0:nt], pb.t[:, 0:nt], AF.Identity, bias=Bkv.t[:, 7 + hh:8 + hh]),
                         reads=[pb, Bkv], writes=[v_])
                    W.vbT = getattr(W, "vbT", {})
                    W.vbT[hh] = v_
                for t in range(ntl):
                    pb = nextps(C)
                    pv = pb.t[:].bitcast(BF16)
                    for hh in range(2):
                        v_ = W.vbT[hh]
                        P.op("pe", lambda e, pv=pv, hh=hh, v_=v_, t=t: e.transpose(pv[:, hh * 128:(hh + 1) * 128], v_.t[:, t * 128:(t + 1) * 128], C.idb),
                             reads=[v_, C.cb], writes=[pb], acc=True)
                    vb_ = vbo[t % 2]
                    P.op("dve", lambda e, pv=pv, vb_=vb_: e.tensor_copy(vb_.t, pv[:, 0:256]), reads=[pb], writes=[vb_])
                    P.dma("sp", Vb.t[row0 // 128 + t, :, :], vb_.t, src=vb_, dst=Vb)
                for h in range(8):
                    pb = nextps(C)
                    for j in range(4):
                        P.mm(pb.t[:, 0:nt], wkvb.t[:, j, h * 256:h * 256 + 128], ckvn.t[:, j, 0:nt], j == 0, j == 3, [wkvb, ckvn], pb)
                    o = ko[nko % 4]; nko += 1
                    if h % 2 == 0:
                        P.op("act", lambda e, pb=pb, o=o: e.activation(o.t[:, 0:nt], pb.t[:, 0:nt], AF.Copy), reads=[pb], writes=[o])
                    else:
                        P.op("dve", lambda e, pb=pb, o=o: e.tensor_copy(o.t[:, 0:nt], pb.t[:, 0:nt]), reads=[pb], writes=[o])
                    P.dma("sp", KaT.t[h, :, row0:row0 + nt], o.t[:, 0:nt], src=o, dst=KaT)
                wv = wkvb.t.rearrange("p k (h two d) -> p k h two d", two=2, d=128)
                for t in range(ntl):
                    v2 = vo[t % 2]
                    for half in range(2):
                        pb = nextps(C)
                        for j in range(4):
                            P.mm(pb.t[:, :].rearrange("p (h d) -> p h d", d=128), ckvn.t[:, j, t * 128:(t + 1) * 128],
                                 wv[:, j, half * 4:(half + 1) * 4, 1, :], j == 0, j == 3, [wkvb, ckvn], pb)
                        if half == 0:
                            P.op("act", lambda e, pb=pb, v2=v2: e.activation(v2.t[:, 0:512], pb.t[:, :], AF.Copy), reads=[pb], writes=[v2], acc=True)
                        else:
                            P.op("dve", lambda e, pb=pb, v2=v2: e.tensor_copy(v2.t[:, 512:1024], pb.t[:, :]), reads=[pb], writes=[v2], acc=True)
                    P.dma("sp", Va.t[row0 // 128 + t, :, :], v2.t, src=v2, dst=Va)

        kv_phase(1, [(0, 256)])
        kv_phase(0, [(256 + i * 512, 512) for i in range(16)])

        ar.reset()
        W = Ctx()
        hTq = ar.alloc("hTq", [16, 2048], BF16)
        mk = ar.mark()
        ln_alloc(C, W)
        W.lnc = 0
        ln_transpose(C, W, xq, 0, 16, hTq)
        ar.release(mk)
        mk = ar.mark()
        Wcq = ar.alloc("Wcq", [16, 768], BF16)
        Bcq = ar.alloc("Bcq", [6], F32)
        wqb = ar.alloc("wqb", [6, 1536], BF16)
        work_alloc(C, W)
        zq = ar.alloc("zq", [6, 512], F32)
        cqn = ar.alloc("cqn", [6, 512], BF16)
        zk = [ar.alloc(f"qzk{i}", [512], F32) for i in range(2)]
        tabA = [ar.alloc(f"qtabA{i}", [2, 512], F32) for i in range(1)]
        ko = [ar.alloc(f"qko{i}", [512], BF16) for i in range(4)]
        for j in range(3):
            prep_panel(C, w_in, 16, j * 256, 256, 0, Wcq, Wcq.t[:, :, j * 256:(j + 1) * 256], Bcq, Bcq.t[:, 2 * j:2 * j + 2])
        for j in range(6):
            prep_panel(C, wq_b, 6, j * 256, 256, None, wqb, wqb.t[:, :, j * 256:(j + 1) * 256])
        nko = 0
        for tbi in range(4):
            tok0 = tbi * 512
            ta = tabA[0]
            P.dma("sp", ta.t[0:64, :, :], ropeA_q[:, :, tok0:tok0 + 512], dst=ta)
            for j in range(6):
                pb = nextps(C)
                proj(C, pb, 128, 512, Wcq, Wcq.t, j * 128, 16, hTq, tok0)
                P.op("act", lambda e, pb=pb, j=j: e.activation(zq.t[:, j, :], pb.t[:, :], AF.Identity, bias=Bcq.t[:, j:j + 1]),
                     reads=[pb, Bcq], writes=[zq], acc=True)
            rstd = rstd_from(C, W, [(zq, zq.t[:, j, :]) for j in range(6)], 768, 512)
            for j in range(6):
                P.op("dve", lambda e, j=j, rstd=rstd: e.scalar_tensor_tensor(cqn.t[:, j, :], zq.t[:, j, :], gn.t[:, j:j + 1], rstd.t[:, :], ALU.mult, ALU.mult),
                     reads=[zq, gn, rstd], writes=[cqn], acc=True)
            for h in range(8):
                pb = nextps(C)
                for j in range(6):
                    P.mm(pb.t[:, :], wqb.t[:, j, h * 192:h * 192 + 128], cqn.t[:, j, :], j == 0, j == 5, [wqb, cqn], pb)
                o = ko[nko % 4]; nko += 1
                P.op("act", lambda e, pb=pb, o=o: e.activation(o.t[:, :], pb.t[:, :], AF.Copy), reads=[pb], writes=[o])
                P.dma("sp", QaT.t[h, :, tok0:tok0 + 512], o.t[:, :], src=o, dst=QaT)
                pb = nextps(C)
                for j in range(6):
                    P.mm(pb.t[0:64, :], wqb.t[:, j, h * 192 + 128:h * 192 + 192], cqn.t[:, j, :], j == 0, j == 5, [wqb, cqn], pb)
                z = zk[h % 2]
                P.op("dve", lambda e, pb=pb, z=z: e.tensor_copy(z.t[0:64, :], pb.t[0:64, :]), reads=[pb], writes=[z])
                o = ko[nko % 4]; nko += 1
                rope(C, W, o, o.t[0:64, :], z, z.t[0:64, :], ta, 64, 512)
                P.dma("sp", QpeT.t[h, :, tok0:tok0 + 512], o.t[0:64, :], src=o, dst=QpeT)
        ar.release(mk)
        mk = ar.mark()
        work_alloc(C, W)
        zk = [ar.alloc(f"qzk{i}", [512], F32) for i in range(2)]
        kn = [ar.alloc(f"qkn{i}", [512], F32) for i in range(2)]
        tabB = [ar.alloc(f"qtabB{i}", [2, 512], F32) for i in range(1)]
        ko = [ar.alloc(f"qko{i}", [512], BF16) for i in range(4)]
        Wp = [ar.alloc(f"Wp{i}", [16, 256], BF16) for i in range(2)]
        Bp = [ar.alloc(f"Bp{i}", [2], F32) for i in range(2)]
        for pi in range(12):
            wp = Wp[pi % 2]
            bp = Bp[pi % 2]
            c0 = 1344 + pi * 256 if pi < 4 else 2880 + (pi - 4) * 256
            prep_panel(C, w_in, 16, c0, 256, 0, wp, wp.t, bp, bp.t)
            for tbi in range(4):
                tok0 = tbi * 512
                if pi < 4:
                    tb = tabB[0]
                    P.dma("sp", tb.t[:, :, :], ropeB_q[:, :, tok0:tok0 + 512], dst=tb)
                for cc in range(2):
                    pb = nextps(C)
                    proj(C, pb, 128, 512, wp, wp.t, cc * 128, 16, hTq, tok0)
                    o = ko[nko % 4]; nko += 1
                    if pi < 4:
                        hh = pi * 2 + cc
                        z = zk[cc]
                        P.op("act", lambda e, pb=pb, z=z, bp=bp, cc=cc: e.activation(z.t[:, :], pb.t[:, :], AF.Identity, bias=bp.t[:, cc:cc + 1]),
                             reads=[pb, bp], writes=[z])
                        rstd = rstd_from(C, W, [(z, z.t[:, :])], 128, 512)
                        k_ = kn[cc]
                        P.op("dve", lambda e, z=z, k_=k_, rstd=rstd: e.scalar_tensor_tensor(k_.t[:, :], z.t[:, :], gn.t[:, 10:11], rstd.t[:, :], ALU.mult, ALU.mult),
                             reads=[z, gn, rstd], writes=[k_])
                        rope(C, W, o, o.t[:, :], k_, k_.t[:, :], tb, 128, 512)
                        P.dma("sp", QbT.t[hh, :, tok0:tok0 + 512], o.t[:, :], src=o, dst=QbT)
                    else:
                        ch = (pi - 4) * 2 + cc
                        P.op("act", lambda e, pb=pb, o=o, bp=bp, cc=cc: e.activation(o.t[:, :], pb.t[:, :], AF.Silu, bias=bp.t[:, cc:cc + 1]),
                             reads=[pb, bp], writes=[o])
                        P.dma("sp", Gd.t[ch, :, tok0:tok0 + 512], o.t[:, :], src=o, dst=Gd)

        ar.reset()
        Kt = [ar.alloc(f"Kt{i}", [NKV], BF16) for i in range(2)]
        Vt = [ar.alloc(f"Vt{i}", [NKT, 128], BF16) for i in range(2)]
        Kpe = ar.alloc("Kpe", [NKV], BF16)
        Qt = [ar.alloc(f"Qt{i}", [2048], BF16) for i in range(2)]
        Qp = [ar.alloc(f"Qp{i}", [2048], BF16) for i in range(2)]
        Gt = [ar.alloc(f"Gt{i}", [2048], BF16) for i in range(2)]
        PT = [ar.alloc(f"PT{i}", [512], BF16) for i in range(6)]
        rc = [ar.alloc(f"rc{i}", [512], F32) for i in range(2)]
        yt = [ar.alloc(f"yt{i}", [512], F32) for i in range(2)]
        yo = [ar.alloc(f"yo{i}", [512], BF16) for i in range(2)]
        P.dma("sp", Kpe.t[0:64, :], KpeT.t, src=KpeT, dst=Kpe)
        SPS = C.psb[0:4]
        OACC = C.psb[4:6]
        SACC = C.psb[6:8]
        kvslot = -1
        u = 0
        for hd in range(16):
            isA = hd < 8
            if isA or (hd - 8) % 4 == 0:
                kvslot += 1
                kt_, vt_ = Kt[kvslot % 2], Vt[kvslot % 2]
                if isA:
                    P.dma("sp", kt_.t, KaT.t[hd], src=KaT, dst=kt_)
                    P.dma("sp", vt_.t, Va.t[:, :, hd * 128:(hd + 1) * 128].rearrange("t p d -> p t d"), src=Va, dst=vt_)
                else:
                    kvh = (hd - 8) // 4
                    P.dma("sp", kt_.t, KbT.t[kvh], src=KbT, dst=kt_)
                    P.dma("sp", vt_.t, Vb.t[:, :, kvh * 128:(kvh + 1) * 128].rearrange("t p d -> p t d"), src=Vb, dst=vt_)
            qt_ = Qt[hd % 2]
            qp_ = Qp[hd % 2]
            gt_ = Gt[hd % 2]
            if isA:
                P.dma("sp", qt_.t, QaT.t[hd], src=QaT, dst=qt_)
                P.dma("sp", qp_.t[0:64, :], QpeT.t[hd], src=QpeT, dst=qp_)
            else:
                P.dma("sp", qt_.t, QbT.t[hd - 8], src=QbT, dst=qt_)
            P.dma("sp", gt_.t, Gd.t[hd], src=Gd, dst=gt_)
            scale = A_SCALE if isA else B_SCALE
            for qb in range(4):
                oT = OACC[u % 2]
                sm = SACC[u % 2]
                qs = slice(qb * 512, (qb + 1) * 512)

                def qk(kt):
                    sp_ = SPS[kt % 4]
                    ks = slice(kt * 128, (kt + 1) * 128)
                    P.mm(sp_.t[:, :], kt_.t[:, ks], qt_.t[:, qs], True, not isA, [kt_, qt_], sp_)
                    if isA:
                        P.mm(sp_.t[:, :], Kpe.t[0:64, ks], qp_.t[0:64, qs], False, True, [Kpe, qp_], sp_)

                def rest(kt):
                    sp_ = SPS[kt % 4]
                    pt = PT[kt % 6]
                    P.op("act", lambda e, pt=pt, sp_=sp_, sc=scale: e.activation(pt.t, sp_.t[:, :], AF.Exp, scale=sc), reads=[sp_], writes=[pt])
                    P.mm(oT.t[:, :], vt_.t[:, kt, :], pt.t, kt == 0, kt == NKT - 1, [vt_, pt], oT)
                    P.mm(sm.t[:, :], C.onesb, pt.t, kt == 0, kt == NKT - 1, [pt, C.cb], sm)

                qk(0)
                qk(1)
                for kt in range(NKT):
                    if kt + 2 < NKT:
                        qk(kt + 2)
                    rest(kt)
                r_ = rc[u % 2]
                y_ = yt[u % 2]
                o_ = yo[u % 2]
                P.op("dve", lambda e, r_=r_, sm=sm: e.reciprocal(r_.t, sm.t[:, :]), reads=[sm], writes=[r_])
                P.op("dve", lambda e, r_=r_, y_=y_, oT=oT: e.tensor_tensor(y_.t, oT.t[:, :], r_.t, ALU.mult), reads=[oT, r_], writes=[y_])
                P.op("pool", lambda e, y_=y_, o_=o_, gt_=gt_, qs=qs: e.tensor_tensor(o_.t, y_.t, gt_.t[:, qs], ALU.mult), reads=[y_, gt_], writes=[o_])
                P.dma("sp", Yd.t[hd, :, qs], o_.t, src=o_, dst=Yd)
                u += 1

        ar.reset()
        W = Ctx()
        Wo = ar.alloc("Wo", [16, 2048], BF16)
        Gys = [ar.alloc(f"Gy{i}", [16, 128], BF16) for i in range(2)]
        gb2 = ar.alloc("gb2", [2048], F32)
        lnbc = ar.alloc("lnbc", [2, 2048], F32)
        xts = [ar.alloc(f"oxt{i}", [2048], F32) for i in range(2)]
        tmp = [ar.alloc(f"otmp{i}", [2048], F32) for i in range(1)]
        st = ar.alloc("ost", [4, 6], F32)
        mv = ar.alloc("omv", [2], F32)
        rs = ar.alloc("ors", [1], F32)
        P.dma("sp", gb2.t, Gbc_d.t, src=Gbc_d, dst=gb2)
        P.dma("sp", lnbc.t[:, 0, :], lngb[0:1, :].partition_broadcast(128), dst=lnbc)
        P.dma("sp", lnbc.t[:, 1, :], lngb[1:2, :].partition_broadcast(128), dst=lnbc)
        for j in range(8):
            prep_panel(C, w_out, 16, j * 256, 256, None, Wo, Wo.t[:, :, j * 256:(j + 1) * 256])
        if fused:
            x1b = P.dram("X1d", [2048, 2048], F32)
            x1 = x1b.t
        else:
            x1b = P.view("x1out", x1)
        for t in range(16):
            xt = xts[t % 2]
            tm = tmp[0]
            P.dma("sp", xt.t, xq[t * 128:(t + 1) * 128, :], dst=xt)
            Gy = Gys[t % 2]
            P.dma("sp", Gy.t, Yd.t[:, :, t * 128:(t + 1) * 128].rearrange("c p t -> p c t"), src=Yd, dst=Gy)
            for nb in range(4):
                pb = nextps(C)
                ns = slice(nb * 512, (nb + 1) * 512)
                for kc in range(16):
                    P.mm(pb.t[:, :], Gy.t[:, kc, :], Wo.t[:, kc, ns], kc == 0, kc == 15, [Gy, Wo], pb)
                P.op("dve", lambda e, pb=pb, ns=ns, tm=tm: e.tensor_tensor(tm.t[:, ns], pb.t[:, :], gb2.t[:, ns], ALU.mult),
                     reads=[pb, gb2], writes=[tm], acc=True)
            P.op("dve", lambda e, xt=xt, tm=tm: e.scalar_tensor_tensor(tm.t, xt.t, ALPHA, tm.t, ALU.mult, ALU.add), reads=[xt, tm], writes=[tm])
            for c in range(4):
                P.op("dve", lambda e, c=c, tm=tm: e.bn_stats(st.t[:, c, :], tm.t[:, c * 512:(c + 1) * 512]), reads=[tm], writes=[st], acc=True)
            P.op("dve", lambda e: e.bn_aggr(mv.t, st.t), reads=[st], writes=[mv])
            P.op("act", lambda e: e.activation(rs.t, mv.t[:, 1:2], AF.Sqrt, bias=C.eps.t[:, 0:1], scale=1.0), reads=[mv, C.eps], writes=[rs])
            P.op("dve", lambda e: e.reciprocal(rs.t, rs.t), reads=[rs], writes=[rs])
            P.op("dve", lambda e, tm=tm: e.tensor_scalar(tm.t, tm.t, mv.t[:, 0:1], rs.t[:, 0:1], ALU.subtract, ALU.mult), reads=[tm, mv, rs], writes=[tm])
            P.op("pool", lambda e, tm=tm: e.tensor_tensor(tm.t, tm.t, lnbc.t[:, 0, :], ALU.mult), reads=[tm, lnbc], writes=[tm])
            P.op("dve", lambda e, tm=tm, xt=xt: e.tensor_tensor(xt.t, tm.t, lnbc.t[:, 1, :], ALU.add), reads=[tm, lnbc], writes=[xt])
            P.dma("sp", x1[t * 128:(t + 1) * 128, :], xt.t, src=xt, dbuf=x1b)
        if fused:
            outb = fused_tail(C, nc, din, x1b)
            P.fence("sp", [outb])
        else:
            P.fence("sp", [x1b])
        P.emit()
    return nc


RS_GROUPS = [[0, 1, 2, 3], [4, 5, 6, 7]]


def fused_tail(C, nc, din, x1b):
    P, ar = C.P, C.ar
    cvec1 = din("cvec1", [128, 32])
    ada_w1 = din("ada_w1", [2048, 6144])
    ada_b1 = din("ada_b1", [2, 6144])
    wf = din("w_in_f", [2048, 8192])
    w_out_f = din("w_out_f", [4096, 2048])
    lngb1 = din("lngb1", [2, 2048])
    sel_d = din("sel", [128, 4])
    cn_d = din("cn", [128, 2, 2, 256], BF16)
    w64_d = din("w64", [128, 128], BF16)
    M_d = din("Mtw", [128, 64, 2, 128], BF16)
    outp = nc.dram_tensor("out", [2048, 2048], F32, kind="ExternalOutput").ap()
    U_in = P.dram("U_in", [4096, 8192], BF16)
    U_out = P.dram("U_out", [1024, 8192], BF16)
    F_in = P.dram("F_in", [4 * 4096, 2048], BF16)
    F_out = P.dram("F_out", [4096, 2048], BF16)
    Gl = P.dram("Gl", [32, 128, 2048], BF16)
    Gbc1 = P.dram("Gbc1", [128, 2048], F32)
    Td = P.dram("Td", [16, 128, 2048], F32)
    sel = P.sb("sel_sb", [128, 4], F32)
    P.dma("sp", sel.t[:], sel_d, dst=sel)

    ar.reset()
    gbc = ar.alloc("gbc1_tmp", [2048], F32)
    emit_modulation(C, cvec1, ada_w1, ada_b1[:, :], gbc, 0)
    P.dma("sp", Gbc1.t, gbc.t, src=gbc, dst=Gbc1)
    ar.reset()
    W = Ctx()
    W.xsrc = x1b
    hT1 = ar.alloc("hT1", [16, 2048], BF16)
    mk = ar.mark()
    ln_alloc(C, W)
    W.lnc = 0
    ln_transpose(C, W, x1b.t, 0, 16, hT1)
    ar.release(mk)
    Wp = [ar.alloc(f"fWp{i}", [16, 256], BF16) for i in range(2)]
    Bp = [ar.alloc(f"fBp{i}", [2], F32) for i in range(2)]
    uo = [ar.alloc(f"fuo{i}", [512], BF16) for i in range(2)]
    us = [ar.alloc(f"fus{i}", [4, 512], BF16) for i in range(2)]
    go = [ar.alloc(f"fgo{i}", [512], BF16) for i in range(2)]
    n = 0
    for pi in range(32):
        wp, bp = Wp[pi % 2], Bp[pi % 2]
        isu = pi < 16
        c0 = pi * 256 if isu else 4096 + (pi - 16) * 256
        prep_panel(C, wf, 16, c0, 256, 0, wp, wp.t, bp, bp.t)
        for tbi in range(4):
            tok0 = tbi * 512
            for cc in range(2):
                pb = nextps(C)
                proj(C, pb, 128, 512, wp, wp.t, cc * 128, 16, hT1, tok0)
                ch = (pi % 16) * 2 + cc
                if isu:
                    o = uo[n % 2]
                    s4 = us[n % 2]
                    n += 1
                    P.op("act", lambda e, pb=pb, o=o, bp=bp, cc=cc: e.activation(o.t, pb.t[:, :], AF.Identity, bias=bp.t[:, cc:cc + 1]),
                         reads=[pb, bp], writes=[o])
                    for j in range(4):
                        if j % 2 == 0:
                            P.op("dve", lambda e, o=o, s4=s4, j=j: e.tensor_scalar_mul(s4.t[:, j, :], o.t, sel.t[:, j:j + 1]),
                                 reads=[o, sel], writes=[s4], acc=True)
                        else:
                            P.op("act", lambda e, o=o, s4=s4, j=j: e.activation(s4.t[:, j, :], o.t, AF.Copy, scale=sel.t[:, j:j + 1]),
                                 reads=[o, sel], writes=[s4], acc=True)
                    dest = ch // 8
                    row0 = dest * 1024 + (ch % 8) * 128
                    dst_ap = U_in.t[row0:row0 + 128, :].rearrange("p (j t) -> p j t", j=4)[:, :, tok0:tok0 + 512]
                    P.dma("sp", dst_ap, s4.t, src=s4, dst=U_in)
                else:
                    o = go[n % 2]
                    n += 1
                    P.op("act", lambda e, pb=pb, o=o, bp=bp, cc=cc: e.activation(o.t, pb.t[:, :], AF.Silu, bias=bp.t[:, cc:cc + 1]),
                         reads=[pb, bp], writes=[o])
                    P.dma("sp", Gl.t[ch, :, tok0:tok0 + 512], o.t, src=o, dst=Gl)
    P.op("pool", lambda e: e.collective_compute("ReduceScatter", ALU.add, replica_groups=RS_GROUPS, ins=[U_in.t], outs=[U_out.t]),
         reads=[U_in], writes=[U_out], dma_dst=U_out, dma_inc=1)

    ar.reset()
    cn = ar.alloc("cn", [2, 2, 256], BF16)
    w64 = ar.alloc("w64", [128], BF16)
    Mt = ar.alloc("Mt", [64, 2, 128], BF16)
    P.dma("sp", cn.t, cn_d, dst=cn)
    P.dma("sp", w64.t, w64_d, dst=w64)
    P.dma("sp", Mt.t, M_d, dst=Mt)
    uT = ar.alloc("uT", [2, 8192], BF16)
    fT = ar.alloc("fT", [8192], BF16)
    z = ar.alloc("z", [128, 128], BF16)
    Y = ar.alloc("Y", [128, 128], BF16)
    fs = [ar.alloc(f"fs{i}", [2048], BF16) for i in range(2)]
    Uv = U_out.t.rearrange("(c p) t -> c p t", p=128)
    nfs = 0
    for g in range(4):
        P.dma("sp", uT.t, Uv[2 * g:2 * g + 2].rearrange("c p t -> p c t"), src=U_out, dst=uT)
        uv = uT.t.rearrange("p c (l1 l2) -> p c l2 l1", l2=128)
        for kh in range(2):
            ks = slice(kh * 128, (kh + 1) * 128)
            for l2p in range(32):
                pb = nextps(C)
                for q in range(4):
                    l2 = l2p * 4 + q
                    for ri in range(2):
                        for cc in range(2):
                            P.mm(pb.t[ri * 64:(ri + 1) * 64, q * 128:(q + 1) * 128], uv[:, cc, l2, :], cn.t[:, cc, ri, ks],
                                 cc == 0, cc == 1, [uT, cn], pb)
                dst = z.t[:, l2p * 4:(l2p + 1) * 4, :]
                src = pb.t[:, :].rearrange("p (a b) -> p a b", b=128)
                if l2p % 2 == 0:
                    P.op("act", lambda e, dst=dst, src=src: e.activation(dst, src, AF.Copy), reads=[pb], writes=[z], acc=True)
                else:
                    P.op("dve", lambda e, dst=dst, src=src: e.tensor_copy(dst, src), reads=[pb], writes=[z], acc=True)
            for k3p in range(32):
                pb = nextps(C)
                for q in range(4):
                    k3 = k3p * 4 + q
                    P.mm(pb.t[:, q * 128:(q + 1) * 128], z.t[:, :, k3], w64.t, True, True, [z, w64], pb)
                dst = Y.t[:, k3p * 4:(k3p + 1) * 4, :]
                src = pb.t[:, :].rearrange("p (a b) -> p a b", b=128)
                if k3p % 2 == 0:
                    P.op("act", lambda e, dst=dst, src=src: e.activation(dst, src, AF.Copy), reads=[pb], writes=[Y], acc=True)
                else:
                    P.op("dve", lambda e, dst=dst, src=src: e.tensor_copy(dst, src), reads=[pb], writes=[Y], acc=True)
            fv = fT.t.rearrange("p (k2 k1) -> p k1 k2", k1=64)
            for k1p in range(16):
                pb = nextps(C)
                for q in range(4):
                    k1 = k1p * 4 + q
                    for ri in range(2):
                        P.mm(pb.t[:, q * 128:(q + 1) * 128], Y.t[:, :, ri * 64 + k1], Mt.t[:, k1, ri, :], ri == 0, ri == 1, [Y, Mt], pb)
                fsl = fv[:, k1p * 4:(k1p + 1) * 4, :]
                src = pb.t[:, :].rearrange("p (a b) -> p a b", b=128)
                if k1p % 2 == 0:
                    P.op("act", lambda e, fsl=fsl, src=src: e.activation(fsl, src, AF.Copy, scale=FSCALE), reads=[pb], writes=[fT], acc=True)
                else:
                    P.op("dve", lambda e, fsl=fsl, src=src: e.tensor_scalar_mul(fsl, src, FSCALE), reads=[pb], writes=[fT], acc=True)
            for d in range(4):
                for j in range(4):
                    f_ = fs[nfs % 2]
                    nfs += 1
                    if j % 2 == 0:
                        P.op("dve", lambda e, f_=f_, d=d, j=j: e.tensor_scalar_mul(f_.t, fT.t[:, d * 2048:(d + 1) * 2048], sel.t[:, j:j + 1]),
                             reads=[fT, sel], writes=[f_])
                    else:
                        P.op("act", lambda e, f_=f_, d=d, j=j: e.activation(f_.t, fT.t[:, d * 2048:(d + 1) * 2048], AF.Copy, scale=sel.t[:, j:j + 1]),
                             reads=[fT, sel], writes=[f_])
                    r0 = d * 4096 + j * 1024 + g * 256 + kh * 128
                    P.dma("sp", F_in.t[r0:r0 + 128, :], f_.t, src=f_, dst=F_in)
    P.op("pool", lambda e: e.collective_compute("ReduceScatter", ALU.add, replica_groups=RS_GROUPS, ins=[F_in.t], outs=[F_out.t]),
         reads=[F_in], writes=[F_out], dma_dst=F_out, dma_inc=1)

    ar.reset()
    Wo = ar.alloc("fWo", [32, 1024], BF16)
    gb2 = ar.alloc("fgb2", [2048], F32)
    lnbc = ar.alloc("flnbc", [2, 2048], F32)
    Fys = [ar.alloc(f"fFy{i}", [32, 128], BF16) for i in range(2)]
    Ggs = [ar.alloc(f"fGg{i}", [32, 128], BF16) for i in range(2)]
    Gys = [ar.alloc(f"fGy{i}", [32, 128], BF16) for i in range(2)]
    tms = [ar.alloc(f"ftm{i}", [1024], F32) for i in range(2)]
    xts = [ar.alloc(f"fxt{i}", [2048], F32) for i in range(1)]
    tmp = ar.alloc("fotmp", [2048], F32)
    st = ar.alloc("fost", [4, 6], F32)
    mv = ar.alloc("fomv", [2], F32)
    rs = ar.alloc("fors", [1], F32)
    P.dma("sp", gb2.t, Gbc1.t, src=Gbc1, dst=gb2)
    P.dma("sp", lnbc.t[:, 0, :], lngb1[0:1, :].partition_broadcast(128), dst=lnbc)
    P.dma("sp", lnbc.t[:, 1, :], lngb1[1:2, :].partition_broadcast(128), dst=lnbc)
    Fv = F_out.t.rearrange("(c p) t -> c p t", p=128)
    n = 0
    for nh in range(2):
        for kh in range(2):
            for j in range(4):
                c0 = nh * 1024 + j * 256
                prep_panel(C, w_out_f[kh * 2048:(kh + 1) * 2048, :], 16, c0, 256, None, Wo,
                           Wo.t[:, kh * 16:(kh + 1) * 16, j * 256:(j + 1) * 256])
        for t in range(16):
            Fy, Gg, Gy, tm = Fys[n % 2], Ggs[n % 2], Gys[n % 2], tms[n % 2]
            n += 1
            P.dma("sp", Fy.t, Fv[:, :, t * 128:(t + 1) * 128].rearrange("c p t -> p c t"), src=F_out, dst=Fy)
            P.dma("sp", Gg.t, Gl.t[:, :, t * 128:(t + 1) * 128].rearrange("c p t -> p c t"), src=Gl, dst=Gg)
            P.op("pool", lambda e, Fy=Fy, Gg=Gg, Gy=Gy: e.tensor_tensor(Gy.t, Fy.t, Gg.t, ALU.mult), reads=[Fy, Gg], writes=[Gy])
            for nb in range(2):
                pb = nextps(C)
                ns = slice(nb * 512, (nb + 1) * 512)
                gs = slice(nh * 1024 + nb * 512, nh * 1024 + (nb + 1) * 512)
                for kc in range(32):
                    P.mm(pb.t[:, :], Gy.t[:, kc, :], Wo.t[:, kc, ns], kc == 0, kc == 31, [Gy, Wo], pb)
                P.op("dve", lambda e, pb=pb, ns=ns, gs=gs, tm=tm: e.tensor_tensor(tm.t[:, ns], pb.t[:, :], gb2.t[:, gs], ALU.mult),
                     reads=[pb, gb2], writes=[tm], acc=True)
            P.dma("sp", Td.t[t, :, nh * 1024:(nh + 1) * 1024], tm.t, src=tm, dst=Td)
    ob = P.view("out_b", outp)
    for t in range(16):
        xt = xts[0]
        tm = tmp
        P.dma("sp", xt.t, x1b.t[t * 128:(t + 1) * 128, :], src=x1b, dst=xt)
        P.dma("sp", tm.t, Td.t[t], src=Td, dst=tm)
        P.op("dve", lambda e, xt=xt, tm=tm: e.scalar_tensor_tensor(tm.t, xt.t, ALPHA, tm.t, ALU.mult, ALU.add), reads=[xt, tm], writes=[tm])
        for c in range(4):
            P.op("dve", lambda e, c=c, tm=tm: e.bn_stats(st.t[:, c, :], tm.t[:, c * 512:(c + 1) * 512]), reads=[tm], writes=[st], acc=True)
        P.op("dve", lambda e: e.bn_aggr(mv.t, st.t), reads=[st], writes=[mv])
        P.op("act", lambda e: e.activation(rs.t, mv.t[:, 1:2], AF.Sqrt, bias=C.eps.t[:, 0:1], scale=1.0), reads=[mv, C.eps], writes=[rs])
        P.op("dve", lambda e: e.reciprocal(rs.t, rs.t), reads=[rs], writes=[rs])
        P.op("dve", lambda e, tm=tm: e.tensor_scalar(tm.t, tm.t, mv.t[:, 0:1], rs.t[:, 0:1], ALU.subtract, ALU.mult), reads=[tm, mv, rs], writes=[tm])
        P.op("pool", lambda e, tm=tm: e.tensor_tensor(tm.t, tm.t, lnbc.t[:, 0, :], ALU.mult), reads=[tm, lnbc], writes=[tm])
        P.op("dve", lambda e, tm=tm, xt=xt: e.tensor_tensor(xt.t, tm.t, lnbc.t[:, 1, :], ALU.add), reads=[tm, lnbc], writes=[xt])
        P.dma("sp", outp[t * 128:(t + 1) * 128, :], xt.t, src=xt, dbuf=ob)
    return ob


def fused_inputs(inp):
    maps = stage_a_inputs(inp)
    cn, w64, M = fft_consts()
    for core in range(8):
        b, r = core // 4, core % 4
        cv = np.zeros((128, 16, 2), np.float32)
        cv[:, :, 0] = _pk(inp["c"][b], 16)
        cv[:, :, 1] = cv[:, :, 0]
        sel = np.zeros((128, 4), np.float32)
        sel[:, r] = 1.0
        maps[core].update({
            "cvec1": cv.reshape(128, 32), "ada_w1": inp["ada_w"][1],
            "ada_b1": np.stack([inp["ada_b"][1], inp["ada_b"][1]]),
            "w_in_f": inp["w_in_fourier"][0], "w_out_f": inp["w_out_fourier"][0],
            "lngb1": np.stack([inp["ln_g"][1], inp["ln_b"][1]]), "sel": sel,
            "cn": cn, "w64": w64, "Mtw": M,
        })
    return maps


def kernel_fused(**inp):
    inp = {k: np.asarray(v) for k, v in inp.items()}
    nc = build_stage_a(fused=True)
    res = run_bass_kernel_spmd(nc, fused_inputs(inp), core_ids=list(range(8)))
    out = np.zeros((2, 8192, 2048), np.float32)
    for core in range(8):
        b, r = core // 4, core % 4
        out[b, r * 2048:(r + 1) * 2048] = res.results[core]["out"]
    return out


def _rope_tables(rot_dim, seq=8192, grid_w=64):
    t = np.arange(seq)
    r = (t // grid_w).astype(np.float32)
    col = (t % grid_w).astype(np.float32)
    nf = rot_dim // 4
    inv = (np.float32(10000.0) ** (-(np.arange(nf, dtype=np.float32)) / np.float32(nf))).astype(np.float32)
    ang = np.concatenate([r[:, None] * inv[None, :], col[:, None] * inv[None, :]], axis=-1).astype(np.float32)
    cos = np.cos(ang).astype(np.float32)
    sin = np.sin(ang).astype(np.float32)
    cosT = np.repeat(cos, 2, axis=1).T
    sgn = np.tile(np.array([-1.0, 1.0], np.float32), rot_dim // 2)
    sinT = (np.repeat(sin, 2, axis=1) * sgn[None, :]).T
    return np.ascontiguousarray(cosT), np.ascontiguousarray(sinT)


def _consts():
    c = np.zeros((128, 3, 128), np.float32)
    c[:, 0, :] = np.eye(128, dtype=np.float32)
    c[:, 1, :] = 1.0
    idx = np.arange(128)
    c[idx, 2, idx ^ 1] = 1.0
    return c


def _pk(v, kc):
    return np.ascontiguousarray(np.asarray(v, np.float32).reshape(kc, 128).T)


def stage_a_inputs(inp):
    cB, sB = _rope_tables(128)
    cA, sA = _rope_tables(64)
    tabB = np.zeros((128, 2, NKV), np.float32)
    tabB[:, 0, :256] = 1.0
    tabB[:, 0, 256:] = cB
    tabB[:, 1, 256:] = sB
    tabA = np.zeros((64, 2, NKV), np.float32)
    tabA[:, 0, :256] = 1.0
    tabA[:, 0, 256:] = cA
    tabA[:, 1, 256:] = sA
    gains = np.zeros((128, 12), np.float32)
    gains[:, 0:6] = _pk(inp["q_lora_norm"][0], 6)
    gains[:, 6:10] = _pk(inp["kv_lora_norm"][0], 4)
    gains[:, 10] = inp["q_norm_b"][0]
    gains[:, 11] = inp["k_norm_b"][0]
    consts = _consts()
    maps = []
    for core in range(8):
        b, qr = core // 4, core % 4
        t0 = qr * 2048
        cv = np.zeros((128, 16, 2), np.float32)
        cv[:, :, 0] = _pk(inp["c"][b], 16)
        cv[:, :, 1] = _pk(inp["c_ctx"], 16)
        maps.append({
            "xkv": np.ascontiguousarray(np.concatenate([inp["ctx"][b], inp["x"][b]], axis=0)),
            "xq": np.ascontiguousarray(inp["x"][b, t0:t0 + 2048]),
            "cvec": cv.reshape(128, 32),
            "ada_w": inp["ada_w"][0], "ada_b": np.stack([inp["ada_b"][0], inp["ada_b"][0]]),
            "w_in": inp["w_in_attn"][0], "wq_b": inp["wq_b"][0], "wkv_b": inp["wkv_b"][0], "w_out": inp["w_out_attn"][0],
            "gains": gains, "lngb": np.stack([inp["ln_g"][0], inp["ln_b"][0]]), "consts": consts,
            "ropeB_kv": tabB, "ropeA_kv": tabA,
            "ropeB_q": np.ascontiguousarray(tabB[:, :, 256 + t0:256 + t0 + 2048]),
            "ropeA_q": np.ascontiguousarray(tabA[:, :, 256 + t0:256 + t0 + 2048]),
        })
    return maps


def run_stage_a(inp):
    nc = build_stage_a()
    res = run_bass_kernel_spmd(nc, stage_a_inputs(inp), core_ids=list(range(8)))
    x1 = np.zeros((2, 8192, 2048), np.float32)
    for core in range(8):
        b, qr = core // 4, core % 4
        x1[b, qr * 2048:(qr + 1) * 2048] = res.results[core]["x1"]
    return x1


FSCALE = float((8192.0 * 256.0) ** -0.5)


def fft_consts():
    import ml_dtypes
    bf = ml_dtypes.bfloat16
    c = np.arange(256)[:, None].astype(np.float64)
    k3 = np.arange(256)[None, :].astype(np.float64)
    a = 2 * np.pi * c * k3 / 256
    cn = np.zeros((128, 2, 2, 256), np.float64)
    for cc in range(2):
        cn[:, cc, 0, :] = np.cos(a[cc * 128:(cc + 1) * 128])
        cn[:, cc, 1, :] = -np.sin(a[cc * 128:(cc + 1) * 128])
    l1 = np.arange(64)[:, None].astype(np.float64)
    k1 = np.arange(64)[None, :].astype(np.float64)
    th = 2 * np.pi * l1 * k1 / 64
    wr, wi = np.cos(th), -np.sin(th)
    w64 = np.zeros((128, 128), np.float64)
    w64[0:64, 0:64] = wr
    w64[64:128, 0:64] = -wi
    w64[0:64, 64:128] = wi
    w64[64:128, 64:128] = wr
    l2 = np.arange(128)[:, None, None].astype(np.float64)
    kk = (np.arange(64)[None, :, None] + 64 * np.arange(128)[None, None, :]).astype(np.float64)
    ph = 2 * np.pi * ((l2 * kk) % 8192) / 8192
    M = np.stack([np.cos(ph), np.sin(ph)], axis=2)
    return cn.astype(np.float32).astype(bf), w64.astype(np.float32).astype(bf), M.astype(np.float32).astype(bf)


def build_stage_b():
    nc = bass.Bass("TRN2", target_bir_lowering=False)

    def din(name, shape, dt=F32):
        return nc.dram_tensor(name, list(shape), dt, kind="ExternalInput").ap()

    x1f = din("x1f", [8192, 2048])
    cvec = din("cvec", [128, 32])
    ada_w = din("ada_w", [2048, 6144])
    ada_b = din("ada_b", [2, 6144])
    wuf = din("wu", [2048, 1024])
    wgf = din("wg", [2048, 1024])
    consts = din("consts", [128, 3, 128])
    cn_d = din("cn", [128, 2, 2, 256], BF16)
    w64_d = din("w64", [128, 128], BF16)
    M_d = din("Mtw", [128, 64, 2, 128], BF16)
    yT = nc.dram_tensor("yT", [1024, 8192], BF16, kind="ExternalOutput").ap()
    gbc_o = nc.dram_tensor("gbc", [128, 2048], F32, kind="ExternalOutput").ap()

    with ExitStack() as es:
        C = setup_common(nc, es, consts)
        P, ar = C.P, C.ar
        Ud = P.dram("Ud", [8, 128, 8192], BF16)
        Gd = P.dram("Gd", [8, 128, 8192], BF16)
        gbc = ar.alloc("gbc_tmp", [2048], F32)
        emit_modulation(C, cvec, ada_w, ada_b[:, :], gbc, 0)
        gbc_b = P.view("gbc_out", gbc_o)
        P.dma("sp", gbc_o, gbc.t, src=gbc, dbuf=gbc_b)
        ar.reset()
        W = Ctx()
        Wu = ar.alloc("Wu", [16, 1024], BF16)
        Wg = ar.alloc("Wg", [16, 1024], BF16)
        Bu = ar.alloc("Bu", [8], F32)
        Bg = ar.alloc("Bg", [8], F32)
        ln_alloc(C, W)
        W.lnc = 0
        hTs = [ar.alloc(f"hT{i}", [16, 512], BF16) for i in range(2)]
        uo = [ar.alloc(f"uo{i}", [512], BF16) for i in range(4)]
        for j in range(4):
            prep_panel(C, wuf, 16, j * 256, 256, 0, Wu, Wu.t[:, :, j * 256:(j + 1) * 256], Bu, Bu.t[:, 2 * j:2 * j + 2])
            prep_panel(C, wgf, 16, j * 256, 256, 0, Wg, Wg.t[:, :, j * 256:(j + 1) * 256], Bg, Bg.t[:, 2 * j:2 * j + 2])
        nuo = 0
        for tb in range(16):
            hT = hTs[tb % 2]
            ln_transpose(C, W, x1f, tb * 512, 4, hT)
            for j in range(16):
                pb = nextps(C)
                isu = j < 8
                jj = j % 8
                proj(C, pb, 128, 512, Wu if isu else Wg, (Wu if isu else Wg).t, jj * 128, 16, hT, 0)
                o = uo[nuo % 4]; nuo += 1
                bb = Bu if isu else Bg
                P.op("act", lambda e, pb=pb, o=o, bb=bb, jj=jj, isu=isu: e.activation(o.t, pb.t[:, :], AF.Identity if isu else AF.Silu, bias=bb.t[:, jj:jj + 1]),
                     reads=[pb, bb], writes=[o])
                dd = Ud if isu else Gd
                P.dma("sp", dd.t[jj, :, tb * 512:(tb + 1) * 512], o.t, src=o, dst=dd)
        ar.reset()
        cn = ar.alloc("cn", [2, 2, 256], BF16)
        w64 = ar.alloc("w64", [128], BF16)
        Mt = ar.alloc("Mt", [64, 2, 128], BF16)
        P.dma("sp", cn.t, cn_d, dst=cn)
        P.dma("sp", w64.t, w64_d, dst=w64)
        P.dma("sp", Mt.t, M_d, dst=Mt)
        uT = ar.alloc("uT", [2, 8192], BF16)
        gT = ar.alloc("gT", [2, 8192], BF16)
        z = ar.alloc("z", [128, 128], BF16)
        Y = ar.alloc("Y", [128, 128], BF16)
        yTb = P.view("yT_out", yT)
        for g in range(4):
            P.dma("sp", uT.t, Ud.t[2 * g:2 * g + 2].rearrange("c p t -> p c t"), src=Ud, dst=uT)
            P.dma("sp", gT.t, Gd.t[2 * g:2 * g + 2].rearrange("c p t -> p c t"), src=Gd, dst=gT)
            uv = uT.t.rearrange("p c (l1 l2) -> p c l2 l1", l2=128)
            for kh in range(2):
                ks = slice(kh * 128, (kh + 1) * 128)
                for l2p in range(32):
                    pb = nextps(C)
                    for q in range(4):
                        l2 = l2p * 4 + q
                        for ri in range(2):
                            for cc in range(2):
                                P.mm(pb.t[ri * 64:(ri + 1) * 64, q * 128:(q + 1) * 128], uv[:, cc, l2, :], cn.t[:, cc, ri, ks],
                                     cc == 0, cc == 1, [uT, cn], pb)
                    dst = z.t[:, l2p * 4:(l2p + 1) * 4, :]
                    src = pb.t[:, :].rearrange("p (a b) -> p a b", b=128)
                    if l2p % 2 == 0:
                        P.op("act", lambda e, dst=dst, src=src: e.activation(dst, src, AF.Copy), reads=[pb], writes=[z], acc=True)
                    else:
                        P.op("dve", lambda e, dst=dst, src=src: e.tensor_copy(dst, src), reads=[pb], writes=[z], acc=True)
                for k3p in range(32):
                    pb = nextps(C)
                    for q in range(4):
                        k3 = k3p * 4 + q
                        P.mm(pb.t[:, q * 128:(q + 1) * 128], z.t[:, :, k3], w64.t, True, True, [z, w64], pb)
                    dst = Y.t[:, k3p * 4:(k3p + 1) * 4, :]
                    src = pb.t[:, :].rearrange("p (a b) -> p a b", b=128)
                    if k3p % 2 == 0:
                        P.op("act", lambda e, dst=dst, src=src: e.activation(dst, src, AF.Copy), reads=[pb], writes=[Y], acc=True)
                    else:
                        P.op("dve", lambda e, dst=dst, src=src: e.tensor_copy(dst, src), reads=[pb], writes=[Y], acc=True)
                gv = gT.t[:, kh, :].rearrange("p (k2 k1) -> p k1 k2", k1=64)
                for k1p in range(16):
                    pb = nextps(C)
                    for q in range(4):
                        k1 = k1p * 4 + q
                        for ri in range(2):
                            P.mm(pb.t[:, q * 128:(q + 1) * 128], Y.t[:, :, ri * 64 + k1], Mt.t[:, k1, ri, :], ri == 0, ri == 1, [Y, Mt], pb)
                    gsl = gv[:, k1p * 4:(k1p + 1) * 4, :]
                    src = pb.t[:, :].rearrange("p (a b) -> p a b", b=128)
                    P.op("dve", lambda e, gsl=gsl, src=src: e.scalar_tensor_tensor(gsl, src, FSCALE, gsl, ALU.mult, ALU.mult),
                         reads=[pb, gT], writes=[gT], acc=True)
            P.dma("sp", yT[g * 256:(g + 1) * 256, :].rearrange("(c p) t -> p c t", p=128), gT.t, src=gT, dbuf=yTb)
        P.fence("sp", [yTb, gbc_b])
        P.emit()
    return nc


def stage_b_inputs(inp, x1):
    cn, w64, M = fft_consts()
    consts = _consts()
    maps = []
    wf = inp["w_in_fourier"][0]
    for core in range(8):
        b, cq = core // 4, core % 4
        cv = np.zeros((128, 16, 2), np.float32)
        cv[:, :, 0] = _pk(inp["c"][b], 16)
        cv[:, :, 1] = cv[:, :, 0]
        maps.append({
            "x1f": np.ascontiguousarray(x1[b]), "cvec": cv.reshape(128, 32),
            "ada_w": inp["ada_w"][1], "ada_b": np.stack([inp["ada_b"][1], inp["ada_b"][1]]),
            "wu": np.ascontiguousarray(wf[:, cq * 1024:(cq + 1) * 1024]),
            "wg": np.ascontiguousarray(wf[:, 4096 + cq * 1024:4096 + (cq + 1) * 1024]),
            "consts": consts, "cn": cn, "w64": w64, "Mtw": M,
        })
    return maps


def run_stage_b(inp, x1):
    nc = build_stage_b()
    res = run_bass_kernel_spmd(nc, stage_b_inputs(inp, x1), core_ids=list(range(8)))
    yT = [np.concatenate([res.results[b * 4 + cq]["yT"] for cq in range(4)], axis=0) for b in range(2)]
    gbc = [res.results[b * 4]["gbc"] for b in range(2)]
    return yT, gbc


def build_stage_c():
    nc = bass.Bass("TRN2", target_bir_lowering=False)

    def din(name, shape, dt=F32):
        return nc.dram_tensor(name, list(shape), dt, kind="ExternalInput").ap()

    yTl = din("yTl", [32, 128, 2048], BF16)
    x1l = din("x1l", [2048, 2048])
    w_out = din("w_out", [4096, 2048])
    gbc_d = din("gbc", [128, 2048])
    lngb = din("lngb", [2, 2048])
    consts = din("consts", [128, 3, 128])
    outp = nc.dram_tensor("out", [2048, 2048], F32, kind="ExternalOutput").ap()

    with ExitStack() as es:
        C = setup_common(nc, es, consts)
        P, ar = C.P, C.ar
        Td = P.dram("Td", [16, 128, 2048], F32)
        Wo = ar.alloc("Wo", [32, 1024], BF16)
        gb2 = ar.alloc("gb2", [2048], F32)
        lnbc = ar.alloc("lnbc", [2, 2048], F32)
        Gys = [ar.alloc(f"Gy{i}", [32, 128], BF16) for i in range(2)]
        tms = [ar.alloc(f"tm{i}", [1024], F32) for i in range(2)]
        xts = [ar.alloc(f"oxt{i}", [2048], F32) for i in range(2)]
        tmp = ar.alloc("otmp", [2048], F32)
        st = ar.alloc("ost", [4, 6], F32)
        mv = ar.alloc("omv", [2], F32)
        rs = ar.alloc("ors", [1], F32)
        P.dma("sp", gb2.t, gbc_d, dst=gb2)
        P.dma("sp", lnbc.t[:, 0, :], lngb[0:1, :].partition_broadcast(128), dst=lnbc)
        P.dma("sp", lnbc.t[:, 1, :], lngb[1:2, :].partition_broadcast(128), dst=lnbc)
        n = 0
        for nh in range(2):
            for kh in range(2):
                for j in range(4):
                    c0 = nh * 1024 + j * 256
                    prep_panel(C, w_out[kh * 2048:(kh + 1) * 2048, :], 16, c0, 256, None, Wo,
                               Wo.t[:, kh * 16:(kh + 1) * 16, j * 256:(j + 1) * 256])
            for t in range(16):
                Gy = Gys[n % 2]
                tm = tms[n % 2]
                n += 1
                P.dma("sp", Gy.t, yTl[:, :, t * 128:(t + 1) * 128].rearrange("c p t -> p c t"), dst=Gy)
                for nb in range(2):
                    pb = nextps(C)
                    ns = slice(nb * 512, (nb + 1) * 512)
                    gs = slice(nh * 1024 + nb * 512, nh * 1024 + (nb + 1) * 512)
                    for kc in range(32):
                        P.mm(pb.t[:, :], Gy.t[:, kc, :], Wo.t[:, kc, ns], kc == 0, kc == 31, [Gy, Wo], pb)
                    P.op("dve", lambda e, pb=pb, ns=ns, gs=gs, tm=tm: e.tensor_tensor(tm.t[:, ns], pb.t[:, :], gb2.t[:, gs], ALU.mult),
                         reads=[pb, gb2], writes=[tm], acc=True)
                P.dma("sp", Td.t[t, :, nh * 1024:(nh + 1) * 1024], tm.t, src=tm, dst=Td)
        ob = P.view("out_b", outp)
        for t in range(16):
            xt = xts[t % 2]
            tm = tmp
            P.dma("sp", xt.t, x1l[t * 128:(t + 1) * 128, :], dst=xt)
            P.dma("sp", tm.t, Td.t[t], src=Td, dst=tm)
            P.op("dve", lambda e, xt=xt, tm=tm: e.scalar_tensor_tensor(tm.t, xt.t, ALPHA, tm.t, ALU.mult, ALU.add), reads=[xt, tm], writes=[tm])
            for c in range(4):
                P.op("dve", lambda e, c=c, tm=tm: e.bn_stats(st.t[:, c, :], tm.t[:, c * 512:(c + 1) * 512]), reads=[tm], writes=[st], acc=True)
            P.op("dve", lambda e: e.bn_aggr(mv.t, st.t), reads=[st], writes=[mv])
            P.op("act", lambda e: e.activation(rs.t, mv.t[:, 1:2], AF.Sqrt, bias=C.eps.t[:, 0:1], scale=1.0), reads=[mv, C.eps], writes=[rs])
            P.op("dve", lambda e: e.reciprocal(rs.t, rs.t), reads=[rs], writes=[rs])
            P.op("dve", lambda e, tm=tm: e.tensor_scalar(tm.t, tm.t, mv.t[:, 0:1], rs.t[:, 0:1], ALU.subtract, ALU.mult), reads=[tm, mv, rs], writes=[tm])
            P.op("pool", lambda e, tm=tm: e.tensor_tensor(tm.t, tm.t, lnbc.t[:, 0, :], ALU.mult), reads=[tm, lnbc], writes=[tm])
            P.op("dve", lambda e, tm=tm, xt=xt: e.tensor_tensor(xt.t, tm.t, lnbc.t[:, 1, :], ALU.add), reads=[tm, lnbc], writes=[xt])
            P.dma("sp", outp[t * 128:(t + 1) * 128, :], xt.t, src=xt, dbuf=ob)
        P.fence("sp", [ob])
        P.emit()
    return nc


def run_stage_c(inp, x1, yT, gbc):
    nc = build_stage_c()
    consts = _consts()
    maps = []
    for core in range(8):
        b, qr = core // 4, core % 4
        t0 = qr * 2048
        maps.append({
            "yTl": np.ascontiguousarray(yT[b][:, t0:t0 + 2048]).reshape(32, 128, 2048),
            "x1l": np.ascontiguousarray(x1[b, t0:t0 + 2048]),
            "w_out": inp["w_out_fourier"][0], "gbc": gbc[b],
            "lngb": np.stack([inp["ln_g"][1], inp["ln_b"][1]]), "consts": consts,
        })
    res = run_bass_kernel_spmd(nc, maps, core_ids=list(range(8)))
    out = np.zeros((2, 8192, 2048), np.float32)
    for core in range(8):
        b, qr = core // 4, core % 4
        out[b, qr * 2048:(qr + 1) * 2048] = res.results[core]["out"]
    return out


def kernel_unfused(**inp):
    inp = {k: np.asarray(v) for k, v in inp.items()}
    x1 = run_stage_a(inp)
    yT, gbc = run_stage_b(inp, x1)
    return run_stage_c(inp, x1, yT, gbc)


def kernel(**inp):
    return kernel_fused(**inp)
```

```python
import numpy as np
from contextlib import ExitStack
import concourse.bass as bass
import concourse.mybir as mybir
from concourse.bass_utils import run_bass_kernel_spmd

F32 = mybir.dt.float32
BF16 = mybir.dt.bfloat16
ALU = mybir.AluOpType
AF = mybir.ActivationFunctionType
AX = mybir.AxisListType

SEM_LIM = 30000


class Buf:
    def __init__(self, name, t=None):
        self.name = name
        self.t = t
        self.w = {}
        self.r = {}
        self.dcnt = 0
        self.dsem = None
        self.is_dram = False

    def __getitem__(self, k):
        return self.t[k]


class Prog:
    ENGS = ("pe", "act", "dve", "pool", "sp")

    def __init__(self, nc, es):
        self.nc = nc
        self.es = es
        self.ops = {e: [] for e in self.ENGS}
        self.seen = {e: {} for e in self.ENGS}
        self.signal = {e: set() for e in self.ENGS}
        self.dbufs = []
        self.nbuf = 0

    def sb(self, name, shape, dt):
        t = self.es.enter_context(self.nc.sbuf_tensor(name, list(shape), dt))
        return Buf(name, t)

    def ps(self, name):
        t = self.es.enter_context(self.nc.psum_tensor(name, [128, 512], F32))
        return Buf(name, t)

    def dram(self, name, shape, dt, kind="Internal"):
        t = self.nc.dram_tensor(name, list(shape), dt, kind=kind)
        b = Buf(name, t.ap())
        b.is_dram = True
        return b

    def view(self, name, ap):
        b = Buf(name, ap)
        b.is_dram = True
        return b

    def op(self, eng, fn, reads=(), writes=(), dma_dst=None, acc=False, dma_inc=16):
        if dma_dst is not None:
            own = ("D", id(dma_dst))
        else:
            own = ("E", eng)
        need = {}

        def merge(d, skip_own=False):
            for k, v in d.items():
                if skip_own and k == own:
                    continue
                if need.get(k, -1) < v:
                    need[k] = v

        for b in reads:
            merge(b.w)
        for b in writes:
            merge(b.w, skip_own=acc)
            merge(b.r)
        waits = []
        seen = self.seen[eng]
        for k, v in need.items():
            if k == ("E", "pe") and eng == "pe":
                continue
            if seen.get(k, -1) >= v:
                continue
            seen[k] = v
            waits.append((k, v))
            if k[0] == "E":
                self.signal[k[1]].add(v)
        idx = len(self.ops[eng])
        if dma_dst is not None:
            if dma_dst.dsem is None:
                dma_dst.dsem = True
                self.dbufs.append(dma_dst)
            dma_dst.dcnt += dma_inc
            assert dma_dst.dcnt < 2 * SEM_LIM, dma_dst.name
            tok = (own, dma_dst.dcnt)
        else:
            tok = (own, idx)
        self.ops[eng].append((fn, waits, dma_dst, idx, dma_inc))
        for b in reads:
            if b.r.get(tok[0], -1) < tok[1]:
                b.r[tok[0]] = tok[1]
        for b in writes:
            if acc:
                b.w[tok[0]] = tok[1]
            else:
                b.w = {tok[0]: tok[1]}
            b.r = {}
        return tok

    def fence(self, eng, bufs):
        self.op(eng, None, reads=bufs)

    def mm(self, out, lhsT, rhs, start, stop, reads, w, **kw):
        self.op("pe", lambda e: e.matmul(out, lhsT, rhs, start=start, stop=stop, **kw),
                reads=reads, writes=[w], acc=True)

    def dma(self, eng, out, in_, src=None, dst=None, dbuf=None, **kw):
        if dst is None:
            dst = dbuf
        reads = [src] if src is not None else []
        writes = [dst] if dst is not None else []
        if src is not None and not src.is_dram and (dst is None or dst.is_dram):
            owner = src
        else:
            owner = dst
        self.op(eng, lambda e: e.dma_start(out=out, in_=in_, **kw), reads=reads, writes=writes,
                dma_dst=owner, acc=True)

    def emit(self):
        nc = self.nc
        es = self.es
        rank = {}
        esems = {}
        for e in self.ENGS:
            sig = sorted(self.signal[e])
            rank[e] = {idx: i + 1 for i, idx in enumerate(sig)}
            n = (len(sig) + SEM_LIM - 1) // SEM_LIM
            esems[e] = [es.enter_context(nc.semaphore(f"s_{e}{i}")) for i in range(max(n, 1))]
        for i, b in enumerate(self.dbufs):
            b.dsem = es.enter_context(nc.semaphore(f"d_{i}"))
        dmap = {id(b): b for b in self.dbufs}
        self.nsem = sum(len(v) for v in esems.values()) + len(self.dbufs)

        def sem_val(k, v):
            if k[0] == "E":
                r = rank[k[1]][v] - 1
                return esems[k[1]][r // SEM_LIM], r % SEM_LIM + 1
            return dmap[k[1]].dsem, v

        def run(e, name):
            for fn, waits, dma_dst, idx, dma_inc in self.ops[name]:
                for k, v in waits:
                    s, val = sem_val(k, v)
                    e.wait_ge(s, val)
                if fn is None:
                    continue
                ins = fn(e)
                if dma_dst is not None:
                    ins.then_inc(dma_dst.dsem, dma_inc)
                elif idx in rank[name]:
                    s, _ = sem_val(("E", name), idx)
                    ins.then_inc(s, 1)

        block = es.enter_context(nc.Block())

        @block.sync
        def _(e):
            run(e, "sp")

        @block.tensor
        def _(e):
            run(e, "pe")

        @block.scalar
        def _(e):
            run(e, "act")

        @block.vector
        def _(e):
            run(e, "dve")

        @block.gpsimd
        def _(e):
            run(e, "pool")


class Arena:
    def __init__(self, P, nbytes):
        self.P = P
        self.n = nbytes // 2
        self.t = P.es.enter_context(P.nc.sbuf_tensor("arena", [128, self.n], BF16))
        self.off = 0
        self.cur = []
        self.prev = {}

    def reset(self):
        for b in self.cur:
            for d in (b.w, b.r):
                for k, v in d.items():
                    if self.prev.get(k, -1) < v:
                        self.prev[k] = v
        self.cur = []
        self.off = 0

    def mark(self):
        return (self.off, len(self.cur))

    def release(self, m):
        for b in self.cur[m[1]:]:
            for d in (b.w, b.r):
                for k, v in d.items():
                    if self.prev.get(k, -1) < v:
                        self.prev[k] = v
        self.cur = self.cur[:m[1]]
        self.off = m[0]

    def alloc(self, name, shape, dt, parts=128):
        esz = 4 if dt == F32 else 2
        n = int(np.prod(shape))
        units = (n * esz + 1) // 2
        units = (units + 15) // 16 * 16
        assert self.off + units <= self.n, (name, self.off * 2, units * 2)
        ap = self.t[0:parts, self.off:self.off + units]
        self.off += units
        if dt == F32:
            ap = ap.bitcast(F32)
        ap = ap[:, 0:n]
        if len(shape) == 2:
            ap = ap.rearrange("p (a b) -> p a b", b=shape[1])
        elif len(shape) == 3:
            ap = ap.rearrange("p (a b c) -> p a b c", b=shape[1], c=shape[2])
        b = Buf(name, ap)
        b.r = dict(self.prev)
        self.cur.append(b)
        return b


ALPHA = (2.0 * 2) ** 0.25
A_SCALE = 192.0 ** -0.5
B_SCALE = 128.0 ** -0.5


class Ctx:
    pass


def setup_common(nc, es, consts_d, arena_kb=170):
    C = Ctx()
    P = Prog(nc, es)
    C.P = P
    C.nc = nc
    C.cst = P.sb("cst", [128, 3, 128], F32)
    P.dma("sp", C.cst.t[:], consts_d, dst=C.cst)
    C.idf = C.cst.t[:, 0, :]
    C.onesf = C.cst.t[:, 1, :]
    C.perm = C.cst.t[:, 2, :]
    C.cb = P.sb("cb", [128, 2, 128], BF16)
    P.op("dve", lambda e: e.tensor_copy(C.cb.t[:], C.cst.t[:, 0:2, :]), reads=[C.cst], writes=[C.cb])
    C.idb = C.cb.t[:, 0, :]
    C.onesb = C.cb.t[:, 1, :]
    C.eps = P.sb("eps", [128, 1], F32)
    P.op("pool", lambda e: e.memset(C.eps.t[:], 1e-6), writes=[C.eps])
    C.modT = P.sb("modT", [128, 32, 2], F32)
    C.stg = [P.sb(f"stg{i}", [128, 16 * 256], F32) for i in range(2)]
    C.nstg = 0
    C.psb = [P.ps(f"ps{i}") for i in range(8)]
    C.nps = 0
    C.ar = Arena(P, arena_kb * 1024)
    return C


def nextps(C, lo=0, hi=8):
    b = C.psb[lo + C.nps % (hi - lo)]
    C.nps += 1
    return b


def emit_modulation(C, cvec_d, ada_w_d, ada_b_d, gbc, want_gate_row=0):
    P, ar = C.P, C.ar
    cv = ar.alloc("cv", [32], F32)
    sv = ar.alloc("sv", [16, 2], F32)
    adb = ar.alloc("adb", [6144], F32, parts=2)
    m2 = ar.alloc("m2", [6144], F32, parts=2)
    P.dma("sp", cv.t, cvec_d, dst=cv)
    P.dma("sp", adb.t, ada_b_d, dst=adb)
    P.op("act", lambda e: e.activation(sv.t.rearrange("p a b -> p (a b)"), cv.t, AF.Silu), reads=[cv], writes=[sv])
    awv = ada_w_d.rearrange("(kc p) n -> p kc n", p=128)
    for nb in range(24):
        stg = C.stg[C.nstg % 2]
        C.nstg += 1
        sv3 = stg.t.rearrange("p (kc n) -> p kc n", n=256)
        P.dma("sp", sv3, awv[:, :, nb * 256:(nb + 1) * 256], dst=stg)
        pb = nextps(C)
        for kc in range(16):
            P.mm(pb.t[0:2, 0:256], sv.t[:, kc, :], sv3[:, kc, :], kc == 0, kc == 15, [sv, stg], pb)
        sl = slice(nb * 256, (nb + 1) * 256)
        P.op("dve", lambda e, pb=pb, sl=sl: e.tensor_tensor(m2.t[:, sl], pb.t[0:2, 0:256], adb.t[:, sl], ALU.add),
             reads=[pb, adb], writes=[m2])
    pT = nextps(C)
    for j in range(32):
        P.mm(pT.t[:, 2 * j:2 * j + 2], m2.t[0:2, j * 128:(j + 1) * 128], C.idf[0:2, 0:2], True, True, [m2, C.cst], pT)
    P.op("dve", lambda e: e.tensor_copy(C.modT.t[:, 0:16, :], pT.t[:, 0:32].rearrange("p (a b) -> p a b", b=2)),
         reads=[pT], writes=[C.modT])
    P.op("dve", lambda e: e.tensor_scalar_add(C.modT.t[:, 16:32, :], pT.t[:, 32:64].rearrange("p (a b) -> p a b", b=2), 1.0),
         reads=[pT], writes=[C.modT], acc=True)
    r = want_gate_row
    for q in range(4):
        pb = nextps(C)
        P.mm(pb.t[:, :], C.onesf[r:r + 1, :], m2.t[r:r + 1, 4096 + q * 512:4096 + (q + 1) * 512], True, True, [m2, C.cst], pb)
        P.op("act", lambda e, pb=pb, q=q: e.activation(gbc.t[:, q * 512:(q + 1) * 512], pb.t[:, :], AF.Copy),
             reads=[pb], writes=[gbc], acc=True)


def prep_panel(C, Wd, KC, c0, ncols, r, Wbuf, Wap, bbuf=None, bap=None):
    P = C.P
    stg = C.stg[C.nstg % 2]
    C.nstg += 1
    s3 = stg.t[:, 0:KC * ncols].rearrange("p (kc n) -> p kc n", n=ncols)
    P.dma("sp", s3, Wd.rearrange("(kc p) n -> p kc n", p=128)[:, :, c0:c0 + ncols], dst=stg)
    if r is None:
        P.op("dve", lambda e: e.tensor_copy(Wap, s3), reads=[stg], writes=[Wbuf], acc=True)
        return
    for kc in range(KC):
        if kc % 2 == 0:
            P.op("dve", lambda e, kc=kc: e.tensor_scalar_mul(Wap[:, kc, :], s3[:, kc, :], C.modT.t[:, 16 + kc, r:r + 1]),
                 reads=[stg, C.modT], writes=[Wbuf], acc=True)
        else:
            P.op("act", lambda e, kc=kc: e.activation(Wap[:, kc, :], s3[:, kc, :], AF.Copy, scale=C.modT.t[:, 16 + kc, r:r + 1]),
                 reads=[stg, C.modT], writes=[Wbuf], acc=True)
    pb = nextps(C)
    nch = (ncols + 127) // 128
    for j in range(nch):
        M = min(128, ncols - j * 128)
        for kc in range(KC):
            P.mm(pb.t[0:M, j:j + 1], s3[:, kc, j * 128:j * 128 + M], C.modT.t[:, kc, r:r + 1], kc == 0, kc == KC - 1,
                 [stg, C.modT], pb)
    nfull = ncols // 128
    if nfull:
        P.op("dve", lambda e: e.tensor_copy(bap[:, 0:nfull], pb.t[:, 0:nfull]), reads=[pb], writes=[bbuf], acc=True)
    if nch > nfull:
        Ml = ncols - nfull * 128
        P.op("dve", lambda e: e.tensor_copy(bap[0:Ml, nfull:nch], pb.t[0:Ml, nfull:nch]), reads=[pb], writes=[bbuf], acc=True)


def ln_tile(C, W, xrows_d, t):
    P = C.P
    xt = W.xt[t % 2]
    st = W.st[t % 2]
    mv = W.mv[t % 2]
    rs = W.rs[t % 2]
    xh = W.xh[t % 2]
    P.dma("sp", xt.t, xrows_d, src=getattr(W, "xsrc", None), dst=xt)
    for c in range(4):
        P.op("dve", lambda e, c=c: e.bn_stats(st.t[:, c, :], xt.t[:, c * 512:(c + 1) * 512]), reads=[xt], writes=[st], acc=True)
    P.op("dve", lambda e: e.bn_aggr(mv.t, st.t), reads=[st], writes=[mv])
    P.op("act", lambda e: e.activation(rs.t, mv.t[:, 1:2], AF.Sqrt, bias=C.eps.t[:, 0:1], scale=1.0), reads=[mv, C.eps], writes=[rs])
    P.op("dve", lambda e: e.reciprocal(rs.t, rs.t), reads=[rs], writes=[rs])
    P.op("dve", lambda e: e.tensor_scalar(xh.t, xt.t, mv.t[:, 0:1], rs.t[:, 0:1], ALU.subtract, ALU.mult),
         reads=[xt, mv, rs], writes=[xh])
    return xh


def ln_alloc(C, W):
    ar = C.ar
    W.xt = [ar.alloc(f"xt{i}", [2048], F32) for i in range(2)]
    W.st = [ar.alloc(f"st{i}", [4, 6], F32) for i in range(2)]
    W.mv = [ar.alloc(f"mv{i}", [2], F32) for i in range(2)]
    W.rs = [ar.alloc(f"rs{i}", [1], F32) for i in range(2)]
    W.xh = [ar.alloc(f"xh{i}", [2048], BF16) for i in range(2)]


def ln_transpose(C, W, x_d, row0, ntiles, hT, col0=0):
    P = C.P
    for t in range(ntiles):
        xh = ln_tile(C, W, x_d[row0 + t * 128: row0 + (t + 1) * 128, :], W.lnc)
        W.lnc += 1
        for half in range(2):
            pb = nextps(C)
            pv = pb.t[:].bitcast(BF16)
            for j in range(8):
                kc = half * 8 + j
                P.op("pe", lambda e, pv=pv, j=j, kc=kc, xh=xh: e.transpose(pv[:, j * 128:(j + 1) * 128], xh.t[:, kc * 128:(kc + 1) * 128], C.idb),
                     reads=[xh, C.cb], writes=[pb], acc=True)
            dst = hT.t[:, half * 8:(half + 1) * 8, col0 + t * 128: col0 + (t + 1) * 128]
            src = pv.rearrange("p (a b) -> p a b", b=128)
            if half == 0:
                P.op("act", lambda e, dst=dst, src=src: e.activation(dst, src, AF.Copy), reads=[pb], writes=[hT], acc=True)
            else:
                P.op("dve", lambda e, dst=dst, src=src: e.tensor_copy(dst, src), reads=[pb], writes=[hT], acc=True)


def proj(C, pb, M, nt, Wbuf, Wap, c0, KC, hT, tok0):
    for kc in range(KC):
        C.P.mm(pb.t[0:M, 0:nt], Wap[:, kc, c0:c0 + M], hT.t[:, kc, tok0:tok0 + nt], kc == 0, kc == KC - 1, [Wbuf, hT], pb)


def rstd_from(C, W, zs, nfeat, nt):
    P = C.P
    pss = nextps(C)
    for i, (zb, zap) in enumerate(zs):
        sq = W.sq[W.nsq % 2]
        W.nsq += 1
        P.op("act", lambda e, sq=sq, zap=zap: e.activation(sq.t[:, 0:nt], zap, AF.Square), reads=[zb], writes=[sq])
        P.mm(pss.t[:, 0:nt], C.onesf, sq.t[:, 0:nt], i == 0, i == len(zs) - 1, [sq, C.cst], pss)
    rstd = W.rstd[W.nrs % 2]
    W.nrs += 1
    P.op("act", lambda e: e.activation(rstd.t[:, 0:nt], pss.t[:, 0:nt], AF.Sqrt, bias=C.eps.t[:, 0:1], scale=1.0 / nfeat),
         reads=[pss, C.eps], writes=[rstd])
    P.op("dve", lambda e: e.reciprocal(rstd.t[:, 0:nt], rstd.t[:, 0:nt]), reads=[rstd], writes=[rstd])
    return rstd


def rope(C, W, dst_buf, dst_ap, src_buf, src_ap, tab, M, nt):
    P = C.P
    pw = nextps(C)
    P.mm(pw.t[0:M, 0:nt], C.perm[0:M, 0:M], src_ap, True, True, [src_buf, C.cst], pw)
    t1 = W.t1[W.nt1 % 2]
    t2 = W.t2[W.nt1 % 2]
    W.nt1 += 1
    P.op("pool", lambda e: e.tensor_tensor(t1.t[0:M, 0:nt], src_ap, tab.t[0:M, 0, 0:nt], ALU.mult), reads=[src_buf, tab], writes=[t1])
    P.op("dve", lambda e: e.tensor_tensor(t2.t[0:M, 0:nt], pw.t[0:M, 0:nt], tab.t[0:M, 1, 0:nt], ALU.mult), reads=[pw, tab], writes=[t2])
    P.op("pool", lambda e: e.tensor_tensor(dst_ap, t1.t[0:M, 0:nt], t2.t[0:M, 0:nt], ALU.add), reads=[t1, t2], writes=[dst_buf])


def work_alloc(C, W):
    ar = C.ar
    W.sq = [ar.alloc(f"sq{i}", [512], F32) for i in range(2)]
    W.rstd = [ar.alloc(f"rstd{i}", [512], F32) for i in range(2)]
    W.t1 = [ar.alloc(f"t1{i}", [512], F32) for i in range(2)]
    W.t2 = [ar.alloc(f"t2{i}", [512], F32) for i in range(2)]
    W.nsq = W.nrs = W.nt1 = 0


NKV = 8448
NKT = 66


def build_stage_a(fused=False):
    nc = bass.Bass("TRN2", target_bir_lowering=False)

    def din(name, shape, dt=F32):
        return nc.dram_tensor(name, list(shape), dt, kind="ExternalInput").ap()

    xkv = din("xkv", [NKV, 2048])
    xq = din("xq", [2048, 2048])
    cvec = din("cvec", [128, 32])
    ada_w = din("ada_w", [2048, 6144])
    ada_b = din("ada_b", [2, 6144])
    w_in = din("w_in", [2048, 4928])
    wq_b = din("wq_b", [768, 1536])
    wkv_b = din("wkv_b", [512, 2048])
    w_out = din("w_out", [2048, 2048])
    gains = din("gains", [128, 12])
    lngb = din("lngb", [2, 2048])
    consts = din("consts", [128, 3, 128])
    ropeB_kv = din("ropeB_kv", [128, 2, NKV])
    ropeA_kv = din("ropeA_kv", [64, 2, NKV])
    ropeB_q = din("ropeB_q", [128, 2, 2048])
    ropeA_q = din("ropeA_q", [64, 2, 2048])
    x1 = None if fused else nc.dram_tensor("x1", [2048, 2048], F32, kind="ExternalOutput").ap()

    with ExitStack() as es:
        C = setup_common(nc, es, consts)
        P, ar = C.P, C.ar
        gn = P.sb("gn", [128, 12], F32)
        P.dma("sp", gn.t[:], gains, dst=gn)
        KaT = P.dram("KaT", [8, 128, NKV], BF16)
        KpeT = P.dram("KpeT", [64, NKV], BF16)
        Va = P.dram("Va", [NKT, 128, 1024], BF16)
        KbT = P.dram("KbT", [2, 128, NKV], BF16)
        Vb = P.dram("Vb", [NKT, 128, 256], BF16)
        QaT = P.dram("QaT", [8, 128, 2048], BF16)
        QpeT = P.dram("QpeT", [8, 64, 2048], BF16)
        QbT = P.dram("QbT", [8, 128, 2048], BF16)
        Gd = P.dram("Gd", [16, 128, 2048], BF16)
        Yd = P.dram("Yd", [16, 128, 2048], BF16)

        gbc = ar.alloc("gbc_tmp", [2048], F32)
        emit_modulation(C, cvec, ada_w, ada_b[:, :], gbc, 0)
        Gbc_d = P.dram("Gbc_d", [128, 2048], F32)
        P.dma("sp", Gbc_d.t, gbc.t, src=gbc, dst=Gbc_d)

        def kv_phase(r, blocks):
            ar.reset()
            W = Ctx()
            Wkv = ar.alloc("Wkv", [16, 1088], BF16)
            Bkv = ar.alloc("Bkv", [9], F32)
            wkvb = ar.alloc("wkvb", [4, 2048], BF16)
            ln_alloc(C, W)
            W.lnc = 0
            work_alloc(C, W)
            hTs = [ar.alloc(f"hT{i}", [16, 512], BF16) for i in range(1)]
            zc = ar.alloc("zc", [4, 512], F32)
            ckvn = ar.alloc("ckvn", [4, 512], BF16)
            zk = [ar.alloc(f"zk{i}", [512], F32) for i in range(2)]
            kn = [ar.alloc(f"kn{i}", [512], F32) for i in range(2)]
            tabB = [ar.alloc(f"tabB{i}", [2, 512], F32) for i in range(1)]
            tabA = [ar.alloc(f"tabA{i}", [2, 512], F32) for i in range(1)]
            ko = [ar.alloc(f"ko{i}", [512], BF16) for i in range(4)]
            vT = [ar.alloc(f"vT{i}", [512], BF16) for i in range(2)]
            vo = [ar.alloc(f"vo{i}", [1024], BF16) for i in range(2)]
            vbo = [ar.alloc(f"vbo{i}", [256], BF16) for i in range(2)]
            panels = [(768, 256, 0, 0), (1024, 256, 256, 2), (1280, 64, 512, 4), (2368, 256, 576, 5), (2624, 256, 832, 7)]
            for (c0, ncols, l0, b0) in panels:
                nch = (ncols + 127) // 128
                prep_panel(C, w_in, 16, c0, ncols, r, Wkv, Wkv.t[:, :, l0:l0 + ncols], Bkv, Bkv.t[:, b0:b0 + nch])
            for j in range(8):
                prep_panel(C, wkv_b, 4, j * 256, 256, None, wkvb, wkvb.t[:, :, j * 256:(j + 1) * 256])
            nko = 0
            for bi, (row0, nt) in enumerate(blocks):
                ntl = nt // 128
                hT = hTs[0]
                ln_transpose(C, W, xkv, row0, ntl, hT)
                tb = tabB[0]
                ta = tabA[0]
                P.dma("sp", tb.t[:, :, 0:nt], ropeB_kv[:, :, row0:row0 + nt], dst=tb)
                P.dma("sp", ta.t[0:64, :, 0:nt], ropeA_kv[:, :, row0:row0 + nt], dst=ta)
                for j in range(4):
                    pb = nextps(C)
                    proj(C, pb, 128, nt, Wkv, Wkv.t, j * 128, 16, hT, 0)
                    P.op("act", lambda e, pb=pb, j=j: e.activation(zc.t[:, j, 0:nt], pb.t[:, 0:nt], AF.Identity, bias=Bkv.t[:, j:j + 1]),
                         reads=[pb, Bkv], writes=[zc], acc=True)
                rstd = rstd_from(C, W, [(zc, zc.t[:, j, 0:nt]) for j in range(4)], 512, nt)
                for j in range(4):
                    P.op("dve", lambda e, j=j, rstd=rstd: e.scalar_tensor_tensor(ckvn.t[:, j, 0:nt], zc.t[:, j, 0:nt], gn.t[:, 6 + j:7 + j], rstd.t[:, 0:nt], ALU.mult, ALU.mult),
                         reads=[zc, gn, rstd], writes=[ckvn], acc=True)
                pb = nextps(C)
                proj(C, pb, 64, nt, Wkv, Wkv.t, 512, 16, hT, 0)
                z = zk[0]
                P.op("act", lambda e, pb=pb, z=z: e.activation(z.t[0:64, 0:nt], pb.t[0:64, 0:nt], AF.Identity, bias=Bkv.t[0:64, 4:5]),
                     reads=[pb, Bkv], writes=[z])
                o = ko[nko % 4]; nko += 1
                rope(C, W, o, o.t[0:64, 0:nt], z, z.t[0:64, 0:nt], ta, 64, nt)
                P.dma("sp", KpeT.t[:, row0:row0 + nt], o.t[0:64, 0:nt], src=o, dst=KpeT)
                for hh in range(2):
                    pb = nextps(C)
                    proj(C, pb, 128, nt, Wkv, Wkv.t, 576 + hh * 128, 16, hT, 0)
                    z = zk[1 - hh % 2] if False else zk[hh % 2]
                    P.op("act", lambda e, pb=pb, z=z, hh=hh: e.activation(z.t[:, 0:nt], pb.t[:, 0:nt], AF.Identity, bias=Bkv.t[:, 5 + hh:6 + hh]),
                         reads=[pb, Bkv], writes=[z])
                    rstd = rstd_from(C, W, [(z, z.t[:, 0:nt])], 128, nt)
                    k_ = kn[hh % 2]
                    P.op("dve", lambda e, z=z, k_=k_, rstd=rstd: e.scalar_tensor_tensor(k_.t[:, 0:nt], z.t[:, 0:nt], gn.t[:, 11:12], rstd.t[:, 0:nt], ALU.mult, ALU.mult),
                         reads=[z, gn, rstd], writes=[k_])
                    o = ko[nko % 4]; nko += 1
                    rope(C, W, o, o.t[:, 0:nt], k_, k_.t[:, 0:nt], tb, 128, nt)
                    P.dma("sp", KbT.t[hh, :, row0:row0 + nt], o.t[:, 0:nt], src=o, dst=KbT)
                for hh in range(2):
                    pb = nextps(C)
                    proj(C, pb, 128, nt, Wkv, Wkv.t, 832 + hh * 128, 16, hT, 0)
                    v_ = vT[hh % 2]
                    P.op("act", lambda e, pb=pb, v_=v_, hh=hh: e.activation(v_.t[:, 0:nt], pb.t[:, 0:nt], AF.Identity, bias=Bkv.t[:, 7 + hh:8 + hh]),
                         reads=[pb, Bkv], writes=[v_])
                    W.vbT = getattr(W, "vbT", {})
                    W.vbT[hh] = v_
                for t in range(ntl):
                    pb = nextps(C)
                    pv = pb.t[:].bitcast(BF16)
                    for hh in range(2):
                        v_ = W.vbT[hh]
                        P.op("pe", lambda e, pv=pv, hh=hh, v_=v_, t=t: e.transpose(pv[:, hh * 128:(hh + 1) * 128], v_.t[:, t * 128:(t + 1) * 128], C.idb),
                             reads=[v_, C.cb], writes=[pb], acc=True)
                    vb_ = vbo[t % 2]
                    P.op("dve", lambda e, pv=pv, vb_=vb_: e.tensor_copy(vb_.t, pv[:, 0:256]), reads=[pb], writes=[vb_])
                    P.dma("sp", Vb.t[row0 // 128 + t, :, :], vb_.t, src=vb_, dst=Vb)
                for h in range(8):
                    pb = nextps(C)
                    for j in range(4):
                        P.mm(pb.t[:, 0:nt], wkvb.t[:, j, h * 256:h * 256 + 128], ckvn.t[:, j, 0:nt], j == 0, j == 3, [wkvb, ckvn], pb)
                    o = ko[nko % 4]; nko += 1
                    if h % 2 == 0:
                        P.op("act", lambda e, pb=pb, o=o: e.activation(o.t[:, 0:nt], pb.t[:, 0:nt], AF.Copy), reads=[pb], writes=[o])
                    else:
                        P.op("dve", lambda e, pb=pb, o=o: e.tensor_copy(o.t[:, 0:nt], pb.t[:, 0:nt]), reads=[pb], writes=[o])
                    P.dma("sp", KaT.t[h, :, row0:row0 + nt], o.t[:, 0:nt], src=o, dst=KaT)
                wv = wkvb.t.rearrange("p k (h two d) -> p k h two d", two=2, d=128)
                for t in range(ntl):
                    v2 = vo[t % 2]
                    for half in range(2):
                        pb = nextps(C)
                        for j in range(4):
                            P.mm(pb.t[:, :].rearrange("p (h d) -> p h d", d=128), ckvn.t[:, j, t * 128:(t + 1) * 128],
                                 wv[:, j, half * 4:(half + 1) * 4, 1, :], j == 0, j == 3, [wkvb, ckvn], pb)
                        if half == 0:
                            P.op("act", lambda e, pb=pb, v2=v2: e.activation(v2.t[:, 0:512], pb.t[:, :], AF.Copy), reads=[pb], writes=[v2], acc=True)
                        else:
                            P.op("dve", lambda e, pb=pb, v2=v2: e.tensor_copy(v2.t[:, 512:1024], pb.t[:, :]), reads=[pb], writes=[v2], acc=True)
                    P.dma("sp", Va.t[row0 // 128 + t, :, :], v2.t, src=v2, dst=Va)

        kv_phase(1, [(0, 256)])
        kv_phase(0, [(256 + i * 512, 512) for i in range(16)])

        ar.reset()
        W = Ctx()
        hTq = ar.alloc("hTq", [16, 2048], BF16)
        mk = ar.mark()
        ln_alloc(C, W)
        W.lnc = 0
        ln_transpose(C, W, xq, 0, 16, hTq)
        ar.release(mk)
        mk = ar.mark()
        Wcq = ar.alloc("Wcq", [16, 768], BF16)
        Bcq = ar.alloc("Bcq", [6], F32)
        wqb = ar.alloc("wqb", [6, 1536], BF16)
        work_alloc(C, W)
        zq = ar.alloc("zq", [6, 512], F32)
        cqn = ar.alloc("cqn", [6, 512], BF16)
        zk = [ar.alloc(f"qzk{i}", [512], F32) for i in range(2)]
        tabA = [ar.alloc(f"qtabA{i}", [2, 512], F32) for i in range(1)]
        ko = [ar.alloc(f"qko{i}", [512], BF16) for i in range(4)]
        for j in range(3):
            prep_panel(C, w_in, 16, j * 256, 256, 0, Wcq, Wcq.t[:, :, j * 256:(j + 1) * 256], Bcq, Bcq.t[:, 2 * j:2 * j + 2])
        for j in range(6):
            prep_panel(C, wq_b, 6, j * 256, 256, None, wqb, wqb.t[:, :, j * 256:(j + 1) * 256])
        nko = 0
        for tbi in range(4):
            tok0 = tbi * 512
            ta = tabA[0]
            P.dma("sp", ta.t[0:64, :, :], ropeA_q[:, :, tok0:tok0 + 512], dst=ta)
            for j in range(6):
                pb = nextps(C)
                proj(C, pb, 128, 512, Wcq, Wcq.t, j * 128, 16, hTq, tok0)
                P.op("act", lambda e, pb=pb, j=j: e.activation(zq.t[:, j, :], pb.t[:, :], AF.Identity, bias=Bcq.t[:, j:j + 1]),
                     reads=[pb, Bcq], writes=[zq], acc=True)
            rstd = rstd_from(C, W, [(zq, zq.t[:, j, :]) for j in range(6)], 768, 512)
            for j in range(6):
                P.op("dve", lambda e, j=j, rstd=rstd: e.scalar_tensor_tensor(cqn.t[:, j, :], zq.t[:, j, :], gn.t[:, j:j + 1], rstd.t[:, :], ALU.mult, ALU.mult),
                     reads=[zq, gn, rstd], writes=[cqn], acc=True)
            for h in range(8):
                pb = nextps(C)
                for j in range(6):
                    P.mm(pb.t[:, :], wqb.t[:, j, h * 192:h * 192 + 128], cqn.t[:, j, :], j == 0, j == 5, [wqb, cqn], pb)
                o = ko[nko % 4]; nko += 1
                P.op("act", lambda e, pb=pb, o=o: e.activation(o.t[:, :], pb.t[:, :], AF.Copy), reads=[pb], writes=[o])
                P.dma("sp", QaT.t[h, :, tok0:tok0 + 512], o.t[:, :], src=o, dst=QaT)
                pb = nextps(C)
                for j in range(6):
                    P.mm(pb.t[0:64, :], wqb.t[:, j, h * 192 + 128:h * 192 + 192], cqn.t[:, j, :], j == 0, j == 5, [wqb, cqn], pb)
                z = zk[h % 2]
                P.op("dve", lambda e, pb=pb, z=z: e.tensor_copy(z.t[0:64, :], pb.t[0:64, :]), reads=[pb], writes=[z])
                o = ko[nko % 4]; nko += 1
                rope(C, W, o, o.t[0:64, :], z, z.t[0:64, :], ta, 64, 512)
                P.dma("sp", QpeT.t[h, :, tok0:tok0 + 512], o.t[0:64, :], src=o, dst=QpeT)
        ar.release(mk)
        mk = ar.mark()
        work_alloc(C, W)
        zk = [ar.alloc(f"qzk{i}", [512], F32) for i in range(2)]
        kn = [ar.alloc(f"qkn{i}", [512], F32) for i in range(2)]
        tabB = [ar.alloc(f"qtabB{i}", [2, 512], F32) for i in range(1)]
        ko = [ar.alloc(f"qko{i}", [512], BF16) for i in range(4)]
        Wp = [ar.alloc(f"Wp{i}", [16, 256], BF16) for i in range(2)]
        Bp = [ar.alloc(f"Bp{i}", [2], F32) for i in range(2)]
        for pi in range(12):
            wp = Wp[pi % 2]
            bp = Bp[pi % 2]
            c0 = 1344 + pi * 256 if pi < 4 else 2880 + (pi - 4) * 256
            prep_panel(C, w_in, 16, c0, 256, 0, wp, wp.t, bp, bp.t)
            for tbi in range(4):
                tok0 = tbi * 512
                if pi < 4:
                    tb = tabB[0]
                    P.dma("sp", tb.t[:, :, :], ropeB_q[:, :, tok0:tok0 + 512], dst=tb)
                for cc in range(2):
                    pb = nextps(C)
                    proj(C, pb, 128, 512, wp, wp.t, cc * 128, 16, hTq, tok0)
                    o = ko[nko % 4]; nko += 1
                    if pi < 4:
                        hh = pi * 2 + cc
                        z = zk[cc]
                        P.op("act", lambda e, pb=pb, z=z, bp=bp, cc=cc: e.activation(z.t[:, :], pb.t[:, :], AF.Identity, bias=bp.t[:, cc:cc + 1]),
                             reads=[pb, bp], writes=[z])
                        rstd = rstd_from(C, W, [(z, z.t[:, :])], 128, 512)
                        k_ = kn[cc]
                        P.op("dve", lambda e, z=z, k_=k_, rstd=rstd: e.scalar_tensor_tensor(k_.t[:, :], z.t[:, :], gn.t[:, 10:11], rstd.t[:, :], ALU.mult, ALU.mult),
                             reads=[z, gn, rstd], writes=[k_])
                        rope(C, W, o, o.t[:, :], k_, k_.t[:, :], tb, 128, 512)
                        P.dma("sp", QbT.t[hh, :, tok0:tok0 + 512], o.t[:, :], src=o, dst=QbT)
                    else:
                        ch = (pi - 4) * 2 + cc
                        P.op("act", lambda e, pb=pb, o=o, bp=bp, cc=cc: e.activation(o.t[:, :], pb.t[:, :], AF.Silu, bias=bp.t[:, cc:cc + 1]),
                             reads=[pb, bp], writes=[o])
                        P.dma("sp", Gd.t[ch, :, tok0:tok0 + 512], o.t[:, :], src=o, dst=Gd)

        ar.reset()
        Kt = [ar.alloc(f"Kt{i}", [NKV], BF16) for i in range(2)]
        Vt = [ar.alloc(f"Vt{i}", [NKT, 128], BF16) for i in range(2)]
        Kpe = ar.alloc("Kpe", [NKV], BF16)
        Qt = [ar.alloc(f"Qt{i}", [2048], BF16) for i in range(2)]
        Qp = [ar.alloc(f"Qp{i}", [2048], BF16) for i in range(2)]
        Gt = [ar.alloc(f"Gt{i}", [2048], BF16) for i in range(2)]
        PT = [ar.alloc(f"PT{i}", [512], BF16) for i in range(6)]
        rc = [ar.alloc(f"rc{i}", [512], F32) for i in range(2)]
        yt = [ar.alloc(f"yt{i}", [512], F32) for i in range(2)]
        yo = [ar.alloc(f"yo{i}", [512], BF16) for i in range(2)]
        P.dma("sp", Kpe.t[0:64, :], KpeT.t, src=KpeT, dst=Kpe)
        SPS = C.psb[0:4]
        OACC = C.psb[4:6]
        SACC = C.psb[6:8]
        kvslot = -1
        u = 0
        for hd in range(16):
            isA = hd < 8
            if isA or (hd - 8) % 4 == 0:
                kvslot += 1
                kt_, vt_ = Kt[kvslot % 2], Vt[kvslot % 2]
                if isA:
                    P.dma("sp", kt_.t, KaT.t[hd], src=KaT, dst=kt_)
                    P.dma("sp", vt_.t, Va.t[:, :, hd * 128:(hd + 1) * 128].rearrange("t p d -> p t d"), src=Va, dst=vt_)
                else:
                    kvh = (hd - 8) // 4
                    P.dma("sp", kt_.t, KbT.t[kvh], src=KbT, dst=kt_)
                    P.dma("sp", vt_.t, Vb.t[:, :, kvh * 128:(kvh + 1) * 128].rearrange("t p d -> p t d"), src=Vb, dst=vt_)
            qt_ = Qt[hd % 2]
            qp_ = Qp[hd % 2]
            gt_ = Gt[hd % 2]
            if isA:
                P.dma("sp", qt_.t, QaT.t[hd], src=QaT, dst=qt_)
                P.dma("sp", qp_.t[0:64, :], QpeT.t[hd], src=QpeT, dst=qp_)
            else:
                P.dma("sp", qt_.t, QbT.t[hd - 8], src=QbT, dst=qt_)
            P.dma("sp", gt_.t, Gd.t[hd], src=Gd, dst=gt_)
            scale = A_SCALE if isA else B_SCALE
            for qb in range(4):
                oT = OACC[u % 2]
                sm = SACC[u % 2]
                qs = slice(qb * 512, (qb + 1) * 512)

                def qk(kt):
                    sp_ = SPS[kt % 4]
                    ks = slice(kt * 128, (kt + 1) * 128)
                    P.mm(sp_.t[:, :], kt_.t[:, ks], qt_.t[:, qs], True, not isA, [kt_, qt_], sp_)
                    if isA:
                        P.mm(sp_.t[:, :], Kpe.t[0:64, ks], qp_.t[0:64, qs], False, True, [Kpe, qp_], sp_)

                def rest(kt):
                    sp_ = SPS[kt % 4]
                    pt = PT[kt % 6]
                    P.op("act", lambda e, pt=pt, sp_=sp_, sc=scale: e.activation(pt.t, sp_.t[:, :], AF.Exp, scale=sc), reads=[sp_], writes=[pt])
                    P.mm(oT.t[:, :], vt_.t[:, kt, :], pt.t, kt == 0, kt == NKT - 1, [vt_, pt], oT)
                    P.mm(sm.t[:, :], C.onesb, pt.t, kt == 0, kt == NKT - 1, [pt, C.cb], sm)

                qk(0)
                qk(1)
                for kt in range(NKT):
                    if kt + 2 < NKT:
                        qk(kt + 2)
                    rest(kt)
                r_ = rc[u % 2]
                y_ = yt[u % 2]
                o_ = yo[u % 2]
                P.op("dve", lambda e, r_=r_, sm=sm: e.reciprocal(r_.t, sm.t[:, :]), reads=[sm], writes=[r_])
                P.op("dve", lambda e, r_=r_, y_=y_, oT=oT: e.tensor_tensor(y_.t, oT.t[:, :], r_.t, ALU.mult), reads=[oT, r_], writes=[y_])
                P.op("pool", lambda e, y_=y_, o_=o_, gt_=gt_, qs=qs: e.tensor_tensor(o_.t, y_.t, gt_.t[:, qs], ALU.mult), reads=[y_, gt_], writes=[o_])
                P.dma("sp", Yd.t[hd, :, qs], o_.t, src=o_, dst=Yd)
                u += 1

        ar.reset()
        W = Ctx()
        Wo = ar.alloc("Wo", [16, 2048], BF16)
        Gys = [ar.alloc(f"Gy{i}", [16, 128], BF16) for i in range(2)]
        gb2 = ar.alloc("gb2", [2048], F32)
        lnbc = ar.alloc("lnbc", [2, 2048], F32)
        xts = [ar.alloc(f"oxt{i}", [2048], F32) for i in range(2)]
        tmp = [ar.alloc(f"otmp{i}", [2048], F32) for i in range(1)]
        st = ar.alloc("ost", [4, 6], F32)
        mv = ar.alloc("omv", [2], F32)
        rs = ar.alloc("ors", [1], F32)
        P.dma("sp", gb2.t, Gbc_d.t, src=Gbc_d, dst=gb2)
        P.dma("sp", lnbc.t[:, 0, :], lngb[0:1, :].partition_broadcast(128), dst=lnbc)
        P.dma("sp", lnbc.t[:, 1, :], lngb[1:2, :].partition_broadcast(128), dst=lnbc)
        for j in range(8):
            prep_panel(C, w_out, 16, j * 256, 256, None, Wo, Wo.t[:, :, j * 256:(j + 1) * 256])
        if fused:
            x1b = P.dram("X1d", [2048, 2048], F32)
            x1 = x1b.t
        else:
            x1b = P.view("x1out", x1)
        for t in range(16):
            xt = xts[t % 2]
            tm = tmp[0]
            P.dma("sp", xt.t, xq[t * 128:(t + 1) * 128, :], dst=xt)
            Gy = Gys[t % 2]
            P.dma("sp", Gy.t, Yd.t[:, :, t * 128:(t + 1) * 128].rearrange("c p t -> p c t"), src=Yd, dst=Gy)
            for nb in range(4):
                pb = nextps(C)
                ns = slice(nb * 512, (nb + 1) * 512)
                for kc in range(16):
                    P.mm(pb.t[:, :], Gy.t[:, kc, :], Wo.t[:, kc, ns], kc == 0, kc == 15, [Gy, Wo], pb)
                P.op("dve", lambda e, pb=pb, ns=ns, tm=tm: e.tensor_tensor(tm.t[:, ns], pb.t[:, :], gb2.t[:, ns], ALU.mult),
                     reads=[pb, gb2], writes=[tm], acc=True)
            P.op("dve", lambda e, xt=xt, tm=tm: e.scalar_tensor_tensor(tm.t, xt.t, ALPHA, tm.t, ALU.mult, ALU.add), reads=[xt, tm], writes=[tm])
            for c in range(4):
                P.op("dve", lambda e, c=c, tm=tm: e.bn_stats(st.t[:, c, :], tm.t[:, c * 512:(c + 1) * 512]), reads=[tm], writes=[st], acc=True)
            P.op("dve", lambda e: e.bn_aggr(mv.t, st.t), reads=[st], writes=[mv])
            P.op("act", lambda e: e.activation(rs.t, mv.t[:, 1:2], AF.Sqrt, bias=C.eps.t[:, 0:1], scale=1.0), reads=[mv, C.eps], writes=[rs])
            P.op("dve", lambda e: e.reciprocal(rs.t, rs.t), reads=[rs], writes=[rs])
            P.op("dve", lambda e, tm=tm: e.tensor_scalar(tm.t, tm.t, mv.t[:, 0:1], rs.t[:, 0:1], ALU.subtract, ALU.mult), reads=[tm, mv, rs], writes=[tm])
            P.op("pool", lambda e, tm=tm: e.tensor_tensor(tm.t, tm.t, lnbc.t[:, 0, :], ALU.mult), reads=[tm, lnbc], writes=[tm])
            P.op("dve", lambda e, tm=tm, xt=xt: e.tensor_tensor(xt.t, tm.t, lnbc.t[:, 1, :], ALU.add), reads=[tm, lnbc], writes=[xt])
            P.dma("sp", x1[t * 128:(t + 1) * 128, :], xt.t, src=xt, dbuf=x1b)
        if fused:
            outb = fused_tail(C, nc, din, x1b)
            P.fence("sp", [outb])
        else:
            P.fence("sp", [x1b])
        P.emit()
    return nc


RS_GROUPS = [[0, 1, 2, 3], [4, 5, 6, 7]]


def fused_tail(C, nc, din, x1b):
    P, ar = C.P, C.ar
    cvec1 = din("cvec1", [128, 32])
    ada_w1 = din("ada_w1", [2048, 6144])
    ada_b1 = din("ada_b1", [2, 6144])
    wf = din("w_in_f", [2048, 8192])
    w_out_f = din("w_out_f", [4096, 2048])
    lngb1 = din("lngb1", [2, 2048])
    sel_d = din("sel", [128, 4])
    cn_d = din("cn", [128, 2, 2, 256], BF16)
    w64_d = din("w64", [128, 128], BF16)
    M_d = din("Mtw", [128, 64, 2, 128], BF16)
    outp = nc.dram_tensor("out", [2048, 2048], F32, kind="ExternalOutput").ap()
    U_in = [P.dram(f"U_in{q}", [4 * 256, 8192], BF16) for q in range(4)]
    U_out = [P.dram(f"U_out{q}", [256, 8192], BF16) for q in range(4)]
    F_in = [P.dram(f"F_in{g}", [4 * 1024, 2048], BF16) for g in range(4)]
    F_out = [P.dram(f"F_out{g}", [1024, 2048], BF16) for g in range(4)]
    Gl = P.dram("Gl", [32, 128, 2048], BF16)
    Gbc1 = P.dram("Gbc1", [128, 2048], F32)
    Td = P.dram("Td", [16, 128, 2048], F32)
    sel = P.sb("sel_sb", [128, 4], F32)
    P.dma("sp", sel.t[:], sel_d, dst=sel)

    ar.reset()
    gbc = ar.alloc("gbc1_tmp", [2048], F32)
    emit_modulation(C, cvec1, ada_w1, ada_b1[:, :], gbc, 0)
    P.dma("sp", Gbc1.t, gbc.t, src=gbc, dst=Gbc1)
    ar.reset()
    W = Ctx()
    W.xsrc = x1b
    hT1 = ar.alloc("hT1", [16, 2048], BF16)
    mk = ar.mark()
    ln_alloc(C, W)
    W.lnc = 0
    ln_transpose(C, W, x1b.t, 0, 16, hT1)
    ar.release(mk)
    Wp = [ar.alloc(f"fWp{i}", [16, 256], BF16) for i in range(2)]
    Bp = [ar.alloc(f"fBp{i}", [2], F32) for i in range(2)]
    uo = [ar.alloc(f"fuo{i}", [512], BF16) for i in range(2)]
    us = [ar.alloc(f"fus{i}", [4, 512], BF16) for i in range(2)]
    go = [ar.alloc(f"fgo{i}", [512], BF16) for i in range(2)]
    n = 0
    order = [(True, d * 4 + q) for q in range(4) for d in range(4)] + [(False, i) for i in range(16)]
    for pi, (isu, pidx) in enumerate(order):
        wp, bp = Wp[pi % 2], Bp[pi % 2]
        c0 = pidx * 256 if isu else 4096 + pidx * 256
        prep_panel(C, wf, 16, c0, 256, 0, wp, wp.t, bp, bp.t)
        for tbi in range(4):
            tok0 = tbi * 512
            for cc in range(2):
                pb = nextps(C)
                proj(C, pb, 128, 512, wp, wp.t, cc * 128, 16, hT1, tok0)
                ch = pidx * 2 + cc
                if isu:
                    o = uo[n % 2]
                    s4 = us[n % 2]
                    n += 1
                    P.op("act", lambda e, pb=pb, o=o, bp=bp, cc=cc: e.activation(o.t, pb.t[:, :], AF.Identity, bias=bp.t[:, cc:cc + 1]),
                         reads=[pb, bp], writes=[o])
                    for j in range(4):
                        if j % 2 == 0:
                            P.op("dve", lambda e, o=o, s4=s4, j=j: e.tensor_scalar_mul(s4.t[:, j, :], o.t, sel.t[:, j:j + 1]),
                                 reads=[o, sel], writes=[s4], acc=True)
                        else:
                            P.op("act", lambda e, o=o, s4=s4, j=j: e.activation(s4.t[:, j, :], o.t, AF.Copy, scale=sel.t[:, j:j + 1]),
                                 reads=[o, sel], writes=[s4], acc=True)
                    dest, q = pidx // 4, pidx % 4
                    row0 = dest * 256 + cc * 128
                    dst_ap = U_in[q].t[row0:row0 + 128, :].rearrange("p (j t) -> p j t", j=4)[:, :, tok0:tok0 + 512]
                    P.dma("sp", dst_ap, s4.t, src=s4, dst=U_in[q])
                else:
                    o = go[n % 2]
                    n += 1
                    P.op("act", lambda e, pb=pb, o=o, bp=bp, cc=cc: e.activation(o.t, pb.t[:, :], AF.Silu, bias=bp.t[:, cc:cc + 1]),
                         reads=[pb, bp], writes=[o])
                    chp = ((ch % 8) // 2) * 8 + (ch // 8) * 2 + ch % 2
                    P.dma("sp", Gl.t[chp, :, tok0:tok0 + 512], o.t, src=o, dst=Gl)
        if isu and pidx // 4 == 3:
            q = pidx % 4
            P.op("pool", lambda e, q=q: e.collective_compute("ReduceScatter", ALU.add, replica_groups=RS_GROUPS, ins=[U_in[q].t], outs=[U_out[q].t]),
                 reads=[U_in[q]], writes=[U_out[q]], dma_dst=U_out[q], dma_inc=1)

    ar.reset()
    cn = ar.alloc("cn", [2, 2, 256], BF16)
    w64 = ar.alloc("w64", [128], BF16)
    Mt = ar.alloc("Mt", [64, 2, 128], BF16)
    P.dma("sp", cn.t, cn_d, dst=cn)
    P.dma("sp", w64.t, w64_d, dst=w64)
    P.dma("sp", Mt.t, M_d, dst=Mt)
    uT = ar.alloc("uT", [2, 8192], BF16)
    fT = ar.alloc("fT", [8192], BF16)
    z = ar.alloc("z", [128, 128], BF16)
    Y = ar.alloc("Y", [128, 128], BF16)
    fs = [ar.alloc(f"fs{i}", [2048], BF16) for i in range(2)]
    nfs = 0
    for g in range(4):
        P.dma("sp", uT.t, U_out[g].t.rearrange("(c p) t -> p c t", p=128), src=U_out[g], dst=uT)
        uv = uT.t.rearrange("p c (l1 l2) -> p c l2 l1", l2=128)
        for kh in range(2):
            ks = slice(kh * 128, (kh + 1) * 128)
            for l2p in range(32):
                pb = nextps(C)
                for q in range(4):
                    l2 = l2p * 4 + q
                    for ri in range(2):
                        for cc in range(2):
                            P.mm(pb.t[ri * 64:(ri + 1) * 64, q * 128:(q + 1) * 128], uv[:, cc, l2, :], cn.t[:, cc, ri, ks],
                                 cc == 0, cc == 1, [uT, cn], pb)
                dst = z.t[:, l2p * 4:(l2p + 1) * 4, :]
                src = pb.t[:, :].rearrange("p (a b) -> p a b", b=128)
                if l2p % 2 == 0:
                    P.op("act", lambda e, dst=dst, src=src: e.activation(dst, src, AF.Copy), reads=[pb], writes=[z], acc=True)
                else:
                    P.op("dve", lambda e, dst=dst, src=src: e.tensor_copy(dst, src), reads=[pb], writes=[z], acc=True)
            for k3p in range(32):
                pb = nextps(C)
                for q in range(4):
                    k3 = k3p * 4 + q
                    P.mm(pb.t[:, q * 128:(q + 1) * 128], z.t[:, :, k3], w64.t, True, True, [z, w64], pb)
                dst = Y.t[:, k3p * 4:(k3p + 1) * 4, :]
                src = pb.t[:, :].rearrange("p (a b) -> p a b", b=128)
                if k3p % 2 == 0:
                    P.op("act", lambda e, dst=dst, src=src: e.activation(dst, src, AF.Copy), reads=[pb], writes=[Y], acc=True)
                else:
                    P.op("dve", lambda e, dst=dst, src=src: e.tensor_copy(dst, src), reads=[pb], writes=[Y], acc=True)
            fv = fT.t.rearrange("p (k2 k1) -> p k1 k2", k1=64)
            for k1p in range(16):
                pb = nextps(C)
                for q in range(4):
                    k1 = k1p * 4 + q
                    for ri in range(2):
                        P.mm(pb.t[:, q * 128:(q + 1) * 128], Y.t[:, :, ri * 64 + k1], Mt.t[:, k1, ri, :], ri == 0, ri == 1, [Y, Mt], pb)
                fsl = fv[:, k1p * 4:(k1p + 1) * 4, :]
                src = pb.t[:, :].rearrange("p (a b) -> p a b", b=128)
                if k1p % 2 == 0:
                    P.op("act", lambda e, fsl=fsl, src=src: e.activation(fsl, src, AF.Copy, scale=FSCALE), reads=[pb], writes=[fT], acc=True)
                else:
                    P.op("dve", lambda e, fsl=fsl, src=src: e.tensor_scalar_mul(fsl, src, FSCALE), reads=[pb], writes=[fT], acc=True)
            for d in range(4):
                for j in range(4):
                    f_ = fs[nfs % 2]
                    nfs += 1
                    if j % 2 == 0:
                        P.op("dve", lambda e, f_=f_, d=d, j=j: e.tensor_scalar_mul(f_.t, fT.t[:, d * 2048:(d + 1) * 2048], sel.t[:, j:j + 1]),
                             reads=[fT, sel], writes=[f_])
                    else:
                        P.op("act", lambda e, f_=f_, d=d, j=j: e.activation(f_.t, fT.t[:, d * 2048:(d + 1) * 2048], AF.Copy, scale=sel.t[:, j:j + 1]),
                             reads=[fT, sel], writes=[f_])
                    r0 = d * 1024 + j * 256 + kh * 128
                    P.dma("sp", F_in[g].t[r0:r0 + 128, :], f_.t, src=f_, dst=F_in[g])
        P.op("pool", lambda e, g=g: e.collective_compute("ReduceScatter", ALU.add, replica_groups=RS_GROUPS, ins=[F_in[g].t], outs=[F_out[g].t]),
             reads=[F_in[g]], writes=[F_out[g]], dma_dst=F_out[g], dma_inc=1)

    ar.reset()
    Wo = ar.alloc("fWo", [32, 1024], BF16)
    gb2 = ar.alloc("fgb2", [2048], F32)
    lnbc = ar.alloc("flnbc", [2, 2048], F32)
    Fys = [ar.alloc(f"fFy{i}", [32, 128], BF16) for i in range(2)]
    Ggs = [ar.alloc(f"fGg{i}", [32, 128], BF16) for i in range(2)]
    Gys = [ar.alloc(f"fGy{i}", [32, 128], BF16) for i in range(2)]
    tms = [ar.alloc(f"ftm{i}", [1024], F32) for i in range(2)]
    xts = [ar.alloc(f"fxt{i}", [2048], F32) for i in range(1)]
    tmp = ar.alloc("fotmp", [2048], F32)
    st = ar.alloc("fost", [4, 6], F32)
    mv = ar.alloc("fomv", [2], F32)
    rs = ar.alloc("fors", [1], F32)
    P.dma("sp", gb2.t, Gbc1.t, src=Gbc1, dst=gb2)
    P.dma("sp", lnbc.t[:, 0, :], lngb1[0:1, :].partition_broadcast(128), dst=lnbc)
    P.dma("sp", lnbc.t[:, 1, :], lngb1[1:2, :].partition_broadcast(128), dst=lnbc)
    n = 0
    for nh in range(2):
        for kh in range(2):
            for j in range(4):
                c0 = nh * 1024 + j * 256
                prep_panel(C, w_out_f[kh * 2048:(kh + 1) * 2048, :], 16, c0, 256, None, Wo,
                           Wo.t[:, kh * 16:(kh + 1) * 16, j * 256:(j + 1) * 256])
        for t in range(16):
            Fy, Gg, Gy, tm = Fys[n % 2], Ggs[n % 2], Gys[n % 2], tms[n % 2]
            n += 1
            for g in range(4):
                P.dma("sp", Fy.t[:, g * 8:(g + 1) * 8, :], F_out[g].t[:, t * 128:(t + 1) * 128].rearrange("(c p) t -> p c t", p=128),
                      src=F_out[g], dst=Fy)
            P.dma("sp", Gg.t, Gl.t[:, :, t * 128:(t + 1) * 128].rearrange("c p t -> p c t"), src=Gl, dst=Gg)
            P.op("pool", lambda e, Fy=Fy, Gg=Gg, Gy=Gy: e.tensor_tensor(Gy.t, Fy.t, Gg.t, ALU.mult), reads=[Fy, Gg], writes=[Gy])
            for nb in range(2):
                pb = nextps(C)
                ns = slice(nb * 512, (nb + 1) * 512)
                gs = slice(nh * 1024 + nb * 512, nh * 1024 + (nb + 1) * 512)
                for kp in range(32):
                    kc = ((kp % 8) // 2) * 8 + (kp // 8) * 2 + kp % 2
                    P.mm(pb.t[:, :], Gy.t[:, kp, :], Wo.t[:, kc, ns], kp == 0, kp == 31, [Gy, Wo], pb)
                P.op("dve", lambda e, pb=pb, ns=ns, gs=gs, tm=tm: e.tensor_tensor(tm.t[:, ns], pb.t[:, :], gb2.t[:, gs], ALU.mult),
                     reads=[pb, gb2], writes=[tm], acc=True)
            P.dma("sp", Td.t[t, :, nh * 1024:(nh + 1) * 1024], tm.t, src=tm, dst=Td)
    ob = P.view("out_b", outp)
    for t in range(16):
        xt = xts[0]
        tm = tmp
        P.dma("sp", xt.t, x1b.t[t * 128:(t + 1) * 128, :], src=x1b, dst=xt)
        P.dma("sp", tm.t, Td.t[t], src=Td, dst=tm)
        P.op("dve", lambda e, xt=xt, tm=tm: e.scalar_tensor_tensor(tm.t, xt.t, ALPHA, tm.t, ALU.mult, ALU.add), reads=[xt, tm], writes=[tm])
        for c in range(4):
            P.op("dve", lambda e, c=c, tm=tm: e.bn_stats(st.t[:, c, :], tm.t[:, c * 512:(c + 1) * 512]), reads=[tm], writes=[st], acc=True)
        P.op("dve", lambda e: e.bn_aggr(mv.t, st.t), reads=[st], writes=[mv])
        P.op("act", lambda e: e.activation(rs.t, mv.t[:, 1:2], AF.Sqrt, bias=C.eps.t[:, 0:1], scale=1.0), reads=[mv, C.eps], writes=[rs])
        P.op("dve", lambda e: e.reciprocal(rs.t, rs.t), reads=[rs], writes=[rs])
        P.op("dve", lambda e, tm=tm: e.tensor_scalar(tm.t, tm.t, mv.t[:, 0:1], rs.t[:, 0:1], ALU.subtract, ALU.mult), reads=[tm, mv, rs], writes=[tm])
        P.op("pool", lambda e, tm=tm: e.tensor_tensor(tm.t, tm.t, lnbc.t[:, 0, :], ALU.mult), reads=[tm, lnbc], writes=[tm])
        P.op("dve", lambda e, tm=tm, xt=xt: e.tensor_tensor(xt.t, tm.t, lnbc.t[:, 1, :], ALU.add), reads=[tm, lnbc], writes=[xt])
        P.dma("sp", outp[t * 128:(t + 1) * 128, :], xt.t, src=xt, dbuf=ob)
    return ob


def fused_inputs(inp):
    maps = stage_a_inputs(inp)
    cn, w64, M = fft_consts()
    for core in range(8):
        b, r = core // 4, core % 4
        cv = np.zeros((128, 16, 2), np.float32)
        cv[:, :, 0] = _pk(inp["c"][b], 16)
        cv[:, :, 1] = cv[:, :, 0]
        sel = np.zeros((128, 4), np.float32)
        sel[:, r] = 1.0
        maps[core].update({
            "cvec1": cv.reshape(128, 32), "ada_w1": inp["ada_w"][1],
            "ada_b1": np.stack([inp["ada_b"][1], inp["ada_b"][1]]),
            "w_in_f": inp["w_in_fourier"][0], "w_out_f": inp["w_out_fourier"][0],
            "lngb1": np.stack([inp["ln_g"][1], inp["ln_b"][1]]), "sel": sel,
            "cn": cn, "w64": w64, "Mtw": M,
        })
    return maps


def kernel_fused(**inp):
    inp = {k: np.asarray(v) for k, v in inp.items()}
    nc = build_stage_a(fused=True)
    res = run_bass_kernel_spmd(nc, fused_inputs(inp), core_ids=list(range(8)))
    out = np.zeros((2, 8192, 2048), np.float32)
    for core in range(8):
        b, r = core // 4, core % 4
        out[b, r * 2048:(r + 1) * 2048] = res.results[core]["out"]
    return out


def _rope_tables(rot_dim, seq=8192, grid_w=64):
    t = np.arange(seq)
    r = (t // grid_w).astype(np.float32)
    col = (t % grid_w).astype(np.float32)
    nf = rot_dim // 4
    inv = (np.float32(10000.0) ** (-(np.arange(nf, dtype=np.float32)) / np.float32(nf))).astype(np.float32)
    ang = np.concatenate([r[:, None] * inv[None, :], col[:, None] * inv[None, :]], axis=-1).astype(np.float32)
    cos = np.cos(ang).astype(np.float32)
    sin = np.sin(ang).astype(np.float32)
    cosT = np.repeat(cos, 2, axis=1).T
    sgn = np.tile(np.array([-1.0, 1.0], np.float32), rot_dim // 2)
    sinT = (np.repeat(sin, 2, axis=1) * sgn[None, :]).T
    return np.ascontiguousarray(cosT), np.ascontiguousarray(sinT)


def _consts():
    c = np.zeros((128, 3, 128), np.float32)
    c[:, 0, :] = np.eye(128, dtype=np.float32)
    c[:, 1, :] = 1.0
    idx = np.arange(128)
    c[idx, 2, idx ^ 1] = 1.0
    return c


def _pk(v, kc):
    return np.ascontiguousarray(np.asarray(v, np.float32).reshape(kc, 128).T)


def stage_a_inputs(inp):
    cB, sB = _rope_tables(128)
    cA, sA = _rope_tables(64)
    tabB = np.zeros((128, 2, NKV), np.float32)
    tabB[:, 0, :256] = 1.0
    tabB[:, 0, 256:] = cB
    tabB[:, 1, 256:] = sB
    tabA = np.zeros((64, 2, NKV), np.float32)
    tabA[:, 0, :256] = 1.0
    tabA[:, 0, 256:] = cA
    tabA[:, 1, 256:] = sA
    gains = np.zeros((128, 12), np.float32)
    gains[:, 0:6] = _pk(inp["q_lora_norm"][0], 6)
    gains[:, 6:10] = _pk(inp["kv_lora_norm"][0], 4)
    gains[:, 10] = inp["q_norm_b"][0]
    gains[:, 11] = inp["k_norm_b"][0]
    consts = _consts()
    maps = []
    for core in range(8):
        b, qr = core // 4, core % 4
        t0 = qr * 2048
        cv = np.zeros((128, 16, 2), np.float32)
        cv[:, :, 0] = _pk(inp["c"][b], 16)
        cv[:, :, 1] = _pk(inp["c_ctx"], 16)
        maps.append({
            "xkv": np.ascontiguousarray(np.concatenate([inp["ctx"][b], inp["x"][b]], axis=0)),
            "xq": np.ascontiguousarray(inp["x"][b, t0:t0 + 2048]),
            "cvec": cv.reshape(128, 32),
            "ada_w": inp["ada_w"][0], "ada_b": np.stack([inp["ada_b"][0], inp["ada_b"][0]]),
            "w_in": inp["w_in_attn"][0], "wq_b": inp["wq_b"][0], "wkv_b": inp["wkv_b"][0], "w_out": inp["w_out_attn"][0],
            "gains": gains, "lngb": np.stack([inp["ln_g"][0], inp["ln_b"][0]]), "consts": consts,
            "ropeB_kv": tabB, "ropeA_kv": tabA,
            "ropeB_q": np.ascontiguousarray(tabB[:, :, 256 + t0:256 + t0 + 2048]),
            "ropeA_q": np.ascontiguousarray(tabA[:, :, 256 + t0:256 + t0 + 2048]),
        })
    return maps


def run_stage_a(inp):
    nc = build_stage_a()
    res = run_bass_kernel_spmd(nc, stage_a_inputs(inp), core_ids=list(range(8)))
    x1 = np.zeros((2, 8192, 2048), np.float32)
    for core in range(8):
        b, qr = core // 4, core % 4
        x1[b, qr * 2048:(qr + 1) * 2048] = res.results[core]["x1"]
    return x1


FSCALE = float((8192.0 * 256.0) ** -0.5)


def fft_consts():
    import ml_dtypes
    bf = ml_dtypes.bfloat16
    c = np.arange(256)[:, None].astype(np.float64)
    k3 = np.arange(256)[None, :].astype(np.float64)
    a = 2 * np.pi * c * k3 / 256
    cn = np.zeros((128, 2, 2, 256), np.float64)
    for cc in range(2):
        cn[:, cc, 0, :] = np.cos(a[cc * 128:(cc + 1) * 128])
        cn[:, cc, 1, :] = -np.sin(a[cc * 128:(cc + 1) * 128])
    l1 = np.arange(64)[:, None].astype(np.float64)
    k1 = np.arange(64)[None, :].astype(np.float64)
    th = 2 * np.pi * l1 * k1 / 64
    wr, wi = np.cos(th), -np.sin(th)
    w64 = np.zeros((128, 128), np.float64)
    w64[0:64, 0:64] = wr
    w64[64:128, 0:64] = -wi
    w64[0:64, 64:128] = wi
    w64[64:128, 64:128] = wr
    l2 = np.arange(128)[:, None, None].astype(np.float64)
    kk = (np.arange(64)[None, :, None] + 64 * np.arange(128)[None, None, :]).astype(np.float64)
    ph = 2 * np.pi * ((l2 * kk) % 8192) / 8192
    M = np.stack([np.cos(ph), np.sin(ph)], axis=2)
    return cn.astype(np.float32).astype(bf), w64.astype(np.float32).astype(bf), M.astype(np.float32).astype(bf)


def build_stage_b():
    nc = bass.Bass("TRN2", target_bir_lowering=False)

    def din(name, shape, dt=F32):
        return nc.dram_tensor(name, list(shape), dt, kind="ExternalInput").ap()

    x1f = din("x1f", [8192, 2048])
    cvec = din("cvec", [128, 32])
    ada_w = din("ada_w", [2048, 6144])
    ada_b = din("ada_b", [2, 6144])
    wuf = din("wu", [2048, 1024])
    wgf = din("wg", [2048, 1024])
    consts = din("consts", [128, 3, 128])
    cn_d = din("cn", [128, 2, 2, 256], BF16)
    w64_d = din("w64", [128, 128], BF16)
    M_d = din("Mtw", [128, 64, 2, 128], BF16)
    yT = nc.dram_tensor("yT", [1024, 8192], BF16, kind="ExternalOutput").ap()
    gbc_o = nc.dram_tensor("gbc", [128, 2048], F32, kind="ExternalOutput").ap()

    with ExitStack() as es:
        C = setup_common(nc, es, consts)
        P, ar = C.P, C.ar
        Ud = P.dram("Ud", [8, 128, 8192], BF16)
        Gd = P.dram("Gd", [8, 128, 8192], BF16)
        gbc = ar.alloc("gbc_tmp", [2048], F32)
        emit_modulation(C, cvec, ada_w, ada_b[:, :], gbc, 0)
        gbc_b = P.view("gbc_out", gbc_o)
        P.dma("sp", gbc_o, gbc.t, src=gbc, dbuf=gbc_b)
        ar.reset()
        W = Ctx()
        Wu = ar.alloc("Wu", [16, 1024], BF16)
        Wg = ar.alloc("Wg", [16, 1024], BF16)
        Bu = ar.alloc("Bu", [8], F32)
        Bg = ar.alloc("Bg", [8], F32)
        ln_alloc(C, W)
        W.lnc = 0
        hTs = [ar.alloc(f"hT{i}", [16, 512], BF16) for i in range(2)]
        uo = [ar.alloc(f"uo{i}", [512], BF16) for i in range(4)]
        for j in range(4):
            prep_panel(C, wuf, 16, j * 256, 256, 0, Wu, Wu.t[:, :, j * 256:(j + 1) * 256], Bu, Bu.t[:, 2 * j:2 * j + 2])
            prep_panel(C, wgf, 16, j * 256, 256, 0, Wg, Wg.t[:, :, j * 256:(j + 1) * 256], Bg, Bg.t[:, 2 * j:2 * j + 2])
        nuo = 0
        for tb in range(16):
            hT = hTs[tb % 2]
            ln_transpose(C, W, x1f, tb * 512, 4, hT)
            for j in range(16):
                pb = nextps(C)
                isu = j < 8
                jj = j % 8
                proj(C, pb, 128, 512, Wu if isu else Wg, (Wu if isu else Wg).t, jj * 128, 16, hT, 0)
                o = uo[nuo % 4]; nuo += 1
                bb = Bu if isu else Bg
                P.op("act", lambda e, pb=pb, o=o, bb=bb, jj=jj, isu=isu: e.activation(o.t, pb.t[:, :], AF.Identity if isu else AF.Silu, bias=bb.t[:, jj:jj + 1]),
                     reads=[pb, bb], writes=[o])
                dd = Ud if isu else Gd
                P.dma("sp", dd.t[jj, :, tb * 512:(tb + 1) * 512], o.t, src=o, dst=dd)
        ar.reset()
        cn = ar.alloc("cn", [2, 2, 256], BF16)
        w64 = ar.alloc("w64", [128], BF16)
        Mt = ar.alloc("Mt", [64, 2, 128], BF16)
        P.dma("sp", cn.t, cn_d, dst=cn)
        P.dma("sp", w64.t, w64_d, dst=w64)
        P.dma("sp", Mt.t, M_d, dst=Mt)
        uT = ar.alloc("uT", [2, 8192], BF16)
        gT = ar.alloc("gT", [2, 8192], BF16)
        z = ar.alloc("z", [128, 128], BF16)
        Y = ar.alloc("Y", [128, 128], BF16)
        yTb = P.view("yT_out", yT)
        for g in range(4):
            P.dma("sp", uT.t, Ud.t[2 * g:2 * g + 2].rearrange("c p t -> p c t"), src=Ud, dst=uT)
            P.dma("sp", gT.t, Gd.t[2 * g:2 * g + 2].rearrange("c p t -> p c t"), src=Gd, dst=gT)
            uv = uT.t.rearrange("p c (l1 l2) -> p c l2 l1", l2=128)
            for kh in range(2):
                ks = slice(kh * 128, (kh + 1) * 128)
                for l2p in range(32):
                    pb = nextps(C)
                    for q in range(4):
                        l2 = l2p * 4 + q
                        for ri in range(2):
                            for cc in range(2):
                                P.mm(pb.t[ri * 64:(ri + 1) * 64, q * 128:(q + 1) * 128], uv[:, cc, l2, :], cn.t[:, cc, ri, ks],
                                     cc == 0, cc == 1, [uT, cn], pb)
                    dst = z.t[:, l2p * 4:(l2p + 1) * 4, :]
                    src = pb.t[:, :].rearrange("p (a b) -> p a b", b=128)
                    if l2p % 2 == 0:
                        P.op("act", lambda e, dst=dst, src=src: e.activation(dst, src, AF.Copy), reads=[pb], writes=[z], acc=True)
                    else:
                        P.op("dve", lambda e, dst=dst, src=src: e.tensor_copy(dst, src), reads=[pb], writes=[z], acc=True)
                for k3p in range(32):
                    pb = nextps(C)
                    for q in range(4):
                        k3 = k3p * 4 + q
                        P.mm(pb.t[:, q * 128:(q + 1) * 128], z.t[:, :, k3], w64.t, True, True, [z, w64], pb)
                    dst = Y.t[:, k3p * 4:(k3p + 1) * 4, :]
                    src = pb.t[:, :].rearrange("p (a b) -> p a b", b=128)
                    if k3p % 2 == 0:
                        P.op("act", lambda e, dst=dst, src=src: e.activation(dst, src, AF.Copy), reads=[pb], writes=[Y], acc=True)
                    else:
                        P.op("dve", lambda e, dst=dst, src=src: e.tensor_copy(dst, src), reads=[pb], writes=[Y], acc=True)
                gv = gT.t[:, kh, :].rearrange("p (k2 k1) -> p k1 k2", k1=64)
                for k1p in range(16):
                    pb = nextps(C)
                    for q in range(4):
                        k1 = k1p * 4 + q
                        for ri in range(2):
                            P.mm(pb.t[:, q * 128:(q + 1) * 128], Y.t[:, :, ri * 64 + k1], Mt.t[:, k1, ri, :], ri == 0, ri == 1, [Y, Mt], pb)
                    gsl = gv[:, k1p * 4:(k1p + 1) * 4, :]
                    src = pb.t[:, :].rearrange("p (a b) -> p a b", b=128)
                    P.op("dve", lambda e, gsl=gsl, src=src: e.scalar_tensor_tensor(gsl, src, FSCALE, gsl, ALU.mult, ALU.mult),
                         reads=[pb, gT], writes=[gT], acc=True)
            P.dma("sp", yT[g * 256:(g + 1) * 256, :].rearrange("(c p) t -> p c t", p=128), gT.t, src=gT, dbuf=yTb)
        P.fence("sp", [yTb, gbc_b])
        P.emit()
    return nc


def stage_b_inputs(inp, x1):
    cn, w64, M = fft_consts()
    consts = _consts()
    maps = []
    wf = inp["w_in_fourier"][0]
    for core in range(8):
        b, cq = core // 4, core % 4
        cv = np.zeros((128, 16, 2), np.float32)
        cv[:, :, 0] = _pk(inp["c"][b], 16)
        cv[:, :, 1] = cv[:, :, 0]
        maps.append({
            "x1f": np.ascontiguousarray(x1[b]), "cvec": cv.reshape(128, 32),
            "ada_w": inp["ada_w"][1], "ada_b": np.stack([inp["ada_b"][1], inp["ada_b"][1]]),
            "wu": np.ascontiguousarray(wf[:, cq * 1024:(cq + 1) * 1024]),
            "wg": np.ascontiguousarray(wf[:, 4096 + cq * 1024:4096 + (cq + 1) * 1024]),
            "consts": consts, "cn": cn, "w64": w64, "Mtw": M,
        })
    return maps


def run_stage_b(inp, x1):
    nc = build_stage_b()
    res = run_bass_kernel_spmd(nc, stage_b_inputs(inp, x1), core_ids=list(range(8)))
    yT = [np.concatenate([res.results[b * 4 + cq]["yT"] for cq in range(4)], axis=0) for b in range(2)]
    gbc = [res.results[b * 4]["gbc"] for b in range(2)]
    return yT, gbc


def build_stage_c():
    nc = bass.Bass("TRN2", target_bir_lowering=False)

    def din(name, shape, dt=F32):
        return nc.dram_tensor(name, list(shape), dt, kind="ExternalInput").ap()

    yTl = din("yTl", [32, 128, 2048], BF16)
    x1l = din("x1l", [2048, 2048])
    w_out = din("w_out", [4096, 2048])
    gbc_d = din("gbc", [128, 2048])
    lngb = din("lngb", [2, 2048])
    consts = din("consts", [128, 3, 128])
    outp = nc.dram_tensor("out", [2048, 2048], F32, kind="ExternalOutput").ap()

    with ExitStack() as es:
        C = setup_common(nc, es, consts)
        P, ar = C.P, C.ar
        Td = P.dram("Td", [16, 128, 2048], F32)
        Wo = ar.alloc("Wo", [32, 1024], BF16)
        gb2 = ar.alloc("gb2", [2048], F32)
        lnbc = ar.alloc("lnbc", [2, 2048], F32)
        Gys = [ar.alloc(f"Gy{i}", [32, 128], BF16) for i in range(2)]
        tms = [ar.alloc(f"tm{i}", [1024], F32) for i in range(2)]
        xts = [ar.alloc(f"oxt{i}", [2048], F32) for i in range(2)]
        tmp = ar.alloc("otmp", [2048], F32)
        st = ar.alloc("ost", [4, 6], F32)
        mv = ar.alloc("omv", [2], F32)
        rs = ar.alloc("ors", [1], F32)
        P.dma("sp", gb2.t, gbc_d, dst=gb2)
        P.dma("sp", lnbc.t[:, 0, :], lngb[0:1, :].partition_broadcast(128), dst=lnbc)
        P.dma("sp", lnbc.t[:, 1, :], lngb[1:2, :].partition_broadcast(128), dst=lnbc)
        n = 0
        for nh in range(2):
            for kh in range(2):
                for j in range(4):
                    c0 = nh * 1024 + j * 256
                    prep_panel(C, w_out[kh * 2048:(kh + 1) * 2048, :], 16, c0, 256, None, Wo,
                               Wo.t[:, kh * 16:(kh + 1) * 16, j * 256:(j + 1) * 256])
            for t in range(16):
                Gy = Gys[n % 2]
                tm = tms[n % 2]
                n += 1
                P.dma("sp", Gy.t, yTl[:, :, t * 128:(t + 1) * 128].rearrange("c p t -> p c t"), dst=Gy)
                for nb in range(2):
                    pb = nextps(C)
                    ns = slice(nb * 512, (nb + 1) * 512)
                    gs = slice(nh * 1024 + nb * 512, nh * 1024 + (nb + 1) * 512)
                    for kc in range(32):
                        P.mm(pb.t[:, :], Gy.t[:, kc, :], Wo.t[:, kc, ns], kc == 0, kc == 31, [Gy, Wo], pb)
                    P.op("dve", lambda e, pb=pb, ns=ns, gs=gs, tm=tm: e.tensor_tensor(tm.t[:, ns], pb.t[:, :], gb2.t[:, gs], ALU.mult),
                         reads=[pb, gb2], writes=[tm], acc=True)
                P.dma("sp", Td.t[t, :, nh * 1024:(nh + 1) * 1024], tm.t, src=tm, dst=Td)
        ob = P.view("out_b", outp)
        for t in range(16):
            xt = xts[t % 2]
            tm = tmp
            P.dma("sp", xt.t, x1l[t * 128:(t + 1) * 128, :], dst=xt)
            P.dma("sp", tm.t, Td.t[t], src=Td, dst=tm)
            P.op("dve", lambda e, xt=xt, tm=tm: e.scalar_tensor_tensor(tm.t, xt.t, ALPHA, tm.t, ALU.mult, ALU.add), reads=[xt, tm], writes=[tm])
            for c in range(4):
                P.op("dve", lambda e, c=c, tm=tm: e.bn_stats(st.t[:, c, :], tm.t[:, c * 512:(c + 1) * 512]), reads=[tm], writes=[st], acc=True)
            P.op("dve", lambda e: e.bn_aggr(mv.t, st.t), reads=[st], writes=[mv])
            P.op("act", lambda e: e.activation(rs.t, mv.t[:, 1:2], AF.Sqrt, bias=C.eps.t[:, 0:1], scale=1.0), reads=[mv, C.eps], writes=[rs])
            P.op("dve", lambda e: e.reciprocal(rs.t, rs.t), reads=[rs], writes=[rs])
            P.op("dve", lambda e, tm=tm: e.tensor_scalar(tm.t, tm.t, mv.t[:, 0:1], rs.t[:, 0:1], ALU.subtract, ALU.mult), reads=[tm, mv, rs], writes=[tm])
            P.op("pool", lambda e, tm=tm: e.tensor_tensor(tm.t, tm.t, lnbc.t[:, 0, :], ALU.mult), reads=[tm, lnbc], writes=[tm])
            P.op("dve", lambda e, tm=tm, xt=xt: e.tensor_tensor(xt.t, tm.t, lnbc.t[:, 1, :], ALU.add), reads=[tm, lnbc], writes=[xt])
            P.dma("sp", outp[t * 128:(t + 1) * 128, :], xt.t, src=xt, dbuf=ob)
        P.fence("sp", [ob])
        P.emit()
    return nc


def run_stage_c(inp, x1, yT, gbc):
    nc = build_stage_c()
    consts = _consts()
    maps = []
    for core in range(8):
        b, qr = core // 4, core % 4
        t0 = qr * 2048
        maps.append({
            "yTl": np.ascontiguousarray(yT[b][:, t0:t0 + 2048]).reshape(32, 128, 2048),
            "x1l": np.ascontiguousarray(x1[b, t0:t0 + 2048]),
            "w_out": inp["w_out_fourier"][0], "gbc": gbc[b],
            "lngb": np.stack([inp["ln_g"][1], inp["ln_b"][1]]), "consts": consts,
        })
    res = run_bass_kernel_spmd(nc, maps, core_ids=list(range(8)))
    out = np.zeros((2, 8192, 2048), np.float32)
    for core in range(8):
        b, qr = core // 4, core % 4
        out[b, qr * 2048:(qr + 1) * 2048] = res.results[core]["out"]
    return out


def kernel_unfused(**inp):
    inp = {k: np.asarray(v) for k, v in inp.items()}
    x1 = run_stage_a(inp)
    yT, gbc = run_stage_b(inp, x1)
    return run_stage_c(inp, x1, yT, gbc)


def kernel(**inp):
    return kernel_fused(**inp)
```

```python
import numpy as np
from contextlib import ExitStack
import concourse.bass as bass
import concourse.mybir as mybir
from concourse.bass_utils import run_bass_kernel_spmd

F32 = mybir.dt.float32
BF16 = mybir.dt.bfloat16
ALU = mybir.AluOpType
AF = mybir.ActivationFunctionType
AX = mybir.AxisListType

SEM_LIM = 30000
PROFILE_SCOPES = False


class Buf:
    def __init__(self, name, t=None):
        self.name = name
        self.t = t
        self.w = {}
        self.r = {}
        self.dcnt = 0
        self.dsem = None
        self.is_dram = False

    def __getitem__(self, k):
        return self.t[k]


class Prog:
    ENGS = ("pe", "act", "dve", "pool", "sp")

    def __init__(self, nc, es):
        self.nc = nc
        self.es = es
        self.ops = {e: [] for e in self.ENGS}
        self.seen = {e: {} for e in self.ENGS}
        self.signal = {e: set() for e in self.ENGS}
        self.dbufs = []
        self.nbuf = 0
        self.phase = None

    def sb(self, name, shape, dt):
        t = self.es.enter_context(self.nc.sbuf_tensor(name, list(shape), dt))
        return Buf(name, t)

    def ps(self, name):
        t = self.es.enter_context(self.nc.psum_tensor(name, [128, 512], F32))
        return Buf(name, t)

    def dram(self, name, shape, dt, kind="Internal"):
        t = self.nc.dram_tensor(name, list(shape), dt, kind=kind)
        b = Buf(name, t.ap())
        b.is_dram = True
        return b

    def view(self, name, ap):
        b = Buf(name, ap)
        b.is_dram = True
        return b

    def op(self, eng, fn, reads=(), writes=(), dma_dst=None, acc=False, dma_inc=16):
        if dma_dst is not None:
            own = ("D", id(dma_dst))
        else:
            own = ("E", eng)
        need = {}

        def merge(d, skip_own=False):
            for k, v in d.items():
                if skip_own and k == own:
                    continue
                if need.get(k, -1) < v:
                    need[k] = v

        for b in reads:
            merge(b.w)
        for b in writes:
            merge(b.w, skip_own=acc)
            merge(b.r)
        waits = []
        seen = self.seen[eng]
        for k, v in need.items():
            if k == ("E", "pe") and eng == "pe":
                continue
            if seen.get(k, -1) >= v:
                continue
            seen[k] = v
            waits.append((k, v))
            if k[0] == "E":
                self.signal[k[1]].add(v)
        idx = len(self.ops[eng])
        if dma_dst is not None:
            if dma_dst.dsem is None:
                dma_dst.dsem = True
                self.dbufs.append(dma_dst)
            dma_dst.dcnt += dma_inc
            assert dma_dst.dcnt < 2 * SEM_LIM, dma_dst.name
            tok = (own, dma_dst.dcnt)
        else:
            tok = (own, idx)
        self.ops[eng].append((fn, waits, dma_dst, idx, dma_inc, self.phase))
        for b in reads:
            if b.r.get(tok[0], -1) < tok[1]:
                b.r[tok[0]] = tok[1]
        for b in writes:
            if acc:
                b.w[tok[0]] = tok[1]
            else:
                b.w = {tok[0]: tok[1]}
            b.r = {}
        return tok

    def fence(self, eng, bufs):
        self.op(eng, None, reads=bufs)

    def mm(self, out, lhsT, rhs, start, stop, reads, w, **kw):
        self.op("pe", lambda e: e.matmul(out, lhsT, rhs, start=start, stop=stop, **kw),
                reads=reads, writes=[w], acc=True)

    def dma(self, eng, out, in_, src=None, dst=None, dbuf=None, **kw):
        if dst is None:
            dst = dbuf
        reads = [src] if src is not None else []
        writes = [dst] if dst is not None else []
        if src is not None and not src.is_dram and (dst is None or dst.is_dram):
            owner = src
        else:
            owner = dst
        self.op(eng, lambda e: e.dma_start(out=out, in_=in_, **kw), reads=reads, writes=writes,
                dma_dst=owner, acc=True)

    def emit(self):
        nc = self.nc
        es = self.es
        rank = {}
        esems = {}
        for e in self.ENGS:
            sig = sorted(self.signal[e])
            rank[e] = {idx: i + 1 for i, idx in enumerate(sig)}
            n = (len(sig) + SEM_LIM - 1) // SEM_LIM
            esems[e] = [es.enter_context(nc.semaphore(f"s_{e}{i}")) for i in range(max(n, 1))]
        for i, b in enumerate(self.dbufs):
            b.dsem = es.enter_context(nc.semaphore(f"d_{i}"))
        dmap = {id(b): b for b in self.dbufs}
        self.nsem = sum(len(v) for v in esems.values()) + len(self.dbufs)

        def sem_val(k, v):
            if k[0] == "E":
                r = rank[k[1]][v] - 1
                return esems[k[1]][r // SEM_LIM], r % SEM_LIM + 1
            return dmap[k[1]].dsem, v

        def run_one(e, name, fn, waits, dma_dst, idx, dma_inc):
            for k, v in waits:
                s, val = sem_val(k, v)
                e.wait_ge(s, val)
            if fn is None:
                return
            ins = fn(e)
            if dma_dst is not None:
                ins.then_inc(dma_dst.dsem, dma_inc)
            elif idx in rank[name]:
                s, _ = sem_val(("E", name), idx)
                ins.then_inc(s, 1)

        def run(e, name):
            ops = self.ops[name]
            i = 0
            while i < len(ops):
                ph = ops[i][5]
                j = i
                while j < len(ops) and ops[j][5] == ph:
                    j += 1
                if PROFILE_SCOPES and ph is not None:
                    with nc.named_scope(ph):
                        for o in ops[i:j]:
                            run_one(e, name, *o[:5])
                else:
                    for o in ops[i:j]:
                        run_one(e, name, *o[:5])
                i = j

        block = es.enter_context(nc.Block())

        @block.sync
        def _(e):
            run(e, "sp")

        @block.tensor
        def _(e):
            run(e, "pe")

        @block.scalar
        def _(e):
            run(e, "act")

        @block.vector
        def _(e):
            run(e, "dve")

        @block.gpsimd
        def _(e):
            run(e, "pool")


class Arena:
    def __init__(self, P, nbytes):
        self.P = P
        self.n = nbytes // 2
        self.t = P.es.enter_context(P.nc.sbuf_tensor("arena", [128, self.n], BF16))
        self.off = 0
        self.cur = []
        self.prev = {}

    def reset(self):
        for b in self.cur:
            for d in (b.w, b.r):
                for k, v in d.items():
                    if self.prev.get(k, -1) < v:
                        self.prev[k] = v
        self.cur = []
        self.off = 0

    def mark(self):
        return (self.off, len(self.cur))

    def release(self, m):
        for b in self.cur[m[1]:]:
            for d in (b.w, b.r):
                for k, v in d.items():
                    if self.prev.get(k, -1) < v:
                        self.prev[k] = v
        self.cur = self.cur[:m[1]]
        self.off = m[0]

    def alloc(self, name, shape, dt, parts=128):
        esz = 4 if dt == F32 else 2
        n = int(np.prod(shape))
        units = (n * esz + 1) // 2
        units = (units + 15) // 16 * 16
        assert self.off + units <= self.n, (name, self.off * 2, units * 2)
        ap = self.t[0:parts, self.off:self.off + units]
        self.off += units
        if dt == F32:
            ap = ap.bitcast(F32)
        ap = ap[:, 0:n]
        if len(shape) == 2:
            ap = ap.rearrange("p (a b) -> p a b", b=shape[1])
        elif len(shape) == 3:
            ap = ap.rearrange("p (a b c) -> p a b c", b=shape[1], c=shape[2])
        b = Buf(name, ap)
        b.r = dict(self.prev)
        self.cur.append(b)
        return b


ALPHA = (2.0 * 2) ** 0.25
A_SCALE = 192.0 ** -0.5
B_SCALE = 128.0 ** -0.5


class Ctx:
    pass


def setup_common(nc, es, consts_d, arena_kb=170):
    C = Ctx()
    P = Prog(nc, es)
    C.P = P
    C.nc = nc
    C.cst = P.sb("cst", [128, 3, 128], F32)
    P.dma("sp", C.cst.t[:], consts_d, dst=C.cst)
    C.idf = C.cst.t[:, 0, :]
    C.onesf = C.cst.t[:, 1, :]
    C.perm = C.cst.t[:, 2, :]
    C.cb = P.sb("cb", [128, 2, 128], BF16)
    P.op("dve", lambda e: e.tensor_copy(C.cb.t[:], C.cst.t[:, 0:2, :]), reads=[C.cst], writes=[C.cb])
    C.idb = C.cb.t[:, 0, :]
    C.onesb = C.cb.t[:, 1, :]
    C.eps = P.sb("eps", [128, 1], F32)
    P.op("pool", lambda e: e.memset(C.eps.t[:], 1e-6), writes=[C.eps])
    C.modT = P.sb("modT", [128, 32, 2], F32)
    C.stg = [P.sb(f"stg{i}", [128, 16 * 256], F32) for i in range(2)]
    C.nstg = 0
    C.psb = [P.ps(f"ps{i}") for i in range(8)]
    C.nps = 0
    C.ar = Arena(P, arena_kb * 1024)
    return C


def nextps(C, lo=0, hi=8):
    b = C.psb[lo + C.nps % (hi - lo)]
    C.nps += 1
    return b


def emit_modulation(C, cvec_d, ada_w_d, ada_b_d, gbc, want_gate_row=0):
    P, ar = C.P, C.ar
    cv = ar.alloc("cv", [32], F32)
    sv = ar.alloc("sv", [16, 2], F32)
    adb = ar.alloc("adb", [6144], F32, parts=2)
    m2 = ar.alloc("m2", [6144], F32, parts=2)
    P.dma("sp", cv.t, cvec_d, dst=cv)
    P.dma("sp", adb.t, ada_b_d, dst=adb)
    P.op("act", lambda e: e.activation(sv.t.rearrange("p a b -> p (a b)"), cv.t, AF.Silu), reads=[cv], writes=[sv])
    awv = ada_w_d.rearrange("(kc p) n -> p kc n", p=128)
    for nb in range(24):
        stg = C.stg[C.nstg % 2]
        C.nstg += 1
        sv3 = stg.t.rearrange("p (kc n) -> p kc n", n=256)
        P.dma("sp", sv3, awv[:, :, nb * 256:(nb + 1) * 256], dst=stg)
        pb = nextps(C)
        for kc in range(16):
            P.mm(pb.t[0:2, 0:256], sv.t[:, kc, :], sv3[:, kc, :], kc == 0, kc == 15, [sv, stg], pb)
        sl = slice(nb * 256, (nb + 1) * 256)
        P.op("dve", lambda e, pb=pb, sl=sl: e.tensor_tensor(m2.t[:, sl], pb.t[0:2, 0:256], adb.t[:, sl], ALU.add),
             reads=[pb, adb], writes=[m2])
    pT = nextps(C)
    for j in range(32):
        P.mm(pT.t[:, 2 * j:2 * j + 2], m2.t[0:2, j * 128:(j + 1) * 128], C.idf[0:2, 0:2], True, True, [m2, C.cst], pT)
    P.op("dve", lambda e: e.tensor_copy(C.modT.t[:, 0:16, :], pT.t[:, 0:32].rearrange("p (a b) -> p a b", b=2)),
         reads=[pT], writes=[C.modT])
    P.op("dve", lambda e: e.tensor_scalar_add(C.modT.t[:, 16:32, :], pT.t[:, 32:64].rearrange("p (a b) -> p a b", b=2), 1.0),
         reads=[pT], writes=[C.modT], acc=True)
    r = want_gate_row
    for q in range(4):
        pb = nextps(C)
        P.mm(pb.t[:, :], C.onesf[r:r + 1, :], m2.t[r:r + 1, 4096 + q * 512:4096 + (q + 1) * 512], True, True, [m2, C.cst], pb)
        P.op("act", lambda e, pb=pb, q=q: e.activation(gbc.t[:, q * 512:(q + 1) * 512], pb.t[:, :], AF.Copy),
             reads=[pb], writes=[gbc], acc=True)


def prep_panel(C, Wd, KC, c0, ncols, r, Wbuf, Wap, bbuf=None, bap=None):
    ld = prep_load(C, Wd, KC, c0, ncols)
    prep_compute(C, ld, KC, ncols, r, Wbuf, Wap, bbuf, bap)


def prep_load(C, Wd, KC, c0, ncols):
    stg = C.stg[C.nstg % 2]
    C.nstg += 1
    s3 = stg.t[:, 0:KC * ncols].rearrange("p (kc n) -> p kc n", n=ncols)
    C.P.dma("sp", s3, Wd.rearrange("(kc p) n -> p kc n", p=128)[:, :, c0:c0 + ncols], dst=stg)
    return stg, s3


def prep_compute(C, ld, KC, ncols, r, Wbuf, Wap, bbuf=None, bap=None):
    P = C.P
    stg, s3 = ld
    if r is None:
        P.op("dve", lambda e: e.tensor_copy(Wap, s3), reads=[stg], writes=[Wbuf], acc=True)
        return
    for kc in range(KC):
        if kc % 2 == 0:
            P.op("dve", lambda e, kc=kc: e.tensor_scalar_mul(Wap[:, kc, :], s3[:, kc, :], C.modT.t[:, 16 + kc, r:r + 1]),
                 reads=[stg, C.modT], writes=[Wbuf], acc=True)
        else:
            P.op("act", lambda e, kc=kc: e.activation(Wap[:, kc, :], s3[:, kc, :], AF.Copy, scale=C.modT.t[:, 16 + kc, r:r + 1]),
                 reads=[stg, C.modT], writes=[Wbuf], acc=True)
    pb = nextps(C)
    nch = (ncols + 127) // 128
    for j in range(nch):
        M = min(128, ncols - j * 128)
        for kc in range(KC):
            P.mm(pb.t[0:M, j:j + 1], s3[:, kc, j * 128:j * 128 + M], C.modT.t[:, kc, r:r + 1], kc == 0, kc == KC - 1,
                 [stg, C.modT], pb)
    nfull = ncols // 128
    if nfull:
        P.op("dve", lambda e: e.tensor_copy(bap[:, 0:nfull], pb.t[:, 0:nfull]), reads=[pb], writes=[bbuf], acc=True)
    if nch > nfull:
        Ml = ncols - nfull * 128
        P.op("dve", lambda e: e.tensor_copy(bap[0:Ml, nfull:nch], pb.t[0:Ml, nfull:nch]), reads=[pb], writes=[bbuf], acc=True)


def ln_tile(C, W, xrows_d, t):
    P = C.P
    xt = W.xt[t % 2]
    st = W.st[t % 2]
    mv = W.mv[t % 2]
    rs = W.rs[t % 2]
    xh = W.xh[t % 2]
    P.dma("sp", xt.t, xrows_d, src=getattr(W, "xsrc", None), dst=xt)
    for c in range(4):
        P.op("dve", lambda e, c=c: e.bn_stats(st.t[:, c, :], xt.t[:, c * 512:(c + 1) * 512]), reads=[xt], writes=[st], acc=True)
    P.op("dve", lambda e: e.bn_aggr(mv.t, st.t), reads=[st], writes=[mv])
    P.op("act", lambda e: e.activation(rs.t, mv.t[:, 1:2], AF.Sqrt, bias=C.eps.t[:, 0:1], scale=1.0), reads=[mv, C.eps], writes=[rs])
    P.op("dve", lambda e: e.reciprocal(rs.t, rs.t), reads=[rs], writes=[rs])
    P.op("dve", lambda e: e.tensor_scalar(xh.t, xt.t, mv.t[:, 0:1], rs.t[:, 0:1], ALU.subtract, ALU.mult),
         reads=[xt, mv, rs], writes=[xh])
    return xh


def ln_alloc(C, W):
    ar = C.ar
    W.xt = [ar.alloc(f"xt{i}", [2048], F32) for i in range(2)]
    W.st = [ar.alloc(f"st{i}", [4, 6], F32) for i in range(2)]
    W.mv = [ar.alloc(f"mv{i}", [2], F32) for i in range(2)]
    W.rs = [ar.alloc(f"rs{i}", [1], F32) for i in range(2)]
    W.xh = [ar.alloc(f"xh{i}", [2048], BF16) for i in range(2)]


def ln_transpose(C, W, x_d, row0, ntiles, hT, col0=0):
    P = C.P
    for t in range(ntiles):
        xh = ln_tile(C, W, x_d[row0 + t * 128: row0 + (t + 1) * 128, :], W.lnc)
        W.lnc += 1
        for half in range(2):
            pb = nextps(C)
            pv = pb.t[:].bitcast(BF16)
            for j in range(8):
                kc = half * 8 + j
                P.op("pe", lambda e, pv=pv, j=j, kc=kc, xh=xh: e.transpose(pv[:, j * 128:(j + 1) * 128], xh.t[:, kc * 128:(kc + 1) * 128], C.idb),
                     reads=[xh, C.cb], writes=[pb], acc=True)
            dst = hT.t[:, half * 8:(half + 1) * 8, col0 + t * 128: col0 + (t + 1) * 128]
            src = pv.rearrange("p (a b) -> p a b", b=128)
            if half == 0:
                P.op("act", lambda e, dst=dst, src=src: e.activation(dst, src, AF.Copy), reads=[pb], writes=[hT], acc=True)
            else:
                P.op("dve", lambda e, dst=dst, src=src: e.tensor_copy(dst, src), reads=[pb], writes=[hT], acc=True)


def proj(C, pb, M, nt, Wbuf, Wap, c0, KC, hT, tok0):
    for kc in range(KC):
        C.P.mm(pb.t[0:M, 0:nt], Wap[:, kc, c0:c0 + M], hT.t[:, kc, tok0:tok0 + nt], kc == 0, kc == KC - 1, [Wbuf, hT], pb)


def rstd_from(C, W, zs, nfeat, nt):
    P = C.P
    pss = nextps(C)
    for i, (zb, zap) in enumerate(zs):
        sq = W.sq[W.nsq % 2]
        W.nsq += 1
        P.op("act", lambda e, sq=sq, zap=zap: e.activation(sq.t[:, 0:nt], zap, AF.Square), reads=[zb], writes=[sq])
        P.mm(pss.t[:, 0:nt], C.onesf, sq.t[:, 0:nt], i == 0, i == len(zs) - 1, [sq, C.cst], pss)
    rstd = W.rstd[W.nrs % 2]
    W.nrs += 1
    P.op("act", lambda e: e.activation(rstd.t[:, 0:nt], pss.t[:, 0:nt], AF.Sqrt, bias=C.eps.t[:, 0:1], scale=1.0 / nfeat),
         reads=[pss, C.eps], writes=[rstd])
    P.op("dve", lambda e: e.reciprocal(rstd.t[:, 0:nt], rstd.t[:, 0:nt]), reads=[rstd], writes=[rstd])
    return rstd


def rope(C, W, dst_buf, dst_ap, src_buf, src_ap, tab, M, nt):
    P = C.P
    pw = nextps(C)
    P.mm(pw.t[0:M, 0:nt], C.perm[0:M, 0:M], src_ap, True, True, [src_buf, C.cst], pw)
    t1 = W.t1[W.nt1 % 2]
    t2 = W.t2[W.nt1 % 2]
    W.nt1 += 1
    P.op("pool", lambda e: e.tensor_tensor(t1.t[0:M, 0:nt], src_ap, tab.t[0:M, 0, 0:nt], ALU.mult), reads=[src_buf, tab], writes=[t1])
    P.op("dve", lambda e: e.tensor_tensor(t2.t[0:M, 0:nt], pw.t[0:M, 0:nt], tab.t[0:M, 1, 0:nt], ALU.mult), reads=[pw, tab], writes=[t2])
    P.op("pool", lambda e: e.tensor_tensor(dst_ap, t1.t[0:M, 0:nt], t2.t[0:M, 0:nt], ALU.add), reads=[t1, t2], writes=[dst_buf])


def work_alloc(C, W):
    ar = C.ar
    W.sq = [ar.alloc(f"sq{i}", [512], F32) for i in range(2)]
    W.rstd = [ar.alloc(f"rstd{i}", [512], F32) for i in range(2)]
    W.t1 = [ar.alloc(f"t1{i}", [512], F32) for i in range(2)]
    W.t2 = [ar.alloc(f"t2{i}", [512], F32) for i in range(2)]
    W.nsq = W.nrs = W.nt1 = 0


NKV = 8448
NKT = 66


def build_stage_a(fused=False):
    nc = bass.Bass("TRN2", target_bir_lowering=False)

    def din(name, shape, dt=F32):
        return nc.dram_tensor(name, list(shape), dt, kind="ExternalInput").ap()

    xkv = din("xkv", [NKV, 2048])
    xq = din("xq", [2048, 2048])
    cvec = din("cvec", [128, 32])
    ada_w = din("ada_w", [2048, 6144])
    ada_b = din("ada_b", [2, 6144])
    w_in = din("w_in", [2048, 4928])
    wq_b = din("wq_b", [768, 1536])
    wkv_b = din("wkv_b", [512, 2048])
    w_out = din("w_out", [2048, 2048])
    gains = din("gains", [128, 12])
    lngb = din("lngb", [2, 2048])
    consts = din("consts", [128, 3, 128])
    ropeB_kv = din("ropeB_kv", [128, 2, NKV])
    ropeA_kv = din("ropeA_kv", [64, 2, NKV])
    ropeB_q = din("ropeB_q", [128, 2, 2048])
    ropeA_q = din("ropeA_q", [64, 2, 2048])
    x1 = None if fused else nc.dram_tensor("x1", [2048, 2048], F32, kind="ExternalOutput").ap()

    with ExitStack() as es:
        C = setup_common(nc, es, consts)
        P, ar = C.P, C.ar
        gn = P.sb("gn", [128, 12], F32)
        P.dma("sp", gn.t[:], gains, dst=gn)
        KaT = P.dram("KaT", [8, 128, NKV], BF16)
        KpeT = P.dram("KpeT", [64, NKV], BF16)
        Va = P.dram("Va", [NKT, 128, 1024], BF16)
        KbT = P.dram("KbT", [2, 128, NKV], BF16)
        Vb = P.dram("Vb", [NKT, 128, 256], BF16)
        QaT = P.dram("QaT", [8, 128, 2048], BF16)
        QpeT = P.dram("QpeT", [8, 64, 2048], BF16)
        QbT = P.dram("QbT", [8, 128, 2048], BF16)
        Gd = P.dram("Gd", [16, 128, 2048], BF16)
        Yd = P.dram("Yd", [16, 128, 2048], BF16)

        P.phase = "MOD"
        gbc = ar.alloc("gbc_tmp", [2048], F32)
        emit_modulation(C, cvec, ada_w, ada_b[:, :], gbc, 0)
        Gbc_d = P.dram("Gbc_d", [128, 2048], F32)
        P.dma("sp", Gbc_d.t, gbc.t, src=gbc, dst=Gbc_d)

        def kv_phase(r, blocks):
            ar.reset()
            W = Ctx()
            Wkv = ar.alloc("Wkv", [16, 1088], BF16)
            Bkv = ar.alloc("Bkv", [9], F32)
            wkvb = ar.alloc("wkvb", [4, 2048], BF16)
            ln_alloc(C, W)
            W.lnc = 0
            work_alloc(C, W)
            hTs = [ar.alloc(f"hT{i}", [16, 512], BF16) for i in range(1)]
            zc = ar.alloc("zc", [4, 512], F32)
            ckvn = ar.alloc("ckvn", [4, 512], BF16)
            zk = [ar.alloc(f"zk{i}", [512], F32) for i in range(2)]
            kn = [ar.alloc(f"kn{i}", [512], F32) for i in range(2)]
            tabB = [ar.alloc(f"tabB{i}", [2, 512], F32) for i in range(1)]
            tabA = [ar.alloc(f"tabA{i}", [2, 512], F32) for i in range(1)]
            ko = [ar.alloc(f"ko{i}", [512], BF16) for i in range(4)]
            vT = [ar.alloc(f"vT{i}", [512], BF16) for i in range(2)]
            vo = [ar.alloc(f"vo{i}", [1024], BF16) for i in range(2)]
            vbo = [ar.alloc(f"vbo{i}", [256], BF16) for i in range(2)]
            panels = [(768, 256, 0, 0), (1024, 256, 256, 2), (1280, 64, 512, 4), (2368, 256, 576, 5), (2624, 256, 832, 7)]
            for (c0, ncols, l0, b0) in panels:
                nch = (ncols + 127) // 128
                prep_panel(C, w_in, 16, c0, ncols, r, Wkv, Wkv.t[:, :, l0:l0 + ncols], Bkv, Bkv.t[:, b0:b0 + nch])
            for j in range(8):
                prep_panel(C, wkv_b, 4, j * 256, 256, None, wkvb, wkvb.t[:, :, j * 256:(j + 1) * 256])
            nko = 0
            for bi, (row0, nt) in enumerate(blocks):
                ntl = nt // 128
                hT = hTs[0]
                ln_transpose(C, W, xkv, row0, ntl, hT)
                tb = tabB[0]
                ta = tabA[0]
                P.dma("sp", tb.t[:, :, 0:nt], ropeB_kv[:, :, row0:row0 + nt], dst=tb)
                P.dma("sp", ta.t[0:64, :, 0:nt], ropeA_kv[:, :, row0:row0 + nt], dst=ta)
                for j in range(4):
                    pb = nextps(C)
                    proj(C, pb, 128, nt, Wkv, Wkv.t, j * 128, 16, hT, 0)
                    P.op("act", lambda e, pb=pb, j=j: e.activation(zc.t[:, j, 0:nt], pb.t[:, 0:nt], AF.Identity, bias=Bkv.t[:, j:j + 1]),
                         reads=[pb, Bkv], writes=[zc], acc=True)
                rstd = rstd_from(C, W, [(zc, zc.t[:, j, 0:nt]) for j in range(4)], 512, nt)
                for j in range(4):
                    P.op("dve", lambda e, j=j, rstd=rstd: e.scalar_tensor_tensor(ckvn.t[:, j, 0:nt], zc.t[:, j, 0:nt], gn.t[:, 6 + j:7 + j], rstd.t[:, 0:nt], ALU.mult, ALU.mult),
                         reads=[zc, gn, rstd], writes=[ckvn], acc=True)
                pb = nextps(C)
                proj(C, pb, 64, nt, Wkv, Wkv.t, 512, 16, hT, 0)
                z = zk[0]
                P.op("act", lambda e, pb=pb, z=z: e.activation(z.t[0:64, 0:nt], pb.t[0:64, 0:nt], AF.Identity, bias=Bkv.t[0:64, 4:5]),
                     reads=[pb, Bkv], writes=[z])
                o = ko[nko % 4]; nko += 1
                rope(C, W, o, o.t[0:64, 0:nt], z, z.t[0:64, 0:nt], ta, 64, nt)
                P.dma("sp", KpeT.t[:, row0:row0 + nt], o.t[0:64, 0:nt], src=o, dst=KpeT)
                for hh in range(2):
                    pb = nextps(C)
                    proj(C, pb, 128, nt, Wkv, Wkv.t, 576 + hh * 128, 16, hT, 0)
                    z = zk[1 - hh % 2] if False else zk[hh % 2]
                    P.op("act", lambda e, pb=pb, z=z, hh=hh: e.activation(z.t[:, 0:nt], pb.t[:, 0:nt], AF.Identity, bias=Bkv.t[:, 5 + hh:6 + hh]),
                         reads=[pb, Bkv], writes=[z])
                    rstd = rstd_from(C, W, [(z, z.t[:, 0:nt])], 128, nt)
                    k_ = kn[hh % 2]
                    P.op("dve", lambda e, z=z, k_=k_, rstd=rstd: e.scalar_tensor_tensor(k_.t[:, 0:nt], z.t[:, 0:nt], gn.t[:, 11:12], rstd.t[:, 0:nt], ALU.mult, ALU.mult),
                         reads=[z, gn, rstd], writes=[k_])
                    o = ko[nko % 4]; nko += 1
                    rope(C, W, o, o.t[:, 0:nt], k_, k_.t[:, 0:nt], tb, 128, nt)
                    P.dma("sp", KbT.t[hh, :, row0:row0 + nt], o.t[:, 0:nt], src=o, dst=KbT)
                for hh in range(2):
                    pb = nextps(C)
                    proj(C, pb, 128, nt, Wkv, Wkv.t, 832 + hh * 128, 16, hT, 0)
                    v_ = vT[hh % 2]
                    P.op("act", lambda e, pb=pb, v_=v_, hh=hh: e.activation(v_.t[:, 0:nt], pb.t[:, 0:nt], AF.Identity, bias=Bkv.t[:, 7 + hh:8 + hh]),
                         reads=[pb, Bkv], writes=[v_])
                    W.vbT = getattr(W, "vbT", {})
                    W.vbT[hh] = v_
                for t in range(ntl):
                    pb = nextps(C)
                    pv = pb.t[:].bitcast(BF16)
                    for hh in range(2):
                        v_ = W.vbT[hh]
                        P.op("pe", lambda e, pv=pv, hh=hh, v_=v_, t=t: e.transpose(pv[:, hh * 128:(hh + 1) * 128], v_.t[:, t * 128:(t + 1) * 128], C.idb),
                             reads=[v_, C.cb], writes=[pb], acc=True)
                    vb_ = vbo[t % 2]
                    P.op("dve", lambda e, pv=pv, vb_=vb_: e.tensor_copy(vb_.t, pv[:, 0:256]), reads=[pb], writes=[vb_])
                    P.dma("sp", Vb.t[row0 // 128 + t, :, :], vb_.t, src=vb_, dst=Vb)
                for h in range(8):
                    pb = nextps(C)
                    for j in range(4):
                        P.mm(pb.t[:, 0:nt], wkvb.t[:, j, h * 256:h * 256 + 128], ckvn.t[:, j, 0:nt], j == 0, j == 3, [wkvb, ckvn], pb)
                    o = ko[nko % 4]; nko += 1
                    if h % 2 == 0:
                        P.op("act", lambda e, pb=pb, o=o: e.activation(o.t[:, 0:nt], pb.t[:, 0:nt], AF.Copy), reads=[pb], writes=[o])
                    else:
                        P.op("dve", lambda e, pb=pb, o=o: e.tensor_copy(o.t[:, 0:nt], pb.t[:, 0:nt]), reads=[pb], writes=[o])
                    P.dma("sp", KaT.t[h, :, row0:row0 + nt], o.t[:, 0:nt], src=o, dst=KaT)
                wv = wkvb.t.rearrange("p k (h two d) -> p k h two d", two=2, d=128)
                for t in range(ntl):
                    v2 = vo[t % 2]
                    for half in range(2):
                        pb = nextps(C)
                        for j in range(4):
                            P.mm(pb.t[:, :].rearrange("p (h d) -> p h d", d=128), ckvn.t[:, j, t * 128:(t + 1) * 128],
                                 wv[:, j, half * 4:(half + 1) * 4, 1, :], j == 0, j == 3, [wkvb, ckvn], pb)
                        if half == 0:
                            P.op("act", lambda e, pb=pb, v2=v2: e.activation(v2.t[:, 0:512], pb.t[:, :], AF.Copy), reads=[pb], writes=[v2], acc=True)
                        else:
                            P.op("dve", lambda e, pb=pb, v2=v2: e.tensor_copy(v2.t[:, 512:1024], pb.t[:, :]), reads=[pb], writes=[v2], acc=True)
                    P.dma("sp", Va.t[row0 // 128 + t, :, :], v2.t, src=v2, dst=Va)

        P.phase = "KVCTX"
        kv_phase(1, [(0, 256)])
        P.phase = "KVLAT"
        kv_phase(0, [(256 + i * 512, 512) for i in range(16)])

        P.phase = "Q"
        ar.reset()
        W = Ctx()
        hTq = ar.alloc("hTq", [16, 2048], BF16)
        mk = ar.mark()
        ln_alloc(C, W)
        W.lnc = 0
        ln_transpose(C, W, xq, 0, 16, hTq)
        ar.release(mk)
        mk = ar.mark()
        Wcq = ar.alloc("Wcq", [16, 768], BF16)
        Bcq = ar.alloc("Bcq", [6], F32)
        wqb = ar.alloc("wqb", [6, 1536], BF16)
        work_alloc(C, W)
        zq = ar.alloc("zq", [6, 512], F32)
        cqn = ar.alloc("cqn", [6, 512], BF16)
        zk = [ar.alloc(f"qzk{i}", [512], F32) for i in range(2)]
        tabA = [ar.alloc(f"qtabA{i}", [2, 512], F32) for i in range(1)]
        ko = [ar.alloc(f"qko{i}", [512], BF16) for i in range(4)]
        for j in range(3):
            prep_panel(C, w_in, 16, j * 256, 256, 0, Wcq, Wcq.t[:, :, j * 256:(j + 1) * 256], Bcq, Bcq.t[:, 2 * j:2 * j + 2])
        for j in range(6):
            prep_panel(C, wq_b, 6, j * 256, 256, None, wqb, wqb.t[:, :, j * 256:(j + 1) * 256])
        nko = 0
        for tbi in range(4):
            tok0 = tbi * 512
            ta = tabA[0]
            P.dma("sp", ta.t[0:64, :, :], ropeA_q[:, :, tok0:tok0 + 512], dst=ta)
            for j in range(6):
                pb = nextps(C)
                proj(C, pb, 128, 512, Wcq, Wcq.t, j * 128, 16, hTq, tok0)
                P.op("act", lambda e, pb=pb, j=j: e.activation(zq.t[:, j, :], pb.t[:, :], AF.Identity, bias=Bcq.t[:, j:j + 1]),
                     reads=[pb, Bcq], writes=[zq], acc=True)
            rstd = rstd_from(C, W, [(zq, zq.t[:, j, :]) for j in range(6)], 768, 512)
            for j in range(6):
                P.op("dve", lambda e, j=j, rstd=rstd: e.scalar_tensor_tensor(cqn.t[:, j, :], zq.t[:, j, :], gn.t[:, j:j + 1], rstd.t[:, :], ALU.mult, ALU.mult),
                     reads=[zq, gn, rstd], writes=[cqn], acc=True)
            for h in range(8):
                pb = nextps(C)
                for j in range(6):
                    P.mm(pb.t[:, :], wqb.t[:, j, h * 192:h * 192 + 128], cqn.t[:, j, :], j == 0, j == 5, [wqb, cqn], pb)
                o = ko[nko % 4]; nko += 1
                P.op("act", lambda e, pb=pb, o=o: e.activation(o.t[:, :], pb.t[:, :], AF.Copy), reads=[pb], writes=[o])
                P.dma("sp", QaT.t[h, :, tok0:tok0 + 512], o.t[:, :], src=o, dst=QaT)
                pb = nextps(C)
                for j in range(6):
                    P.mm(pb.t[0:64, :], wqb.t[:, j, h * 192 + 128:h * 192 + 192], cqn.t[:, j, :], j == 0, j == 5, [wqb, cqn], pb)
                z = zk[h % 2]
                P.op("dve", lambda e, pb=pb, z=z: e.tensor_copy(z.t[0:64, :], pb.t[0:64, :]), reads=[pb], writes=[z])
                o = ko[nko % 4]; nko += 1
                rope(C, W, o, o.t[0:64, :], z, z.t[0:64, :], ta, 64, 512)
                P.dma("sp", QpeT.t[h, :, tok0:tok0 + 512], o.t[0:64, :], src=o, dst=QpeT)
        ar.release(mk)
        mk = ar.mark()
        work_alloc(C, W)
        zk = [ar.alloc(f"qzk{i}", [512], F32) for i in range(2)]
        kn = [ar.alloc(f"qkn{i}", [512], F32) for i in range(2)]
        tabB = [ar.alloc(f"qtabB{i}", [2, 512], F32) for i in range(1)]
        ko = [ar.alloc(f"qko{i}", [512], BF16) for i in range(4)]
        Wp = [ar.alloc(f"Wp{i}", [16, 256], BF16) for i in range(2)]
        Bp = [ar.alloc(f"Bp{i}", [2], F32) for i in range(2)]
        def qcol0(k):
            return 1344 + k * 256 if k < 4 else 2880 + (k - 4) * 256

        ld_next = prep_load(C, w_in, 16, qcol0(0), 256)
        for pi in range(12):
            wp = Wp[pi % 2]
            bp = Bp[pi % 2]
            prep_compute(C, ld_next, 16, 256, 0, wp, wp.t, bp, bp.t)
            if pi + 1 < 12:
                ld_next = prep_load(C, w_in, 16, qcol0(pi + 1), 256)
            for tbi in range(4):
                tok0 = tbi * 512
                if pi < 4:
                    tb = tabB[0]
                    P.dma("sp", tb.t[:, :, :], ropeB_q[:, :, tok0:tok0 + 512], dst=tb)
                for cc in range(2):
                    pb = nextps(C)
                    proj(C, pb, 128, 512, wp, wp.t, cc * 128, 16, hTq, tok0)
                    o = ko[nko % 4]; nko += 1
                    if pi < 4:
                        hh = pi * 2 + cc
                        z = zk[cc]
                        P.op("act", lambda e, pb=pb, z=z, bp=bp, cc=cc: e.activation(z.t[:, :], pb.t[:, :], AF.Identity, bias=bp.t[:, cc:cc + 1]),
                             reads=[pb, bp], writes=[z])
                        rstd = rstd_from(C, W, [(z, z.t[:, :])], 128, 512)
                        k_ = kn[cc]
                        P.op("dve", lambda e, z=z, k_=k_, rstd=rstd: e.scalar_tensor_tensor(k_.t[:, :], z.t[:, :], gn.t[:, 10:11], rstd.t[:, :], ALU.mult, ALU.mult),
                             reads=[z, gn, rstd], writes=[k_])
                        rope(C, W, o, o.t[:, :], k_, k_.t[:, :], tb, 128, 512)
                        P.dma("sp", QbT.t[hh, :, tok0:tok0 + 512], o.t[:, :], src=o, dst=QbT)
                    else:
                        ch = (pi - 4) * 2 + cc
                        P.op("act", lambda e, pb=pb, o=o, bp=bp, cc=cc: e.activation(o.t[:, :], pb.t[:, :], AF.Silu, bias=bp.t[:, cc:cc + 1]),
                             reads=[pb, bp], writes=[o])
                        P.dma("sp", Gd.t[ch, :, tok0:tok0 + 512], o.t[:, :], src=o, dst=Gd)

        P.phase = "ATT"
        ar.reset()
        Kt = [ar.alloc(f"Kt{i}", [NKV], BF16) for i in range(2)]
        Vt = [ar.alloc(f"Vt{i}", [NKT, 128], BF16) for i in range(2)]
        Kpe = ar.alloc("Kpe", [NKV], BF16)
        Qt = [ar.alloc(f"Qt{i}", [2048], BF16) for i in range(2)]
        Qp = [ar.alloc(f"Qp{i}", [2048], BF16) for i in range(2)]
        Gt = [ar.alloc(f"Gt{i}", [2048], BF16) for i in range(2)]
        PT = [ar.alloc(f"PT{i}", [512], BF16) for i in range(6)]
        rc = [ar.alloc(f"rc{i}", [512], F32) for i in range(2)]
        yt = [ar.alloc(f"yt{i}", [512], F32) for i in range(2)]
        yo = [ar.alloc(f"yo{i}", [512], BF16) for i in range(2)]
        P.dma("sp", Kpe.t[0:64, :], KpeT.t, src=KpeT, dst=Kpe)
        SPS = C.psb[0:4]
        OACC = C.psb[4:6]
        SACC = C.psb[6:8]
        kvslot = -1
        u = 0
        for hd in range(16):
            isA = hd < 8
            if isA or (hd - 8) % 4 == 0:
                kvslot += 1
                kt_, vt_ = Kt[kvslot % 2], Vt[kvslot % 2]
                if isA:
                    P.dma("sp", kt_.t, KaT.t[hd], src=KaT, dst=kt_)
                    P.dma("sp", vt_.t, Va.t[:, :, hd * 128:(hd + 1) * 128].rearrange("t p d -> p t d"), src=Va, dst=vt_)
                else:
                    kvh = (hd - 8) // 4
                    P.dma("sp", kt_.t, KbT.t[kvh], src=KbT, dst=kt_)
                    P.dma("sp", vt_.t, Vb.t[:, :, kvh * 128:(kvh + 1) * 128].rearrange("t p d -> p t d"), src=Vb, dst=vt_)
            qt_ = Qt[hd % 2]
            qp_ = Qp[hd % 2]
            gt_ = Gt[hd % 2]
            if isA:
                P.dma("sp", qt_.t, QaT.t[hd], src=QaT, dst=qt_)
                P.dma("sp", qp_.t[0:64, :], QpeT.t[hd], src=QpeT, dst=qp_)
            else:
                P.dma("sp", qt_.t, QbT.t[hd - 8], src=QbT, dst=qt_)
            P.dma("sp", gt_.t, Gd.t[hd], src=Gd, dst=gt_)
            scale = A_SCALE if isA else B_SCALE
            for qb in range(4):
                oT = OACC[u % 2]
                sm = SACC[u % 2]
                qs = slice(qb * 512, (qb + 1) * 512)

                def qk(kt):
                    sp_ = SPS[kt % 4]
                    ks = slice(kt * 128, (kt + 1) * 128)
                    P.mm(sp_.t[:, :], kt_.t[:, ks], qt_.t[:, qs], True, not isA, [kt_, qt_], sp_)
                    if isA:
                        P.mm(sp_.t[:, :], Kpe.t[0:64, ks], qp_.t[0:64, qs], False, True, [Kpe, qp_], sp_)

                def rest(kt):
                    sp_ = SPS[kt % 4]
                    pt = PT[kt % 6]
                    P.op("act", lambda e, pt=pt, sp_=sp_, sc=scale: e.activation(pt.t, sp_.t[:, :], AF.Exp, scale=sc), reads=[sp_], writes=[pt])
                    P.mm(oT.t[:, :], vt_.t[:, kt, :], pt.t, kt == 0, kt == NKT - 1, [vt_, pt], oT)
                    P.mm(sm.t[:, :], C.onesb, pt.t, kt == 0, kt == NKT - 1, [pt, C.cb], sm)

                qk(0)
                qk(1)
                for kt in range(NKT):
                    if kt + 2 < NKT:
                        qk(kt + 2)
                    rest(kt)
                r_ = rc[u % 2]
                y_ = yt[u % 2]
                o_ = yo[u % 2]
                P.op("dve", lambda e, r_=r_, sm=sm: e.reciprocal(r_.t, sm.t[:, :]), reads=[sm], writes=[r_])
                P.op("dve", lambda e, r_=r_, y_=y_, oT=oT: e.tensor_tensor(y_.t, oT.t[:, :], r_.t, ALU.mult), reads=[oT, r_], writes=[y_])
                P.op("pool", lambda e, y_=y_, o_=o_, gt_=gt_, qs=qs: e.tensor_tensor(o_.t, y_.t, gt_.t[:, qs], ALU.mult), reads=[y_, gt_], writes=[o_])
                P.dma("sp", Yd.t[hd, :, qs], o_.t, src=o_, dst=Yd)
                u += 1

        P.phase = "OUT"
        ar.reset()
        W = Ctx()
        Wo = ar.alloc("Wo", [16, 2048], BF16)
        Gys = [ar.alloc(f"Gy{i}", [16, 128], BF16) for i in range(2)]
        gb2 = ar.alloc("gb2", [2048], F32)
        lnbc = ar.alloc("lnbc", [2, 2048], F32)
        xts = [ar.alloc(f"oxt{i}", [2048], F32) for i in range(2)]
        tmp = [ar.alloc(f"otmp{i}", [2048], F32) for i in range(1)]
        st = ar.alloc("ost", [4, 6], F32)
        mv = ar.alloc("omv", [2], F32)
        rs = ar.alloc("ors", [1], F32)
        P.dma("sp", gb2.t, Gbc_d.t, src=Gbc_d, dst=gb2)
        P.dma("sp", lnbc.t[:, 0, :], lngb[0:1, :].partition_broadcast(128), dst=lnbc)
        P.dma("sp", lnbc.t[:, 1, :], lngb[1:2, :].partition_broadcast(128), dst=lnbc)
        for j in range(8):
            prep_panel(C, w_out, 16, j * 256, 256, None, Wo, Wo.t[:, :, j * 256:(j + 1) * 256])
        if fused:
            x1b = P.dram("X1d", [2048, 2048], F32)
            x1 = x1b.t
        else:
            x1b = P.view("x1out", x1)
        for t in range(16):
            xt = xts[t % 2]
            tm = tmp[0]
            P.dma("sp", xt.t, xq[t * 128:(t + 1) * 128, :], dst=xt)
            Gy = Gys[t % 2]
            P.dma("sp", Gy.t, Yd.t[:, :, t * 128:(t + 1) * 128].rearrange("c p t -> p c t"), src=Yd, dst=Gy)
            for nb in range(4):
                pb = nextps(C)
                ns = slice(nb * 512, (nb + 1) * 512)
                for kc in range(16):
                    P.mm(pb.t[:, :], Gy.t[:, kc, :], Wo.t[:, kc, ns], kc == 0, kc == 15, [Gy, Wo], pb)
                P.op("dve", lambda e, pb=pb, ns=ns, tm=tm: e.tensor_tensor(tm.t[:, ns], pb.t[:, :], gb2.t[:, ns], ALU.mult),
                     reads=[pb, gb2], writes=[tm], acc=True)
            P.op("dve", lambda e, xt=xt, tm=tm: e.scalar_tensor_tensor(tm.t, xt.t, ALPHA, tm.t, ALU.mult, ALU.add), reads=[xt, tm], writes=[tm])
            for c in range(4):
                P.op("dve", lambda e, c=c, tm=tm: e.bn_stats(st.t[:, c, :], tm.t[:, c * 512:(c + 1) * 512]), reads=[tm], writes=[st], acc=True)
            P.op("dve", lambda e: e.bn_aggr(mv.t, st.t), reads=[st], writes=[mv])
            P.op("act", lambda e: e.activation(rs.t, mv.t[:, 1:2], AF.Sqrt, bias=C.eps.t[:, 0:1], scale=1.0), reads=[mv, C.eps], writes=[rs])
            P.op("dve", lambda e: e.reciprocal(rs.t, rs.t), reads=[rs], writes=[rs])
            P.op("dve", lambda e, tm=tm: e.tensor_scalar(tm.t, tm.t, mv.t[:, 0:1], rs.t[:, 0:1], ALU.subtract, ALU.mult), reads=[tm, mv, rs], writes=[tm])
            P.op("pool", lambda e, tm=tm: e.tensor_tensor(tm.t, tm.t, lnbc.t[:, 0, :], ALU.mult), reads=[tm, lnbc], writes=[tm])
            P.op("dve", lambda e, tm=tm, xt=xt: e.tensor_tensor(xt.t, tm.t, lnbc.t[:, 1, :], ALU.add), reads=[tm, lnbc], writes=[xt])
            P.dma("sp", x1[t * 128:(t + 1) * 128, :], xt.t, src=xt, dbuf=x1b)
        if fused:
            outb = fused_tail(C, nc, din, x1b)
            P.fence("sp", [outb])
        else:
            P.fence("sp", [x1b])
        P.emit()
    return nc


RS_GROUPS = [[0, 1, 2, 3], [4, 5, 6, 7]]


def fused_tail(C, nc, din, x1b):
    P, ar = C.P, C.ar
    cvec1 = din("cvec1", [128, 32])
    ada_w1 = din("ada_w1", [2048, 6144])
    ada_b1 = din("ada_b1", [2, 6144])
    wf = din("w_in_f", [2048, 8192])
    w_out_f = din("w_out_f", [4096, 2048])
    lngb1 = din("lngb1", [2, 2048])
    sel_d = din("sel", [128, 4])
    cn_d = din("cn", [128, 2, 2, 256], BF16)
    w64_d = din("w64", [128, 128], BF16)
    M_d = din("Mtw", [128, 64, 2, 128], BF16)
    outp = nc.dram_tensor("out", [2048, 2048], F32, kind="ExternalOutput").ap()
    U_in = [P.dram(f"U_in{q}", [4 * 256, 8192], BF16) for q in range(4)]
    U_out = [P.dram(f"U_out{q}", [256, 8192], BF16) for q in range(4)]
    F_in = [P.dram(f"F_in{g}", [4 * 1024, 2048], BF16) for g in range(4)]
    F_out = [P.dram(f"F_out{g}", [1024, 2048], BF16) for g in range(4)]
    Gl = P.dram("Gl", [32, 128, 2048], BF16)
    Gbc1 = P.dram("Gbc1", [128, 2048], F32)
    Td = P.dram("Td", [16, 128, 2048], F32)
    sel = P.sb("sel_sb", [128, 4], F32)
    P.dma("sp", sel.t[:], sel_d, dst=sel)

    P.phase = "B1MOD"
    ar.reset()
    gbc = ar.alloc("gbc1_tmp", [2048], F32)
    emit_modulation(C, cvec1, ada_w1, ada_b1[:, :], gbc, 0)
    P.dma("sp", Gbc1.t, gbc.t, src=gbc, dst=Gbc1)
    ar.reset()
    P.phase = "B1"
    W = Ctx()
    W.xsrc = x1b
    hT1 = ar.alloc("hT1", [16, 2048], BF16)
    mk = ar.mark()
    ln_alloc(C, W)
    W.lnc = 0
    ln_transpose(C, W, x1b.t, 0, 16, hT1)
    ar.release(mk)
    Wp = [ar.alloc(f"fWp{i}", [16, 256], BF16) for i in range(2)]
    Bp = [ar.alloc(f"fBp{i}", [2], F32) for i in range(2)]
    uo = [ar.alloc(f"fuo{i}", [512], BF16) for i in range(2)]
    us = [ar.alloc(f"fus{i}", [4, 512], BF16) for i in range(2)]
    go = [ar.alloc(f"fgo{i}", [512], BF16) for i in range(2)]
    n = 0
    order = [(True, d * 4 + q) for q in range(4) for d in range(4)] + [(False, i) for i in range(16)]
    def col0(k):
        return order[k][1] * 256 if order[k][0] else 4096 + order[k][1] * 256

    ld_next = prep_load(C, wf, 16, col0(0), 256)
    for pi, (isu, pidx) in enumerate(order):
        wp, bp = Wp[pi % 2], Bp[pi % 2]
        prep_compute(C, ld_next, 16, 256, 0, wp, wp.t, bp, bp.t)
        if pi + 1 < len(order):
            ld_next = prep_load(C, wf, 16, col0(pi + 1), 256)
        for tbi in range(4):
            tok0 = tbi * 512
            for cc in range(2):
                pb = nextps(C)
                proj(C, pb, 128, 512, wp, wp.t, cc * 128, 16, hT1, tok0)
                ch = pidx * 2 + cc
                if isu:
                    o = uo[n % 2]
                    s4 = us[n % 2]
                    n += 1
                    P.op("act", lambda e, pb=pb, o=o, bp=bp, cc=cc: e.activation(o.t, pb.t[:, :], AF.Identity, bias=bp.t[:, cc:cc + 1]),
                         reads=[pb, bp], writes=[o])
                    for j in range(4):
                        if j % 2 == 0:
                            P.op("dve", lambda e, o=o, s4=s4, j=j: e.tensor_scalar_mul(s4.t[:, j, :], o.t, sel.t[:, j:j + 1]),
                                 reads=[o, sel], writes=[s4], acc=True)
                        else:
                            P.op("act", lambda e, o=o, s4=s4, j=j: e.activation(s4.t[:, j, :], o.t, AF.Copy, scale=sel.t[:, j:j + 1]),
                                 reads=[o, sel], writes=[s4], acc=True)
                    dest, q = pidx // 4, pidx % 4
                    row0 = dest * 256 + cc * 128
                    dst_ap = U_in[q].t[row0:row0 + 128, :].rearrange("p (j t) -> p j t", j=4)[:, :, tok0:tok0 + 512]
                    P.dma("sp", dst_ap, s4.t, src=s4, dst=U_in[q])
                else:
                    o = go[n % 2]
                    n += 1
                    P.op("act", lambda e, pb=pb, o=o, bp=bp, cc=cc: e.activation(o.t, pb.t[:, :], AF.Silu, bias=bp.t[:, cc:cc + 1]),
                         reads=[pb, bp], writes=[o])
                    chp = ((ch % 8) // 2) * 8 + (ch // 8) * 2 + ch % 2
                    P.dma("sp", Gl.t[chp, :, tok0:tok0 + 512], o.t, src=o, dst=Gl)
        if isu and pidx // 4 == 3:
            q = pidx % 4
            P.op("pool", lambda e, q=q: e.collective_compute("ReduceScatter", ALU.add, replica_groups=RS_GROUPS, ins=[U_in[q].t], outs=[U_out[q].t]),
                 reads=[U_in[q]], writes=[U_out[q]], dma_dst=U_out[q], dma_inc=1)

    P.phase = "B2"
    ar.reset()
    cn = ar.alloc("cn", [2, 2, 256], BF16)
    w64 = ar.alloc("w64", [128], BF16)
    Mt = ar.alloc("Mt", [64, 2, 128], BF16)
    P.dma("sp", cn.t, cn_d, dst=cn)
    P.dma("sp", w64.t, w64_d, dst=w64)
    P.dma("sp", Mt.t, M_d, dst=Mt)
    uT = ar.alloc("uT", [2, 8192], BF16)
    fT = ar.alloc("fT", [8192], BF16)
    z = ar.alloc("z", [128, 128], BF16)
    Y = ar.alloc("Y", [128, 128], BF16)
    fs = [ar.alloc(f"fs{i}", [2048], BF16) for i in range(2)]
    nfs = 0
    for g in range(4):
        P.dma("sp", uT.t, U_out[g].t.rearrange("(c p) t -> p c t", p=128), src=U_out[g], dst=uT)
        uv = uT.t.rearrange("p c (l1 l2) -> p c l2 l1", l2=128)
        for kh in range(2):
            ks = slice(kh * 128, (kh + 1) * 128)
            for l2p in range(32):
                pb = nextps(C)
                for q in range(4):
                    l2 = l2p * 4 + q
                    for ri in range(2):
                        for cc in range(2):
                            P.mm(pb.t[ri * 64:(ri + 1) * 64, q * 128:(q + 1) * 128], uv[:, cc, l2, :], cn.t[:, cc, ri, ks],
                                 cc == 0, cc == 1, [uT, cn], pb)
                dst = z.t[:, l2p * 4:(l2p + 1) * 4, :]
                src = pb.t[:, :].rearrange("p (a b) -> p a b", b=128)
                if l2p % 2 == 0:
                    P.op("act", lambda e, dst=dst, src=src: e.activation(dst, src, AF.Copy), reads=[pb], writes=[z], acc=True)
                else:
                    P.op("dve", lambda e, dst=dst, src=src: e.tensor_copy(dst, src), reads=[pb], writes=[z], acc=True)
            for k3p in range(32):
                pb = nextps(C)
                for q in range(4):
                    k3 = k3p * 4 + q
                    P.mm(pb.t[:, q * 128:(q + 1) * 128], z.t[:, :, k3], w64.t, True, True, [z, w64], pb)
                dst = Y.t[:, k3p * 4:(k3p + 1) * 4, :]
                src = pb.t[:, :].rearrange("p (a b) -> p a b", b=128)
                if k3p % 2 == 0:
                    P.op("act", lambda e, dst=dst, src=src: e.activation(dst, src, AF.Copy), reads=[pb], writes=[Y], acc=True)
                else:
                    P.op("dve", lambda e, dst=dst, src=src: e.tensor_copy(dst, src), reads=[pb], writes=[Y], acc=True)
            fv = fT.t.rearrange("p (k2 k1) -> p k1 k2", k1=64)
            for k1p in range(16):
                pb = nextps(C)
                for q in range(4):
                    k1 = k1p * 4 + q
                    for ri in range(2):
                        P.mm(pb.t[:, q * 128:(q + 1) * 128], Y.t[:, :, ri * 64 + k1], Mt.t[:, k1, ri, :], ri == 0, ri == 1, [Y, Mt], pb)
                fsl = fv[:, k1p * 4:(k1p + 1) * 4, :]
                src = pb.t[:, :].rearrange("p (a b) -> p a b", b=128)
                if k1p % 2 == 0:
                    P.op("act", lambda e, fsl=fsl, src=src: e.activation(fsl, src, AF.Copy, scale=FSCALE), reads=[pb], writes=[fT], acc=True)
                else:
                    P.op("dve", lambda e, fsl=fsl, src=src: e.tensor_scalar_mul(fsl, src, FSCALE), reads=[pb], writes=[fT], acc=True)
            for d in range(4):
                for j in range(4):
                    f_ = fs[nfs % 2]
                    nfs += 1
                    if j % 2 == 0:
                        P.op("dve", lambda e, f_=f_, d=d, j=j: e.tensor_scalar_mul(f_.t, fT.t[:, d * 2048:(d + 1) * 2048], sel.t[:, j:j + 1]),
                             reads=[fT, sel], writes=[f_])
                    else:
                        P.op("act", lambda e, f_=f_, d=d, j=j: e.activation(f_.t, fT.t[:, d * 2048:(d + 1) * 2048], AF.Copy, scale=sel.t[:, j:j + 1]),
                             reads=[fT, sel], writes=[f_])
                    r0 = d * 1024 + j * 256 + kh * 128
                    P.dma("sp", F_in[g].t[r0:r0 + 128, :], f_.t, src=f_, dst=F_in[g])
        P.op("pool", lambda e, g=g: e.collective_compute("ReduceScatter", ALU.add, replica_groups=RS_GROUPS, ins=[F_in[g].t], outs=[F_out[g].t]),
             reads=[F_in[g]], writes=[F_out[g]], dma_dst=F_out[g], dma_inc=1)

    P.phase = "C"
    ar.reset()
    Wo = ar.alloc("fWo", [32, 1024], BF16)
    gb2 = ar.alloc("fgb2", [2048], F32)
    lnbc = ar.alloc("flnbc", [2, 2048], F32)
    Fys = [ar.alloc(f"fFy{i}", [32, 128], BF16) for i in range(2)]
    Ggs = [ar.alloc(f"fGg{i}", [32, 128], BF16) for i in range(2)]
    Gys = [ar.alloc(f"fGy{i}", [32, 128], BF16) for i in range(2)]
    tms = [ar.alloc(f"ftm{i}", [1024], F32) for i in range(2)]
    xts = [ar.alloc(f"fxt{i}", [2048], F32) for i in range(1)]
    tmp = ar.alloc("fotmp", [2048], F32)
    st = ar.alloc("fost", [4, 6], F32)
    mv = ar.alloc("fomv", [2], F32)
    rs = ar.alloc("fors", [1], F32)
    P.dma("sp", gb2.t, Gbc1.t, src=Gbc1, dst=gb2)
    P.dma("sp", lnbc.t[:, 0, :], lngb1[0:1, :].partition_broadcast(128), dst=lnbc)
    P.dma("sp", lnbc.t[:, 1, :], lngb1[1:2, :].partition_broadcast(128), dst=lnbc)
    n = 0
    for nh in range(2):
        for kh in range(2):
            for j in range(4):
                c0 = nh * 1024 + j * 256
                prep_panel(C, w_out_f[kh * 2048:(kh + 1) * 2048, :], 16, c0, 256, None, Wo,
                           Wo.t[:, kh * 16:(kh + 1) * 16, j * 256:(j + 1) * 256])
        for t in range(16):
            Fy, Gg, Gy, tm = Fys[n % 2], Ggs[n % 2], Gys[n % 2], tms[n % 2]
            n += 1
            for g in range(4):
                P.dma("sp", Fy.t[:, g * 8:(g + 1) * 8, :], F_out[g].t[:, t * 128:(t + 1) * 128].rearrange("(c p) t -> p c t", p=128),
                      src=F_out[g], dst=Fy)
            P.dma("sp", Gg.t, Gl.t[:, :, t * 128:(t + 1) * 128].rearrange("c p t -> p c t"), src=Gl, dst=Gg)
            P.op("pool", lambda e, Fy=Fy, Gg=Gg, Gy=Gy: e.tensor_tensor(Gy.t, Fy.t, Gg.t, ALU.mult), reads=[Fy, Gg], writes=[Gy])
            for nb in range(2):
                pb = nextps(C)
                ns = slice(nb * 512, (nb + 1) * 512)
                gs = slice(nh * 1024 + nb * 512, nh * 1024 + (nb + 1) * 512)
                for kp in range(32):
                    kc = ((kp % 8) // 2) * 8 + (kp // 8) * 2 + kp % 2
                    P.mm(pb.t[:, :], Gy.t[:, kp, :], Wo.t[:, kc, ns], kp == 0, kp == 31, [Gy, Wo], pb)
                P.op("dve", lambda e, pb=pb, ns=ns, gs=gs, tm=tm: e.tensor_tensor(tm.t[:, ns], pb.t[:, :], gb2.t[:, gs], ALU.mult),
                     reads=[pb, gb2], writes=[tm], acc=True)
            P.dma("sp", Td.t[t, :, nh * 1024:(nh + 1) * 1024], tm.t, src=tm, dst=Td)
    ob = P.view("out_b", outp)
    for t in range(16):
        xt = xts[0]
        tm = tmp
        P.dma("sp", xt.t, x1b.t[t * 128:(t + 1) * 128, :], src=x1b, dst=xt)
        P.dma("sp", tm.t, Td.t[t], src=Td, dst=tm)
        P.op("dve", lambda e, xt=xt, tm=tm: e.scalar_tensor_tensor(tm.t, xt.t, ALPHA, tm.t, ALU.mult, ALU.add), reads=[xt, tm], writes=[tm])
        for c in range(4):
            P.op("dve", lambda e, c=c, tm=tm: e.bn_stats(st.t[:, c, :], tm.t[:, c * 512:(c + 1) * 512]), reads=[tm], writes=[st], acc=True)
        P.op("dve", lambda e: e.bn_aggr(mv.t, st.t), reads=[st], writes=[mv])
        P.op("act", lambda e: e.activation(rs.t, mv.t[:, 1:2], AF.Sqrt, bias=C.eps.t[:, 0:1], scale=1.0), reads=[mv, C.eps], writes=[rs])
        P.op("dve", lambda e: e.reciprocal(rs.t, rs.t), reads=[rs], writes=[rs])
        P.op("dve", lambda e, tm=tm: e.tensor_scalar(tm.t, tm.t, mv.t[:, 0:1], rs.t[:, 0:1], ALU.subtract, ALU.mult), reads=[tm, mv, rs], writes=[tm])
        P.op("pool", lambda e, tm=tm: e.tensor_tensor(tm.t, tm.t, lnbc.t[:, 0, :], ALU.mult), reads=[tm, lnbc], writes=[tm])
        P.op("dve", lambda e, tm=tm, xt=xt: e.tensor_tensor(xt.t, tm.t, lnbc.t[:, 1, :], ALU.add), reads=[tm, lnbc], writes=[xt])
        P.dma("sp", outp[t * 128:(t + 1) * 128, :], xt.t, src=xt, dbuf=ob)
    return ob


def fused_inputs(inp):
    maps = stage_a_inputs(inp)
    cn, w64, M = fft_consts()
    for core in range(8):
        b, r = core // 4, core % 4
        cv = np.zeros((128, 16, 2), np.float32)
        cv[:, :, 0] = _pk(inp["c"][b], 16)
        cv[:, :, 1] = cv[:, :, 0]
        sel = np.zeros((128, 4), np.float32)
        sel[:, r] = 1.0
        maps[core].update({
            "cvec1": cv.reshape(128, 32), "ada_w1": inp["ada_w"][1],
            "ada_b1": np.stack([inp["ada_b"][1], inp["ada_b"][1]]),
            "w_in_f": inp["w_in_fourier"][0], "w_out_f": inp["w_out_fourier"][0],
            "lngb1": np.stack([inp["ln_g"][1], inp["ln_b"][1]]), "sel": sel,
            "cn": cn, "w64": w64, "Mtw": M,
        })
    return maps


def kernel_fused(**inp):
    inp = {k: np.asarray(v) for k, v in inp.items()}
    nc = build_stage_a(fused=True)
    res = run_bass_kernel_spmd(nc, fused_inputs(inp), core_ids=list(range(8)))
    out = np.zeros((2, 8192, 2048), np.float32)
    for core in range(8):
        b, r = core // 4, core % 4
        out[b, r * 2048:(r + 1) * 2048] = res.results[core]["out"]
    return out


def _rope_tables(rot_dim, seq=8192, grid_w=64):
    t = np.arange(seq)
    r = (t // grid_w).astype(np.float32)
    col = (t % grid_w).astype(np.float32)
    nf = rot_dim // 4
    inv = (np.float32(10000.0) ** (-(np.arange(nf, dtype=np.float32)) / np.float32(nf))).astype(np.float32)
    ang = np.concatenate([r[:, None] * inv[None, :], col[:, None] * inv[None, :]], axis=-1).astype(np.float32)
    cos = np.cos(ang).astype(np.float32)
    sin = np.sin(ang).astype(np.float32)
    cosT = np.repeat(cos, 2, axis=1).T
    sgn = np.tile(np.array([-1.0, 1.0], np.float32), rot_dim // 2)
    sinT = (np.repeat(sin, 2, axis=1) * sgn[None, :]).T
    return np.ascontiguousarray(cosT), np.ascontiguousarray(sinT)


def _consts():
    c = np.zeros((128, 3, 128), np.float32)
    c[:, 0, :] = np.eye(128, dtype=np.float32)
    c[:, 1, :] = 1.0
    idx = np.arange(128)
    c[idx, 2, idx ^ 1] = 1.0
    return c


def _pk(v, kc):
    return np.ascontiguousarray(np.asarray(v, np.float32).reshape(kc, 128).T)


def stage_a_inputs(inp):
    cB, sB = _rope_tables(128)
    cA, sA = _rope_tables(64)
    tabB = np.zeros((128, 2, NKV), np.float32)
    tabB[:, 0, :256] = 1.0
    tabB[:, 0, 256:] = cB
    tabB[:, 1, 256:] = sB
    tabA = np.zeros((64, 2, NKV), np.float32)
    tabA[:, 0, :256] = 1.0
    tabA[:, 0, 256:] = cA
    tabA[:, 1, 256:] = sA
    gains = np.zeros((128, 12), np.float32)
    gains[:, 0:6] = _pk(inp["q_lora_norm"][0], 6)
    gains[:, 6:10] = _pk(inp["kv_lora_norm"][0], 4)
    gains[:, 10] = inp["q_norm_b"][0]
    gains[:, 11] = inp["k_norm_b"][0]
    consts = _consts()
    maps = []
    for core in range(8):
        b, qr = core // 4, core % 4
        t0 = qr * 2048
        cv = np.zeros((128, 16, 2), np.float32)
        cv[:, :, 0] = _pk(inp["c"][b], 16)
        cv[:, :, 1] = _pk(inp["c_ctx"], 16)
        maps.append({
            "xkv": np.ascontiguousarray(np.concatenate([inp["ctx"][b], inp["x"][b]], axis=0)),
            "xq": np.ascontiguousarray(inp["x"][b, t0:t0 + 2048]),
            "cvec": cv.reshape(128, 32),
            "ada_w": inp["ada_w"][0], "ada_b": np.stack([inp["ada_b"][0], inp["ada_b"][0]]),
            "w_in": inp["w_in_attn"][0], "wq_b": inp["wq_b"][0], "wkv_b": inp["wkv_b"][0], "w_out": inp["w_out_attn"][0],
            "gains": gains, "lngb": np.stack([inp["ln_g"][0], inp["ln_b"][0]]), "consts": consts,
            "ropeB_kv": tabB, "ropeA_kv": tabA,
            "ropeB_q": np.ascontiguousarray(tabB[:, :, 256 + t0:256 + t0 + 2048]),
            "ropeA_q": np.ascontiguousarray(tabA[:, :, 256 + t0:256 + t0 + 2048]),
        })
    return maps


def run_stage_a(inp):
    nc = build_stage_a()
    res = run_bass_kernel_spmd(nc, stage_a_inputs(inp), core_ids=list(range(8)))
    x1 = np.zeros((2, 8192, 2048), np.float32)
    for core in range(8):
        b, qr = core // 4, core % 4
        x1[b, qr * 2048:(qr + 1) * 2048] = res.results[core]["x1"]
    return x1


FSCALE = float((8192.0 * 256.0) ** -0.5)


def fft_consts():
    import ml_dtypes
    bf = ml_dtypes.bfloat16
    c = np.arange(256)[:, None].astype(np.float64)
    k3 = np.arange(256)[None, :].astype(np.float64)
    a = 2 * np.pi * c * k3 / 256
    cn = np.zeros((128, 2, 2, 256), np.float64)
    for cc in range(2):
        cn[:, cc, 0, :] = np.cos(a[cc * 128:(cc + 1) * 128])
        cn[:, cc, 1, :] = -np.sin(a[cc * 128:(cc + 1) * 128])
    l1 = np.arange(64)[:, None].astype(np.float64)
    k1 = np.arange(64)[None, :].astype(np.float64)
    th = 2 * np.pi * l1 * k1 / 64
    wr, wi = np.cos(th), -np.sin(th)
    w64 = np.zeros((128, 128), np.float64)
    w64[0:64, 0:64] = wr
    w64[64:128, 0:64] = -wi
    w64[0:64, 64:128] = wi
    w64[64:128, 64:128] = wr
    l2 = np.arange(128)[:, None, None].astype(np.float64)
    kk = (np.arange(64)[None, :, None] + 64 * np.arange(128)[None, None, :]).astype(np.float64)
    ph = 2 * np.pi * ((l2 * kk) % 8192) / 8192
    M = np.stack([np.cos(ph), np.sin(ph)], axis=2)
    return cn.astype(np.float32).astype(bf), w64.astype(np.float32).astype(bf), M.astype(np.float32).astype(bf)


def build_stage_b():
    nc = bass.Bass("TRN2", target_bir_lowering=False)

    def din(name, shape, dt=F32):
        return nc.dram_tensor(name, list(shape), dt, kind="ExternalInput").ap()

    x1f = din("x1f", [8192, 2048])
    cvec = din("cvec", [128, 32])
    ada_w = din("ada_w", [2048, 6144])
    ada_b = din("ada_b", [2, 6144])
    wuf = din("wu", [2048, 1024])
    wgf = din("wg", [2048, 1024])
    consts = din("consts", [128, 3, 128])
    cn_d = din("cn", [128, 2, 2, 256], BF16)
    w64_d = din("w64", [128, 128], BF16)
    M_d = din("Mtw", [128, 64, 2, 128], BF16)
    yT = nc.dram_tensor("yT", [1024, 8192], BF16, kind="ExternalOutput").ap()
    gbc_o = nc.dram_tensor("gbc", [128, 2048], F32, kind="ExternalOutput").ap()

    with ExitStack() as es:
        C = setup_common(nc, es, consts)
        P, ar = C.P, C.ar
        Ud = P.dram("Ud", [8, 128, 8192], BF16)
        Gd = P.dram("Gd", [8, 128, 8192], BF16)
        gbc = ar.alloc("gbc_tmp", [2048], F32)
        emit_modulation(C, cvec, ada_w, ada_b[:, :], gbc, 0)
        gbc_b = P.view("gbc_out", gbc_o)
        P.dma("sp", gbc_o, gbc.t, src=gbc, dbuf=gbc_b)
        ar.reset()
        W = Ctx()
        Wu = ar.alloc("Wu", [16, 1024], BF16)
        Wg = ar.alloc("Wg", [16, 1024], BF16)
        Bu = ar.alloc("Bu", [8], F32)
        Bg = ar.alloc("Bg", [8], F32)
        ln_alloc(C, W)
        W.lnc = 0
        hTs = [ar.alloc(f"hT{i}", [16, 512], BF16) for i in range(2)]
        uo = [ar.alloc(f"uo{i}", [512], BF16) for i in range(4)]
        for j in range(4):
            prep_panel(C, wuf, 16, j * 256, 256, 0, Wu, Wu.t[:, :, j * 256:(j + 1) * 256], Bu, Bu.t[:, 2 * j:2 * j + 2])
            prep_panel(C, wgf, 16, j * 256, 256, 0, Wg, Wg.t[:, :, j * 256:(j + 1) * 256], Bg, Bg.t[:, 2 * j:2 * j + 2])
        nuo = 0
        for tb in range(16):
            hT = hTs[tb % 2]
            ln_transpose(C, W, x1f, tb * 512, 4, hT)
            for j in range(16):
                pb = nextps(C)
                isu = j < 8
                jj = j % 8
                proj(C, pb, 128, 512, Wu if isu else Wg, (Wu if isu else Wg).t, jj * 128, 16, hT, 0)
                o = uo[nuo % 4]; nuo += 1
                bb = Bu if isu else Bg
                P.op("act", lambda e, pb=pb, o=o, bb=bb, jj=jj, isu=isu: e.activation(o.t, pb.t[:, :], AF.Identity if isu else AF.Silu, bias=bb.t[:, jj:jj + 1]),
                     reads=[pb, bb], writes=[o])
                dd = Ud if isu else Gd
                P.dma("sp", dd.t[jj, :, tb * 512:(tb + 1) * 512], o.t, src=o, dst=dd)
        ar.reset()
        cn = ar.alloc("cn", [2, 2, 256], BF16)
        w64 = ar.alloc("w64", [128], BF16)
        Mt = ar.alloc("Mt", [64, 2, 128], BF16)
        P.dma("sp", cn.t, cn_d, dst=cn)
        P.dma("sp", w64.t, w64_d, dst=w64)
        P.dma("sp", Mt.t, M_d, dst=Mt)
        uT = ar.alloc("uT", [2, 8192], BF16)
        gT = ar.alloc("gT", [2, 8192], BF16)
        z = ar.alloc("z", [128, 128], BF16)
        Y = ar.alloc("Y", [128, 128], BF16)
        yTb = P.view("yT_out", yT)
        for g in range(4):
            P.dma("sp", uT.t, Ud.t[2 * g:2 * g + 2].rearrange("c p t -> p c t"), src=Ud, dst=uT)
            P.dma("sp", gT.t, Gd.t[2 * g:2 * g + 2].rearrange("c p t -> p c t"), src=Gd, dst=gT)
            uv = uT.t.rearrange("p c (l1 l2) -> p c l2 l1", l2=128)
            for kh in range(2):
                ks = slice(kh * 128, (kh + 1) * 128)
                for l2p in range(32):
                    pb = nextps(C)
                    for q in range(4):
                        l2 = l2p * 4 + q
                        for ri in range(2):
                            for cc in range(2):
                                P.mm(pb.t[ri * 64:(ri + 1) * 64, q * 128:(q + 1) * 128], uv[:, cc, l2, :], cn.t[:, cc, ri, ks],
                                     cc == 0, cc == 1, [uT, cn], pb)
                    dst = z.t[:, l2p * 4:(l2p + 1) * 4, :]
                    src = pb.t[:, :].rearrange("p (a b) -> p a b", b=128)
                    if l2p % 2 == 0:
                        P.op("act", lambda e, dst=dst, src=src: e.activation(dst, src, AF.Copy), reads=[pb], writes=[z], acc=True)
                    else:
                        P.op("dve", lambda e, dst=dst, src=src: e.tensor_copy(dst, src), reads=[pb], writes=[z], acc=True)
                for k3p in range(32):
                    pb = nextps(C)
                    for q in range(4):
                        k3 = k3p * 4 + q
                        P.mm(pb.t[:, q * 128:(q + 1) * 128], z.t[:, :, k3], w64.t, True, True, [z, w64], pb)
                    dst = Y.t[:, k3p * 4:(k3p + 1) * 4, :]
                    src = pb.t[:, :].rearrange("p (a b) -> p a b", b=128)
                    if k3p % 2 == 0:
                        P.op("act", lambda e, dst=dst, src=src: e.activation(dst, src, AF.Copy), reads=[pb], writes=[Y], acc=True)
                    else:
                        P.op("dve", lambda e, dst=dst, src=src: e.tensor_copy(dst, src), reads=[pb], writes=[Y], acc=True)
                gv = gT.t[:, kh, :].rearrange("p (k2 k1) -> p k1 k2", k1=64)
                for k1p in range(16):
                    pb = nextps(C)
                    for q in range(4):
                        k1 = k1p * 4 + q
                        for ri in range(2):
                            P.mm(pb.t[:, q * 128:(q + 1) * 128], Y.t[:, :, ri * 64 + k1], Mt.t[:, k1, ri, :], ri == 0, ri == 1, [Y, Mt], pb)
                    gsl = gv[:, k1p * 4:(k1p + 1) * 4, :]
                    src = pb.t[:, :].rearrange("p (a b) -> p a b", b=128)
                    P.op("dve", lambda e, gsl=gsl, src=src: e.scalar_tensor_tensor(gsl, src, FSCALE, gsl, ALU.mult, ALU.mult),
                         reads=[pb, gT], writes=[gT], acc=True)
            P.dma("sp", yT[g * 256:(g + 1) * 256, :].rearrange("(c p) t -> p c t", p=128), gT.t, src=gT, dbuf=yTb)
        P.fence("sp", [yTb, gbc_b])
        P.emit()
    return nc


def stage_b_inputs(inp, x1):
    cn, w64, M = fft_consts()
    consts = _consts()
    maps = []
    wf = inp["w_in_fourier"][0]
    for core in range(8):
        b, cq = core // 4, core % 4
        cv = np.zeros((128, 16, 2), np.float32)
        cv[:, :, 0] = _pk(inp["c"][b], 16)
        cv[:, :, 1] = cv[:, :, 0]
        maps.append({
            "x1f": np.ascontiguousarray(x1[b]), "cvec": cv.reshape(128, 32),
            "ada_w": inp["ada_w"][1], "ada_b": np.stack([inp["ada_b"][1], inp["ada_b"][1]]),
            "wu": np.ascontiguousarray(wf[:, cq * 1024:(cq + 1) * 1024]),
            "wg": np.ascontiguousarray(wf[:, 4096 + cq * 1024:4096 + (cq + 1) * 1024]),
            "consts": consts, "cn": cn, "w64": w64, "Mtw": M,
        })
    return maps


def run_stage_b(inp, x1):
    nc = build_stage_b()
    res = run_bass_kernel_spmd(nc, stage_b_inputs(inp, x1), core_ids=list(range(8)))
    yT = [np.concatenate([res.results[b * 4 + cq]["yT"] for cq in range(4)], axis=0) for b in range(2)]
    gbc = [res.results[b * 4]["gbc"] for b in range(2)]
    return yT, gbc


def build_stage_c():
    nc = bass.Bass("TRN2", target_bir_lowering=False)

    def din(name, shape, dt=F32):
        return nc.dram_tensor(name, list(shape), dt, kind="ExternalInput").ap()

    yTl = din("yTl", [32, 128, 2048], BF16)
    x1l = din("x1l", [2048, 2048])
    w_out = din("w_out", [4096, 2048])
    gbc_d = din("gbc", [128, 2048])
    lngb = din("lngb", [2, 2048])
    consts = din("consts", [128, 3, 128])
    outp = nc.dram_tensor("out", [2048, 2048], F32, kind="ExternalOutput").ap()

    with ExitStack() as es:
        C = setup_common(nc, es, consts)
        P, ar = C.P, C.ar
        Td = P.dram("Td", [16, 128, 2048], F32)
        Wo = ar.alloc("Wo", [32, 1024], BF16)
        gb2 = ar.alloc("gb2", [2048], F32)
        lnbc = ar.alloc("lnbc", [2, 2048], F32)
        Gys = [ar.alloc(f"Gy{i}", [32, 128], BF16) for i in range(2)]
        tms = [ar.alloc(f"tm{i}", [1024], F32) for i in range(2)]
        xts = [ar.alloc(f"oxt{i}", [2048], F32) for i in range(2)]
        tmp = ar.alloc("otmp", [2048], F32)
        st = ar.alloc("ost", [4, 6], F32)
        mv = ar.alloc("omv", [2], F32)
        rs = ar.alloc("ors", [1], F32)
        P.dma("sp", gb2.t, gbc_d, dst=gb2)
        P.dma("sp", lnbc.t[:, 0, :], lngb[0:1, :].partition_broadcast(128), dst=lnbc)
        P.dma("sp", lnbc.t[:, 1, :], lngb[1:2, :].partition_broadcast(128), dst=lnbc)
        n = 0
        for nh in range(2):
            for kh in range(2):
                for j in range(4):
                    c0 = nh * 1024 + j * 256
                    prep_panel(C, w_out[kh * 2048:(kh + 1) * 2048, :], 16, c0, 256, None, Wo,
                               Wo.t[:, kh * 16:(kh + 1) * 16, j * 256:(j + 1) * 256])
            for t in range(16):
                Gy = Gys[n % 2]
                tm = tms[n % 2]
                n += 1
                P.dma("sp", Gy.t, yTl[:, :, t * 128:(t + 1) * 128].rearrange("c p t -> p c t"), dst=Gy)
                for nb in range(2):
                    pb = nextps(C)
                    ns = slice(nb * 512, (nb + 1) * 512)
                    gs = slice(nh * 1024 + nb * 512, nh * 1024 + (nb + 1) * 512)
                    for kc in range(32):
                        P.mm(pb.t[:, :], Gy.t[:, kc, :], Wo.t[:, kc, ns], kc == 0, kc == 31, [Gy, Wo], pb)
                    P.op("dve", lambda e, pb=pb, ns=ns, gs=gs, tm=tm: e.tensor_tensor(tm.t[:, ns], pb.t[:, :], gb2.t[:, gs], ALU.mult),
                         reads=[pb, gb2], writes=[tm], acc=True)
                P.dma("sp", Td.t[t, :, nh * 1024:(nh + 1) * 1024], tm.t, src=tm, dst=Td)
        ob = P.view("out_b", outp)
        for t in range(16):
            xt = xts[t % 2]
            tm = tmp
            P.dma("sp", xt.t, x1l[t * 128:(t + 1) * 128, :], dst=xt)
            P.dma("sp", tm.t, Td.t[t], src=Td, dst=tm)
            P.op("dve", lambda e, xt=xt, tm=tm: e.scalar_tensor_tensor(tm.t, xt.t, ALPHA, tm.t, ALU.mult, ALU.add), reads=[xt, tm], writes=[tm])
            for c in range(4):
                P.op("dve", lambda e, c=c, tm=tm: e.bn_stats(st.t[:, c, :], tm.t[:, c * 512:(c + 1) * 512]), reads=[tm], writes=[st], acc=True)
            P.op("dve", lambda e: e.bn_aggr(mv.t, st.t), reads=[st], writes=[mv])
            P.op("act", lambda e: e.activation(rs.t, mv.t[:, 1:2], AF.Sqrt, bias=C.eps.t[:, 0:1], scale=1.0), reads=[mv, C.eps], writes=[rs])
            P.op("dve", lambda e: e.reciprocal(rs.t, rs.t), reads=[rs], writes=[rs])
            P.op("dve", lambda e, tm=tm: e.tensor_scalar(tm.t, tm.t, mv.t[:, 0:1], rs.t[:, 0:1], ALU.subtract, ALU.mult), reads=[tm, mv, rs], writes=[tm])
            P.op("pool", lambda e, tm=tm: e.tensor_tensor(tm.t, tm.t, lnbc.t[:, 0, :], ALU.mult), reads=[tm, lnbc], writes=[tm])
            P.op("dve", lambda e, tm=tm, xt=xt: e.tensor_tensor(xt.t, tm.t, lnbc.t[:, 1, :], ALU.add), reads=[tm, lnbc], writes=[xt])
            P.dma("sp", outp[t * 128:(t + 1) * 128, :], xt.t, src=xt, dbuf=ob)
        P.fence("sp", [ob])
        P.emit()
    return nc


def run_stage_c(inp, x1, yT, gbc):
    nc = build_stage_c()
    consts = _consts()
    maps = []
    for core in range(8):
        b, qr = core // 4, core % 4
        t0 = qr * 2048
        maps.append({
            "yTl": np.ascontiguousarray(yT[b][:, t0:t0 + 2048]).reshape(32, 128, 2048),
            "x1l": np.ascontiguousarray(x1[b, t0:t0 + 2048]),
            "w_out": inp["w_out_fourier"][0], "gbc": gbc[b],
            "lngb": np.stack([inp["ln_g"][1], inp["ln_b"][1]]), "consts": consts,
        })
    res = run_bass_kernel_spmd(nc, maps, core_ids=list(range(8)))
    out = np.zeros((2, 8192, 2048), np.float32)
    for core in range(8):
        b, qr = core // 4, core % 4
        out[b, qr * 2048:(qr + 1) * 2048] = res.results[core]["out"]
    return out


def kernel_unfused(**inp):
    inp = {k: np.asarray(v) for k, v in inp.items()}
    x1 = run_stage_a(inp)
    yT, gbc = run_stage_b(inp, x1)
    return run_stage_c(inp, x1, yT, gbc)


def kernel(**inp):
    return kernel_fused(**inp)
```

```python
import numpy as np
from contextlib import ExitStack
import concourse.bass as bass
import concourse.mybir as mybir
from concourse.bass_utils import run_bass_kernel_spmd

F32 = mybir.dt.float32
BF16 = mybir.dt.bfloat16
ALU = mybir.AluOpType
AF = mybir.ActivationFunctionType
AX = mybir.AxisListType

SEM_LIM = 30000
PROFILE_SCOPES = False


class Buf:
    def __init__(self, name, t=None):
        self.name = name
        self.t = t
        self.w = {}
        self.r = {}
        self.dcnt = 0
        self.dsem = None
        self.is_dram = False

    def __getitem__(self, k):
        return self.t[k]


class Prog:
    ENGS = ("pe", "act", "dve", "pool", "sp")

    def __init__(self, nc, es):
        self.nc = nc
        self.es = es
        self.ops = {e: [] for e in self.ENGS}
        self.seen = {e: {} for e in self.ENGS}
        self.signal = {e: set() for e in self.ENGS}
        self.dbufs = []
        self.nbuf = 0
        self.phase = None

    def sb(self, name, shape, dt):
        t = self.es.enter_context(self.nc.sbuf_tensor(name, list(shape), dt))
        return Buf(name, t)

    def ps(self, name):
        t = self.es.enter_context(self.nc.psum_tensor(name, [128, 512], F32))
        return Buf(name, t)

    def dram(self, name, shape, dt, kind="Internal"):
        t = self.nc.dram_tensor(name, list(shape), dt, kind=kind)
        b = Buf(name, t.ap())
        b.is_dram = True
        return b

    def view(self, name, ap):
        b = Buf(name, ap)
        b.is_dram = True
        return b

    def op(self, eng, fn, reads=(), writes=(), dma_dst=None, acc=False, dma_inc=16):
        if dma_dst is not None:
            own = ("D", id(dma_dst))
        else:
            own = ("E", eng)
        need = {}

        def merge(d, skip_own=False):
            for k, v in d.items():
                if skip_own and k == own:
                    continue
                if need.get(k, -1) < v:
                    need[k] = v

        for b in reads:
            merge(b.w)
        for b in writes:
            merge(b.w, skip_own=acc)
            merge(b.r)
        waits = []
        seen = self.seen[eng]
        for k, v in need.items():
            if k == ("E", "pe") and eng == "pe":
                continue
            if seen.get(k, -1) >= v:
                continue
            seen[k] = v
            waits.append((k, v))
            if k[0] == "E":
                self.signal[k[1]].add(v)
        idx = len(self.ops[eng])
        if dma_dst is not None:
            if dma_dst.dsem is None:
                dma_dst.dsem = True
                self.dbufs.append(dma_dst)
            dma_dst.dcnt += dma_inc
            assert dma_dst.dcnt < 2 * SEM_LIM, dma_dst.name
            tok = (own, dma_dst.dcnt)
        else:
            tok = (own, idx)
        self.ops[eng].append((fn, waits, dma_dst, idx, dma_inc, self.phase))
        for b in reads:
            if b.r.get(tok[0], -1) < tok[1]:
                b.r[tok[0]] = tok[1]
        for b in writes:
            if acc:
                b.w[tok[0]] = tok[1]
            else:
                b.w = {tok[0]: tok[1]}
            b.r = {}
        return tok

    def fence(self, eng, bufs):
        self.op(eng, None, reads=bufs)

    def mm(self, out, lhsT, rhs, start, stop, reads, w, **kw):
        self.op("pe", lambda e: e.matmul(out, lhsT, rhs, start=start, stop=stop, **kw),
                reads=reads, writes=[w], acc=True)

    def dma(self, eng, out, in_, src=None, dst=None, dbuf=None, **kw):
        if dst is None:
            dst = dbuf
        reads = [src] if src is not None else []
        writes = [dst] if dst is not None else []
        if src is not None and not src.is_dram and (dst is None or dst.is_dram):
            owner = src
        else:
            owner = dst
        self.op(eng, lambda e: e.dma_start(out=out, in_=in_, **kw), reads=reads, writes=writes,
                dma_dst=owner, acc=True)

    def emit(self):
        nc = self.nc
        es = self.es
        rank = {}
        esems = {}
        for e in self.ENGS:
            sig = sorted(self.signal[e])
            rank[e] = {idx: i + 1 for i, idx in enumerate(sig)}
            n = (len(sig) + SEM_LIM - 1) // SEM_LIM
            esems[e] = [es.enter_context(nc.semaphore(f"s_{e}{i}")) for i in range(max(n, 1))]
        for i, b in enumerate(self.dbufs):
            b.dsem = es.enter_context(nc.semaphore(f"d_{i}"))
        dmap = {id(b): b for b in self.dbufs}
        self.nsem = sum(len(v) for v in esems.values()) + len(self.dbufs)

        def sem_val(k, v):
            if k[0] == "E":
                r = rank[k[1]][v] - 1
                return esems[k[1]][r // SEM_LIM], r % SEM_LIM + 1
            return dmap[k[1]].dsem, v

        def run_one(e, name, fn, waits, dma_dst, idx, dma_inc):
            for k, v in waits:
                s, val = sem_val(k, v)
                e.wait_ge(s, val)
            if fn is None:
                return
            ins = fn(e)
            if dma_dst is not None:
                ins.then_inc(dma_dst.dsem, dma_inc)
            elif idx in rank[name]:
                s, _ = sem_val(("E", name), idx)
                ins.then_inc(s, 1)

        def run(e, name):
            ops = self.ops[name]
            i = 0
            while i < len(ops):
                ph = ops[i][5]
                j = i
                while j < len(ops) and ops[j][5] == ph:
                    j += 1
                if PROFILE_SCOPES and ph is not None:
                    with nc.named_scope(ph):
                        for o in ops[i:j]:
                            run_one(e, name, *o[:5])
                else:
                    for o in ops[i:j]:
                        run_one(e, name, *o[:5])
                i = j

        block = es.enter_context(nc.Block())

        @block.sync
        def _(e):
            run(e, "sp")

        @block.tensor
        def _(e):
            run(e, "pe")

        @block.scalar
        def _(e):
            run(e, "act")

        @block.vector
        def _(e):
            run(e, "dve")

        @block.gpsimd
        def _(e):
            run(e, "pool")


class Arena:
    def __init__(self, P, nbytes):
        self.P = P
        self.n = nbytes // 2
        self.t = P.es.enter_context(P.nc.sbuf_tensor("arena", [128, self.n], BF16))
        self.off = 0
        self.cur = []
        self.prev = {}

    def reset(self):
        for b in self.cur:
            for d in (b.w, b.r):
                for k, v in d.items():
                    if self.prev.get(k, -1) < v:
                        self.prev[k] = v
        self.cur = []
        self.off = 0

    def mark(self):
        return (self.off, len(self.cur))

    def release(self, m):
        for b in self.cur[m[1]:]:
            for d in (b.w, b.r):
                for k, v in d.items():
                    if self.prev.get(k, -1) < v:
                        self.prev[k] = v
        self.cur = self.cur[:m[1]]
        self.off = m[0]

    def alloc(self, name, shape, dt, parts=128):
        esz = 4 if dt == F32 else 2
        n = int(np.prod(shape))
        units = (n * esz + 1) // 2
        units = (units + 15) // 16 * 16
        assert self.off + units <= self.n, (name, self.off * 2, units * 2)
        ap = self.t[0:parts, self.off:self.off + units]
        self.off += units
        if dt == F32:
            ap = ap.bitcast(F32)
        ap = ap[:, 0:n]
        if len(shape) == 2:
            ap = ap.rearrange("p (a b) -> p a b", b=shape[1])
        elif len(shape) == 3:
            ap = ap.rearrange("p (a b c) -> p a b c", b=shape[1], c=shape[2])
        b = Buf(name, ap)
        b.r = dict(self.prev)
        self.cur.append(b)
        return b


ALPHA = (2.0 * 2) ** 0.25
A_SCALE = 192.0 ** -0.5
B_SCALE = 128.0 ** -0.5


class Ctx:
    pass


def setup_common(nc, es, consts_d, arena_kb=170):
    C = Ctx()
    P = Prog(nc, es)
    C.P = P
    C.nc = nc
    C.cst = P.sb("cst", [128, 3, 128], F32)
    P.dma("sp", C.cst.t[:], consts_d, dst=C.cst)
    C.idf = C.cst.t[:, 0, :]
    C.onesf = C.cst.t[:, 1, :]
    C.perm = C.cst.t[:, 2, :]
    C.cb = P.sb("cb", [128, 2, 128], BF16)
    P.op("dve", lambda e: e.tensor_copy(C.cb.t[:], C.cst.t[:, 0:2, :]), reads=[C.cst], writes=[C.cb])
    C.idb = C.cb.t[:, 0, :]
    C.onesb = C.cb.t[:, 1, :]
    C.eps = P.sb("eps", [128, 1], F32)
    P.op("pool", lambda e: e.memset(C.eps.t[:], 1e-6), writes=[C.eps])
    C.modT = P.sb("modT", [128, 32, 2], F32)
    C.stg = [P.sb(f"stg{i}", [128, 16 * 256], F32) for i in range(2)]
    C.nstg = 0
    C.psb = [P.ps(f"ps{i}") for i in range(8)]
    C.nps = 0
    C.ar = Arena(P, arena_kb * 1024)
    return C


def nextps(C, lo=0, hi=8):
    b = C.psb[lo + C.nps % (hi - lo)]
    C.nps += 1
    return b


def emit_modulation(C, cvec_d, ada_w_d, ada_b_d, gbc, want_gate_row=0):
    P, ar = C.P, C.ar
    cv = ar.alloc("cv", [32], F32)
    sv = ar.alloc("sv", [16, 2], F32)
    adb = ar.alloc("adb", [6144], F32, parts=2)
    m2 = ar.alloc("m2", [6144], F32, parts=2)
    P.dma("sp", cv.t, cvec_d, dst=cv)
    P.dma("sp", adb.t, ada_b_d, dst=adb)
    P.op("act", lambda e: e.activation(sv.t.rearrange("p a b -> p (a b)"), cv.t, AF.Silu), reads=[cv], writes=[sv])
    awv = ada_w_d.rearrange("(kc p) n -> p kc n", p=128)
    for nb in range(24):
        stg = C.stg[C.nstg % 2]
        C.nstg += 1
        sv3 = stg.t.rearrange("p (kc n) -> p kc n", n=256)
        P.dma("sp", sv3, awv[:, :, nb * 256:(nb + 1) * 256], dst=stg)
        pb = nextps(C)
        for kc in range(16):
            P.mm(pb.t[0:2, 0:256], sv.t[:, kc, :], sv3[:, kc, :], kc == 0, kc == 15, [sv, stg], pb)
        sl = slice(nb * 256, (nb + 1) * 256)
        P.op("dve", lambda e, pb=pb, sl=sl: e.tensor_tensor(m2.t[:, sl], pb.t[0:2, 0:256], adb.t[:, sl], ALU.add),
             reads=[pb, adb], writes=[m2])
    pT = nextps(C)
    for j in range(32):
        P.mm(pT.t[:, 2 * j:2 * j + 2], m2.t[0:2, j * 128:(j + 1) * 128], C.idf[0:2, 0:2], True, True, [m2, C.cst], pT)
    P.op("dve", lambda e: e.tensor_copy(C.modT.t[:, 0:16, :], pT.t[:, 0:32].rearrange("p (a b) -> p a b", b=2)),
         reads=[pT], writes=[C.modT])
    P.op("dve", lambda e: e.tensor_scalar_add(C.modT.t[:, 16:32, :], pT.t[:, 32:64].rearrange("p (a b) -> p a b", b=2), 1.0),
         reads=[pT], writes=[C.modT], acc=True)
    r = want_gate_row
    for q in range(4):
        pb = nextps(C)
        P.mm(pb.t[:, :], C.onesf[r:r + 1, :], m2.t[r:r + 1, 4096 + q * 512:4096 + (q + 1) * 512], True, True, [m2, C.cst], pb)
        P.op("act", lambda e, pb=pb, q=q: e.activation(gbc.t[:, q * 512:(q + 1) * 512], pb.t[:, :], AF.Copy),
             reads=[pb], writes=[gbc], acc=True)


def prep_panel(C, Wd, KC, c0, ncols, r, Wbuf, Wap, bbuf=None, bap=None):
    ld = prep_load(C, Wd, KC, c0, ncols)
    prep_compute(C, ld, KC, ncols, r, Wbuf, Wap, bbuf, bap)


def prep_load(C, Wd, KC, c0, ncols):
    stg = C.stg[C.nstg % 2]
    C.nstg += 1
    s3 = stg.t[:, 0:KC * ncols].rearrange("p (kc n) -> p kc n", n=ncols)
    C.P.dma("sp", s3, Wd.rearrange("(kc p) n -> p kc n", p=128)[:, :, c0:c0 + ncols], dst=stg)
    return stg, s3


def prep_compute(C, ld, KC, ncols, r, Wbuf, Wap, bbuf=None, bap=None):
    P = C.P
    stg, s3 = ld
    if r is None:
        P.op("dve", lambda e: e.tensor_copy(Wap, s3), reads=[stg], writes=[Wbuf], acc=True)
        return
    for kc in range(KC):
        if kc % 2 == 0:
            P.op("dve", lambda e, kc=kc: e.tensor_scalar_mul(Wap[:, kc, :], s3[:, kc, :], C.modT.t[:, 16 + kc, r:r + 1]),
                 reads=[stg, C.modT], writes=[Wbuf], acc=True)
        else:
            P.op("act", lambda e, kc=kc: e.activation(Wap[:, kc, :], s3[:, kc, :], AF.Copy, scale=C.modT.t[:, 16 + kc, r:r + 1]),
                 reads=[stg, C.modT], writes=[Wbuf], acc=True)
    pb = nextps(C)
    nch = (ncols + 127) // 128
    for j in range(nch):
        M = min(128, ncols - j * 128)
        for kc in range(KC):
            P.mm(pb.t[0:M, j:j + 1], s3[:, kc, j * 128:j * 128 + M], C.modT.t[:, kc, r:r + 1], kc == 0, kc == KC - 1,
                 [stg, C.modT], pb)
    nfull = ncols // 128
    if nfull:
        P.op("dve", lambda e: e.tensor_copy(bap[:, 0:nfull], pb.t[:, 0:nfull]), reads=[pb], writes=[bbuf], acc=True)
    if nch > nfull:
        Ml = ncols - nfull * 128
        P.op("dve", lambda e: e.tensor_copy(bap[0:Ml, nfull:nch], pb.t[0:Ml, nfull:nch]), reads=[pb], writes=[bbuf], acc=True)


def ln_tile(C, W, xrows_d, t):
    P = C.P
    xt = W.xt[t % 2]
    st = W.st[t % 2]
    mv = W.mv[t % 2]
    rs = W.rs[t % 2]
    xh = W.xh[t % 2]
    P.dma("sp", xt.t, xrows_d, src=getattr(W, "xsrc", None), dst=xt)
    for c in range(4):
        P.op("dve", lambda e, c=c: e.bn_stats(st.t[:, c, :], xt.t[:, c * 512:(c + 1) * 512]), reads=[xt], writes=[st], acc=True)
    P.op("dve", lambda e: e.bn_aggr(mv.t, st.t), reads=[st], writes=[mv])
    P.op("act", lambda e: e.activation(rs.t, mv.t[:, 1:2], AF.Sqrt, bias=C.eps.t[:, 0:1], scale=1.0), reads=[mv, C.eps], writes=[rs])
    P.op("dve", lambda e: e.reciprocal(rs.t, rs.t), reads=[rs], writes=[rs])
    P.op("dve", lambda e: e.tensor_scalar(xh.t, xt.t, mv.t[:, 0:1], rs.t[:, 0:1], ALU.subtract, ALU.mult),
         reads=[xt, mv, rs], writes=[xh])
    return xh


def ln_alloc(C, W):
    ar = C.ar
    W.xt = [ar.alloc(f"xt{i}", [2048], F32) for i in range(2)]
    W.st = [ar.alloc(f"st{i}", [4, 6], F32) for i in range(2)]
    W.mv = [ar.alloc(f"mv{i}", [2], F32) for i in range(2)]
    W.rs = [ar.alloc(f"rs{i}", [1], F32) for i in range(2)]
    W.xh = [ar.alloc(f"xh{i}", [2048], BF16) for i in range(2)]


def ln_transpose(C, W, x_d, row0, ntiles, hT, col0=0):
    P = C.P
    for t in range(ntiles):
        xh = ln_tile(C, W, x_d[row0 + t * 128: row0 + (t + 1) * 128, :], W.lnc)
        W.lnc += 1
        for half in range(2):
            pb = nextps(C)
            pv = pb.t[:].bitcast(BF16)
            for j in range(8):
                kc = half * 8 + j
                P.op("pe", lambda e, pv=pv, j=j, kc=kc, xh=xh: e.transpose(pv[:, j * 128:(j + 1) * 128], xh.t[:, kc * 128:(kc + 1) * 128], C.idb),
                     reads=[xh, C.cb], writes=[pb], acc=True)
            dst = hT.t[:, half * 8:(half + 1) * 8, col0 + t * 128: col0 + (t + 1) * 128]
            src = pv.rearrange("p (a b) -> p a b", b=128)
            if half == 0:
                P.op("act", lambda e, dst=dst, src=src: e.activation(dst, src, AF.Copy), reads=[pb], writes=[hT], acc=True)
            else:
                P.op("dve", lambda e, dst=dst, src=src: e.tensor_copy(dst, src), reads=[pb], writes=[hT], acc=True)


def proj(C, pb, M, nt, Wbuf, Wap, c0, KC, hT, tok0):
    for kc in range(KC):
        C.P.mm(pb.t[0:M, 0:nt], Wap[:, kc, c0:c0 + M], hT.t[:, kc, tok0:tok0 + nt], kc == 0, kc == KC - 1, [Wbuf, hT], pb)


def rstd_from(C, W, zs, nfeat, nt):
    P = C.P
    pss = nextps(C)
    for i, (zb, zap) in enumerate(zs):
        sq = W.sq[W.nsq % 2]
        W.nsq += 1
        P.op("act", lambda e, sq=sq, zap=zap: e.activation(sq.t[:, 0:nt], zap, AF.Square), reads=[zb], writes=[sq])
        P.mm(pss.t[:, 0:nt], C.onesf, sq.t[:, 0:nt], i == 0, i == len(zs) - 1, [sq, C.cst], pss)
    rstd = W.rstd[W.nrs % 2]
    W.nrs += 1
    P.op("act", lambda e: e.activation(rstd.t[:, 0:nt], pss.t[:, 0:nt], AF.Sqrt, bias=C.eps.t[:, 0:1], scale=1.0 / nfeat),
         reads=[pss, C.eps], writes=[rstd])
    P.op("dve", lambda e: e.reciprocal(rstd.t[:, 0:nt], rstd.t[:, 0:nt]), reads=[rstd], writes=[rstd])
    return rstd


def rope(C, W, dst_buf, dst_ap, src_buf, src_ap, tab, M, nt):
    P = C.P
    pw = nextps(C)
    P.mm(pw.t[0:M, 0:nt], C.perm[0:M, 0:M], src_ap, True, True, [src_buf, C.cst], pw)
    t1 = W.t1[W.nt1 % 2]
    t2 = W.t2[W.nt1 % 2]
    W.nt1 += 1
    P.op("pool", lambda e: e.tensor_tensor(t1.t[0:M, 0:nt], src_ap, tab.t[0:M, 0, 0:nt], ALU.mult), reads=[src_buf, tab], writes=[t1])
    P.op("dve", lambda e: e.tensor_tensor(t2.t[0:M, 0:nt], pw.t[0:M, 0:nt], tab.t[0:M, 1, 0:nt], ALU.mult), reads=[pw, tab], writes=[t2])
    P.op("pool", lambda e: e.tensor_tensor(dst_ap, t1.t[0:M, 0:nt], t2.t[0:M, 0:nt], ALU.add), reads=[t1, t2], writes=[dst_buf])


def work_alloc(C, W):
    ar = C.ar
    W.sq = [ar.alloc(f"sq{i}", [512], F32) for i in range(2)]
    W.rstd = [ar.alloc(f"rstd{i}", [512], F32) for i in range(2)]
    W.t1 = [ar.alloc(f"t1{i}", [512], F32) for i in range(2)]
    W.t2 = [ar.alloc(f"t2{i}", [512], F32) for i in range(2)]
    W.nsq = W.nrs = W.nt1 = 0


NKV = 8448
NKT = 66


def build_stage_a(fused=False):
    nc = bass.Bass("TRN2", target_bir_lowering=False)

    def din(name, shape, dt=F32):
        return nc.dram_tensor(name, list(shape), dt, kind="ExternalInput").ap()

    xkv = din("xkv", [NKV, 2048])
    xq = din("xq", [2048, 2048])
    cvec = din("cvec", [128, 32])
    ada_w = din("ada_w", [2048, 6144])
    ada_b = din("ada_b", [2, 6144])
    w_in = din("w_in", [2048, 4928])
    wq_b = din("wq_b", [768, 1536])
    wkv_b = din("wkv_b", [512, 2048])
    w_out = din("w_out", [2048, 2048])
    gains = din("gains", [128, 12])
    lngb = din("lngb", [2, 2048])
    consts = din("consts", [128, 3, 128])
    ropeB_kv = din("ropeB_kv", [128, 2, NKV])
    ropeA_kv = din("ropeA_kv", [64, 2, NKV])
    ropeB_q = din("ropeB_q", [128, 2, 2048])
    ropeA_q = din("ropeA_q", [64, 2, 2048])
    x1 = None if fused else nc.dram_tensor("x1", [2048, 2048], F32, kind="ExternalOutput").ap()

    with ExitStack() as es:
        C = setup_common(nc, es, consts)
        P, ar = C.P, C.ar
        gn = P.sb("gn", [128, 12], F32)
        P.dma("sp", gn.t[:], gains, dst=gn)
        KaT = P.dram("KaT", [8, 128, NKV], BF16)
        KpeT = P.dram("KpeT", [64, NKV], BF16)
        Va = P.dram("Va", [NKT, 128, 1024], BF16)
        KbT = P.dram("KbT", [2, 128, NKV], BF16)
        Vb = P.dram("Vb", [NKT, 128, 256], BF16)
        QaT = P.dram("QaT", [8, 128, 2048], BF16)
        QpeT = P.dram("QpeT", [8, 64, 2048], BF16)
        QbT = P.dram("QbT", [8, 128, 2048], BF16)
        Gd = P.dram("Gd", [16, 128, 2048], BF16)
        Yd = P.dram("Yd", [16, 128, 2048], BF16)

        P.phase = "MOD"
        gbc = ar.alloc("gbc_tmp", [2048], F32)
        emit_modulation(C, cvec, ada_w, ada_b[:, :], gbc, 0)
        Gbc_d = P.dram("Gbc_d", [128, 2048], F32)
        P.dma("sp", Gbc_d.t, gbc.t, src=gbc, dst=Gbc_d)

        def kv_phase(r, blocks):
            ar.reset()
            W = Ctx()
            Wkv = ar.alloc("Wkv", [16, 1088], BF16)
            Bkv = ar.alloc("Bkv", [9], F32)
            wkvb = ar.alloc("wkvb", [4, 2048], BF16)
            ln_alloc(C, W)
            W.lnc = 0
            work_alloc(C, W)
            hTs = [ar.alloc(f"hT{i}", [16, 512], BF16) for i in range(1)]
            zc = ar.alloc("zc", [4, 512], F32)
            ckvn = ar.alloc("ckvn", [4, 512], BF16)
            zk = [ar.alloc(f"zk{i}", [512], F32) for i in range(2)]
            kn = [ar.alloc(f"kn{i}", [512], F32) for i in range(2)]
            tabB = [ar.alloc(f"tabB{i}", [2, 512], F32) for i in range(1)]
            tabA = [ar.alloc(f"tabA{i}", [2, 512], F32) for i in range(1)]
            ko = [ar.alloc(f"ko{i}", [512], BF16) for i in range(4)]
            vT = [ar.alloc(f"vT{i}", [512], BF16) for i in range(2)]
            vo = [ar.alloc(f"vo{i}", [1024], BF16) for i in range(2)]
            vbo = [ar.alloc(f"vbo{i}", [256], BF16) for i in range(2)]
            panels = [(768, 256, 0, 0), (1024, 256, 256, 2), (1280, 64, 512, 4), (2368, 256, 576, 5), (2624, 256, 832, 7)]
            for (c0, ncols, l0, b0) in panels:
                nch = (ncols + 127) // 128
                prep_panel(C, w_in, 16, c0, ncols, r, Wkv, Wkv.t[:, :, l0:l0 + ncols], Bkv, Bkv.t[:, b0:b0 + nch])
            for j in range(8):
                prep_panel(C, wkv_b, 4, j * 256, 256, None, wkvb, wkvb.t[:, :, j * 256:(j + 1) * 256])
            nko = 0
            for bi, (row0, nt) in enumerate(blocks):
                ntl = nt // 128
                hT = hTs[0]
                ln_transpose(C, W, xkv, row0, ntl, hT)
                tb = tabB[0]
                ta = tabA[0]
                P.dma("sp", tb.t[:, :, 0:nt], ropeB_kv[:, :, row0:row0 + nt], dst=tb)
                P.dma("sp", ta.t[0:64, :, 0:nt], ropeA_kv[:, :, row0:row0 + nt], dst=ta)
                for j in range(4):
                    pb = nextps(C)
                    proj(C, pb, 128, nt, Wkv, Wkv.t, j * 128, 16, hT, 0)
                    P.op("act", lambda e, pb=pb, j=j: e.activation(zc.t[:, j, 0:nt], pb.t[:, 0:nt], AF.Identity, bias=Bkv.t[:, j:j + 1]),
                         reads=[pb, Bkv], writes=[zc], acc=True)
                rstd = rstd_from(C, W, [(zc, zc.t[:, j, 0:nt]) for j in range(4)], 512, nt)
                for j in range(4):
                    P.op("dve", lambda e, j=j, rstd=rstd: e.scalar_tensor_tensor(ckvn.t[:, j, 0:nt], zc.t[:, j, 0:nt], gn.t[:, 6 + j:7 + j], rstd.t[:, 0:nt], ALU.mult, ALU.mult),
                         reads=[zc, gn, rstd], writes=[ckvn], acc=True)
                pb = nextps(C)
                proj(C, pb, 64, nt, Wkv, Wkv.t, 512, 16, hT, 0)
                z = zk[0]
                P.op("act", lambda e, pb=pb, z=z: e.activation(z.t[0:64, 0:nt], pb.t[0:64, 0:nt], AF.Identity, bias=Bkv.t[0:64, 4:5]),
                     reads=[pb, Bkv], writes=[z])
                o = ko[nko % 4]; nko += 1
                rope(C, W, o, o.t[0:64, 0:nt], z, z.t[0:64, 0:nt], ta, 64, nt)
                P.dma("sp", KpeT.t[:, row0:row0 + nt], o.t[0:64, 0:nt], src=o, dst=KpeT)
                for hh in range(2):
                    pb = nextps(C)
                    proj(C, pb, 128, nt, Wkv, Wkv.t, 576 + hh * 128, 16, hT, 0)
                    z = zk[1 - hh % 2] if False else zk[hh % 2]
                    P.op("act", lambda e, pb=pb, z=z, hh=hh: e.activation(z.t[:, 0:nt], pb.t[:, 0:nt], AF.Identity, bias=Bkv.t[:, 5 + hh:6 + hh]),
                         reads=[pb, Bkv], writes=[z])
                    rstd = rstd_from(C, W, [(z, z.t[:, 0:nt])], 128, nt)
                    k_ = kn[hh % 2]
                    P.op("dve", lambda e, z=z, k_=k_, rstd=rstd: e.scalar_tensor_tensor(k_.t[:, 0:nt], z.t[:, 0:nt], gn.t[:, 11:12], rstd.t[:, 0:nt], ALU.mult, ALU.mult),
                         reads=[z, gn, rstd], writes=[k_])
                    o = ko[nko % 4]; nko += 1
                    rope(C, W, o, o.t[:, 0:nt], k_, k_.t[:, 0:nt], tb, 128, nt)
                    P.dma("sp", KbT.t[hh, :, row0:row0 + nt], o.t[:, 0:nt], src=o, dst=KbT)
                for hh in range(2):
                    pb = nextps(C)
                    proj(C, pb, 128, nt, Wkv, Wkv.t, 832 + hh * 128, 16, hT, 0)
                    v_ = vT[hh % 2]
                    P.op("act", lambda e, pb=pb, v_=v_, hh=hh: e.activation(v_.t[:, 0:nt], pb.t[:, 0:nt], AF.Identity, bias=Bkv.t[:, 7 + hh:8 + hh]),
                         reads=[pb, Bkv], writes=[v_])
                    W.vbT = getattr(W, "vbT", {})
                    W.vbT[hh] = v_
                for t in range(ntl):
                    pb = nextps(C)
                    pv = pb.t[:].bitcast(BF16)
                    for hh in range(2):
                        v_ = W.vbT[hh]
                        P.op("pe", lambda e, pv=pv, hh=hh, v_=v_, t=t: e.transpose(pv[:, hh * 128:(hh + 1) * 128], v_.t[:, t * 128:(t + 1) * 128], C.idb),
                             reads=[v_, C.cb], writes=[pb], acc=True)
                    vb_ = vbo[t % 2]
                    P.op("dve", lambda e, pv=pv, vb_=vb_: e.tensor_copy(vb_.t, pv[:, 0:256]), reads=[pb], writes=[vb_])
                    P.dma("sp", Vb.t[row0 // 128 + t, :, :], vb_.t, src=vb_, dst=Vb)
                for h in range(8):
                    pb = nextps(C)
                    for j in range(4):
                        P.mm(pb.t[:, 0:nt], wkvb.t[:, j, h * 256:h * 256 + 128], ckvn.t[:, j, 0:nt], j == 0, j == 3, [wkvb, ckvn], pb)
                    o = ko[nko % 4]; nko += 1
                    if h % 2 == 0:
                        P.op("act", lambda e, pb=pb, o=o: e.activation(o.t[:, 0:nt], pb.t[:, 0:nt], AF.Copy), reads=[pb], writes=[o])
                    else:
                        P.op("dve", lambda e, pb=pb, o=o: e.tensor_copy(o.t[:, 0:nt], pb.t[:, 0:nt]), reads=[pb], writes=[o])
                    P.dma("sp", KaT.t[h, :, row0:row0 + nt], o.t[:, 0:nt], src=o, dst=KaT)
                wv = wkvb.t.rearrange("p k (h two d) -> p k h two d", two=2, d=128)
                for t in range(ntl):
                    v2 = vo[t % 2]
                    for half in range(2):
                        pb = nextps(C)
                        for j in range(4):
                            P.mm(pb.t[:, :].rearrange("p (h d) -> p h d", d=128), ckvn.t[:, j, t * 128:(t + 1) * 128],
                                 wv[:, j, half * 4:(half + 1) * 4, 1, :], j == 0, j == 3, [wkvb, ckvn], pb)
                        if half == 0:
                            P.op("act", lambda e, pb=pb, v2=v2: e.activation(v2.t[:, 0:512], pb.t[:, :], AF.Copy), reads=[pb], writes=[v2], acc=True)
                        else:
                            P.op("dve", lambda e, pb=pb, v2=v2: e.tensor_copy(v2.t[:, 512:1024], pb.t[:, :]), reads=[pb], writes=[v2], acc=True)
                    P.dma("sp", Va.t[row0 // 128 + t, :, :], v2.t, src=v2, dst=Va)

        P.phase = "KVCTX"
        kv_phase(1, [(0, 256)])
        P.phase = "KVLAT"
        kv_phase(0, [(256 + i * 512, 512) for i in range(16)])

        P.phase = "Q"
        ar.reset()
        W = Ctx()
        hTq = ar.alloc("hTq", [16, 2048], BF16)
        mk = ar.mark()
        ln_alloc(C, W)
        W.lnc = 0
        ln_transpose(C, W, xq, 0, 16, hTq)
        ar.release(mk)
        mk = ar.mark()
        Wcq = ar.alloc("Wcq", [16, 768], BF16)
        Bcq = ar.alloc("Bcq", [6], F32)
        wqb = ar.alloc("wqb", [6, 1536], BF16)
        work_alloc(C, W)
        zq = ar.alloc("zq", [6, 512], F32)
        cqn = ar.alloc("cqn", [6, 512], BF16)
        zk = [ar.alloc(f"qzk{i}", [512], F32) for i in range(2)]
        tabA = [ar.alloc(f"qtabA{i}", [2, 512], F32) for i in range(1)]
        ko = [ar.alloc(f"qko{i}", [512], BF16) for i in range(4)]
        for j in range(3):
            prep_panel(C, w_in, 16, j * 256, 256, 0, Wcq, Wcq.t[:, :, j * 256:(j + 1) * 256], Bcq, Bcq.t[:, 2 * j:2 * j + 2])
        for j in range(6):
            prep_panel(C, wq_b, 6, j * 256, 256, None, wqb, wqb.t[:, :, j * 256:(j + 1) * 256])
        nko = 0
        for tbi in range(4):
            tok0 = tbi * 512
            ta = tabA[0]
            P.dma("sp", ta.t[0:64, :, :], ropeA_q[:, :, tok0:tok0 + 512], dst=ta)
            for j in range(6):
                pb = nextps(C)
                proj(C, pb, 128, 512, Wcq, Wcq.t, j * 128, 16, hTq, tok0)
                P.op("act", lambda e, pb=pb, j=j: e.activation(zq.t[:, j, :], pb.t[:, :], AF.Identity, bias=Bcq.t[:, j:j + 1]),
                     reads=[pb, Bcq], writes=[zq], acc=True)
            rstd = rstd_from(C, W, [(zq, zq.t[:, j, :]) for j in range(6)], 768, 512)
            for j in range(6):
                P.op("dve", lambda e, j=j, rstd=rstd: e.scalar_tensor_tensor(cqn.t[:, j, :], zq.t[:, j, :], gn.t[:, j:j + 1], rstd.t[:, :], ALU.mult, ALU.mult),
                     reads=[zq, gn, rstd], writes=[cqn], acc=True)
            for h in range(8):
                pb = nextps(C)
                for j in range(6):
                    P.mm(pb.t[:, :], wqb.t[:, j, h * 192:h * 192 + 128], cqn.t[:, j, :], j == 0, j == 5, [wqb, cqn], pb)
                o = ko[nko % 4]; nko += 1
                P.op("act", lambda e, pb=pb, o=o: e.activation(o.t[:, :], pb.t[:, :], AF.Copy), reads=[pb], writes=[o])
                P.dma("sp", QaT.t[h, :, tok0:tok0 + 512], o.t[:, :], src=o, dst=QaT)
                pb = nextps(C)
                for j in range(6):
                    P.mm(pb.t[0:64, :], wqb.t[:, j, h * 192 + 128:h * 192 + 192], cqn.t[:, j, :], j == 0, j == 5, [wqb, cqn], pb)
                z = zk[h % 2]
                P.op("dve", lambda e, pb=pb, z=z: e.tensor_copy(z.t[0:64, :], pb.t[0:64, :]), reads=[pb], writes=[z])
                o = ko[nko % 4]; nko += 1
                rope(C, W, o, o.t[0:64, :], z, z.t[0:64, :], ta, 64, 512)
                P.dma("sp", QpeT.t[h, :, tok0:tok0 + 512], o.t[0:64, :], src=o, dst=QpeT)
        ar.release(mk)
        mk = ar.mark()
        work_alloc(C, W)
        zk = [ar.alloc(f"qzk{i}", [512], F32) for i in range(2)]
        kn = [ar.alloc(f"qkn{i}", [512], F32) for i in range(2)]
        tabB = [ar.alloc(f"qtabB{i}", [2, 512], F32) for i in range(1)]
        ko = [ar.alloc(f"qko{i}", [512], BF16) for i in range(4)]
        Wp = [ar.alloc(f"Wp{i}", [16, 256], BF16) for i in range(2)]
        Bp = [ar.alloc(f"Bp{i}", [2], F32) for i in range(2)]
        def qcol0(k):
            return 1344 + k * 256 if k < 4 else 2880 + (k - 4) * 256

        ld_next = prep_load(C, w_in, 16, qcol0(0), 256)
        for pi in range(12):
            wp = Wp[pi % 2]
            bp = Bp[pi % 2]
            prep_compute(C, ld_next, 16, 256, 0, wp, wp.t, bp, bp.t)
            if pi + 1 < 12:
                ld_next = prep_load(C, w_in, 16, qcol0(pi + 1), 256)
            for tbi in range(4):
                tok0 = tbi * 512
                if pi < 4:
                    tb = tabB[0]
                    P.dma("sp", tb.t[:, :, :], ropeB_q[:, :, tok0:tok0 + 512], dst=tb)
                for cc in range(2):
                    pb = nextps(C)
                    proj(C, pb, 128, 512, wp, wp.t, cc * 128, 16, hTq, tok0)
                    o = ko[nko % 4]; nko += 1
                    if pi < 4:
                        hh = pi * 2 + cc
                        z = zk[cc]
                        P.op("act", lambda e, pb=pb, z=z, bp=bp, cc=cc: e.activation(z.t[:, :], pb.t[:, :], AF.Identity, bias=bp.t[:, cc:cc + 1]),
                             reads=[pb, bp], writes=[z])
                        rstd = rstd_from(C, W, [(z, z.t[:, :])], 128, 512)
                        k_ = kn[cc]
                        P.op("dve", lambda e, z=z, k_=k_, rstd=rstd: e.scalar_tensor_tensor(k_.t[:, :], z.t[:, :], gn.t[:, 10:11], rstd.t[:, :], ALU.mult, ALU.mult),
                             reads=[z, gn, rstd], writes=[k_])
                        rope(C, W, o, o.t[:, :], k_, k_.t[:, :], tb, 128, 512)
                        P.dma("sp", QbT.t[hh, :, tok0:tok0 + 512], o.t[:, :], src=o, dst=QbT)
                    else:
                        ch = (pi - 4) * 2 + cc
                        P.op("act", lambda e, pb=pb, o=o, bp=bp, cc=cc: e.activation(o.t[:, :], pb.t[:, :], AF.Silu, bias=bp.t[:, cc:cc + 1]),
                             reads=[pb, bp], writes=[o])
                        P.dma("sp", Gd.t[ch, :, tok0:tok0 + 512], o.t[:, :], src=o, dst=Gd)

        P.phase = "ATT"
        ar.reset()
        Kt = [ar.alloc(f"Kt{i}", [NKV], BF16) for i in range(2)]
        Vt = [ar.alloc(f"Vt{i}", [NKT, 128], BF16) for i in range(2)]
        Kpe = ar.alloc("Kpe", [NKV], BF16)
        Qt = [ar.alloc(f"Qt{i}", [2048], BF16) for i in range(2)]
        Qp = [ar.alloc(f"Qp{i}", [2048], BF16) for i in range(2)]
        Gt = [ar.alloc(f"Gt{i}", [2048], BF16) for i in range(2)]
        PT = [ar.alloc(f"PT{i}", [512], BF16) for i in range(6)]
        rc = [ar.alloc(f"rc{i}", [512], F32) for i in range(2)]
        yt = [ar.alloc(f"yt{i}", [512], F32) for i in range(2)]
        yo = [ar.alloc(f"yo{i}", [512], BF16) for i in range(2)]
        P.dma("sp", Kpe.t[0:64, :], KpeT.t, src=KpeT, dst=Kpe)
        SPS = C.psb[0:4]
        OACC = C.psb[4:6]
        SACC = C.psb[6:8]
        kvst = {"slot": -1, "kv": None}

        def load_head(hd):
            isA = hd < 8
            if isA or (hd - 8) % 4 == 0:
                kvst["slot"] += 1
                kt_, vt_ = Kt[kvst["slot"] % 2], Vt[kvst["slot"] % 2]
                if isA:
                    P.dma("sp", kt_.t, KaT.t[hd], src=KaT, dst=kt_)
                    P.dma("sp", vt_.t, Va.t[:, :, hd * 128:(hd + 1) * 128].rearrange("t p d -> p t d"), src=Va, dst=vt_)
                else:
                    kvh = (hd - 8) // 4
                    P.dma("sp", kt_.t, KbT.t[kvh], src=KbT, dst=kt_)
                    P.dma("sp", vt_.t, Vb.t[:, :, kvh * 128:(kvh + 1) * 128].rearrange("t p d -> p t d"), src=Vb, dst=vt_)
                kvst["kv"] = (kt_, vt_)
            kt_, vt_ = kvst["kv"]
            qt_ = Qt[hd % 2]
            qp_ = Qp[hd % 2]
            gt_ = Gt[hd % 2]
            if isA:
                P.dma("sp", qt_.t, QaT.t[hd], src=QaT, dst=qt_)
                P.dma("sp", qp_.t[0:64, :], QpeT.t[hd], src=QpeT, dst=qp_)
            else:
                P.dma("sp", qt_.t, QbT.t[hd - 8], src=QbT, dst=qt_)
            P.dma("sp", gt_.t, Gd.t[hd], src=Gd, dst=gt_)
            return kt_, vt_, qt_, qp_, gt_

        u = 0
        nxt = load_head(0)
        for hd in range(16):
            isA = hd < 8
            kt_, vt_, qt_, qp_, gt_ = nxt
            if hd + 1 < 16:
                nxt = load_head(hd + 1)
            scale = A_SCALE if isA else B_SCALE
            for qb in range(4):
                oT = OACC[u % 2]
                sm = SACC[u % 2]
                qs = slice(qb * 512, (qb + 1) * 512)

                def qk(kt):
                    sp_ = SPS[kt % 4]
                    ks = slice(kt * 128, (kt + 1) * 128)
                    P.mm(sp_.t[:, :], kt_.t[:, ks], qt_.t[:, qs], True, not isA, [kt_, qt_], sp_)
                    if isA:
                        P.mm(sp_.t[:, :], Kpe.t[0:64, ks], qp_.t[0:64, qs], False, True, [Kpe, qp_], sp_)

                def rest(kt):
                    sp_ = SPS[kt % 4]
                    pt = PT[kt % 6]
                    P.op("act", lambda e, pt=pt, sp_=sp_, sc=scale: e.activation(pt.t, sp_.t[:, :], AF.Exp, scale=sc), reads=[sp_], writes=[pt])
                    P.mm(oT.t[:, :], vt_.t[:, kt, :], pt.t, kt == 0, kt == NKT - 1, [vt_, pt], oT)
                    P.mm(sm.t[:, :], C.onesb, pt.t, kt == 0, kt == NKT - 1, [pt, C.cb], sm)

                qk(0)
                qk(1)
                for kt in range(NKT):
                    if kt + 2 < NKT:
                        qk(kt + 2)
                    rest(kt)
                r_ = rc[u % 2]
                y_ = yt[u % 2]
                o_ = yo[u % 2]
                P.op("dve", lambda e, r_=r_, sm=sm: e.reciprocal(r_.t, sm.t[:, :]), reads=[sm], writes=[r_])
                P.op("dve", lambda e, r_=r_, y_=y_, oT=oT: e.tensor_tensor(y_.t, oT.t[:, :], r_.t, ALU.mult), reads=[oT, r_], writes=[y_])
                P.op("pool", lambda e, y_=y_, o_=o_, gt_=gt_, qs=qs: e.tensor_tensor(o_.t, y_.t, gt_.t[:, qs], ALU.mult), reads=[y_, gt_], writes=[o_])
                P.dma("sp", Yd.t[hd, :, qs], o_.t, src=o_, dst=Yd)
                u += 1

        P.phase = "OUT"
        ar.reset()
        W = Ctx()
        Wo = ar.alloc("Wo", [16, 2048], BF16)
        Gys = [ar.alloc(f"Gy{i}", [16, 128], BF16) for i in range(2)]
        gb2 = ar.alloc("gb2", [2048], F32)
        lnbc = ar.alloc("lnbc", [2, 2048], F32)
        xts = [ar.alloc(f"oxt{i}", [2048], F32) for i in range(2)]
        tmp = [ar.alloc(f"otmp{i}", [2048], F32) for i in range(1)]
        st = ar.alloc("ost", [4, 6], F32)
        mv = ar.alloc("omv", [2], F32)
        rs = ar.alloc("ors", [1], F32)
        P.dma("sp", gb2.t, Gbc_d.t, src=Gbc_d, dst=gb2)
        P.dma("sp", lnbc.t[:, 0, :], lngb[0:1, :].partition_broadcast(128), dst=lnbc)
        P.dma("sp", lnbc.t[:, 1, :], lngb[1:2, :].partition_broadcast(128), dst=lnbc)
        for j in range(8):
            prep_panel(C, w_out, 16, j * 256, 256, None, Wo, Wo.t[:, :, j * 256:(j + 1) * 256])
        if fused:
            x1b = P.dram("X1d", [2048, 2048], F32)
            x1 = x1b.t
        else:
            x1b = P.view("x1out", x1)
        for t in range(16):
            xt = xts[t % 2]
            tm = tmp[0]
            P.dma("sp", xt.t, xq[t * 128:(t + 1) * 128, :], dst=xt)
            Gy = Gys[t % 2]
            P.dma("sp", Gy.t, Yd.t[:, :, t * 128:(t + 1) * 128].rearrange("c p t -> p c t"), src=Yd, dst=Gy)
            for nb in range(4):
                pb = nextps(C)
                ns = slice(nb * 512, (nb + 1) * 512)
                for kc in range(16):
                    P.mm(pb.t[:, :], Gy.t[:, kc, :], Wo.t[:, kc, ns], kc == 0, kc == 15, [Gy, Wo], pb)
                P.op("dve", lambda e, pb=pb, ns=ns, tm=tm: e.tensor_tensor(tm.t[:, ns], pb.t[:, :], gb2.t[:, ns], ALU.mult),
                     reads=[pb, gb2], writes=[tm], acc=True)
            P.op("dve", lambda e, xt=xt, tm=tm: e.scalar_tensor_tensor(tm.t, xt.t, ALPHA, tm.t, ALU.mult, ALU.add), reads=[xt, tm], writes=[tm])
            for c in range(4):
                P.op("dve", lambda e, c=c, tm=tm: e.bn_stats(st.t[:, c, :], tm.t[:, c * 512:(c + 1) * 512]), reads=[tm], writes=[st], acc=True)
            P.op("dve", lambda e: e.bn_aggr(mv.t, st.t), reads=[st], writes=[mv])
            P.op("act", lambda e: e.activation(rs.t, mv.t[:, 1:2], AF.Sqrt, bias=C.eps.t[:, 0:1], scale=1.0), reads=[mv, C.eps], writes=[rs])
            P.op("dve", lambda e: e.reciprocal(rs.t, rs.t), reads=[rs], writes=[rs])
            P.op("dve", lambda e, tm=tm: e.tensor_scalar(tm.t, tm.t, mv.t[:, 0:1], rs.t[:, 0:1], ALU.subtract, ALU.mult), reads=[tm, mv, rs], writes=[tm])
            P.op("pool", lambda e, tm=tm: e.tensor_tensor(tm.t, tm.t, lnbc.t[:, 0, :], ALU.mult), reads=[tm, lnbc], writes=[tm])
            P.op("dve", lambda e, tm=tm, xt=xt: e.tensor_tensor(xt.t, tm.t, lnbc.t[:, 1, :], ALU.add), reads=[tm, lnbc], writes=[xt])
            P.dma("sp", x1[t * 128:(t + 1) * 128, :], xt.t, src=xt, dbuf=x1b)
        if fused:
            outb = fused_tail(C, nc, din, x1b)
            P.fence("sp", [outb])
        else:
            P.fence("sp", [x1b])
        P.emit()
    return nc


RS_GROUPS = [[0, 1, 2, 3], [4, 5, 6, 7]]


def fused_tail(C, nc, din, x1b):
    P, ar = C.P, C.ar
    cvec1 = din("cvec1", [128, 32])
    ada_w1 = din("ada_w1", [2048, 6144])
    ada_b1 = din("ada_b1", [2, 6144])
    wf = din("w_in_f", [2048, 8192])
    w_out_f = din("w_out_f", [4096, 2048])
    lngb1 = din("lngb1", [2, 2048])
    sel_d = din("sel", [128, 4])
    cn_d = din("cn", [128, 2, 2, 256], BF16)
    w64_d = din("w64", [128, 128], BF16)
    M_d = din("Mtw", [128, 64, 2, 128], BF16)
    outp = nc.dram_tensor("out", [2048, 2048], F32, kind="ExternalOutput").ap()
    U_in = [P.dram(f"U_in{q}", [4 * 256, 8192], BF16) for q in range(4)]
    U_out = [P.dram(f"U_out{q}", [256, 8192], BF16) for q in range(4)]
    F_in = [P.dram(f"F_in{g}", [4 * 1024, 2048], BF16) for g in range(4)]
    F_out = [P.dram(f"F_out{g}", [1024, 2048], BF16) for g in range(4)]
    Gl = P.dram("Gl", [32, 128, 2048], BF16)
    Gbc1 = P.dram("Gbc1", [128, 2048], F32)
    Td = P.dram("Td", [16, 128, 2048], F32)
    sel = P.sb("sel_sb", [128, 4], F32)
    P.dma("sp", sel.t[:], sel_d, dst=sel)

    P.phase = "B1MOD"
    ar.reset()
    gbc = ar.alloc("gbc1_tmp", [2048], F32)
    emit_modulation(C, cvec1, ada_w1, ada_b1[:, :], gbc, 0)
    P.dma("sp", Gbc1.t, gbc.t, src=gbc, dst=Gbc1)
    ar.reset()
    P.phase = "B1"
    W = Ctx()
    W.xsrc = x1b
    hT1 = ar.alloc("hT1", [16, 2048], BF16)
    mk = ar.mark()
    ln_alloc(C, W)
    W.lnc = 0
    ln_transpose(C, W, x1b.t, 0, 16, hT1)
    ar.release(mk)
    Wp = [ar.alloc(f"fWp{i}", [16, 256], BF16) for i in range(2)]
    Bp = [ar.alloc(f"fBp{i}", [2], F32) for i in range(2)]
    uo = [ar.alloc(f"fuo{i}", [512], BF16) for i in range(2)]
    us = [ar.alloc(f"fus{i}", [4, 512], BF16) for i in range(2)]
    go = [ar.alloc(f"fgo{i}", [512], BF16) for i in range(2)]
    n = 0
    order = [(True, d * 4 + q) for q in range(4) for d in range(4)] + [(False, i) for i in range(16)]
    def col0(k):
        return order[k][1] * 256 if order[k][0] else 4096 + order[k][1] * 256

    ld_next = prep_load(C, wf, 16, col0(0), 256)
    for pi, (isu, pidx) in enumerate(order):
        wp, bp = Wp[pi % 2], Bp[pi % 2]
        prep_compute(C, ld_next, 16, 256, 0, wp, wp.t, bp, bp.t)
        if pi + 1 < len(order):
            ld_next = prep_load(C, wf, 16, col0(pi + 1), 256)
        for tbi in range(4):
            tok0 = tbi * 512
            for cc in range(2):
                pb = nextps(C)
                proj(C, pb, 128, 512, wp, wp.t, cc * 128, 16, hT1, tok0)
                ch = pidx * 2 + cc
                if isu:
                    o = uo[n % 2]
                    s4 = us[n % 2]
                    n += 1
                    P.op("act", lambda e, pb=pb, o=o, bp=bp, cc=cc: e.activation(o.t, pb.t[:, :], AF.Identity, bias=bp.t[:, cc:cc + 1]),
                         reads=[pb, bp], writes=[o])
                    for j in range(4):
                        if j % 2 == 0:
                            P.op("dve", lambda e, o=o, s4=s4, j=j: e.tensor_scalar_mul(s4.t[:, j, :], o.t, sel.t[:, j:j + 1]),
                                 reads=[o, sel], writes=[s4], acc=True)
                        else:
                            P.op("act", lambda e, o=o, s4=s4, j=j: e.activation(s4.t[:, j, :], o.t, AF.Copy, scale=sel.t[:, j:j + 1]),
                                 reads=[o, sel], writes=[s4], acc=True)
                    dest, q = pidx // 4, pidx % 4
                    row0 = dest * 256 + cc * 128
                    dst_ap = U_in[q].t[row0:row0 + 128, :].rearrange("p (j t) -> p j t", j=4)[:, :, tok0:tok0 + 512]
                    P.dma("sp", dst_ap, s4.t, src=s4, dst=U_in[q])
                else:
                    o = go[n % 2]
                    n += 1
                    P.op("act", lambda e, pb=pb, o=o, bp=bp, cc=cc: e.activation(o.t, pb.t[:, :], AF.Silu, bias=bp.t[:, cc:cc + 1]),
                         reads=[pb, bp], writes=[o])
                    chp = ((ch % 8) // 2) * 8 + (ch // 8) * 2 + ch % 2
                    P.dma("sp", Gl.t[chp, :, tok0:tok0 + 512], o.t, src=o, dst=Gl)
        if isu and pidx // 4 == 3:
            q = pidx % 4
            P.op("pool", lambda e, q=q: e.collective_compute("ReduceScatter", ALU.add, replica_groups=RS_GROUPS, ins=[U_in[q].t], outs=[U_out[q].t]),
                 reads=[U_in[q]], writes=[U_out[q]], dma_dst=U_out[q], dma_inc=1)

    P.phase = "B2"
    ar.reset()
    cn = ar.alloc("cn", [2, 2, 256], BF16)
    w64 = ar.alloc("w64", [128], BF16)
    Mt = ar.alloc("Mt", [64, 2, 128], BF16)
    P.dma("sp", cn.t, cn_d, dst=cn)
    P.dma("sp", w64.t, w64_d, dst=w64)
    P.dma("sp", Mt.t, M_d, dst=Mt)
    uT = ar.alloc("uT", [2, 8192], BF16)
    fT = ar.alloc("fT", [8192], BF16)
    z = ar.alloc("z", [128, 128], BF16)
    Y = ar.alloc("Y", [128, 128], BF16)
    fs = [ar.alloc(f"fs{i}", [2048], BF16) for i in range(2)]
    nfs = 0
    for g in range(4):
        P.dma("sp", uT.t, U_out[g].t.rearrange("(c p) t -> p c t", p=128), src=U_out[g], dst=uT)
        uv = uT.t.rearrange("p c (l1 l2) -> p c l2 l1", l2=128)
        for kh in range(2):
            ks = slice(kh * 128, (kh + 1) * 128)
            for l2p in range(32):
                pb = nextps(C)
                for q in range(4):
                    l2 = l2p * 4 + q
                    for ri in range(2):
                        for cc in range(2):
                            P.mm(pb.t[ri * 64:(ri + 1) * 64, q * 128:(q + 1) * 128], uv[:, cc, l2, :], cn.t[:, cc, ri, ks],
                                 cc == 0, cc == 1, [uT, cn], pb)
                dst = z.t[:, l2p * 4:(l2p + 1) * 4, :]
                src = pb.t[:, :].rearrange("p (a b) -> p a b", b=128)
                if l2p % 2 == 0:
                    P.op("act", lambda e, dst=dst, src=src: e.activation(dst, src, AF.Copy), reads=[pb], writes=[z], acc=True)
                else:
                    P.op("dve", lambda e, dst=dst, src=src: e.tensor_copy(dst, src), reads=[pb], writes=[z], acc=True)
            for k3p in range(32):
                pb = nextps(C)
                for q in range(4):
                    k3 = k3p * 4 + q
                    P.mm(pb.t[:, q * 128:(q + 1) * 128], z.t[:, :, k3], w64.t, True, True, [z, w64], pb)
                dst = Y.t[:, k3p * 4:(k3p + 1) * 4, :]
                src = pb.t[:, :].rearrange("p (a b) -> p a b", b=128)
                if k3p % 2 == 0:
                    P.op("act", lambda e, dst=dst, src=src: e.activation(dst, src, AF.Copy), reads=[pb], writes=[Y], acc=True)
                else:
                    P.op("dve", lambda e, dst=dst, src=src: e.tensor_copy(dst, src), reads=[pb], writes=[Y], acc=True)
            fv = fT.t.rearrange("p (k2 k1) -> p k1 k2", k1=64)
            for k1p in range(16):
                pb = nextps(C)
                for q in range(4):
                    k1 = k1p * 4 + q
                    for ri in range(2):
                        P.mm(pb.t[:, q * 128:(q + 1) * 128], Y.t[:, :, ri * 64 + k1], Mt.t[:, k1, ri, :], ri == 0, ri == 1, [Y, Mt], pb)
                fsl = fv[:, k1p * 4:(k1p + 1) * 4, :]
                src = pb.t[:, :].rearrange("p (a b) -> p a b", b=128)
                if k1p % 2 == 0:
                    P.op("act", lambda e, fsl=fsl, src=src: e.activation(fsl, src, AF.Copy, scale=FSCALE), reads=[pb], writes=[fT], acc=True)
                else:
                    P.op("dve", lambda e, fsl=fsl, src=src: e.tensor_scalar_mul(fsl, src, FSCALE), reads=[pb], writes=[fT], acc=True)
            for d in range(4):
                for j in range(4):
                    f_ = fs[nfs % 2]
                    nfs += 1
                    if j % 2 == 0:
                        P.op("dve", lambda e, f_=f_, d=d, j=j: e.tensor_scalar_mul(f_.t, fT.t[:, d * 2048:(d + 1) * 2048], sel.t[:, j:j + 1]),
                             reads=[fT, sel], writes=[f_])
                    else:
                        P.op("act", lambda e, f_=f_, d=d, j=j: e.activation(f_.t, fT.t[:, d * 2048:(d + 1) * 2048], AF.Copy, scale=sel.t[:, j:j + 1]),
                             reads=[fT, sel], writes=[f_])
                    r0 = d * 1024 + j * 256 + kh * 128
                    P.dma("sp", F_in[g].t[r0:r0 + 128, :], f_.t, src=f_, dst=F_in[g])
        P.op("pool", lambda e, g=g: e.collective_compute("ReduceScatter", ALU.add, replica_groups=RS_GROUPS, ins=[F_in[g].t], outs=[F_out[g].t]),
             reads=[F_in[g]], writes=[F_out[g]], dma_dst=F_out[g], dma_inc=1)

    P.phase = "C"
    ar.reset()
    Wo = ar.alloc("fWo", [32, 1024], BF16)
    gb2 = ar.alloc("fgb2", [2048], F32)
    lnbc = ar.alloc("flnbc", [2, 2048], F32)
    Fys = [ar.alloc(f"fFy{i}", [32, 128], BF16) for i in range(2)]
    Ggs = [ar.alloc(f"fGg{i}", [32, 128], BF16) for i in range(2)]
    Gys = [ar.alloc(f"fGy{i}", [32, 128], BF16) for i in range(2)]
    tms = [ar.alloc(f"ftm{i}", [1024], F32) for i in range(2)]
    xts = [ar.alloc(f"fxt{i}", [2048], F32) for i in range(1)]
    tmp = ar.alloc("fotmp", [2048], F32)
    st = ar.alloc("fost", [4, 6], F32)
    mv = ar.alloc("fomv", [2], F32)
    rs = ar.alloc("fors", [1], F32)
    P.dma("sp", gb2.t, Gbc1.t, src=Gbc1, dst=gb2)
    P.dma("sp", lnbc.t[:, 0, :], lngb1[0:1, :].partition_broadcast(128), dst=lnbc)
    P.dma("sp", lnbc.t[:, 1, :], lngb1[1:2, :].partition_broadcast(128), dst=lnbc)
    n = 0
    for nh in range(2):
        for kh in range(2):
            for j in range(4):
                c0 = nh * 1024 + j * 256
                prep_panel(C, w_out_f[kh * 2048:(kh + 1) * 2048, :], 16, c0, 256, None, Wo,
                           Wo.t[:, kh * 16:(kh + 1) * 16, j * 256:(j + 1) * 256])
        def load_tile(k):
            Fy, Gg = Fys[k % 2], Ggs[k % 2]
            t = k % 16
            for g in range(4):
                P.dma("sp", Fy.t[:, g * 8:(g + 1) * 8, :], F_out[g].t[:, t * 128:(t + 1) * 128].rearrange("(c p) t -> p c t", p=128),
                      src=F_out[g], dst=Fy)
            P.dma("sp", Gg.t, Gl.t[:, :, t * 128:(t + 1) * 128].rearrange("c p t -> p c t"), src=Gl, dst=Gg)

        if nh == 0:
            load_tile(0)
        for t in range(16):
            Fy, Gg, Gy, tm = Fys[n % 2], Ggs[n % 2], Gys[n % 2], tms[n % 2]
            n += 1
            if n < 32:
                load_tile(n)
            P.op("pool", lambda e, Fy=Fy, Gg=Gg, Gy=Gy: e.tensor_tensor(Gy.t, Fy.t, Gg.t, ALU.mult), reads=[Fy, Gg], writes=[Gy])
            for nb in range(2):
                pb = nextps(C)
                ns = slice(nb * 512, (nb + 1) * 512)
                gs = slice(nh * 1024 + nb * 512, nh * 1024 + (nb + 1) * 512)
                for kp in range(32):
                    kc = ((kp % 8) // 2) * 8 + (kp // 8) * 2 + kp % 2
                    P.mm(pb.t[:, :], Gy.t[:, kp, :], Wo.t[:, kc, ns], kp == 0, kp == 31, [Gy, Wo], pb)
                P.op("dve", lambda e, pb=pb, ns=ns, gs=gs, tm=tm: e.tensor_tensor(tm.t[:, ns], pb.t[:, :], gb2.t[:, gs], ALU.mult),
                     reads=[pb, gb2], writes=[tm], acc=True)
            P.dma("sp", Td.t[t, :, nh * 1024:(nh + 1) * 1024], tm.t, src=tm, dst=Td)
    ob = P.view("out_b", outp)
    for t in range(16):
        xt = xts[0]
        tm = tmp
        P.dma("sp", xt.t, x1b.t[t * 128:(t + 1) * 128, :], src=x1b, dst=xt)
        P.dma("sp", tm.t, Td.t[t], src=Td, dst=tm)
        P.op("dve", lambda e, xt=xt, tm=tm: e.scalar_tensor_tensor(tm.t, xt.t, ALPHA, tm.t, ALU.mult, ALU.add), reads=[xt, tm], writes=[tm])
        for c in range(4):
            P.op("dve", lambda e, c=c, tm=tm: e.bn_stats(st.t[:, c, :], tm.t[:, c * 512:(c + 1) * 512]), reads=[tm], writes=[st], acc=True)
        P.op("dve", lambda e: e.bn_aggr(mv.t, st.t), reads=[st], writes=[mv])
        P.op("act", lambda e: e.activation(rs.t, mv.t[:, 1:2], AF.Sqrt, bias=C.eps.t[:, 0:1], scale=1.0), reads=[mv, C.eps], writes=[rs])
        P.op("dve", lambda e: e.reciprocal(rs.t, rs.t), reads=[rs], writes=[rs])
        P.op("dve", lambda e, tm=tm: e.tensor_scalar(tm.t, tm.t, mv.t[:, 0:1], rs.t[:, 0:1], ALU.subtract, ALU.mult), reads=[tm, mv, rs], writes=[tm])
        P.op("pool", lambda e, tm=tm: e.tensor_tensor(tm.t, tm.t, lnbc.t[:, 0, :], ALU.mult), reads=[tm, lnbc], writes=[tm])
        P.op("dve", lambda e, tm=tm, xt=xt: e.tensor_tensor(xt.t, tm.t, lnbc.t[:, 1, :], ALU.add), reads=[tm, lnbc], writes=[xt])
        P.dma("sp", outp[t * 128:(t + 1) * 128, :], xt.t, src=xt, dbuf=ob)
    return ob


def fused_inputs(inp):
    maps = stage_a_inputs(inp)
    cn, w64, M = fft_consts()
    for core in range(8):
        b, r = core // 4, core % 4
        cv = np.zeros((128, 16, 2), np.float32)
        cv[:, :, 0] = _pk(inp["c"][b], 16)
        cv[:, :, 1] = cv[:, :, 0]
        sel = np.zeros((128, 4), np.float32)
        sel[:, r] = 1.0
        maps[core].update({
            "cvec1": cv.reshape(128, 32), "ada_w1": inp["ada_w"][1],
            "ada_b1": np.stack([inp["ada_b"][1], inp["ada_b"][1]]),
            "w_in_f": inp["w_in_fourier"][0], "w_out_f": inp["w_out_fourier"][0],
            "lngb1": np.stack([inp["ln_g"][1], inp["ln_b"][1]]), "sel": sel,
            "cn": cn, "w64": w64, "Mtw": M,
        })
    return maps


def kernel_fused(**inp):
    inp = {k: np.asarray(v) for k, v in inp.items()}
    nc = build_stage_a(fused=True)
    res = run_bass_kernel_spmd(nc, fused_inputs(inp), core_ids=list(range(8)))
    out = np.zeros((2, 8192, 2048), np.float32)
    for core in range(8):
        b, r = core // 4, core % 4
        out[b, r * 2048:(r + 1) * 2048] = res.results[core]["out"]
    return out


def _rope_tables(rot_dim, seq=8192, grid_w=64):
    t = np.arange(seq)
    r = (t // grid_w).astype(np.float32)
    col = (t % grid_w).astype(np.float32)
    nf = rot_dim // 4
    inv = (np.float32(10000.0) ** (-(np.arange(nf, dtype=np.float32)) / np.float32(nf))).astype(np.float32)
    ang = np.concatenate([r[:, None] * inv[None, :], col[:, None] * inv[None, :]], axis=-1).astype(np.float32)
    cos = np.cos(ang).astype(np.float32)
    sin = np.sin(ang).astype(np.float32)
    cosT = np.repeat(cos, 2, axis=1).T
    sgn = np.tile(np.array([-1.0, 1.0], np.float32), rot_dim // 2)
    sinT = (np.repeat(sin, 2, axis=1) * sgn[None, :]).T
    return np.ascontiguousarray(cosT), np.ascontiguousarray(sinT)


def _consts():
    c = np.zeros((128, 3, 128), np.float32)
    c[:, 0, :] = np.eye(128, dtype=np.float32)
    c[:, 1, :] = 1.0
    idx = np.arange(128)
    c[idx, 2, idx ^ 1] = 1.0
    return c


def _pk(v, kc):
    return np.ascontiguousarray(np.asarray(v, np.float32).reshape(kc, 128).T)


def stage_a_inputs(inp):
    cB, sB = _rope_tables(128)
    cA, sA = _rope_tables(64)
    tabB = np.zeros((128, 2, NKV), np.float32)
    tabB[:, 0, :256] = 1.0
    tabB[:, 0, 256:] = cB
    tabB[:, 1, 256:] = sB
    tabA = np.zeros((64, 2, NKV), np.float32)
    tabA[:, 0, :256] = 1.0
    tabA[:, 0, 256:] = cA
    tabA[:, 1, 256:] = sA
    gains = np.zeros((128, 12), np.float32)
    gains[:, 0:6] = _pk(inp["q_lora_norm"][0], 6)
    gains[:, 6:10] = _pk(inp["kv_lora_norm"][0], 4)
    gains[:, 10] = inp["q_norm_b"][0]
    gains[:, 11] = inp["k_norm_b"][0]
    consts = _consts()
    maps = []
    for core in range(8):
        b, qr = core // 4, core % 4
        t0 = qr * 2048
        cv = np.zeros((128, 16, 2), np.float32)
        cv[:, :, 0] = _pk(inp["c"][b], 16)
        cv[:, :, 1] = _pk(inp["c_ctx"], 16)
        maps.append({
            "xkv": np.ascontiguousarray(np.concatenate([inp["ctx"][b], inp["x"][b]], axis=0)),
            "xq": np.ascontiguousarray(inp["x"][b, t0:t0 + 2048]),
            "cvec": cv.reshape(128, 32),
            "ada_w": inp["ada_w"][0], "ada_b": np.stack([inp["ada_b"][0], inp["ada_b"][0]]),
            "w_in": inp["w_in_attn"][0], "wq_b": inp["wq_b"][0], "wkv_b": inp["wkv_b"][0], "w_out": inp["w_out_attn"][0],
            "gains": gains, "lngb": np.stack([inp["ln_g"][0], inp["ln_b"][0]]), "consts": consts,
            "ropeB_kv": tabB, "ropeA_kv": tabA,
            "ropeB_q": np.ascontiguousarray(tabB[:, :, 256 + t0:256 + t0 + 2048]),
            "ropeA_q": np.ascontiguousarray(tabA[:, :, 256 + t0:256 + t0 + 2048]),
        })
    return maps


def run_stage_a(inp):
    nc = build_stage_a()
    res = run_bass_kernel_spmd(nc, stage_a_inputs(inp), core_ids=list(range(8)))
    x1 = np.zeros((2, 8192, 2048), np.float32)
    for core in range(8):
        b, qr = core // 4, core % 4
        x1[b, qr * 2048:(qr + 1) * 2048] = res.results[core]["x1"]
    return x1


FSCALE = float((8192.0 * 256.0) ** -0.5)


def fft_consts():
    import ml_dtypes
    bf = ml_dtypes.bfloat16
    c = np.arange(256)[:, None].astype(np.float64)
    k3 = np.arange(256)[None, :].astype(np.float64)
    a = 2 * np.pi * c * k3 / 256
    cn = np.zeros((128, 2, 2, 256), np.float64)
    for cc in range(2):
        cn[:, cc, 0, :] = np.cos(a[cc * 128:(cc + 1) * 128])
        cn[:, cc, 1, :] = -np.sin(a[cc * 128:(cc + 1) * 128])
    l1 = np.arange(64)[:, None].astype(np.float64)
    k1 = np.arange(64)[None, :].astype(np.float64)
    th = 2 * np.pi * l1 * k1 / 64
    wr, wi = np.cos(th), -np.sin(th)
    w64 = np.zeros((128, 128), np.float64)
    w64[0:64, 0:64] = wr
    w64[64:128, 0:64] = -wi
    w64[0:64, 64:128] = wi
    w64[64:128, 64:128] = wr
    l2 = np.arange(128)[:, None, None].astype(np.float64)
    kk = (np.arange(64)[None, :, None] + 64 * np.arange(128)[None, None, :]).astype(np.float64)
    ph = 2 * np.pi * ((l2 * kk) % 8192) / 8192
    M = np.stack([np.cos(ph), np.sin(ph)], axis=2)
    return cn.astype(np.float32).astype(bf), w64.astype(np.float32).astype(bf), M.astype(np.float32).astype(bf)


def build_stage_b():
    nc = bass.Bass("TRN2", target_bir_lowering=False)

    def din(name, shape, dt=F32):
        return nc.dram_tensor(name, list(shape), dt, kind="ExternalInput").ap()

    x1f = din("x1f", [8192, 2048])
    cvec = din("cvec", [128, 32])
    ada_w = din("ada_w", [2048, 6144])
    ada_b = din("ada_b", [2, 6144])
    wuf = din("wu", [2048, 1024])
    wgf = din("wg", [2048, 1024])
    consts = din("consts", [128, 3, 128])
    cn_d = din("cn", [128, 2, 2, 256], BF16)
    w64_d = din("w64", [128, 128], BF16)
    M_d = din("Mtw", [128, 64, 2, 128], BF16)
    yT = nc.dram_tensor("yT", [1024, 8192], BF16, kind="ExternalOutput").ap()
    gbc_o = nc.dram_tensor("gbc", [128, 2048], F32, kind="ExternalOutput").ap()

    with ExitStack() as es:
        C = setup_common(nc, es, consts)
        P, ar = C.P, C.ar
        Ud = P.dram("Ud", [8, 128, 8192], BF16)
        Gd = P.dram("Gd", [8, 128, 8192], BF16)
        gbc = ar.alloc("gbc_tmp", [2048], F32)
        emit_modulation(C, cvec, ada_w, ada_b[:, :], gbc, 0)
        gbc_b = P.view("gbc_out", gbc_o)
        P.dma("sp", gbc_o, gbc.t, src=gbc, dbuf=gbc_b)
        ar.reset()
        W = Ctx()
        Wu = ar.alloc("Wu", [16, 1024], BF16)
        Wg = ar.alloc("Wg", [16, 1024], BF16)
        Bu = ar.alloc("Bu", [8], F32)
        Bg = ar.alloc("Bg", [8], F32)
        ln_alloc(C, W)
        W.lnc = 0
        hTs = [ar.alloc(f"hT{i}", [16, 512], BF16) for i in range(2)]
        uo = [ar.alloc(f"uo{i}", [512], BF16) for i in range(4)]
        for j in range(4):
            prep_panel(C, wuf, 16, j * 256, 256, 0, Wu, Wu.t[:, :, j * 256:(j + 1) * 256], Bu, Bu.t[:, 2 * j:2 * j + 2])
            prep_panel(C, wgf, 16, j * 256, 256, 0, Wg, Wg.t[:, :, j * 256:(j + 1) * 256], Bg, Bg.t[:, 2 * j:2 * j + 2])
        nuo = 0
        for tb in range(16):
            hT = hTs[tb % 2]
            ln_transpose(C, W, x1f, tb * 512, 4, hT)
            for j in range(16):
                pb = nextps(C)
                isu = j < 8
                jj = j % 8
                proj(C, pb, 128, 512, Wu if isu else Wg, (Wu if isu else Wg).t, jj * 128, 16, hT, 0)
                o = uo[nuo % 4]; nuo += 1
                bb = Bu if isu else Bg
                P.op("act", lambda e, pb=pb, o=o, bb=bb, jj=jj, isu=isu: e.activation(o.t, pb.t[:, :], AF.Identity if isu else AF.Silu, bias=bb.t[:, jj:jj + 1]),
                     reads=[pb, bb], writes=[o])
                dd = Ud if isu else Gd
                P.dma("sp", dd.t[jj, :, tb * 512:(tb + 1) * 512], o.t, src=o, dst=dd)
        ar.reset()
        cn = ar.alloc("cn", [2, 2, 256], BF16)
        w64 = ar.alloc("w64", [128], BF16)
        Mt = ar.alloc("Mt", [64, 2, 128], BF16)
        P.dma("sp", cn.t, cn_d, dst=cn)
        P.dma("sp", w64.t, w64_d, dst=w64)
        P.dma("sp", Mt.t, M_d, dst=Mt)
        uT = ar.alloc("uT", [2, 8192], BF16)
        gT = ar.alloc("gT", [2, 8192], BF16)
        z = ar.alloc("z", [128, 128], BF16)
        Y = ar.alloc("Y", [128, 128], BF16)
        yTb = P.view("yT_out", yT)
        for g in range(4):
            P.dma("sp", uT.t, Ud.t[2 * g:2 * g + 2].rearrange("c p t -> p c t"), src=Ud, dst=uT)
            P.dma("sp", gT.t, Gd.t[2 * g:2 * g + 2].rearrange("c p t -> p c t"), src=Gd, dst=gT)
            uv = uT.t.rearrange("p c (l1 l2) -> p c l2 l1", l2=128)
            for kh in range(2):
                ks = slice(kh * 128, (kh + 1) * 128)
                for l2p in range(32):
                    pb = nextps(C)
                    for q in range(4):
                        l2 = l2p * 4 + q
                        for ri in range(2):
                            for cc in range(2):
                                P.mm(pb.t[ri * 64:(ri + 1) * 64, q * 128:(q + 1) * 128], uv[:, cc, l2, :], cn.t[:, cc, ri, ks],
                                     cc == 0, cc == 1, [uT, cn], pb)
                    dst = z.t[:, l2p * 4:(l2p + 1) * 4, :]
                    src = pb.t[:, :].rearrange("p (a b) -> p a b", b=128)
                    if l2p % 2 == 0:
                        P.op("act", lambda e, dst=dst, src=src: e.activation(dst, src, AF.Copy), reads=[pb], writes=[z], acc=True)
                    else:
                        P.op("dve", lambda e, dst=dst, src=src: e.tensor_copy(dst, src), reads=[pb], writes=[z], acc=True)
                for k3p in range(32):
                    pb = nextps(C)
                    for q in range(4):
                        k3 = k3p * 4 + q
                        P.mm(pb.t[:, q * 128:(q + 1) * 128], z.t[:, :, k3], w64.t, True, True, [z, w64], pb)
                    dst = Y.t[:, k3p * 4:(k3p + 1) * 4, :]
                    src = pb.t[:, :].rearrange("p (a b) -> p a b", b=128)
                    if k3p % 2 == 0:
                        P.op("act", lambda e, dst=dst, src=src: e.activation(dst, src, AF.Copy), reads=[pb], writes=[Y], acc=True)
                    else:
                        P.op("dve", lambda e, dst=dst, src=src: e.tensor_copy(dst, src), reads=[pb], writes=[Y], acc=True)
                gv = gT.t[:, kh, :].rearrange("p (k2 k1) -> p k1 k2", k1=64)
                for k1p in range(16):
                    pb = nextps(C)
                    for q in range(4):
                        k1 = k1p * 4 + q
                        for ri in range(2):
                            P.mm(pb.t[:, q * 128:(q + 1) * 128], Y.t[:, :, ri * 64 + k1], Mt.t[:, k1, ri, :], ri == 0, ri == 1, [Y, Mt], pb)
                    gsl = gv[:, k1p * 4:(k1p + 1) * 4, :]
                    src = pb.t[:, :].rearrange("p (a b) -> p a b", b=128)
                    P.op("dve", lambda e, gsl=gsl, src=src: e.scalar_tensor_tensor(gsl, src, FSCALE, gsl, ALU.mult, ALU.mult),
                         reads=[pb, gT], writes=[gT], acc=True)
            P.dma("sp", yT[g * 256:(g + 1) * 256, :].rearrange("(c p) t -> p c t", p=128), gT.t, src=gT, dbuf=yTb)
        P.fence("sp", [yTb, gbc_b])
        P.emit()
    return nc


def stage_b_inputs(inp, x1):
    cn, w64, M = fft_consts()
    consts = _consts()
    maps = []
    wf = inp["w_in_fourier"][0]
    for core in range(8):
        b, cq = core // 4, core % 4
        cv = np.zeros((128, 16, 2), np.float32)
        cv[:, :, 0] = _pk(inp["c"][b], 16)
        cv[:, :, 1] = cv[:, :, 0]
        maps.append({
            "x1f": np.ascontiguousarray(x1[b]), "cvec": cv.reshape(128, 32),
            "ada_w": inp["ada_w"][1], "ada_b": np.stack([inp["ada_b"][1], inp["ada_b"][1]]),
            "wu": np.ascontiguousarray(wf[:, cq * 1024:(cq + 1) * 1024]),
            "wg": np.ascontiguousarray(wf[:, 4096 + cq * 1024:4096 + (cq + 1) * 1024]),
            "consts": consts, "cn": cn, "w64": w64, "Mtw": M,
        })
    return maps


def run_stage_b(inp, x1):
    nc = build_stage_b()
    res = run_bass_kernel_spmd(nc, stage_b_inputs(inp, x1), core_ids=list(range(8)))
    yT = [np.concatenate([res.results[b * 4 + cq]["yT"] for cq in range(4)], axis=0) for b in range(2)]
    gbc = [res.results[b * 4]["gbc"] for b in range(2)]
    return yT, gbc


def build_stage_c():
    nc = bass.Bass("TRN2", target_bir_lowering=False)

    def din(name, shape, dt=F32):
        return nc.dram_tensor(name, list(shape), dt, kind="ExternalInput").ap()

    yTl = din("yTl", [32, 128, 2048], BF16)
    x1l = din("x1l", [2048, 2048])
    w_out = din("w_out", [4096, 2048])
    gbc_d = din("gbc", [128, 2048])
    lngb = din("lngb", [2, 2048])
    consts = din("consts", [128, 3, 128])
    outp = nc.dram_tensor("out", [2048, 2048], F32, kind="ExternalOutput").ap()

    with ExitStack() as es:
        C = setup_common(nc, es, consts)
        P, ar = C.P, C.ar
        Td = P.dram("Td", [16, 128, 2048], F32)
        Wo = ar.alloc("Wo", [32, 1024], BF16)
        gb2 = ar.alloc("gb2", [2048], F32)
        lnbc = ar.alloc("lnbc", [2, 2048], F32)
        Gys = [ar.alloc(f"Gy{i}", [32, 128], BF16) for i in range(2)]
        tms = [ar.alloc(f"tm{i}", [1024], F32) for i in range(2)]
        xts = [ar.alloc(f"oxt{i}", [2048], F32) for i in range(2)]
        tmp = ar.alloc("otmp", [2048], F32)
        st = ar.alloc("ost", [4, 6], F32)
        mv = ar.alloc("omv", [2], F32)
        rs = ar.alloc("ors", [1], F32)
        P.dma("sp", gb2.t, gbc_d, dst=gb2)
        P.dma("sp", lnbc.t[:, 0, :], lngb[0:1, :].partition_broadcast(128), dst=lnbc)
        P.dma("sp", lnbc.t[:, 1, :], lngb[1:2, :].partition_broadcast(128), dst=lnbc)
        n = 0
        for nh in range(2):
            for kh in range(2):
                for j in range(4):
                    c0 = nh * 1024 + j * 256
                    prep_panel(C, w_out[kh * 2048:(kh + 1) * 2048, :], 16, c0, 256, None, Wo,
                               Wo.t[:, kh * 16:(kh + 1) * 16, j * 256:(j + 1) * 256])
            for t in range(16):
                Gy = Gys[n % 2]
                tm = tms[n % 2]
                n += 1
                P.dma("sp", Gy.t, yTl[:, :, t * 128:(t + 1) * 128].rearrange("c p t -> p c t"), dst=Gy)
                for nb in range(2):
                    pb = nextps(C)
                    ns = slice(nb * 512, (nb + 1) * 512)
                    gs = slice(nh * 1024 + nb * 512, nh * 1024 + (nb + 1) * 512)
                    for kc in range(32):
                        P.mm(pb.t[:, :], Gy.t[:, kc, :], Wo.t[:, kc, ns], kc == 0, kc == 31, [Gy, Wo], pb)
                    P.op("dve", lambda e, pb=pb, ns=ns, gs=gs, tm=tm: e.tensor_tensor(tm.t[:, ns], pb.t[:, :], gb2.t[:, gs], ALU.mult),
                         reads=[pb, gb2], writes=[tm], acc=True)
                P.dma("sp", Td.t[t, :, nh * 1024:(nh + 1) * 1024], tm.t, src=tm, dst=Td)
        ob = P.view("out_b", outp)
        for t in range(16):
            xt = xts[t % 2]
            tm = tmp
            P.dma("sp", xt.t, x1l[t * 128:(t + 1) * 128, :], dst=xt)
            P.dma("sp", tm.t, Td.t[t], src=Td, dst=tm)
            P.op("dve", lambda e, xt=xt, tm=tm: e.scalar_tensor_tensor(tm.t, xt.t, ALPHA, tm.t, ALU.mult, ALU.add), reads=[xt, tm], writes=[tm])
            for c in range(4):
                P.op("dve", lambda e, c=c, tm=tm: e.bn_stats(st.t[:, c, :], tm.t[:, c * 512:(c + 1) * 512]), reads=[tm], writes=[st], acc=True)
            P.op("dve", lambda e: e.bn_aggr(mv.t, st.t), reads=[st], writes=[mv])
            P.op("act", lambda e: e.activation(rs.t, mv.t[:, 1:2], AF.Sqrt, bias=C.eps.t[:, 0:1], scale=1.0), reads=[mv, C.eps], writes=[rs])
            P.op("dve", lambda e: e.reciprocal(rs.t, rs.t), reads=[rs], writes=[rs])
            P.op("dve", lambda e, tm=tm: e.tensor_scalar(tm.t, tm.t, mv.t[:, 0:1], rs.t[:, 0:1], ALU.subtract, ALU.mult), reads=[tm, mv, rs], writes=[tm])
            P.op("pool", lambda e, tm=tm: e.tensor_tensor(tm.t, tm.t, lnbc.t[:, 0, :], ALU.mult), reads=[tm, lnbc], writes=[tm])
            P.op("dve", lambda e, tm=tm, xt=xt: e.tensor_tensor(xt.t, tm.t, lnbc.t[:, 1, :], ALU.add), reads=[tm, lnbc], writes=[xt])
            P.dma("sp", outp[t * 128:(t + 1) * 128, :], xt.t, src=xt, dbuf=ob)
        P.fence("sp", [ob])
        P.emit()
    return nc


def run_stage_c(inp, x1, yT, gbc):
    nc = build_stage_c()
    consts = _consts()
    maps = []
    for core in range(8):
        b, qr = core // 4, core % 4
        t0 = qr * 2048
        maps.append({
            "yTl": np.ascontiguousarray(yT[b][:, t0:t0 + 2048]).reshape(32, 128, 2048),
            "x1l": np.ascontiguousarray(x1[b, t0:t0 + 2048]),
            "w_out": inp["w_out_fourier"][0], "gbc": gbc[b],
            "lngb": np.stack([inp["ln_g"][1], inp["ln_b"][1]]), "consts": consts,
        })
    res = run_bass_kernel_spmd(nc, maps, core_ids=list(range(8)))
    out = np.zeros((2, 8192, 2048), np.float32)
    for core in range(8):
        b, qr = core // 4, core % 4
        out[b, qr * 2048:(qr + 1) * 2048] = res.results[core]["out"]
    return out


def kernel_unfused(**inp):
    inp = {k: np.asarray(v) for k, v in inp.items()}
    x1 = run_stage_a(inp)
    yT, gbc = run_stage_b(inp, x1)
    return run_stage_c(inp, x1, yT, gbc)


def kernel(**inp):
    return kernel_fused(**inp)
```

```python
import numpy as np
from contextlib import ExitStack
import concourse.bass as bass
import concourse.mybir as mybir
from concourse.bass_utils import run_bass_kernel_spmd

F32 = mybir.dt.float32
BF16 = mybir.dt.bfloat16
ALU = mybir.AluOpType
AF = mybir.ActivationFunctionType
AX = mybir.AxisListType

SEM_LIM = 30000
PROFILE_SCOPES = False


class Buf:
    def __init__(self, name, t=None):
        self.name = name
        self.t = t
        self.w = {}
        self.r = {}
        self.dcnt = 0
        self.dsem = None
        self.is_dram = False

    def __getitem__(self, k):
        return self.t[k]


class Prog:
    ENGS = ("pe", "act", "dve", "pool", "sp")

    def __init__(self, nc, es):
        self.nc = nc
        self.es = es
        self.ops = {e: [] for e in self.ENGS}
        self.seen = {e: {} for e in self.ENGS}
        self.signal = {e: set() for e in self.ENGS}
        self.dbufs = []
        self.nbuf = 0
        self.phase = None

    def sb(self, name, shape, dt):
        t = self.es.enter_context(self.nc.sbuf_tensor(name, list(shape), dt))
        return Buf(name, t)

    def ps(self, name):
        t = self.es.enter_context(self.nc.psum_tensor(name, [128, 512], F32))
        return Buf(name, t)

    def dram(self, name, shape, dt, kind="Internal"):
        t = self.nc.dram_tensor(name, list(shape), dt, kind=kind)
        b = Buf(name, t.ap())
        b.is_dram = True
        return b

    def view(self, name, ap):
        b = Buf(name, ap)
        b.is_dram = True
        return b

    def op(self, eng, fn, reads=(), writes=(), dma_dst=None, acc=False, dma_inc=16):
        if dma_dst is not None:
            own = ("D", id(dma_dst))
        else:
            own = ("E", eng)
        need = {}

        def merge(d, skip_own=False):
            for k, v in d.items():
                if skip_own and k == own:
                    continue
                if need.get(k, -1) < v:
                    need[k] = v

        for b in reads:
            merge(b.w)
        for b in writes:
            merge(b.w, skip_own=acc)
            merge(b.r)
        waits = []
        seen = self.seen[eng]
        for k, v in need.items():
            if k == ("E", "pe") and eng == "pe":
                continue
            if seen.get(k, -1) >= v:
                continue
            seen[k] = v
            waits.append((k, v))
            if k[0] == "E":
                self.signal[k[1]].add(v)
        idx = len(self.ops[eng])
        if dma_dst is not None:
            if dma_dst.dsem is None:
                dma_dst.dsem = True
                self.dbufs.append(dma_dst)
            dma_dst.dcnt += dma_inc
            assert dma_dst.dcnt < 2 * SEM_LIM, dma_dst.name
            tok = (own, dma_dst.dcnt)
        else:
            tok = (own, idx)
        self.ops[eng].append((fn, waits, dma_dst, idx, dma_inc, self.phase))
        for b in reads:
            if b.r.get(tok[0], -1) < tok[1]:
                b.r[tok[0]] = tok[1]
        for b in writes:
            if acc:
                b.w[tok[0]] = tok[1]
            else:
                b.w = {tok[0]: tok[1]}
            b.r = {}
        return tok

    def fence(self, eng, bufs):
        self.op(eng, None, reads=bufs)

    def mm(self, out, lhsT, rhs, start, stop, reads, w, **kw):
        self.op("pe", lambda e: e.matmul(out, lhsT, rhs, start=start, stop=stop, **kw),
                reads=reads, writes=[w], acc=True)

    def dma(self, eng, out, in_, src=None, dst=None, dbuf=None, **kw):
        if dst is None:
            dst = dbuf
        reads = [src] if src is not None else []
        writes = [dst] if dst is not None else []
        if src is not None and not src.is_dram and (dst is None or dst.is_dram):
            owner = src
        else:
            owner = dst
        self.op(eng, lambda e: e.dma_start(out=out, in_=in_, **kw), reads=reads, writes=writes,
                dma_dst=owner, acc=True)

    def emit(self):
        nc = self.nc
        es = self.es
        rank = {}
        esems = {}
        for e in self.ENGS:
            sig = sorted(self.signal[e])
            rank[e] = {idx: i + 1 for i, idx in enumerate(sig)}
            n = (len(sig) + SEM_LIM - 1) // SEM_LIM
            esems[e] = [es.enter_context(nc.semaphore(f"s_{e}{i}")) for i in range(max(n, 1))]
        for i, b in enumerate(self.dbufs):
            b.dsem = es.enter_context(nc.semaphore(f"d_{i}"))
        dmap = {id(b): b for b in self.dbufs}
        self.nsem = sum(len(v) for v in esems.values()) + len(self.dbufs)

        def sem_val(k, v):
            if k[0] == "E":
                r = rank[k[1]][v] - 1
                return esems[k[1]][r // SEM_LIM], r % SEM_LIM + 1
            return dmap[k[1]].dsem, v

        def run_one(e, name, fn, waits, dma_dst, idx, dma_inc):
            for k, v in waits:
                s, val = sem_val(k, v)
                e.wait_ge(s, val)
            if fn is None:
                return
            ins = fn(e)
            if dma_dst is not None:
                ins.then_inc(dma_dst.dsem, dma_inc)
            elif idx in rank[name]:
                s, _ = sem_val(("E", name), idx)
                ins.then_inc(s, 1)

        def run(e, name):
            ops = self.ops[name]
            i = 0
            while i < len(ops):
                ph = ops[i][5]
                j = i
                while j < len(ops) and ops[j][5] == ph:
                    j += 1
                if PROFILE_SCOPES and ph is not None:
                    with nc.named_scope(ph):
                        for o in ops[i:j]:
                            run_one(e, name, *o[:5])
                else:
                    for o in ops[i:j]:
                        run_one(e, name, *o[:5])
                i = j

        block = es.enter_context(nc.Block())

        @block.sync
        def _(e):
            run(e, "sp")

        @block.tensor
        def _(e):
            run(e, "pe")

        @block.scalar
        def _(e):
            run(e, "act")

        @block.vector
        def _(e):
            run(e, "dve")

        @block.gpsimd
        def _(e):
            run(e, "pool")


class Arena:
    def __init__(self, P, nbytes):
        self.P = P
        self.n = nbytes // 2
        self.t = P.es.enter_context(P.nc.sbuf_tensor("arena", [128, self.n], BF16))
        self.off = 0
        self.cur = []
        self.prev = {}

    def reset(self):
        for b in self.cur:
            for d in (b.w, b.r):
                for k, v in d.items():
                    if self.prev.get(k, -1) < v:
                        self.prev[k] = v
        self.cur = []
        self.off = 0

    def mark(self):
        return (self.off, len(self.cur))

    def release(self, m):
        for b in self.cur[m[1]:]:
            for d in (b.w, b.r):
                for k, v in d.items():
                    if self.prev.get(k, -1) < v:
                        self.prev[k] = v
        self.cur = self.cur[:m[1]]
        self.off = m[0]

    def alloc(self, name, shape, dt, parts=128):
        esz = 4 if dt == F32 else 2
        n = int(np.prod(shape))
        units = (n * esz + 1) // 2
        units = (units + 15) // 16 * 16
        assert self.off + units <= self.n, (name, self.off * 2, units * 2)
        ap = self.t[0:parts, self.off:self.off + units]
        self.off += units
        if dt == F32:
            ap = ap.bitcast(F32)
        ap = ap[:, 0:n]
        if len(shape) == 2:
            ap = ap.rearrange("p (a b) -> p a b", b=shape[1])
        elif len(shape) == 3:
            ap = ap.rearrange("p (a b c) -> p a b c", b=shape[1], c=shape[2])
        b = Buf(name, ap)
        b.r = dict(self.prev)
        self.cur.append(b)
        return b


ALPHA = (2.0 * 2) ** 0.25
A_SCALE = 192.0 ** -0.5
B_SCALE = 128.0 ** -0.5


class Ctx:
    pass


def setup_common(nc, es, consts_d, arena_kb=170):
    C = Ctx()
    P = Prog(nc, es)
    C.P = P
    C.nc = nc
    C.cst = P.sb("cst", [128, 3, 128], F32)
    P.dma("sp", C.cst.t[:], consts_d, dst=C.cst)
    C.idf = C.cst.t[:, 0, :]
    C.onesf = C.cst.t[:, 1, :]
    C.perm = C.cst.t[:, 2, :]
    C.cb = P.sb("cb", [128, 2, 128], BF16)
    P.op("dve", lambda e: e.tensor_copy(C.cb.t[:], C.cst.t[:, 0:2, :]), reads=[C.cst], writes=[C.cb])
    C.idb = C.cb.t[:, 0, :]
    C.onesb = C.cb.t[:, 1, :]
    C.eps = P.sb("eps", [128, 1], F32)
    P.op("pool", lambda e: e.memset(C.eps.t[:], 1e-6), writes=[C.eps])
    C.modT = P.sb("modT", [128, 32, 2], F32)
    C.stg = [P.sb(f"stg{i}", [128, 16 * 256], F32) for i in range(2)]
    C.nstg = 0
    C.psb = [P.ps(f"ps{i}") for i in range(8)]
    C.nps = 0
    C.ar = Arena(P, arena_kb * 1024)
    return C


def nextps(C, lo=0, hi=8):
    b = C.psb[lo + C.nps % (hi - lo)]
    C.nps += 1
    return b


def emit_modulation(C, cvec_d, ada_w_d, ada_b_d, gbc, want_gate_row=0):
    P, ar = C.P, C.ar
    cv = ar.alloc("cv", [32], F32)
    sv = ar.alloc("sv", [16, 2], F32)
    adb = ar.alloc("adb", [6144], F32, parts=2)
    m2 = ar.alloc("m2", [6144], F32, parts=2)
    P.dma("sp", cv.t, cvec_d, dst=cv)
    P.dma("sp", adb.t, ada_b_d, dst=adb)
    P.op("act", lambda e: e.activation(sv.t.rearrange("p a b -> p (a b)"), cv.t, AF.Silu), reads=[cv], writes=[sv])
    awv = ada_w_d.rearrange("(kc p) n -> p kc n", p=128)
    for nb in range(24):
        stg = C.stg[C.nstg % 2]
        C.nstg += 1
        sv3 = stg.t.rearrange("p (kc n) -> p kc n", n=256)
        P.dma("sp", sv3, awv[:, :, nb * 256:(nb + 1) * 256], dst=stg)
        pb = nextps(C)
        for kc in range(16):
            P.mm(pb.t[0:2, 0:256], sv.t[:, kc, :], sv3[:, kc, :], kc == 0, kc == 15, [sv, stg], pb)
        sl = slice(nb * 256, (nb + 1) * 256)
        P.op("dve", lambda e, pb=pb, sl=sl: e.tensor_tensor(m2.t[:, sl], pb.t[0:2, 0:256], adb.t[:, sl], ALU.add),
             reads=[pb, adb], writes=[m2])
    pT = nextps(C)
    for j in range(32):
        P.mm(pT.t[:, 2 * j:2 * j + 2], m2.t[0:2, j * 128:(j + 1) * 128], C.idf[0:2, 0:2], True, True, [m2, C.cst], pT)
    P.op("dve", lambda e: e.tensor_copy(C.modT.t[:, 0:16, :], pT.t[:, 0:32].rearrange("p (a b) -> p a b", b=2)),
         reads=[pT], writes=[C.modT])
    P.op("dve", lambda e: e.tensor_scalar_add(C.modT.t[:, 16:32, :], pT.t[:, 32:64].rearrange("p (a b) -> p a b", b=2), 1.0),
         reads=[pT], writes=[C.modT], acc=True)
    r = want_gate_row
    for q in range(4):
        pb = nextps(C)
        P.mm(pb.t[:, :], C.onesf[r:r + 1, :], m2.t[r:r + 1, 4096 + q * 512:4096 + (q + 1) * 512], True, True, [m2, C.cst], pb)
        P.op("act", lambda e, pb=pb, q=q: e.activation(gbc.t[:, q * 512:(q + 1) * 512], pb.t[:, :], AF.Copy),
             reads=[pb], writes=[gbc], acc=True)


def prep_panel(C, Wd, KC, c0, ncols, r, Wbuf, Wap, bbuf=None, bap=None):
    ld = prep_load(C, Wd, KC, c0, ncols)
    prep_compute(C, ld, KC, ncols, r, Wbuf, Wap, bbuf, bap)


def prep_load(C, Wd, KC, c0, ncols):
    stg = C.stg[C.nstg % 2]
    C.nstg += 1
    s3 = stg.t[:, 0:KC * ncols].rearrange("p (kc n) -> p kc n", n=ncols)
    C.P.dma("sp", s3, Wd.rearrange("(kc p) n -> p kc n", p=128)[:, :, c0:c0 + ncols], dst=stg)
    return stg, s3


def prep_compute(C, ld, KC, ncols, r, Wbuf, Wap, bbuf=None, bap=None):
    P = C.P
    stg, s3 = ld
    if r is None:
        P.op("dve", lambda e: e.tensor_copy(Wap, s3), reads=[stg], writes=[Wbuf], acc=True)
        return
    for kc in range(KC):
        if kc % 2 == 0:
            P.op("dve", lambda e, kc=kc: e.tensor_scalar_mul(Wap[:, kc, :], s3[:, kc, :], C.modT.t[:, 16 + kc, r:r + 1]),
                 reads=[stg, C.modT], writes=[Wbuf], acc=True)
        else:
            P.op("act", lambda e, kc=kc: e.activation(Wap[:, kc, :], s3[:, kc, :], AF.Copy, scale=C.modT.t[:, 16 + kc, r:r + 1]),
                 reads=[stg, C.modT], writes=[Wbuf], acc=True)
    pb = nextps(C)
    nch = (ncols + 127) // 128
    for j in range(nch):
        M = min(128, ncols - j * 128)
        for kc in range(KC):
            P.mm(pb.t[0:M, j:j + 1], s3[:, kc, j * 128:j * 128 + M], C.modT.t[:, kc, r:r + 1], kc == 0, kc == KC - 1,
                 [stg, C.modT], pb)
    nfull = ncols // 128
    if nfull:
        P.op("dve", lambda e: e.tensor_copy(bap[:, 0:nfull], pb.t[:, 0:nfull]), reads=[pb], writes=[bbuf], acc=True)
    if nch > nfull:
        Ml = ncols - nfull * 128
        P.op("dve", lambda e: e.tensor_copy(bap[0:Ml, nfull:nch], pb.t[0:Ml, nfull:nch]), reads=[pb], writes=[bbuf], acc=True)


def ln_tile(C, W, xrows_d, t):
    P = C.P
    xt = W.xt[t % 2]
    st = W.st[t % 2]
    mv = W.mv[t % 2]
    rs = W.rs[t % 2]
    xh = W.xh[t % 2]
    if getattr(W, "ldc", 0) <= t:
        ln_load(C, W, xrows_d)
    for c in range(4):
        P.op("dve", lambda e, c=c: e.bn_stats(st.t[:, c, :], xt.t[:, c * 512:(c + 1) * 512]), reads=[xt], writes=[st], acc=True)
    P.op("dve", lambda e: e.bn_aggr(mv.t, st.t), reads=[st], writes=[mv])
    P.op("act", lambda e: e.activation(rs.t, mv.t[:, 1:2], AF.Sqrt, bias=C.eps.t[:, 0:1], scale=1.0), reads=[mv, C.eps], writes=[rs])
    P.op("dve", lambda e: e.reciprocal(rs.t, rs.t), reads=[rs], writes=[rs])
    P.op("dve", lambda e: e.tensor_scalar(xh.t, xt.t, mv.t[:, 0:1], rs.t[:, 0:1], ALU.subtract, ALU.mult),
         reads=[xt, mv, rs], writes=[xh])
    return xh


def ln_load(C, W, xrows_d):
    W.ldc = getattr(W, "ldc", 0)
    xt = W.xt[W.ldc % 2]
    W.ldc += 1
    C.P.dma("sp", xt.t, xrows_d, src=getattr(W, "xsrc", None), dst=xt)


def ln_alloc(C, W):
    ar = C.ar
    W.xt = [ar.alloc(f"xt{i}", [2048], F32) for i in range(2)]
    W.st = [ar.alloc(f"st{i}", [4, 6], F32) for i in range(2)]
    W.mv = [ar.alloc(f"mv{i}", [2], F32) for i in range(2)]
    W.rs = [ar.alloc(f"rs{i}", [1], F32) for i in range(2)]
    W.xh = [ar.alloc(f"xh{i}", [2048], BF16) for i in range(2)]


def ln_transpose(C, W, x_d, row0, ntiles, hT, col0=0):
    P = C.P
    for t in range(ntiles):
        xh = ln_tile(C, W, x_d[row0 + t * 128: row0 + (t + 1) * 128, :], W.lnc)
        W.lnc += 1
        for half in range(2):
            pb = nextps(C)
            pv = pb.t[:].bitcast(BF16)
            for j in range(8):
                kc = half * 8 + j
                P.op("pe", lambda e, pv=pv, j=j, kc=kc, xh=xh: e.transpose(pv[:, j * 128:(j + 1) * 128], xh.t[:, kc * 128:(kc + 1) * 128], C.idb),
                     reads=[xh, C.cb], writes=[pb], acc=True)
            dst = hT.t[:, half * 8:(half + 1) * 8, col0 + t * 128: col0 + (t + 1) * 128]
            src = pv.rearrange("p (a b) -> p a b", b=128)
            if half == 0:
                P.op("act", lambda e, dst=dst, src=src: e.activation(dst, src, AF.Copy), reads=[pb], writes=[hT], acc=True)
            else:
                P.op("dve", lambda e, dst=dst, src=src: e.tensor_copy(dst, src), reads=[pb], writes=[hT], acc=True)


def proj(C, pb, M, nt, Wbuf, Wap, c0, KC, hT, tok0):
    for kc in range(KC):
        C.P.mm(pb.t[0:M, 0:nt], Wap[:, kc, c0:c0 + M], hT.t[:, kc, tok0:tok0 + nt], kc == 0, kc == KC - 1, [Wbuf, hT], pb)


def rstd_from(C, W, zs, nfeat, nt):
    P = C.P
    pss = nextps(C)
    for i, (zb, zap) in enumerate(zs):
        sq = W.sq[W.nsq % 2]
        W.nsq += 1
        P.op("act", lambda e, sq=sq, zap=zap: e.activation(sq.t[:, 0:nt], zap, AF.Square), reads=[zb], writes=[sq])
        P.mm(pss.t[:, 0:nt], C.onesf, sq.t[:, 0:nt], i == 0, i == len(zs) - 1, [sq, C.cst], pss)
    rstd = W.rstd[W.nrs % 2]
    W.nrs += 1
    P.op("act", lambda e: e.activation(rstd.t[:, 0:nt], pss.t[:, 0:nt], AF.Sqrt, bias=C.eps.t[:, 0:1], scale=1.0 / nfeat),
         reads=[pss, C.eps], writes=[rstd])
    P.op("dve", lambda e: e.reciprocal(rstd.t[:, 0:nt], rstd.t[:, 0:nt]), reads=[rstd], writes=[rstd])
    return rstd


def rope(C, W, dst_buf, dst_ap, src_buf, src_ap, tab, M, nt):
    P = C.P
    pw = nextps(C)
    P.mm(pw.t[0:M, 0:nt], C.perm[0:M, 0:M], src_ap, True, True, [src_buf, C.cst], pw)
    t1 = W.t1[W.nt1 % 2]
    t2 = W.t2[W.nt1 % 2]
    W.nt1 += 1
    P.op("pool", lambda e: e.tensor_tensor(t1.t[0:M, 0:nt], src_ap, tab.t[0:M, 0, 0:nt], ALU.mult), reads=[src_buf, tab], writes=[t1])
    P.op("dve", lambda e: e.tensor_tensor(t2.t[0:M, 0:nt], pw.t[0:M, 0:nt], tab.t[0:M, 1, 0:nt], ALU.mult), reads=[pw, tab], writes=[t2])
    P.op("pool", lambda e: e.tensor_tensor(dst_ap, t1.t[0:M, 0:nt], t2.t[0:M, 0:nt], ALU.add), reads=[t1, t2], writes=[dst_buf])


def work_alloc(C, W):
    ar = C.ar
    W.sq = [ar.alloc(f"sq{i}", [512], F32) for i in range(2)]
    W.rstd = [ar.alloc(f"rstd{i}", [512], F32) for i in range(2)]
    W.t1 = [ar.alloc(f"t1{i}", [512], F32) for i in range(2)]
    W.t2 = [ar.alloc(f"t2{i}", [512], F32) for i in range(2)]
    W.nsq = W.nrs = W.nt1 = 0


NKV = 8448
NKT = 66


def build_stage_a(fused=False):
    nc = bass.Bass("TRN2", target_bir_lowering=False)

    def din(name, shape, dt=F32):
        return nc.dram_tensor(name, list(shape), dt, kind="ExternalInput").ap()

    xkv = din("xkv", [NKV, 2048])
    xq = din("xq", [2048, 2048])
    cvec = din("cvec", [128, 32])
    ada_w = din("ada_w", [2048, 6144])
    ada_b = din("ada_b", [2, 6144])
    w_in = din("w_in", [2048, 4928])
    wq_b = din("wq_b", [768, 1536])
    wkv_b = din("wkv_b", [512, 2048])
    w_out = din("w_out", [2048, 2048])
    gains = din("gains", [128, 12])
    lngb = din("lngb", [2, 2048])
    consts = din("consts", [128, 3, 128])
    ropeB_kv = din("ropeB_kv", [128, 2, NKV])
    ropeA_kv = din("ropeA_kv", [64, 2, NKV])
    ropeB_q = din("ropeB_q", [128, 2, 2048])
    ropeA_q = din("ropeA_q", [64, 2, 2048])
    x1 = None if fused else nc.dram_tensor("x1", [2048, 2048], F32, kind="ExternalOutput").ap()

    with ExitStack() as es:
        C = setup_common(nc, es, consts)
        P, ar = C.P, C.ar
        gn = P.sb("gn", [128, 12], F32)
        P.dma("sp", gn.t[:], gains, dst=gn)
        KaT = P.dram("KaT", [8, 128, NKV], BF16)
        KpeT = P.dram("KpeT", [64, NKV], BF16)
        Va = P.dram("Va", [NKT, 128, 1024], BF16)
        KbT = P.dram("KbT", [2, 128, NKV], BF16)
        Vb = P.dram("Vb", [NKT, 128, 256], BF16)
        QaT = P.dram("QaT", [8, 128, 2048], BF16)
        QpeT = P.dram("QpeT", [8, 64, 2048], BF16)
        QbT = P.dram("QbT", [8, 128, 2048], BF16)
        Gd = P.dram("Gd", [16, 128, 2048], BF16)
        Yd = P.dram("Yd", [16, 128, 2048], BF16)

        P.phase = "MOD"
        gbc = ar.alloc("gbc_tmp", [2048], F32)
        emit_modulation(C, cvec, ada_w, ada_b[:, :], gbc, 0)
        Gbc_d = P.dram("Gbc_d", [128, 2048], F32)
        P.dma("sp", Gbc_d.t, gbc.t, src=gbc, dst=Gbc_d)

        def kv_phase(r, blocks):
            ar.reset()
            W = Ctx()
            Wkv = ar.alloc("Wkv", [16, 1088], BF16)
            Bkv = ar.alloc("Bkv", [9], F32)
            wkvb = ar.alloc("wkvb", [4, 2048], BF16)
            ln_alloc(C, W)
            W.lnc = 0
            work_alloc(C, W)
            hTs = [ar.alloc(f"hT{i}", [16, 512], BF16) for i in range(1)]
            zc = ar.alloc("zc", [4, 512], F32)
            ckvn = ar.alloc("ckvn", [4, 512], BF16)
            zk = [ar.alloc(f"zk{i}", [512], F32) for i in range(2)]
            kn = [ar.alloc(f"kn{i}", [512], F32) for i in range(2)]
            tabB = [ar.alloc(f"tabB{i}", [2, 512], F32) for i in range(1)]
            tabA = [ar.alloc(f"tabA{i}", [2, 512], F32) for i in range(1)]
            ko = [ar.alloc(f"ko{i}", [512], BF16) for i in range(4)]
            vT = [ar.alloc(f"vT{i}", [512], BF16) for i in range(2)]
            vo = [ar.alloc(f"vo{i}", [1024], BF16) for i in range(2)]
            vbo = [ar.alloc(f"vbo{i}", [256], BF16) for i in range(2)]
            panels = [(768, 256, 0, 0), (1024, 256, 256, 2), (1280, 64, 512, 4), (2368, 256, 576, 5), (2624, 256, 832, 7)]
            for (c0, ncols, l0, b0) in panels:
                nch = (ncols + 127) // 128
                prep_panel(C, w_in, 16, c0, ncols, r, Wkv, Wkv.t[:, :, l0:l0 + ncols], Bkv, Bkv.t[:, b0:b0 + nch])
            for j in range(8):
                prep_panel(C, wkv_b, 4, j * 256, 256, None, wkvb, wkvb.t[:, :, j * 256:(j + 1) * 256])
            nko = 0
            for bi, (row0, nt) in enumerate(blocks):
                ntl = nt // 128
                hT = hTs[0]
                ln_transpose(C, W, xkv, row0, ntl, hT)
                tb = tabB[0]
                ta = tabA[0]
                P.dma("sp", tb.t[:, :, 0:nt], ropeB_kv[:, :, row0:row0 + nt], dst=tb)
                P.dma("sp", ta.t[0:64, :, 0:nt], ropeA_kv[:, :, row0:row0 + nt], dst=ta)
                if bi + 1 < len(blocks):
                    r1, n1 = blocks[bi + 1]
                    for tt in range(2):
                        ln_load(C, W, xkv[r1 + tt * 128:r1 + (tt + 1) * 128, :])
                for j in range(4):
                    pb = nextps(C)
                    proj(C, pb, 128, nt, Wkv, Wkv.t, j * 128, 16, hT, 0)
                    P.op("act", lambda e, pb=pb, j=j: e.activation(zc.t[:, j, 0:nt], pb.t[:, 0:nt], AF.Identity, bias=Bkv.t[:, j:j + 1]),
                         reads=[pb, Bkv], writes=[zc], acc=True)
                rstd = rstd_from(C, W, [(zc, zc.t[:, j, 0:nt]) for j in range(4)], 512, nt)
                for j in range(4):
                    P.op("dve", lambda e, j=j, rstd=rstd: e.scalar_tensor_tensor(ckvn.t[:, j, 0:nt], zc.t[:, j, 0:nt], gn.t[:, 6 + j:7 + j], rstd.t[:, 0:nt], ALU.mult, ALU.mult),
                         reads=[zc, gn, rstd], writes=[ckvn], acc=True)
                pb = nextps(C)
                proj(C, pb, 64, nt, Wkv, Wkv.t, 512, 16, hT, 0)
                z = zk[0]
                P.op("act", lambda e, pb=pb, z=z: e.activation(z.t[0:64, 0:nt], pb.t[0:64, 0:nt], AF.Identity, bias=Bkv.t[0:64, 4:5]),
                     reads=[pb, Bkv], writes=[z])
                o = ko[nko % 4]; nko += 1
                rope(C, W, o, o.t[0:64, 0:nt], z, z.t[0:64, 0:nt], ta, 64, nt)
                P.dma("sp", KpeT.t[:, row0:row0 + nt], o.t[0:64, 0:nt], src=o, dst=KpeT)
                for hh in range(2):
                    pb = nextps(C)
                    proj(C, pb, 128, nt, Wkv, Wkv.t, 576 + hh * 128, 16, hT, 0)
                    z = zk[1 - hh % 2] if False else zk[hh % 2]
                    P.op("act", lambda e, pb=pb, z=z, hh=hh: e.activation(z.t[:, 0:nt], pb.t[:, 0:nt], AF.Identity, bias=Bkv.t[:, 5 + hh:6 + hh]),
                         reads=[pb, Bkv], writes=[z])
                    rstd = rstd_from(C, W, [(z, z.t[:, 0:nt])], 128, nt)
                    k_ = kn[hh % 2]
                    P.op("dve", lambda e, z=z, k_=k_, rstd=rstd: e.scalar_tensor_tensor(k_.t[:, 0:nt], z.t[:, 0:nt], gn.t[:, 11:12], rstd.t[:, 0:nt], ALU.mult, ALU.mult),
                         reads=[z, gn, rstd], writes=[k_])
                    o = ko[nko % 4]; nko += 1
                    rope(C, W, o, o.t[:, 0:nt], k_, k_.t[:, 0:nt], tb, 128, nt)
                    P.dma("sp", KbT.t[hh, :, row0:row0 + nt], o.t[:, 0:nt], src=o, dst=KbT)
                for hh in range(2):
                    pb = nextps(C)
                    proj(C, pb, 128, nt, Wkv, Wkv.t, 832 + hh * 128, 16, hT, 0)
                    v_ = vT[hh % 2]
                    P.op("act", lambda e, pb=pb, v_=v_, hh=hh: e.activation(v_.t[:, 0:nt], pb.t[:, 0:nt], AF.Identity, bias=Bkv.t[:, 7 + hh:8 + hh]),
                         reads=[pb, Bkv], writes=[v_])
                    W.vbT = getattr(W, "vbT", {})
                    W.vbT[hh] = v_
                for t in range(ntl):
                    pb = nextps(C)
                    pv = pb.t[:].bitcast(BF16)
                    for hh in range(2):
                        v_ = W.vbT[hh]
                        P.op("pe", lambda e, pv=pv, hh=hh, v_=v_, t=t: e.transpose(pv[:, hh * 128:(hh + 1) * 128], v_.t[:, t * 128:(t + 1) * 128], C.idb),
                             reads=[v_, C.cb], writes=[pb], acc=True)
                    vb_ = vbo[t % 2]
                    P.op("dve", lambda e, pv=pv, vb_=vb_: e.tensor_copy(vb_.t, pv[:, 0:256]), reads=[pb], writes=[vb_])
                    P.dma("sp", Vb.t[row0 // 128 + t, :, :], vb_.t, src=vb_, dst=Vb)
                for h in range(8):
                    pb = nextps(C)
                    for j in range(4):
                        P.mm(pb.t[:, 0:nt], wkvb.t[:, j, h * 256:h * 256 + 128], ckvn.t[:, j, 0:nt], j == 0, j == 3, [wkvb, ckvn], pb)
                    o = ko[nko % 4]; nko += 1
                    if h % 2 == 0:
                        P.op("act", lambda e, pb=pb, o=o: e.activation(o.t[:, 0:nt], pb.t[:, 0:nt], AF.Copy), reads=[pb], writes=[o])
                    else:
                        P.op("dve", lambda e, pb=pb, o=o: e.tensor_copy(o.t[:, 0:nt], pb.t[:, 0:nt]), reads=[pb], writes=[o])
                    P.dma("sp", KaT.t[h, :, row0:row0 + nt], o.t[:, 0:nt], src=o, dst=KaT)
                wv = wkvb.t.rearrange("p k (h two d) -> p k h two d", two=2, d=128)
                for t in range(ntl):
                    v2 = vo[t % 2]
                    for half in range(2):
                        pb = nextps(C)
                        for j in range(4):
                            P.mm(pb.t[:, :].rearrange("p (h d) -> p h d", d=128), ckvn.t[:, j, t * 128:(t + 1) * 128],
                                 wv[:, j, half * 4:(half + 1) * 4, 1, :], j == 0, j == 3, [wkvb, ckvn], pb)
                        if half == 0:
                            P.op("act", lambda e, pb=pb, v2=v2: e.activation(v2.t[:, 0:512], pb.t[:, :], AF.Copy), reads=[pb], writes=[v2], acc=True)
                        else:
                            P.op("dve", lambda e, pb=pb, v2=v2: e.tensor_copy(v2.t[:, 512:1024], pb.t[:, :]), reads=[pb], writes=[v2], acc=True)
                    P.dma("sp", Va.t[row0 // 128 + t, :, :], v2.t, src=v2, dst=Va)

        P.phase = "KVCTX"
        kv_phase(1, [(0, 256)])
        P.phase = "KVLAT"
        kv_phase(0, [(256 + i * 512, 512) for i in range(16)])

        P.phase = "Q"
        ar.reset()
        W = Ctx()
        hTq = ar.alloc("hTq", [16, 2048], BF16)
        mk = ar.mark()
        ln_alloc(C, W)
        W.lnc = 0
        ln_transpose(C, W, xq, 0, 16, hTq)
        ar.release(mk)
        mk = ar.mark()
        Wcq = ar.alloc("Wcq", [16, 768], BF16)
        Bcq = ar.alloc("Bcq", [6], F32)
        wqb = ar.alloc("wqb", [6, 1536], BF16)
        work_alloc(C, W)
        zq = ar.alloc("zq", [6, 512], F32)
        cqn = ar.alloc("cqn", [6, 512], BF16)
        zk = [ar.alloc(f"qzk{i}", [512], F32) for i in range(2)]
        tabA = [ar.alloc(f"qtabA{i}", [2, 512], F32) for i in range(1)]
        ko = [ar.alloc(f"qko{i}", [512], BF16) for i in range(4)]
        for j in range(3):
            prep_panel(C, w_in, 16, j * 256, 256, 0, Wcq, Wcq.t[:, :, j * 256:(j + 1) * 256], Bcq, Bcq.t[:, 2 * j:2 * j + 2])
        for j in range(6):
            prep_panel(C, wq_b, 6, j * 256, 256, None, wqb, wqb.t[:, :, j * 256:(j + 1) * 256])
        nko = 0
        for tbi in range(4):
            tok0 = tbi * 512
            ta = tabA[0]
            P.dma("sp", ta.t[0:64, :, :], ropeA_q[:, :, tok0:tok0 + 512], dst=ta)
            for j in range(6):
                pb = nextps(C)
                proj(C, pb, 128, 512, Wcq, Wcq.t, j * 128, 16, hTq, tok0)
                P.op("act", lambda e, pb=pb, j=j: e.activation(zq.t[:, j, :], pb.t[:, :], AF.Identity, bias=Bcq.t[:, j:j + 1]),
                     reads=[pb, Bcq], writes=[zq], acc=True)
            rstd = rstd_from(C, W, [(zq, zq.t[:, j, :]) for j in range(6)], 768, 512)
            for j in range(6):
                P.op("dve", lambda e, j=j, rstd=rstd: e.scalar_tensor_tensor(cqn.t[:, j, :], zq.t[:, j, :], gn.t[:, j:j + 1], rstd.t[:, :], ALU.mult, ALU.mult),
                     reads=[zq, gn, rstd], writes=[cqn], acc=True)
            for h in range(8):
                pb = nextps(C)
                for j in range(6):
                    P.mm(pb.t[:, :], wqb.t[:, j, h * 192:h * 192 + 128], cqn.t[:, j, :], j == 0, j == 5, [wqb, cqn], pb)
                o = ko[nko % 4]; nko += 1
                P.op("act", lambda e, pb=pb, o=o: e.activation(o.t[:, :], pb.t[:, :], AF.Copy), reads=[pb], writes=[o])
                P.dma("sp", QaT.t[h, :, tok0:tok0 + 512], o.t[:, :], src=o, dst=QaT)
                pb = nextps(C)
                for j in range(6):
                    P.mm(pb.t[0:64, :], wqb.t[:, j, h * 192 + 128:h * 192 + 192], cqn.t[:, j, :], j == 0, j == 5, [wqb, cqn], pb)
                z = zk[h % 2]
                P.op("dve", lambda e, pb=pb, z=z: e.tensor_copy(z.t[0:64, :], pb.t[0:64, :]), reads=[pb], writes=[z])
                o = ko[nko % 4]; nko += 1
                rope(C, W, o, o.t[0:64, :], z, z.t[0:64, :], ta, 64, 512)
                P.dma("sp", QpeT.t[h, :, tok0:tok0 + 512], o.t[0:64, :], src=o, dst=QpeT)
        ar.release(mk)
        mk = ar.mark()
        work_alloc(C, W)
        zk = [ar.alloc(f"qzk{i}", [512], F32) for i in range(2)]
        kn = [ar.alloc(f"qkn{i}", [512], F32) for i in range(2)]
        tabB = [ar.alloc(f"qtabB{i}", [2, 512], F32) for i in range(1)]
        ko = [ar.alloc(f"qko{i}", [512], BF16) for i in range(4)]
        Wp = [ar.alloc(f"Wp{i}", [16, 256], BF16) for i in range(2)]
        Bp = [ar.alloc(f"Bp{i}", [2], F32) for i in range(2)]
        def qcol0(k):
            return 1344 + k * 256 if k < 4 else 2880 + (k - 4) * 256

        ld_next = prep_load(C, w_in, 16, qcol0(0), 256)
        for pi in range(12):
            wp = Wp[pi % 2]
            bp = Bp[pi % 2]
            prep_compute(C, ld_next, 16, 256, 0, wp, wp.t, bp, bp.t)
            if pi + 1 < 12:
                ld_next = prep_load(C, w_in, 16, qcol0(pi + 1), 256)
            for tbi in range(4):
                tok0 = tbi * 512
                if pi < 4:
                    tb = tabB[0]
                    P.dma("sp", tb.t[:, :, :], ropeB_q[:, :, tok0:tok0 + 512], dst=tb)
                for cc in range(2):
                    pb = nextps(C)
                    proj(C, pb, 128, 512, wp, wp.t, cc * 128, 16, hTq, tok0)
                    o = ko[nko % 4]; nko += 1
                    if pi < 4:
                        hh = pi * 2 + cc
                        z = zk[cc]
                        P.op("act", lambda e, pb=pb, z=z, bp=bp, cc=cc: e.activation(z.t[:, :], pb.t[:, :], AF.Identity, bias=bp.t[:, cc:cc + 1]),
                             reads=[pb, bp], writes=[z])
                        rstd = rstd_from(C, W, [(z, z.t[:, :])], 128, 512)
                        k_ = kn[cc]
                        P.op("dve", lambda e, z=z, k_=k_, rstd=rstd: e.scalar_tensor_tensor(k_.t[:, :], z.t[:, :], gn.t[:, 10:11], rstd.t[:, :], ALU.mult, ALU.mult),
                             reads=[z, gn, rstd], writes=[k_])
                        rope(C, W, o, o.t[:, :], k_, k_.t[:, :], tb, 128, 512)
                        P.dma("sp", QbT.t[hh, :, tok0:tok0 + 512], o.t[:, :], src=o, dst=QbT)
                    else:
                        ch = (pi - 4) * 2 + cc
                        P.op("act", lambda e, pb=pb, o=o, bp=bp, cc=cc: e.activation(o.t[:, :], pb.t[:, :], AF.Silu, bias=bp.t[:, cc:cc + 1]),
                             reads=[pb, bp], writes=[o])
                        P.dma("sp", Gd.t[ch, :, tok0:tok0 + 512], o.t[:, :], src=o, dst=Gd)

        P.phase = "ATT"
        ar.reset()
        Kt = [ar.alloc(f"Kt{i}", [NKV], BF16) for i in range(2)]
        Vt = [ar.alloc(f"Vt{i}", [NKT, 128], BF16) for i in range(2)]
        Kpe = ar.alloc("Kpe", [NKV], BF16)
        Qt = [ar.alloc(f"Qt{i}", [2048], BF16) for i in range(2)]
        Qp = [ar.alloc(f"Qp{i}", [2048], BF16) for i in range(2)]
        Gt = [ar.alloc(f"Gt{i}", [2048], BF16) for i in range(2)]
        PT = [ar.alloc(f"PT{i}", [512], BF16) for i in range(6)]
        rc = [ar.alloc(f"rc{i}", [512], F32) for i in range(2)]
        yt = [ar.alloc(f"yt{i}", [512], F32) for i in range(2)]
        yo = [ar.alloc(f"yo{i}", [512], BF16) for i in range(2)]
        P.dma("sp", Kpe.t[0:64, :], KpeT.t, src=KpeT, dst=Kpe)
        SPS = C.psb[0:4]
        OACC = C.psb[4:6]
        SACC = C.psb[6:8]
        kvst = {"slot": -1, "kv": None}

        def load_head(hd):
            isA = hd < 8
            if isA or (hd - 8) % 4 == 0:
                kvst["slot"] += 1
                kt_, vt_ = Kt[kvst["slot"] % 2], Vt[kvst["slot"] % 2]
                if isA:
                    P.dma("sp", kt_.t, KaT.t[hd], src=KaT, dst=kt_)
                    P.dma("sp", vt_.t, Va.t[:, :, hd * 128:(hd + 1) * 128].rearrange("t p d -> p t d"), src=Va, dst=vt_)
                else:
                    kvh = (hd - 8) // 4
                    P.dma("sp", kt_.t, KbT.t[kvh], src=KbT, dst=kt_)
                    P.dma("sp", vt_.t, Vb.t[:, :, kvh * 128:(kvh + 1) * 128].rearrange("t p d -> p t d"), src=Vb, dst=vt_)
                kvst["kv"] = (kt_, vt_)
            kt_, vt_ = kvst["kv"]
            qt_ = Qt[hd % 2]
            qp_ = Qp[hd % 2]
            gt_ = Gt[hd % 2]
            if isA:
                P.dma("sp", qt_.t, QaT.t[hd], src=QaT, dst=qt_)
                P.dma("sp", qp_.t[0:64, :], QpeT.t[hd], src=QpeT, dst=qp_)
            else:
                P.dma("sp", qt_.t, QbT.t[hd - 8], src=QbT, dst=qt_)
            P.dma("sp", gt_.t, Gd.t[hd], src=Gd, dst=gt_)
            return kt_, vt_, qt_, qp_, gt_

        u = 0
        nxt = load_head(0)
        for hd in range(16):
            isA = hd < 8
            kt_, vt_, qt_, qp_, gt_ = nxt
            if hd + 1 < 16:
                nxt = load_head(hd + 1)
            scale = A_SCALE if isA else B_SCALE
            for qb in range(4):
                oT = OACC[u % 2]
                sm = SACC[u % 2]
                qs = slice(qb * 512, (qb + 1) * 512)

                def qk(kt):
                    sp_ = SPS[kt % 4]
                    ks = slice(kt * 128, (kt + 1) * 128)
                    P.mm(sp_.t[:, :], kt_.t[:, ks], qt_.t[:, qs], True, not isA, [kt_, qt_], sp_)
                    if isA:
                        P.mm(sp_.t[:, :], Kpe.t[0:64, ks], qp_.t[0:64, qs], False, True, [Kpe, qp_], sp_)

                def rest(kt):
                    sp_ = SPS[kt % 4]
                    pt = PT[kt % 6]
                    P.op("act", lambda e, pt=pt, sp_=sp_, sc=scale: e.activation(pt.t, sp_.t[:, :], AF.Exp, scale=sc), reads=[sp_], writes=[pt])
                    P.mm(oT.t[:, :], vt_.t[:, kt, :], pt.t, kt == 0, kt == NKT - 1, [vt_, pt], oT)
                    P.mm(sm.t[:, :], C.onesb, pt.t, kt == 0, kt == NKT - 1, [pt, C.cb], sm)

                qk(0)
                qk(1)
                for kt in range(NKT):
                    if kt + 2 < NKT:
                        qk(kt + 2)
                    rest(kt)
                r_ = rc[u % 2]
                y_ = yt[u % 2]
                o_ = yo[u % 2]
                P.op("dve", lambda e, r_=r_, sm=sm: e.reciprocal(r_.t, sm.t[:, :]), reads=[sm], writes=[r_])
                P.op("dve", lambda e, r_=r_, y_=y_, oT=oT: e.tensor_tensor(y_.t, oT.t[:, :], r_.t, ALU.mult), reads=[oT, r_], writes=[y_])
                P.op("pool", lambda e, y_=y_, o_=o_, gt_=gt_, qs=qs: e.tensor_tensor(o_.t, y_.t, gt_.t[:, qs], ALU.mult), reads=[y_, gt_], writes=[o_])
                P.dma("sp", Yd.t[hd, :, qs], o_.t, src=o_, dst=Yd)
                u += 1

        P.phase = "OUT"
        ar.reset()
        W = Ctx()
        Wo = ar.alloc("Wo", [16, 2048], BF16)
        Gys = [ar.alloc(f"Gy{i}", [16, 128], BF16) for i in range(2)]
        gb2 = ar.alloc("gb2", [2048], F32)
        lnbc = ar.alloc("lnbc", [2, 2048], F32)
        xts = [ar.alloc(f"oxt{i}", [2048], F32) for i in range(2)]
        tmp = [ar.alloc(f"otmp{i}", [2048], F32) for i in range(1)]
        st = ar.alloc("ost", [4, 6], F32)
        mv = ar.alloc("omv", [2], F32)
        rs = ar.alloc("ors", [1], F32)
        P.dma("sp", gb2.t, Gbc_d.t, src=Gbc_d, dst=gb2)
        P.dma("sp", lnbc.t[:, 0, :], lngb[0:1, :].partition_broadcast(128), dst=lnbc)
        P.dma("sp", lnbc.t[:, 1, :], lngb[1:2, :].partition_broadcast(128), dst=lnbc)
        for j in range(8):
            prep_panel(C, w_out, 16, j * 256, 256, None, Wo, Wo.t[:, :, j * 256:(j + 1) * 256])
        if fused:
            x1b = P.dram("X1d", [2048, 2048], F32)
            x1 = x1b.t
        else:
            x1b = P.view("x1out", x1)
        def out_load(t):
            P.dma("sp", xts[t % 2].t, xq[t * 128:(t + 1) * 128, :], dst=xts[t % 2])
            P.dma("sp", Gys[t % 2].t, Yd.t[:, :, t * 128:(t + 1) * 128].rearrange("c p t -> p c t"), src=Yd, dst=Gys[t % 2])

        out_load(0)
        for t in range(16):
            xt = xts[t % 2]
            tm = tmp[0]
            Gy = Gys[t % 2]
            if t + 1 < 16:
                out_load(t + 1)
            for nb in range(4):
                pb = nextps(C)
                ns = slice(nb * 512, (nb + 1) * 512)
                for kc in range(16):
                    P.mm(pb.t[:, :], Gy.t[:, kc, :], Wo.t[:, kc, ns], kc == 0, kc == 15, [Gy, Wo], pb)
                P.op("dve", lambda e, pb=pb, ns=ns, tm=tm: e.tensor_tensor(tm.t[:, ns], pb.t[:, :], gb2.t[:, ns], ALU.mult),
                     reads=[pb, gb2], writes=[tm], acc=True)
            P.op("dve", lambda e, xt=xt, tm=tm: e.scalar_tensor_tensor(tm.t, xt.t, ALPHA, tm.t, ALU.mult, ALU.add), reads=[xt, tm], writes=[tm])
            for c in range(4):
                P.op("dve", lambda e, c=c, tm=tm: e.bn_stats(st.t[:, c, :], tm.t[:, c * 512:(c + 1) * 512]), reads=[tm], writes=[st], acc=True)
            P.op("dve", lambda e: e.bn_aggr(mv.t, st.t), reads=[st], writes=[mv])
            P.op("act", lambda e: e.activation(rs.t, mv.t[:, 1:2], AF.Sqrt, bias=C.eps.t[:, 0:1], scale=1.0), reads=[mv, C.eps], writes=[rs])
            P.op("dve", lambda e: e.reciprocal(rs.t, rs.t), reads=[rs], writes=[rs])
            P.op("dve", lambda e, tm=tm: e.tensor_scalar(tm.t, tm.t, mv.t[:, 0:1], rs.t[:, 0:1], ALU.subtract, ALU.mult), reads=[tm, mv, rs], writes=[tm])
            P.op("pool", lambda e, tm=tm: e.tensor_tensor(tm.t, tm.t, lnbc.t[:, 0, :], ALU.mult), reads=[tm, lnbc], writes=[tm])
            P.op("dve", lambda e, tm=tm, xt=xt: e.tensor_tensor(xt.t, tm.t, lnbc.t[:, 1, :], ALU.add), reads=[tm, lnbc], writes=[xt])
            P.dma("sp", x1[t * 128:(t + 1) * 128, :], xt.t, src=xt, dbuf=x1b)
        if fused:
            outb = fused_tail(C, nc, din, x1b)
            P.fence("sp", [outb])
        else:
            P.fence("sp", [x1b])
        P.emit()
    return nc


RS_GROUPS = [[0, 1, 2, 3], [4, 5, 6, 7]]


def fused_tail(C, nc, din, x1b):
    P, ar = C.P, C.ar
    cvec1 = din("cvec1", [128, 32])
    ada_w1 = din("ada_w1", [2048, 6144])
    ada_b1 = din("ada_b1", [2, 6144])
    wf = din("w_in_f", [2048, 8192])
    w_out_f = din("w_out_f", [4096, 2048])
    lngb1 = din("lngb1", [2, 2048])
    sel_d = din("sel", [128, 4])
    cn_d = din("cn", [128, 2, 2, 256], BF16)
    w64_d = din("w64", [128, 128], BF16)
    M_d = din("Mtw", [128, 64, 2, 128], BF16)
    outp = nc.dram_tensor("out", [2048, 2048], F32, kind="ExternalOutput").ap()
    U_in = [P.dram(f"U_in{q}", [4 * 256, 8192], BF16) for q in range(4)]
    U_out = [P.dram(f"U_out{q}", [256, 8192], BF16) for q in range(4)]
    F_in = [P.dram(f"F_in{g}", [4 * 1024, 2048], BF16) for g in range(4)]
    F_out = [P.dram(f"F_out{g}", [1024, 2048], BF16) for g in range(4)]
    Gl = P.dram("Gl", [32, 128, 2048], BF16)
    Gbc1 = P.dram("Gbc1", [128, 2048], F32)
    Td = P.dram("Td", [16, 128, 2048], F32)
    sel = P.sb("sel_sb", [128, 4], F32)
    P.dma("sp", sel.t[:], sel_d, dst=sel)

    P.phase = "B1MOD"
    ar.reset()
    gbc = ar.alloc("gbc1_tmp", [2048], F32)
    emit_modulation(C, cvec1, ada_w1, ada_b1[:, :], gbc, 0)
    P.dma("sp", Gbc1.t, gbc.t, src=gbc, dst=Gbc1)
    ar.reset()
    P.phase = "B1"
    W = Ctx()
    W.xsrc = x1b
    hT1 = ar.alloc("hT1", [16, 2048], BF16)
    mk = ar.mark()
    ln_alloc(C, W)
    W.lnc = 0
    ln_transpose(C, W, x1b.t, 0, 16, hT1)
    ar.release(mk)
    Wp = [ar.alloc(f"fWp{i}", [16, 256], BF16) for i in range(2)]
    Bp = [ar.alloc(f"fBp{i}", [2], F32) for i in range(2)]
    uo = [ar.alloc(f"fuo{i}", [512], BF16) for i in range(2)]
    us = [ar.alloc(f"fus{i}", [4, 512], BF16) for i in range(2)]
    go = [ar.alloc(f"fgo{i}", [512], BF16) for i in range(2)]
    n = 0
    order = [(True, d * 4 + q) for q in range(4) for d in range(4)] + [(False, i) for i in range(16)]
    def col0(k):
        return order[k][1] * 256 if order[k][0] else 4096 + order[k][1] * 256

    ld_next = prep_load(C, wf, 16, col0(0), 256)
    for pi, (isu, pidx) in enumerate(order):
        wp, bp = Wp[pi % 2], Bp[pi % 2]
        prep_compute(C, ld_next, 16, 256, 0, wp, wp.t, bp, bp.t)
        if pi + 1 < len(order):
            ld_next = prep_load(C, wf, 16, col0(pi + 1), 256)
        for tbi in range(4):
            tok0 = tbi * 512
            for cc in range(2):
                pb = nextps(C)
                proj(C, pb, 128, 512, wp, wp.t, cc * 128, 16, hT1, tok0)
                ch = pidx * 2 + cc
                if isu:
                    o = uo[n % 2]
                    s4 = us[n % 2]
                    n += 1
                    P.op("act", lambda e, pb=pb, o=o, bp=bp, cc=cc: e.activation(o.t, pb.t[:, :], AF.Identity, bias=bp.t[:, cc:cc + 1]),
                         reads=[pb, bp], writes=[o])
                    for j in range(4):
                        if j % 2 == 0:
                            P.op("dve", lambda e, o=o, s4=s4, j=j: e.tensor_scalar_mul(s4.t[:, j, :], o.t, sel.t[:, j:j + 1]),
                                 reads=[o, sel], writes=[s4], acc=True)
                        else:
                            P.op("act", lambda e, o=o, s4=s4, j=j: e.activation(s4.t[:, j, :], o.t, AF.Copy, scale=sel.t[:, j:j + 1]),
                                 reads=[o, sel], writes=[s4], acc=True)
                    dest, q = pidx // 4, pidx % 4
                    row0 = dest * 256 + cc * 128
                    dst_ap = U_in[q].t[row0:row0 + 128, :].rearrange("p (j t) -> p j t", j=4)[:, :, tok0:tok0 + 512]
                    P.dma("sp", dst_ap, s4.t, src=s4, dst=U_in[q])
                else:
                    o = go[n % 2]
                    n += 1
                    P.op("act", lambda e, pb=pb, o=o, bp=bp, cc=cc: e.activation(o.t, pb.t[:, :], AF.Silu, bias=bp.t[:, cc:cc + 1]),
                         reads=[pb, bp], writes=[o])
                    chp = ((ch % 8) // 2) * 8 + (ch // 8) * 2 + ch % 2
                    P.dma("sp", Gl.t[chp, :, tok0:tok0 + 512], o.t, src=o, dst=Gl)
        if isu and pidx // 4 == 3:
            q = pidx % 4
            P.op("pool", lambda e, q=q: e.collective_compute("ReduceScatter", ALU.add, replica_groups=RS_GROUPS, ins=[U_in[q].t], outs=[U_out[q].t]),
                 reads=[U_in[q]], writes=[U_out[q]], dma_dst=U_out[q], dma_inc=1)

    P.phase = "B2"
    ar.reset()
    cn = ar.alloc("cn", [2, 2, 256], BF16)
    w64 = ar.alloc("w64", [128], BF16)
    Mt = ar.alloc("Mt", [64, 2, 128], BF16)
    P.dma("sp", cn.t, cn_d, dst=cn)
    P.dma("sp", w64.t, w64_d, dst=w64)
    P.dma("sp", Mt.t, M_d, dst=Mt)
    uT = ar.alloc("uT", [2, 8192], BF16)
    fT = ar.alloc("fT", [8192], BF16)
    z = ar.alloc("z", [128, 128], BF16)
    Y = ar.alloc("Y", [128, 128], BF16)
    fs = [ar.alloc(f"fs{i}", [2048], BF16) for i in range(2)]
    nfs = 0
    for g in range(4):
        P.dma("sp", uT.t, U_out[g].t.rearrange("(c p) t -> p c t", p=128), src=U_out[g], dst=uT)
        uv = uT.t.rearrange("p c (l1 l2) -> p c l2 l1", l2=128)
        for kh in range(2):
            ks = slice(kh * 128, (kh + 1) * 128)
            for l2p in range(32):
                pb = nextps(C)
                for q in range(4):
                    l2 = l2p * 4 + q
                    for ri in range(2):
                        for cc in range(2):
                            P.mm(pb.t[ri * 64:(ri + 1) * 64, q * 128:(q + 1) * 128], uv[:, cc, l2, :], cn.t[:, cc, ri, ks],
                                 cc == 0, cc == 1, [uT, cn], pb)
                dst = z.t[:, l2p * 4:(l2p + 1) * 4, :]
                src = pb.t[:, :].rearrange("p (a b) -> p a b", b=128)
                if l2p % 2 == 0:
                    P.op("act", lambda e, dst=dst, src=src: e.activation(dst, src, AF.Copy), reads=[pb], writes=[z], acc=True)
                else:
                    P.op("dve", lambda e, dst=dst, src=src: e.tensor_copy(dst, src), reads=[pb], writes=[z], acc=True)
            for k3p in range(32):
                pb = nextps(C)
                for q in range(4):
                    k3 = k3p * 4 + q
                    P.mm(pb.t[:, q * 128:(q + 1) * 128], z.t[:, :, k3], w64.t, True, True, [z, w64], pb)
                dst = Y.t[:, k3p * 4:(k3p + 1) * 4, :]
                src = pb.t[:, :].rearrange("p (a b) -> p a b", b=128)
                if k3p % 2 == 0:
                    P.op("act", lambda e, dst=dst, src=src: e.activation(dst, src, AF.Copy), reads=[pb], writes=[Y], acc=True)
                else:
                    P.op("dve", lambda e, dst=dst, src=src: e.tensor_copy(dst, src), reads=[pb], writes=[Y], acc=True)
            fv = fT.t.rearrange("p (k2 k1) -> p k1 k2", k1=64)
            for k1p in range(16):
                pb = nextps(C)
                for q in range(4):
                    k1 = k1p * 4 + q
                    for ri in range(2):
                        P.mm(pb.t[:, q * 128:(q + 1) * 128], Y.t[:, :, ri * 64 + k1], Mt.t[:, k1, ri, :], ri == 0, ri == 1, [Y, Mt], pb)
                fsl = fv[:, k1p * 4:(k1p + 1) * 4, :]
                src = pb.t[:, :].rearrange("p (a b) -> p a b", b=128)
                if k1p % 2 == 0:
                    P.op("act", lambda e, fsl=fsl, src=src: e.activation(fsl, src, AF.Copy, scale=FSCALE), reads=[pb], writes=[fT], acc=True)
                else:
                    P.op("dve", lambda e, fsl=fsl, src=src: e.tensor_scalar_mul(fsl, src, FSCALE), reads=[pb], writes=[fT], acc=True)
            for d in range(4):
                for j in range(4):
                    f_ = fs[nfs % 2]
                    nfs += 1
                    if j % 2 == 0:
                        P.op("dve", lambda e, f_=f_, d=d, j=j: e.tensor_scalar_mul(f_.t, fT.t[:, d * 2048:(d + 1) * 2048], sel.t[:, j:j + 1]),
                             reads=[fT, sel], writes=[f_])
                    else:
                        P.op("act", lambda e, f_=f_, d=d, j=j: e.activation(f_.t, fT.t[:, d * 2048:(d + 1) * 2048], AF.Copy, scale=sel.t[:, j:j + 1]),
                             reads=[fT, sel], writes=[f_])
                    r0 = d * 1024 + j * 256 + kh * 128
                    P.dma("sp", F_in[g].t[r0:r0 + 128, :], f_.t, src=f_, dst=F_in[g])
        P.op("pool", lambda e, g=g: e.collective_compute("ReduceScatter", ALU.add, replica_groups=RS_GROUPS, ins=[F_in[g].t], outs=[F_out[g].t]),
             reads=[F_in[g]], writes=[F_out[g]], dma_dst=F_out[g], dma_inc=1)

    P.phase = "C"
    ar.reset()
    Wo = ar.alloc("fWo", [32, 1024], BF16)
    gb2 = ar.alloc("fgb2", [2048], F32)
    lnbc = ar.alloc("flnbc", [2, 2048], F32)
    Fys = [ar.alloc(f"fFy{i}", [32, 128], BF16) for i in range(2)]
    Ggs = [ar.alloc(f"fGg{i}", [32, 128], BF16) for i in range(2)]
    Gys = [ar.alloc(f"fGy{i}", [32, 128], BF16) for i in range(2)]
    tms = [ar.alloc(f"ftm{i}", [1024], F32) for i in range(2)]
    xts = [ar.alloc(f"fxt{i}", [2048], F32) for i in range(1)]
    tmp = ar.alloc("fotmp", [2048], F32)
    st = ar.alloc("fost", [4, 6], F32)
    mv = ar.alloc("fomv", [2], F32)
    rs = ar.alloc("fors", [1], F32)
    P.dma("sp", gb2.t, Gbc1.t, src=Gbc1, dst=gb2)
    P.dma("sp", lnbc.t[:, 0, :], lngb1[0:1, :].partition_broadcast(128), dst=lnbc)
    P.dma("sp", lnbc.t[:, 1, :], lngb1[1:2, :].partition_broadcast(128), dst=lnbc)
    n = 0
    for nh in range(2):
        for kh in range(2):
            for j in range(4):
                c0 = nh * 1024 + j * 256
                prep_panel(C, w_out_f[kh * 2048:(kh + 1) * 2048, :], 16, c0, 256, None, Wo,
                           Wo.t[:, kh * 16:(kh + 1) * 16, j * 256:(j + 1) * 256])
        def load_tile(k):
            Fy, Gg = Fys[k % 2], Ggs[k % 2]
            t = k % 16
            for g in range(4):
                P.dma("sp", Fy.t[:, g * 8:(g + 1) * 8, :], F_out[g].t[:, t * 128:(t + 1) * 128].rearrange("(c p) t -> p c t", p=128),
                      src=F_out[g], dst=Fy)
            P.dma("sp", Gg.t, Gl.t[:, :, t * 128:(t + 1) * 128].rearrange("c p t -> p c t"), src=Gl, dst=Gg)

        if nh == 0:
            load_tile(0)
        for t in range(16):
            Fy, Gg, Gy, tm = Fys[n % 2], Ggs[n % 2], Gys[n % 2], tms[n % 2]
            n += 1
            if n < 32:
                load_tile(n)
            P.op("pool", lambda e, Fy=Fy, Gg=Gg, Gy=Gy: e.tensor_tensor(Gy.t, Fy.t, Gg.t, ALU.mult), reads=[Fy, Gg], writes=[Gy])
            for nb in range(2):
                pb = nextps(C)
                ns = slice(nb * 512, (nb + 1) * 512)
                gs = slice(nh * 1024 + nb * 512, nh * 1024 + (nb + 1) * 512)
                for kp in range(32):
                    kc = ((kp % 8) // 2) * 8 + (kp // 8) * 2 + kp % 2
                    P.mm(pb.t[:, :], Gy.t[:, kp, :], Wo.t[:, kc, ns], kp == 0, kp == 31, [Gy, Wo], pb)
                P.op("dve", lambda e, pb=pb, ns=ns, gs=gs, tm=tm: e.tensor_tensor(tm.t[:, ns], pb.t[:, :], gb2.t[:, gs], ALU.mult),
                     reads=[pb, gb2], writes=[tm], acc=True)
            P.dma("sp", Td.t[t, :, nh * 1024:(nh + 1) * 1024], tm.t, src=tm, dst=Td)
    ob = P.view("out_b", outp)
    for t in range(16):
        xt = xts[0]
        tm = tmp
        P.dma("sp", xt.t, x1b.t[t * 128:(t + 1) * 128, :], src=x1b, dst=xt)
        P.dma("sp", tm.t, Td.t[t], src=Td, dst=tm)
        P.op("dve", lambda e, xt=xt, tm=tm: e.scalar_tensor_tensor(tm.t, xt.t, ALPHA, tm.t, ALU.mult, ALU.add), reads=[xt, tm], writes=[tm])
        for c in range(4):
            P.op("dve", lambda e, c=c, tm=tm: e.bn_stats(st.t[:, c, :], tm.t[:, c * 512:(c + 1) * 512]), reads=[tm], writes=[st], acc=True)
        P.op("dve", lambda e: e.bn_aggr(mv.t, st.t), reads=[st], writes=[mv])
        P.op("act", lambda e: e.activation(rs.t, mv.t[:, 1:2], AF.Sqrt, bias=C.eps.t[:, 0:1], scale=1.0), reads=[mv, C.eps], writes=[rs])
        P.op("dve", lambda e: e.reciprocal(rs.t, rs.t), reads=[rs], writes=[rs])
        P.op("dve", lambda e, tm=tm: e.tensor_scalar(tm.t, tm.t, mv.t[:, 0:1], rs.t[:, 0:1], ALU.subtract, ALU.mult), reads=[tm, mv, rs], writes=[tm])
        P.op("pool", lambda e, tm=tm: e.tensor_tensor(tm.t, tm.t, lnbc.t[:, 0, :], ALU.mult), reads=[tm, lnbc], writes=[tm])
        P.op("dve", lambda e, tm=tm, xt=xt: e.tensor_tensor(xt.t, tm.t, lnbc.t[:, 1, :], ALU.add), reads=[tm, lnbc], writes=[xt])
        P.dma("sp", outp[t * 128:(t + 1) * 128, :], xt.t, src=xt, dbuf=ob)
    return ob


def fused_inputs(inp):
    maps = stage_a_inputs(inp)
    cn, w64, M = fft_consts()
    for core in range(8):
        b, r = core // 4, core % 4
        cv = np.zeros((128, 16, 2), np.float32)
        cv[:, :, 0] = _pk(inp["c"][b], 16)
        cv[:, :, 1] = cv[:, :, 0]
        sel = np.zeros((128, 4), np.float32)
        sel[:, r] = 1.0
        maps[core].update({
            "cvec1": cv.reshape(128, 32), "ada_w1": inp["ada_w"][1],
            "ada_b1": np.stack([inp["ada_b"][1], inp["ada_b"][1]]),
            "w_in_f": inp["w_in_fourier"][0], "w_out_f": inp["w_out_fourier"][0],
            "lngb1": np.stack([inp["ln_g"][1], inp["ln_b"][1]]), "sel": sel,
            "cn": cn, "w64": w64, "Mtw": M,
        })
    return maps


def kernel_fused(**inp):
    inp = {k: np.asarray(v) for k, v in inp.items()}
    nc = build_stage_a(fused=True)
    res = run_bass_kernel_spmd(nc, fused_inputs(inp), core_ids=list(range(8)))
    out = np.zeros((2, 8192, 2048), np.float32)
    for core in range(8):
        b, r = core // 4, core % 4
        out[b, r * 2048:(r + 1) * 2048] = res.results[core]["out"]
    return out


def _rope_tables(rot_dim, seq=8192, grid_w=64):
    t = np.arange(seq)
    r = (t // grid_w).astype(np.float32)
    col = (t % grid_w).astype(np.float32)
    nf = rot_dim // 4
    inv = (np.float32(10000.0) ** (-(np.arange(nf, dtype=np.float32)) / np.float32(nf))).astype(np.float32)
    ang = np.concatenate([r[:, None] * inv[None, :], col[:, None] * inv[None, :]], axis=-1).astype(np.float32)
    cos = np.cos(ang).astype(np.float32)
    sin = np.sin(ang).astype(np.float32)
    cosT = np.repeat(cos, 2, axis=1).T
    sgn = np.tile(np.array([-1.0, 1.0], np.float32), rot_dim // 2)
    sinT = (np.repeat(sin, 2, axis=1) * sgn[None, :]).T
    return np.ascontiguousarray(cosT), np.ascontiguousarray(sinT)


def _consts():
    c = np.zeros((128, 3, 128), np.float32)
    c[:, 0, :] = np.eye(128, dtype=np.float32)
    c[:, 1, :] = 1.0
    idx = np.arange(128)
    c[idx, 2, idx ^ 1] = 1.0
    return c


def _pk(v, kc):
    return np.ascontiguousarray(np.asarray(v, np.float32).reshape(kc, 128).T)


def stage_a_inputs(inp):
    cB, sB = _rope_tables(128)
    cA, sA = _rope_tables(64)
    tabB = np.zeros((128, 2, NKV), np.float32)
    tabB[:, 0, :256] = 1.0
    tabB[:, 0, 256:] = cB
    tabB[:, 1, 256:] = sB
    tabA = np.zeros((64, 2, NKV), np.float32)
    tabA[:, 0, :256] = 1.0
    tabA[:, 0, 256:] = cA
    tabA[:, 1, 256:] = sA
    gains = np.zeros((128, 12), np.float32)
    gains[:, 0:6] = _pk(inp["q_lora_norm"][0], 6)
    gains[:, 6:10] = _pk(inp["kv_lora_norm"][0], 4)
    gains[:, 10] = inp["q_norm_b"][0]
    gains[:, 11] = inp["k_norm_b"][0]
    consts = _consts()
    maps = []
    for core in range(8):
        b, qr = core // 4, core % 4
        t0 = qr * 2048
        cv = np.zeros((128, 16, 2), np.float32)
        cv[:, :, 0] = _pk(inp["c"][b], 16)
        cv[:, :, 1] = _pk(inp["c_ctx"], 16)
        maps.append({
            "xkv": np.ascontiguousarray(np.concatenate([inp["ctx"][b], inp["x"][b]], axis=0)),
            "xq": np.ascontiguousarray(inp["x"][b, t0:t0 + 2048]),
            "cvec": cv.reshape(128, 32),
            "ada_w": inp["ada_w"][0], "ada_b": np.stack([inp["ada_b"][0], inp["ada_b"][0]]),
            "w_in": inp["w_in_attn"][0], "wq_b": inp["wq_b"][0], "wkv_b": inp["wkv_b"][0], "w_out": inp["w_out_attn"][0],
            "gains": gains, "lngb": np.stack([inp["ln_g"][0], inp["ln_b"][0]]), "consts": consts,
            "ropeB_kv": tabB, "ropeA_kv": tabA,
            "ropeB_q": np.ascontiguousarray(tabB[:, :, 256 + t0:256 + t0 + 2048]),
            "ropeA_q": np.ascontiguousarray(tabA[:, :, 256 + t0:256 + t0 + 2048]),
        })
    return maps


def run_stage_a(inp):
    nc = build_stage_a()
    res = run_bass_kernel_spmd(nc, stage_a_inputs(inp), core_ids=list(range(8)))
    x1 = np.zeros((2, 8192, 2048), np.float32)
    for core in range(8):
        b, qr = core // 4, core % 4
        x1[b, qr * 2048:(qr + 1) * 2048] = res.results[core]["x1"]
    return x1


FSCALE = float((8192.0 * 256.0) ** -0.5)


def fft_consts():
    import ml_dtypes
    bf = ml_dtypes.bfloat16
    c = np.arange(256)[:, None].astype(np.float64)
    k3 = np.arange(256)[None, :].astype(np.float64)
    a = 2 * np.pi * c * k3 / 256
    cn = np.zeros((128, 2, 2, 256), np.float64)
    for cc in range(2):
        cn[:, cc, 0, :] = np.cos(a[cc * 128:(cc + 1) * 128])
        cn[:, cc, 1, :] = -np.sin(a[cc * 128:(cc + 1) * 128])
    l1 = np.arange(64)[:, None].astype(np.float64)
    k1 = np.arange(64)[None, :].astype(np.float64)
    th = 2 * np.pi * l1 * k1 / 64
    wr, wi = np.cos(th), -np.sin(th)
    w64 = np.zeros((128, 128), np.float64)
    w64[0:64, 0:64] = wr
    w64[64:128, 0:64] = -wi
    w64[0:64, 64:128] = wi
    w64[64:128, 64:128] = wr
    l2 = np.arange(128)[:, None, None].astype(np.float64)
    kk = (np.arange(64)[None, :, None] + 64 * np.arange(128)[None, None, :]).astype(np.float64)
    ph = 2 * np.pi * ((l2 * kk) % 8192) / 8192
    M = np.stack([np.cos(ph), np.sin(ph)], axis=2)
    return cn.astype(np.float32).astype(bf), w64.astype(np.float32).astype(bf), M.astype(np.float32).astype(bf)


def build_stage_b():
    nc = bass.Bass("TRN2", target_bir_lowering=False)

    def din(name, shape, dt=F32):
        return nc.dram_tensor(name, list(shape), dt, kind="ExternalInput").ap()

    x1f = din("x1f", [8192, 2048])
    cvec = din("cvec", [128, 32])
    ada_w = din("ada_w", [2048, 6144])
    ada_b = din("ada_b", [2, 6144])
    wuf = din("wu", [2048, 1024])
    wgf = din("wg", [2048, 1024])
    consts = din("consts", [128, 3, 128])
    cn_d = din("cn", [128, 2, 2, 256], BF16)
    w64_d = din("w64", [128, 128], BF16)
    M_d = din("Mtw", [128, 64, 2, 128], BF16)
    yT = nc.dram_tensor("yT", [1024, 8192], BF16, kind="ExternalOutput").ap()
    gbc_o = nc.dram_tensor("gbc", [128, 2048], F32, kind="ExternalOutput").ap()

    with ExitStack() as es:
        C = setup_common(nc, es, consts)
        P, ar = C.P, C.ar
        Ud = P.dram("Ud", [8, 128, 8192], BF16)
        Gd = P.dram("Gd", [8, 128, 8192], BF16)
        gbc = ar.alloc("gbc_tmp", [2048], F32)
        emit_modulation(C, cvec, ada_w, ada_b[:, :], gbc, 0)
        gbc_b = P.view("gbc_out", gbc_o)
        P.dma("sp", gbc_o, gbc.t, src=gbc, dbuf=gbc_b)
        ar.reset()
        W = Ctx()
        Wu = ar.alloc("Wu", [16, 1024], BF16)
        Wg = ar.alloc("Wg", [16, 1024], BF16)
        Bu = ar.alloc("Bu", [8], F32)
        Bg = ar.alloc("Bg", [8], F32)
        ln_alloc(C, W)
        W.lnc = 0
        hTs = [ar.alloc(f"hT{i}", [16, 512], BF16) for i in range(2)]
        uo = [ar.alloc(f"uo{i}", [512], BF16) for i in range(4)]
        for j in range(4):
            prep_panel(C, wuf, 16, j * 256, 256, 0, Wu, Wu.t[:, :, j * 256:(j + 1) * 256], Bu, Bu.t[:, 2 * j:2 * j + 2])
            prep_panel(C, wgf, 16, j * 256, 256, 0, Wg, Wg.t[:, :, j * 256:(j + 1) * 256], Bg, Bg.t[:, 2 * j:2 * j + 2])
        nuo = 0
        for tb in range(16):
            hT = hTs[tb % 2]
            ln_transpose(C, W, x1f, tb * 512, 4, hT)
            for j in range(16):
                pb = nextps(C)
                isu = j < 8
                jj = j % 8
                proj(C, pb, 128, 512, Wu if isu else Wg, (Wu if isu else Wg).t, jj * 128, 16, hT, 0)
                o = uo[nuo % 4]; nuo += 1
                bb = Bu if isu else Bg
                P.op("act", lambda e, pb=pb, o=o, bb=bb, jj=jj, isu=isu: e.activation(o.t, pb.t[:, :], AF.Identity if isu else AF.Silu, bias=bb.t[:, jj:jj + 1]),
                     reads=[pb, bb], writes=[o])
                dd = Ud if isu else Gd
                P.dma("sp", dd.t[jj, :, tb * 512:(tb + 1) * 512], o.t, src=o, dst=dd)
        ar.reset()
        cn = ar.alloc("cn", [2, 2, 256], BF16)
        w64 = ar.alloc("w64", [128], BF16)
        Mt = ar.alloc("Mt", [64, 2, 128], BF16)
        P.dma("sp", cn.t, cn_d, dst=cn)
        P.dma("sp", w64.t, w64_d, dst=w64)
        P.dma("sp", Mt.t, M_d, dst=Mt)
        uT = ar.alloc("uT", [2, 8192], BF16)
        gT = ar.alloc("gT", [2, 8192], BF16)
        z = ar.alloc("z", [128, 128], BF16)
        Y = ar.alloc("Y", [128, 128], BF16)
        yTb = P.view("yT_out", yT)
        for g in range(4):
            P.dma("sp", uT.t, Ud.t[2 * g:2 * g + 2].rearrange("c p t -> p c t"), src=Ud, dst=uT)
            P.dma("sp", gT.t, Gd.t[2 * g:2 * g + 2].rearrange("c p t -> p c t"), src=Gd, dst=gT)
            uv = uT.t.rearrange("p c (l1 l2) -> p c l2 l1", l2=128)
            for kh in range(2):
                ks = slice(kh * 128, (kh + 1) * 128)
                for l2p in range(32):
                    pb = nextps(C)
                    for q in range(4):
                        l2 = l2p * 4 + q
                        for ri in range(2):
                            for cc in range(2):
                                P.mm(pb.t[ri * 64:(ri + 1) * 64, q * 128:(q + 1) * 128], uv[:, cc, l2, :], cn.t[:, cc, ri, ks],
                                     cc == 0, cc == 1, [uT, cn], pb)
                    dst = z.t[:, l2p * 4:(l2p + 1) * 4, :]
                    src = pb.t[:, :].rearrange("p (a b) -> p a b", b=128)
                    if l2p % 2 == 0:
                        P.op("act", lambda e, dst=dst, src=src: e.activation(dst, src, AF.Copy), reads=[pb], writes=[z], acc=True)
                    else:
                        P.op("dve", lambda e, dst=dst, src=src: e.tensor_copy(dst, src), reads=[pb], writes=[z], acc=True)
                for k3p in range(32):
                    pb = nextps(C)
                    for q in range(4):
                        k3 = k3p * 4 + q
                        P.mm(pb.t[:, q * 128:(q + 1) * 128], z.t[:, :, k3], w64.t, True, True, [z, w64], pb)
                    dst = Y.t[:, k3p * 4:(k3p + 1) * 4, :]
                    src = pb.t[:, :].rearrange("p (a b) -> p a b", b=128)
                    if k3p % 2 == 0:
                        P.op("act", lambda e, dst=dst, src=src: e.activation(dst, src, AF.Copy), reads=[pb], writes=[Y], acc=True)
                    else:
                        P.op("dve", lambda e, dst=dst, src=src: e.tensor_copy(dst, src), reads=[pb], writes=[Y], acc=True)
                gv = gT.t[:, kh, :].rearrange("p (k2 k1) -> p k1 k2", k1=64)
                for k1p in range(16):
                    pb = nextps(C)
                    for q in range(4):
                        k1 = k1p * 4 + q
                        for ri in range(2):
                            P.mm(pb.t[:, q * 128:(q + 1) * 128], Y.t[:, :, ri * 64 + k1], Mt.t[:, k1, ri, :], ri == 0, ri == 1, [Y, Mt], pb)
                    gsl = gv[:, k1p * 4:(k1p + 1) * 4, :]
                    src = pb.t[:, :].rearrange("p (a b) -> p a b", b=128)
                    P.op("dve", lambda e, gsl=gsl, src=src: e.scalar_tensor_tensor(gsl, src, FSCALE, gsl, ALU.mult, ALU.mult),
                         reads=[pb, gT], writes=[gT], acc=True)
            P.dma("sp", yT[g * 256:(g + 1) * 256, :].rearrange("(c p) t -> p c t", p=128), gT.t, src=gT, dbuf=yTb)
        P.fence("sp", [yTb, gbc_b])
        P.emit()
    return nc


def stage_b_inputs(inp, x1):
    cn, w64, M = fft_consts()
    consts = _consts()
    maps = []
    wf = inp["w_in_fourier"][0]
    for core in range(8):
        b, cq = core // 4, core % 4
        cv = np.zeros((128, 16, 2), np.float32)
        cv[:, :, 0] = _pk(inp["c"][b], 16)
        cv[:, :, 1] = cv[:, :, 0]
        maps.append({
            "x1f": np.ascontiguousarray(x1[b]), "cvec": cv.reshape(128, 32),
            "ada_w": inp["ada_w"][1], "ada_b": np.stack([inp["ada_b"][1], inp["ada_b"][1]]),
            "wu": np.ascontiguousarray(wf[:, cq * 1024:(cq + 1) * 1024]),
            "wg": np.ascontiguousarray(wf[:, 4096 + cq * 1024:4096 + (cq + 1) * 1024]),
            "consts": consts, "cn": cn, "w64": w64, "Mtw": M,
        })
    return maps


def run_stage_b(inp, x1):
    nc = build_stage_b()
    res = run_bass_kernel_spmd(nc, stage_b_inputs(inp, x1), core_ids=list(range(8)))
    yT = [np.concatenate([res.results[b * 4 + cq]["yT"] for cq in range(4)], axis=0) for b in range(2)]
    gbc = [res.results[b * 4]["gbc"] for b in range(2)]
    return yT, gbc


def build_stage_c():
    nc = bass.Bass("TRN2", target_bir_lowering=False)

    def din(name, shape, dt=F32):
        return nc.dram_tensor(name, list(shape), dt, kind="ExternalInput").ap()

    yTl = din("yTl", [32, 128, 2048], BF16)
    x1l = din("x1l", [2048, 2048])
    w_out = din("w_out", [4096, 2048])
    gbc_d = din("gbc", [128, 2048])
    lngb = din("lngb", [2, 2048])
    consts = din("consts", [128, 3, 128])
    outp = nc.dram_tensor("out", [2048, 2048], F32, kind="ExternalOutput").ap()

    with ExitStack() as es:
        C = setup_common(nc, es, consts)
        P, ar = C.P, C.ar
        Td = P.dram("Td", [16, 128, 2048], F32)
        Wo = ar.alloc("Wo", [32, 1024], BF16)
        gb2 = ar.alloc("gb2", [2048], F32)
        lnbc = ar.alloc("lnbc", [2, 2048], F32)
        Gys = [ar.alloc(f"Gy{i}", [32, 128], BF16) for i in range(2)]
        tms = [ar.alloc(f"tm{i}", [1024], F32) for i in range(2)]
        xts = [ar.alloc(f"oxt{i}", [2048], F32) for i in range(2)]
        tmp = ar.alloc("otmp", [2048], F32)
        st = ar.alloc("ost", [4, 6], F32)
        mv = ar.alloc("omv", [2], F32)
        rs = ar.alloc("ors", [1], F32)
        P.dma("sp", gb2.t, gbc_d, dst=gb2)
        P.dma("sp", lnbc.t[:, 0, :], lngb[0:1, :].partition_broadcast(128), dst=lnbc)
        P.dma("sp", lnbc.t[:, 1, :], lngb[1:2, :].partition_broadcast(128), dst=lnbc)
        n = 0
        for nh in range(2):
            for kh in range(2):
                for j in range(4):
                    c0 = nh * 1024 + j * 256
                    prep_panel(C, w_out[kh * 2048:(kh + 1) * 2048, :], 16, c0, 256, None, Wo,
                               Wo.t[:, kh * 16:(kh + 1) * 16, j * 256:(j + 1) * 256])
            for t in range(16):
                Gy = Gys[n % 2]
                tm = tms[n % 2]
                n += 1
                P.dma("sp", Gy.t, yTl[:, :, t * 128:(t + 1) * 128].rearrange("c p t -> p c t"), dst=Gy)
                for nb in range(2):
                    pb = nextps(C)
                    ns = slice(nb * 512, (nb + 1) * 512)
                    gs = slice(nh * 1024 + nb * 512, nh * 1024 + (nb + 1) * 512)
                    for kc in range(32):
                        P.mm(pb.t[:, :], Gy.t[:, kc, :], Wo.t[:, kc, ns], kc == 0, kc == 31, [Gy, Wo], pb)
                    P.op("dve", lambda e, pb=pb, ns=ns, gs=gs, tm=tm: e.tensor_tensor(tm.t[:, ns], pb.t[:, :], gb2.t[:, gs], ALU.mult),
                         reads=[pb, gb2], writes=[tm], acc=True)
                P.dma("sp", Td.t[t, :, nh * 1024:(nh + 1) * 1024], tm.t, src=tm, dst=Td)
        ob = P.view("out_b", outp)
        for t in range(16):
            xt = xts[t % 2]
            tm = tmp
            P.dma("sp", xt.t, x1l[t * 128:(t + 1) * 128, :], dst=xt)
            P.dma("sp", tm.t, Td.t[t], src=Td, dst=tm)
            P.op("dve", lambda e, xt=xt, tm=tm: e.scalar_tensor_tensor(tm.t, xt.t, ALPHA, tm.t, ALU.mult, ALU.add), reads=[xt, tm], writes=[tm])
            for c in range(4):
                P.op("dve", lambda e, c=c, tm=tm: e.bn_stats(st.t[:, c, :], tm.t[:, c * 512:(c + 1) * 512]), reads=[tm], writes=[st], acc=True)
            P.op("dve", lambda e: e.bn_aggr(mv.t, st.t), reads=[st], writes=[mv])
            P.op("act", lambda e: e.activation(rs.t, mv.t[:, 1:2], AF.Sqrt, bias=C.eps.t[:, 0:1], scale=1.0), reads=[mv, C.eps], writes=[rs])
            P.op("dve", lambda e: e.reciprocal(rs.t, rs.t), reads=[rs], writes=[rs])
            P.op("dve", lambda e, tm=tm: e.tensor_scalar(tm.t, tm.t, mv.t[:, 0:1], rs.t[:, 0:1], ALU.subtract, ALU.mult), reads=[tm, mv, rs], writes=[tm])
            P.op("pool", lambda e, tm=tm: e.tensor_tensor(tm.t, tm.t, lnbc.t[:, 0, :], ALU.mult), reads=[tm, lnbc], writes=[tm])
            P.op("dve", lambda e, tm=tm, xt=xt: e.tensor_tensor(xt.t, tm.t, lnbc.t[:, 1, :], ALU.add), reads=[tm, lnbc], writes=[xt])
            P.dma("sp", outp[t * 128:(t + 1) * 128, :], xt.t, src=xt, dbuf=ob)
        P.fence("sp", [ob])
        P.emit()
    return nc


def run_stage_c(inp, x1, yT, gbc):
    nc = build_stage_c()
    consts = _consts()
    maps = []
    for core in range(8):
        b, qr = core // 4, core % 4
        t0 = qr * 2048
        maps.append({
            "yTl": np.ascontiguousarray(yT[b][:, t0:t0 + 2048]).reshape(32, 128, 2048),
            "x1l": np.ascontiguousarray(x1[b, t0:t0 + 2048]),
            "w_out": inp["w_out_fourier"][0], "gbc": gbc[b],
            "lngb": np.stack([inp["ln_g"][1], inp["ln_b"][1]]), "consts": consts,
        })
    res = run_bass_kernel_spmd(nc, maps, core_ids=list(range(8)))
    out = np.zeros((2, 8192, 2048), np.float32)
    for core in range(8):
        b, qr = core // 4, core % 4
        out[b, qr * 2048:(qr + 1) * 2048] = res.results[core]["out"]
    return out


def kernel_unfused(**inp):
    inp = {k: np.asarray(v) for k, v in inp.items()}
    x1 = run_stage_a(inp)
    yT, gbc = run_stage_b(inp, x1)
    return run_stage_c(inp, x1, yT, gbc)


def kernel(**inp):
    return kernel_fused(**inp)
```
